# Optimizing a Trainium2 kernel written in Bass

```python
import math
import jax, jax.numpy as jnp
from jax import lax
import numpy as np

D_MODEL = 1024
BATCH = 16
SEQ = 256
DEPTH = 2
DEC_BATCH = 4
DEC_SEQ = 2048
PAST_LEN = 512

GRID_W = 64
GROUP_W = D_MODEL // 4
HEAD_DIM = 64
MLA_HEADS = GROUP_W // HEAD_DIM
MLA_NOPE = 64
MLA_ROPE = 32
MLA_VDIM = HEAD_DIM
MLA_KV_RANK = 128
MLA_QK = MLA_NOPE + MLA_ROPE
SWA_HEADS = GROUP_W // HEAD_DIM
SWA_KV_HEADS = 2
SWA_HD = HEAD_DIM
SWA_WINDOW = 128
SWA_BLOCK = 128
GDN_HEADS = GROUP_W // HEAD_DIM
GDN_DK = HEAD_DIM
GDN_DV = HEAD_DIM
GDN_CONV = 3
GDN_CHUNK = 64
HY_WIDTH = GROUP_W
HY_ORDER = 2
HY_CONV = 3
HY_BANDS = 8
HY_EMB = 1 + 2 * HY_BANDS
HY_FF = 64
D_FF = 4 * D_MODEL
Q_BLOCK = 128
ROPE_BASE = 10000.0
EPS = 1e-6
F32 = jnp.float32
IN_SIZES = (MLA_HEADS * MLA_QK, MLA_KV_RANK, MLA_ROPE,
            SWA_HEADS * SWA_HD, SWA_KV_HEADS * SWA_HD, SWA_KV_HEADS * SWA_HD,
            GDN_HEADS * (2 * GDN_DK + GDN_DV), GDN_HEADS * GDN_DV, 2 * GDN_HEADS, 2 * GDN_HEADS,
            (HY_ORDER + 1) * HY_WIDTH)
IN_COLS = sum(IN_SIZES)
SPLIT_POINTS = tuple(int(s) for s in np.cumsum(IN_SIZES)[:-1])

kernel_name = 'hybrid_diffusion_prefix_trunk'


def rms_norm(x, g):
    xf = x.astype(F32)
    y = xf * lax.rsqrt(jnp.mean(xf * xf, axis=-1, keepdims=True) + EPS)
    return (y * g.astype(F32)).astype(x.dtype)


def l2norm(x):
    xf = x.astype(F32)
    return xf * lax.rsqrt(jnp.sum(xf * xf, axis=-1, keepdims=True) + EPS)


def axial_rope_angles(row, col, dim):
    n_freq = dim // 4
    inv = ROPE_BASE ** (-jnp.arange(n_freq, dtype=F32) / n_freq)
    ang = jnp.concatenate([row.astype(F32)[:, None] * inv, col.astype(F32)[:, None] * inv], axis=-1)
    return jnp.cos(ang), jnp.sin(ang)


def apply_rope(x, cs):
    cos, sin = cs
    xf = x.astype(F32)
    x1, x2 = xf[..., 0::2], xf[..., 1::2]
    c = cos[None, :, None, :]
    s = sin[None, :, None, :]
    return jnp.stack([x1 * c - x2 * s, x1 * s + x2 * c], axis=-1).reshape(x.shape).astype(x.dtype)


def softmax_with_sink(s, sink):
    m = jnp.maximum(jnp.max(s, axis=-1, keepdims=True), sink)
    p = jnp.exp(s - m)
    return p / (jnp.sum(p, axis=-1, keepdims=True) + jnp.exp(sink - m))


def dense_attention(q, k, v, scale, sink):
    B, Lq, H, dq = q.shape
    Hkv = k.shape[2]
    G = H // Hkv
    nb = Lq // Q_BLOCK
    qb = jnp.moveaxis(q.reshape(B, nb, Q_BLOCK, Hkv, G, dq), 1, 0)

    def one_block(qi):
        s = jnp.einsum('bqhgd,bshd->bhgqs', qi, k, preferred_element_type=F32) * scale
        if sink is None:
            p = jax.nn.softmax(s, axis=-1)
        else:
            p = softmax_with_sink(s, sink.astype(F32).reshape(1, Hkv, G, 1, 1))
        return jnp.einsum('bhgqs,bshd->bqhgd', p.astype(v.dtype), v)

    o = lax.map(one_block, qb)
    return jnp.moveaxis(o, 0, 1).reshape(B, Lq, H, v.shape[-1])


def banded_attention(q, k, v, k_ctx, v_ctx, sink):
    B, L, H, d = q.shape
    Hkv = k.shape[2]
    G = H // Hkv
    W = SWA_BLOCK
    nb = L // W
    pad = ((0, 0), (W, W), (0, 0), (0, 0))
    kp = jnp.pad(k, pad)
    vp = jnp.pad(v, pad)
    idx = jnp.arange(nb)[:, None] * W + jnp.arange(3 * W)[None, :]
    kb = kp[:, idx]
    vb = vp[:, idx]
    qb = q.reshape(B, nb, W, Hkv, G, d)
    scale = d ** -0.5
    s_loc = jnp.einsum('bnqhgd,bnshd->bnhgqs', qb, kb, preferred_element_type=F32) * scale
    qpos = jnp.arange(L).reshape(nb, W)
    kpos = idx - W
    valid = (kpos[:, None, :] >= 0) & (kpos[:, None, :] < L) & (jnp.abs(qpos[:, :, None] - kpos[:, None, :]) <= SWA_WINDOW)
    s_loc = jnp.where(valid[None, :, None, None], s_loc, -jnp.inf)
    s_ctx = jnp.einsum('bnqhgd,bshd->bnhgqs', qb, k_ctx, preferred_element_type=F32) * scale
    s = jnp.concatenate([s_loc, s_ctx], axis=-1)
    p = softmax_with_sink(s, sink.astype(F32).reshape(1, 1, Hkv, G, 1, 1)).astype(v.dtype)
    o = (jnp.einsum('bnhgqs,bnshd->bnqhgd', p[..., :3 * W], vb)
         + jnp.einsum('bnhgqs,bshd->bnqhgd', p[..., 3 * W:], v_ctx))
    return o.reshape(B, L, H, d)


def short_conv(x, w):
    K, C = w.shape
    return lax.conv_general_dilated(x, w.astype(x.dtype)[:, None, :], window_strides=(1,),
                                    padding=[(K // 2, K // 2)],
                                    dimension_numbers=('NWC', 'WIO', 'NWC'),
                                    feature_group_count=C)


def gated_delta_chunked(q, k, v, g_log, beta, s0):
    B, L, H, dk = k.shape
    dv = v.shape[-1]
    C = GDN_CHUNK
    n = L // C
    f = lambda t: t.astype(F32).reshape(B, n, C, H, -1).transpose(1, 0, 3, 2, 4)
    q_, k_, v_ = f(q), f(k), f(v)
    g_ = g_log.astype(F32).reshape(B, n, C, H).transpose(1, 0, 3, 2)
    b_ = beta.astype(F32).reshape(B, n, C, H).transpose(1, 0, 3, 2)
    decay = jnp.cumsum(g_, axis=-1)
    tri = jnp.tril(jnp.ones((C, C), dtype=bool))
    tri_strict = jnp.tril(jnp.ones((C, C), dtype=bool), -1)
    diff = decay[..., :, None] - decay[..., None, :]
    gam = jnp.where(tri, jnp.exp(jnp.where(tri, diff, 0.0)), 0.0)
    kb = k_ * b_[..., None]
    a = jnp.where(tri_strict, jnp.einsum('nbhcd,nbhsd->nbhcs', kb, k_) * gam, 0.0)
    rhs = jnp.concatenate([v_ * b_[..., None], kb * jnp.exp(decay)[..., None]], axis=-1)
    sol = lax.linalg.triangular_solve(a + jnp.eye(C, dtype=F32), rhs, left_side=True, lower=True,
                                      unit_diagonal=True)
    u_v, w = sol[..., :dv], sol[..., dv:]
    attn_qk = jnp.einsum('nbhcd,nbhsd->nbhcs', q_, k_) * gam
    q_dec = q_ * jnp.exp(decay)[..., None]
    k_tail = k_ * jnp.exp(decay[..., -1:] - decay)[..., None]
    tail = jnp.exp(decay[..., -1])

    def step(S, xs):
        uv_c, w_c, aqk_c, qd_c, kt_c, tl_c = xs
        u = uv_c - jnp.einsum('bhck,bhkv->bhcv', w_c, S)
        o = jnp.einsum('bhck,bhkv->bhcv', qd_c, S) + jnp.einsum('bhcs,bhsv->bhcv', aqk_c, u)
        S = S * tl_c[..., None, None] + jnp.einsum('bhck,bhcv->bhkv', kt_c, u)
        return S, o

    s_final, o = lax.scan(step, s0.astype(F32), (u_v, w, attn_qk, q_dec, k_tail, tail))
    return o.transpose(1, 0, 3, 2, 4).reshape(B, L, H, dv), s_final


def gdn_mixer(gqkv, gz, ga, gb, P, s0_f, s0_b):
    B, L, _ = gqkv.shape
    qkv = jax.nn.silu(short_conv(gqkv, P['gdn_conv']))
    q, k, v = jnp.split(qkv, 3, axis=-1)
    q = l2norm(q.reshape(B, L, GDN_HEADS, GDN_DK)) * (GDN_DK ** -0.5)
    k = l2norm(k.reshape(B, L, GDN_HEADS, GDN_DK))
    v = v.reshape(B, L, GDN_HEADS, GDN_DV)
    a = ga.reshape(B, L, 2, GDN_HEADS).astype(F32)
    b = gb.reshape(B, L, 2, GDN_HEADS).astype(F32)
    g_log = -jnp.exp(P['gdn_a_log'].astype(F32)) * jax.nn.softplus(a + P['gdn_dt_bias'].astype(F32))
    beta = jax.nn.sigmoid(b)
    o_f, s_f = gated_delta_chunked(q, k, v, g_log[:, :, 0], beta[:, :, 0], s0_f)
    flip = lambda t: jnp.flip(t, axis=1)
    o_b, s_b = gated_delta_chunked(flip(q), flip(k), flip(v), flip(g_log[:, :, 1]), flip(beta[:, :, 1]), s0_b)
    o = o_f + flip(o_b)
    o = rms_norm(o, P['gdn_norm']) * jax.nn.silu(gz.reshape(B, L, GDN_HEADS, GDN_DV).astype(F32))
    return o.reshape(B, L, GDN_HEADS * GDN_DV), s_f, s_b


def hyena_filters(L, P):
    t = jnp.arange(L, dtype=F32)
    t01 = t / max(L - 1, 1)
    w = 2.0 * math.pi * t / L
    bands = jnp.linspace(1e-4, HY_BANDS - 1, HY_BANDS, dtype=F32)
    feats = jnp.concatenate([t01[:, None], jnp.cos(w[:, None] * bands), -jnp.sin(w[:, None] * bands)], axis=-1)
    freq = P['hy_freq'].astype(F32)
    h = jnp.sin(freq[0] * (feats @ P['hy_w1'].astype(F32) + P['hy_b1'].astype(F32)))
    h = jnp.sin(freq[1] * (h @ P['hy_w2'].astype(F32) + P['hy_b2'].astype(F32)))
    h = h @ P['hy_w3'].astype(F32)
    dist = jnp.abs(t - (L // 2)) / (L / 2)
    h = h * jnp.exp(-dist[:, None] * jnp.abs(P['hy_decay'].astype(F32)))
    return jnp.transpose(h.reshape(L, HY_ORDER, HY_WIDTH), (1, 0, 2))


def long_conv_centred(u, h, bias):
    L = u.shape[1]
    uf = jnp.fft.rfft(u.astype(F32), n=2 * L, axis=1)
    hf = jnp.fft.rfft(h, n=2 * L, axis=0)
    y = jnp.fft.irfft(uf * hf[None], n=2 * L, axis=1)[:, L // 2: L // 2 + L]
    return y + u.astype(F32) * bias.astype(F32)


def hyena_mixer(hu, P):
    L = hu.shape[1]
    u = short_conv(hu, P['hy_conv'])
    v, x1, x2 = jnp.split(u.astype(F32), 3, axis=-1)
    filt = hyena_filters(L, P)
    z = x1 * long_conv_centred(v, filt[0], P['hy_bias'][0])
    return x2 * long_conv_centred(z, filt[1], P['hy_bias'][1])


def project(h, w_in):
    return jnp.split(jnp.einsum('bld,de->ble', h, w_in), SPLIT_POINTS, axis=-1)


def mla_expand(ckv, kpe, w_ukv):
    B, L, _ = ckv.shape
    kv = jnp.einsum('blr,re->ble', ckv, w_ukv).reshape(B, L, MLA_HEADS, MLA_NOPE + MLA_VDIM)
    k = jnp.concatenate([kv[..., :MLA_NOPE], jnp.broadcast_to(kpe[:, :, None, :], (B, L, MLA_HEADS, MLA_ROPE))], axis=-1)
    return k, kv[..., MLA_NOPE:]


def merge_heads(o_a, o_b, o_c, o_d, w_out, dtype):
    B, L = o_a.shape[:2]
    o = jnp.concatenate([t.reshape(B, L, -1).astype(dtype) for t in (o_a, o_b, o_c, o_d)], axis=-1)
    return jnp.einsum('ble,ed->bld', o, w_out)


def mixer_context(h, P):
    B, L, _ = h.shape
    mq, ckv, kpe, sq, sk, sv, gqkv, gz, ga, gb, hu = project(h, P['w_in'])
    ckv = rms_norm(ckv, P['mla_kv_norm'])
    k_a, v_a = mla_expand(ckv, kpe, P['mla_w_ukv'])
    o_a = dense_attention(mq.reshape(B, L, MLA_HEADS, MLA_QK), k_a, v_a, MLA_QK ** -0.5, None)
    k_b = sk.reshape(B, L, SWA_KV_HEADS, SWA_HD)
    v_b = sv.reshape(B, L, SWA_KV_HEADS, SWA_HD)
    o_b = dense_attention(sq.reshape(B, L, SWA_HEADS, SWA_HD), k_b, v_b, SWA_HD ** -0.5, P['swa_sink'])
    zero = jnp.zeros((B, GDN_HEADS, GDN_DK, GDN_DV), F32)
    o_c, s_f, s_b = gdn_mixer(gqkv, gz, ga, gb, P, zero, zero)
    o_d = hyena_mixer(hu, P)
    out = merge_heads(o_a, o_b, o_c, o_d, P['w_out'], h.dtype)
    return out, (ckv, kpe, k_b, v_b, jnp.stack([s_f, s_b], axis=1))


def mixer_latent(h, P, rope_a, rope_b, ctx):
    B, L, _ = h.shape
    ckv_c, kpe_c, k_c, v_c, st = ctx
    mq, ckv, kpe, sq, sk, sv, gqkv, gz, ga, gb, hu = project(h, P['w_in'])
    ckv = rms_norm(ckv, P['mla_kv_norm'])
    kpe = apply_rope(kpe[:, :, None, :], rope_a)[:, :, 0, :]
    q_a = mq.reshape(B, L, MLA_HEADS, MLA_QK)
    q_a = jnp.concatenate([q_a[..., :MLA_NOPE], apply_rope(q_a[..., MLA_NOPE:], rope_a)], axis=-1)
    k_lat, v_lat = mla_expand(ckv, kpe, P['mla_w_ukv'])
    k_ctx, v_ctx = mla_expand(ckv_c, kpe_c, P['mla_w_ukv'])
    o_a = dense_attention(q_a, jnp.concatenate([k_lat, k_ctx], axis=1),
                          jnp.concatenate([v_lat, v_ctx], axis=1), MLA_QK ** -0.5, None)
    q_b = apply_rope(sq.reshape(B, L, SWA_HEADS, SWA_HD), rope_b)
    k_b = apply_rope(sk.reshape(B, L, SWA_KV_HEADS, SWA_HD), rope_b)
    v_b = sv.reshape(B, L, SWA_KV_HEADS, SWA_HD)
    o_b = banded_attention(q_b, k_b, v_b, k_c, v_c, P['swa_sink'])
    o_c, _, _ = gdn_mixer(gqkv, gz, ga, gb, P, st[:, 0], st[:, 1])
    o_d = hyena_mixer(hu, P)
    return merge_heads(o_a, o_b, o_c, o_d, P['w_out'], h.dtype), None


def trunk_layer(x, cond, P, mixer):
    mod = jnp.einsum('bd,de->be', jax.nn.silu(cond), P['w_ada']) + P['b_ada']
    sh1, sc1, gt1, sh2, sc2, gt2 = jnp.split(mod[:, None, :], 6, axis=-1)
    h = rms_norm(x, P['g_pre_mix']) * (1.0 + sc1) + sh1
    o, extra = mixer(h)
    x = x + gt1 * rms_norm(o, P['g_post_mix'])
    h = rms_norm(x, P['g_pre_mlp']) * (1.0 + sc2) + sh2
    m = jnp.einsum('blf,fd->bld', jnp.square(jax.nn.relu(jnp.einsum('bld,df->blf', h, P['mlp_w1']))), P['mlp_w2'])
    x = x + gt2 * rms_norm(m, P['g_post_mlp'])
    return x, extra


def setup_inputs(seed: int = 0) -> dict:
    key = jax.random.key(seed)
    ks = jax.random.split(key, 36)
    D = D_MODEL

    def nrm(i, shape, scale):
        return jax.random.normal(ks[i], shape, F32) * scale

    def unif(i, shape, lo, hi):
        return jax.random.uniform(ks[i], shape, F32, lo, hi)

    dt = jnp.exp(unif(22, (DEPTH, 2, GDN_HEADS), math.log(1e-3), math.log(1e-1)))
    return {
        'x_prompt': nrm(0, (BATCH, SEQ, D), 1.0),
        'x_sample': nrm(1, (DEC_BATCH, DEC_SEQ, D), 1.0),
        'cache_mla_ckv': nrm(2, (DEC_BATCH, DEPTH, PAST_LEN, MLA_KV_RANK), 1.0),
        'cache_mla_kpe': nrm(3, (DEC_BATCH, DEPTH, PAST_LEN, MLA_ROPE), 1.0),
        'cache_swa_k': nrm(4, (DEC_BATCH, DEPTH, PAST_LEN, SWA_KV_HEADS, SWA_HD), 1.0),
        'cache_swa_v': nrm(5, (DEC_BATCH, DEPTH, PAST_LEN, SWA_KV_HEADS, SWA_HD), 1.0),
        'state_gdn': nrm(6, (DEC_BATCH, DEPTH, 2, GDN_HEADS, GDN_DK, GDN_DV), 0.1),
        'c': nrm(7, (DEC_BATCH, D), 1.0),
        'c_ctx': nrm(8, (D,), 1.0),
        'w_ada': nrm(9, (DEPTH, D, 6 * D), 0.5 * D ** -0.5),
        'b_ada': nrm(10, (DEPTH, 6 * D), 0.02),
        'g_pre_mix': 1.0 + nrm(11, (DEPTH, D), 0.05),
        'g_post_mix': 1.0 + nrm(12, (DEPTH, D), 0.05),
        'g_pre_mlp': 1.0 + nrm(13, (DEPTH, D), 0.05),
        'g_post_mlp': 1.0 + nrm(14, (DEPTH, D), 0.05),
        'w_in': nrm(15, (DEPTH, D, IN_COLS), D ** -0.5),
        'w_out': nrm(16, (DEPTH, D, D), D ** -0.5),
        'mla_kv_norm': 1.0 + nrm(17, (DEPTH, MLA_KV_RANK), 0.05),
        'mla_w_ukv': nrm(18, (DEPTH, MLA_KV_RANK, MLA_HEADS * (MLA_NOPE + MLA_VDIM)), MLA_KV_RANK ** -0.5),
        'swa_sink': nrm(19, (DEPTH, SWA_HEADS), 0.5),
        'gdn_conv': nrm(20, (DEPTH, GDN_CONV, GDN_HEADS * (2 * GDN_DK + GDN_DV)), GDN_CONV ** -0.5),
        'gdn_a_log': jnp.log(unif(21, (DEPTH, 2, GDN_HEADS), 1.0, 16.0)),
        'gdn_dt_bias': dt + jnp.log(-jnp.expm1(-dt)),
        'gdn_norm': 1.0 + nrm(23, (DEPTH, GDN_DV), 0.05),
        'hy_conv': nrm(24, (DEPTH, HY_CONV, (HY_ORDER + 1) * HY_WIDTH), HY_CONV ** -0.5),
        'hy_w1': nrm(25, (DEPTH, HY_EMB, HY_FF), HY_EMB ** -0.5),
        'hy_b1': nrm(26, (DEPTH, HY_FF), 0.1),
        'hy_w2': nrm(27, (DEPTH, HY_FF, HY_FF), HY_FF ** -0.5),
        'hy_b2': nrm(28, (DEPTH, HY_FF), 0.1),
        'hy_w3': nrm(29, (DEPTH, HY_FF, HY_ORDER * HY_WIDTH), 0.05 * HY_FF ** -0.5),
        'hy_freq': 1.0 + nrm(30, (DEPTH, 2, HY_FF), 0.1),
        'hy_decay': unif(31, (DEPTH, HY_ORDER * HY_WIDTH), 3.0, 15.0),
        'hy_bias': nrm(32, (DEPTH, HY_ORDER, HY_WIDTH), 0.1),
        'mlp_w1': nrm(33, (DEPTH, D, D_FF), D ** -0.5),
        'mlp_w2': nrm(34, (DEPTH, D_FF, D), D_FF ** -0.5),
    }


def reference(x_prompt, x_sample, cache_mla_ckv, cache_mla_kpe, cache_swa_k, cache_swa_v, state_gdn,
              c, c_ctx, w_ada, b_ada, g_pre_mix, g_post_mix, g_pre_mlp, g_post_mlp, w_in, w_out,
              mla_kv_norm, mla_w_ukv, swa_sink, gdn_conv, gdn_a_log, gdn_dt_bias, gdn_norm,
              hy_conv, hy_w1, hy_b1, hy_w2, hy_b2, hy_w3, hy_freq, hy_decay, hy_bias, mlp_w1, mlp_w2):
    cond_ctx = jnp.broadcast_to(c_ctx[None, :], (x_prompt.shape[0], c_ctx.shape[0]))
    n_lat = x_sample.shape[1]
    rows = n_lat // GRID_W
    row = jnp.repeat(jnp.arange(rows), GRID_W)
    col = jnp.tile(jnp.arange(GRID_W), rows)
    rope_a = axial_rope_angles(row, col, MLA_ROPE)
    rope_b = axial_rope_angles(row, col, SWA_HD)
    xp = x_prompt
    xs = x_sample
    ckv_l, kpe_l, k_l, v_l, st_l = [], [], [], [], []
    for l in range(DEPTH):
        P = dict(w_ada=w_ada[l], b_ada=b_ada[l], g_pre_mix=g_pre_mix[l], g_post_mix=g_post_mix[l],
                 g_pre_mlp=g_pre_mlp[l], g_post_mlp=g_post_mlp[l], w_in=w_in[l], w_out=w_out[l],
                 mla_kv_norm=mla_kv_norm[l], mla_w_ukv=mla_w_ukv[l], swa_sink=swa_sink[l],
                 gdn_conv=gdn_conv[l], gdn_a_log=gdn_a_log[l], gdn_dt_bias=gdn_dt_bias[l],
                 gdn_norm=gdn_norm[l], hy_conv=hy_conv[l], hy_w1=hy_w1[l], hy_b1=hy_b1[l],
                 hy_w2=hy_w2[l], hy_b2=hy_b2[l], hy_w3=hy_w3[l], hy_freq=hy_freq[l],
                 hy_decay=hy_decay[l], hy_bias=hy_bias[l], mlp_w1=mlp_w1[l], mlp_w2=mlp_w2[l])
        xp, new = trunk_layer(xp, cond_ctx, P, lambda h: mixer_context(h, P))
        ckv_l.append(new[0])
        kpe_l.append(new[1])
        k_l.append(new[2])
        v_l.append(new[3])
        st_l.append(new[4])
        ctx = (cache_mla_ckv[:, l], cache_mla_kpe[:, l], cache_swa_k[:, l], cache_swa_v[:, l], state_gdn[:, l])
        xs, _ = trunk_layer(xs, c, P, lambda h: mixer_latent(h, P, rope_a, rope_b, ctx))
    new_mla_ckv = jnp.stack(ckv_l, axis=1)
    new_mla_kpe = jnp.stack(kpe_l, axis=1)
    new_swa_k = jnp.stack(k_l, axis=1)
    new_swa_v = jnp.stack(v_l, axis=1)
    new_gdn_state = jnp.stack(st_l, axis=1)
    return (xp, xs, new_mla_ckv, new_mla_kpe, new_swa_k, new_swa_v, new_gdn_state)
```

```python
import math
import contextlib
import numpy as np
import ml_dtypes
import concourse.bass as bass
import concourse.mybir as mybir
from concourse.bass_utils import run_bass_kernel_spmd

F32 = mybir.dt.float32
BF16 = mybir.dt.bfloat16
AF = mybir.ActivationFunctionType
ALU = mybir.AluOpType

COMPUTE = ("pe", "act", "dve", "pool")
QUEUES = ("pe", "act", "dve", "pool", "sp")
NDMASEM = 12

D = 1024
NCH = 8
TT = 2560
TB = 512
DEPTH = 2
IN_COLS = 2864
EPS = 1e-6
C_MQ, C_CKV, C_KPE, C_SQ, C_SK, C_SV, C_GQKV, C_GZ, C_GA, C_GB, C_HU = (
    0, 384, 512, 544, 800, 928, 1056, 1824, 2080, 2088, 2096)


class Buf:
    __slots__ = ("w", "r", "excl", "rg")

    def __init__(self, excl=False):
        self.w = []
        self.r = []
        self.excl = excl
        self.rg = None


class Prog:
    def __init__(self, nc):
        self.nc = nc
        self.ops = []
        self.last = {q: None for q in QUEUES}
        self.dmas_since = []
        self.force = set()

    def op(self, eng, fn, reads=(), writes=(), dma=False, pe_force=False, bg=False):
        writes = list(writes) + [b for b in reads if b.excl]
        reads = [b for b in reads if not b.excl]
        deps = set()
        for b in reads:
            deps.update(b.w)
        for b in writes:
            if dma and not b.r:
                deps.update(w for w in b.w if not self.ops[w][3])
            else:
                deps.update(b.w)
            deps.update(b.r)
        oid = len(self.ops)
        self.ops.append((eng, fn, sorted(deps), dma))
        if pe_force:
            self.force.add(oid)
        for b in reads:
            b.r.append(oid)
        for b in writes:
            if dma and not b.r and b.w and all(self.ops[w][3] for w in b.w):
                b.w = b.w + [oid]
            else:
                b.w = [oid]
            b.r = []
        if dma:
            if not bg:
                self.dmas_since.append(oid)
        else:
            self.last[eng] = oid
        return oid

    def barrier(self):
        lasts = [v for v in self.last.values() if v is not None]
        deps = sorted(set(lasts + self.dmas_since))
        self.dmas_since = []
        for q in QUEUES:
            self.ops.append((q, None, deps, False))

    def emit(self):
        nc = self.nc
        ops = self.ops
        all_dma = [i for i, o in enumerate(ops) if o[3]]
        ops.append(("sp", None, all_dma, False))
        n = len(ops)
        eng_idx = [0] * n
        cnt = {q: 0 for q in QUEUES}
        for i, o in enumerate(ops):
            cnt[o[0]] += 1
            eng_idx[i] = cnt[o[0]]
        clock = {q: ({c: 0 for c in QUEUES}, set()) for q in QUEUES}
        opclock = [None] * n
        waits = [None] * n
        needed = [False] * n
        for i, (eng, fn, deps, isdma) in enumerate(ops):
            ck, dset = clock[eng]
            w = []
            for d in deps:
                deng, dfn, _, disdma = ops[d]
                if disdma:
                    if d in dset:
                        continue
                    w.append(d)
                    needed[d] = True
                    dset.add(d)
                else:
                    if dfn is None:
                        continue
                    if deng == eng and eng == "pe" and i not in self.force:
                        continue
                    if ck[deng] >= eng_idx[d]:
                        continue
                    w.append(d)
                    needed[d] = True
                    ck[deng] = eng_idx[d]
                ock = opclock[d]
                for c in QUEUES:
                    if ock[c] > ck[c]:
                        ck[c] = ock[c]
            waits[i] = w
            opclock[i] = dict(ck)
        stack = contextlib.ExitStack()
        sems = {q: stack.enter_context(nc.semaphore("s_" + q)) for q in COMPUTE}
        dsem = {q: [stack.enter_context(nc.semaphore("d_%s_%d" % (q, j))) for j in range(NDMASEM)]
                for q in QUEUES}
        sigval = [None] * n
        ccount = {q: 0 for q in COMPUTE}
        dcount = {q: 0 for q in QUEUES}
        dslot_val = {q: [0] * NDMASEM for q in QUEUES}
        pre_wait = [None] * n
        for i, (eng, fn, deps, isdma) in enumerate(ops):
            if isdma:
                k = dcount[eng] % NDMASEM
                dcount[eng] += 1
                if dslot_val[eng][k] > 0:
                    pre_wait[i] = (dsem[eng][k], dslot_val[eng][k])
                dslot_val[eng][k] += 16
                sigval[i] = (dsem[eng][k], dslot_val[eng][k])
            elif needed[i]:
                ccount[eng] += 1
                sigval[i] = (sems[eng], ccount[eng])
        per = {q: [] for q in QUEUES}
        for i, o in enumerate(ops):
            per[o[0]].append(i)
        self.stats = {q: len(per[q]) for q in QUEUES}
        self.stats["waits"] = sum(len(w) for w in waits)
        with nc.Block() as block:
            def mk(q):
                def body(e):
                    for i in per[q]:
                        eng, fn, deps, isdma = ops[i]
                        if pre_wait[i] is not None:
                            e.wait_ge(pre_wait[i][0], pre_wait[i][1])
                        for d in waits[i]:
                            e.wait_ge(sigval[d][0], sigval[d][1])
                        if fn is None:
                            continue
                        ins = fn(e)
                        if isdma:
                            ins.then_inc(sigval[i][0], 16)
                        elif needed[i]:
                            ins.then_inc(sigval[i][0], 1)
                return body
            block.tensor(mk("pe"))
            block.scalar(mk("act"))
            block.vector(mk("dve"))
            block.gpsimd(mk("pool"))
            block.sync(mk("sp"))
        stack.close()


class StopBuild(Exception):
    pass


class Arena:
    def __init__(self, tile, nbytes):
        self.t = tile
        self.n = nbytes
        self.top = 0

    def alloc(self, shape, dt):
        n = int(np.prod(shape))
        nb = n * (2 if dt == BF16 else 4)
        off = self.top
        self.top = off + (nb + 63) // 64 * 64
        assert self.top <= self.n, ("SBUF arena overflow", self.top, self.n)
        if dt == BF16:
            v = self.t[:, off // 2: off // 2 + n]
        else:
            v = self.t[:, off // 2: off // 2 + 2 * n].bitcast(dt)
        if len(shape) == 2:
            v = v.rearrange("p (a b) -> p a b", a=shape[0])
        elif len(shape) == 3:
            v = v.rearrange("p (a b c) -> p a b c", a=shape[0], b=shape[1])
        return v


class K:
    def __init__(self, nc, P, arena, psum):
        self.nc = nc
        self.P = P
        self.A = arena
        self.psum = psum
        self.PS = [Buf(excl=True) for _ in range(8)]
        self.rr = 0

    def bank(self, i):
        return self.psum[:, i * 512:(i + 1) * 512]

    def _rg(self, stat, W):
        key = (stat.base_partition(), stat.shape[0])
        force = False
        for b in W:
            if b.excl:
                if b.rg is not None and b.rg != key:
                    force = True
                b.rg = key
        return force

    def mm(self, out, lhsT, rhs, start, stop, R, W):
        f = self._rg(lhsT, W)
        self.P.op("pe", lambda e: e.matmul(out, lhsT=lhsT, rhs=rhs, start=start, stop=stop), R, W, pe_force=f)

    def tr(self, out, in_, ident, R, W):
        f = self._rg(in_, W)
        self.P.op("pe", lambda e: e.transpose(out, in_, ident), R, W, pe_force=f)

    def act(self, out, in_, func, R, W, scale=None, bias=None):
        kw = {}
        if scale is not None:
            kw["scale"] = scale
        if bias is not None:
            kw["bias"] = bias
        self.P.op("act", lambda e: e.activation(out=out, in_=in_, func=func, **kw), R, W)

    def tt(self, out, in0, in1, op, R, W, eng="dve"):
        self.P.op(eng, lambda e: e.tensor_tensor(out=out, in0=in0, in1=in1, op=op), R, W)

    def ts(self, out, in0, s1, s2, op0, op1, R, W, eng="dve"):
        if op1 is None:
            self.P.op(eng, lambda e: e.tensor_scalar(out=out, in0=in0, scalar1=s1, scalar2=None, op0=op0), R, W)
        else:
            self.P.op(eng, lambda e: e.tensor_scalar(out=out, in0=in0, scalar1=s1, scalar2=s2, op0=op0, op1=op1), R, W)

    def stt(self, out, in0, scalar, in1, op0, op1, R, W):
        self.P.op("dve", lambda e: e.scalar_tensor_tensor(out=out, in0=in0, scalar=scalar, in1=in1, op0=op0, op1=op1), R, W)

    def copy(self, out, in_, R, W, eng=None):
        if eng is None:
            self.rr ^= 1
            eng = "dve" if self.rr else "act"
        if eng == "act":
            self.P.op("act", lambda e: e.activation(out=out, in_=in_, func=AF.Copy), R, W)
        else:
            self.P.op(eng, lambda e: e.tensor_copy(out=out, in_=in_), R, W)

    def recip(self, out, in_, R, W):
        self.P.op("dve", lambda e: e.reciprocal(out=out, in_=in_), R, W)

    def memset(self, out, val, W, eng="pool"):
        self.P.op(eng, lambda e: e.memset(out, val), (), W)

    def dma(self, q, out, in_, R, W, slow=False, bg=False):
        if bg:
            self.P.op(q, lambda e: e.dma_start(out=out, in_=in_), R, W, dma=True, bg=True)
        elif slow:
            self.P.op(q, lambda e: e.dma_start(out=out, in_=in_, allow_slow_non_contiguous=True), R, W, dma=True)
        else:
            self.P.op(q, lambda e: e.dma_start(out=out, in_=in_), R, W, dma=True)


def bc(ap, shape):
    return ap.to_broadcast(shape)


def build_program(dbg=None, stub_mixer=False, nlayers=DEPTH, stop=None):
    nc = bass.Bass("TRN2", target_bir_lowering=False)
    P = Prog(nc)
    ES = contextlib.ExitStack()

    def din(name, shape, dt=F32):
        return nc.dram_tensor(name, list(shape), dt, kind="ExternalInput").ap()

    def dout(name, shape, dt=F32):
        return nc.dram_tensor(name, list(shape), dt, kind="ExternalOutput").ap()

    def dscr(name, shape, dt=F32):
        return nc.dram_tensor(name, list(shape), dt, kind="Internal").ap()

    x_tok = din("x_tok", [TT, D])
    conds = din("conds", [2, D])
    w_ada = din("w_ada", [DEPTH, D, 6 * D])
    b_ada = din("b_ada", [DEPTH, 6 * D])
    gvec = din("gvec", [4, DEPTH, D])
    w_in = din("w_in", [DEPTH, D, IN_COLS])
    w_out = din("w_out", [DEPTH, D, D])
    mlp_w1 = din("mlp_w1", [DEPTH, D, 4 * D])
    mlp_w2 = din("mlp_w2", [DEPTH, 4 * D, D])
    y_tok = dout("y_tok", [TT, D])
    mla_kv_norm = din("mla_kv_norm", [DEPTH, 128])
    mla_w_ukv = din("mla_w_ukv", [DEPTH, 128, 512])
    swa_sink = din("swa_sink", [DEPTH, 4])
    c_ckv = din("c_ckv", [DEPTH, 512, 128])
    c_kpe = din("c_kpe", [DEPTH, 512, 32])
    c_swk = din("c_swk", [DEPTH, 512, 128])
    c_swv = din("c_swv", [DEPTH, 512, 128])
    ropeA = din("ropeA", [2, 128, 2048])
    ropeB = din("ropeB", [2, 64, 2048])
    pswap_d = din("pswap", [128, 128])
    masks_d = din("masks", [2, 128, 128], BF16)
    hy_conv = din("hy_conv", [DEPTH, 3, 768])
    hy_w1 = din("hy_w1", [DEPTH, 17, 64])
    hy_b1 = din("hy_b1", [DEPTH, 64])
    hy_w2 = din("hy_w2", [DEPTH, 64, 64])
    hy_b2 = din("hy_b2", [DEPTH, 64])
    hy_w3 = din("hy_w3", [DEPTH, 64, 512])
    hy_freq = din("hy_freq", [DEPTH, 2, 64])
    hy_decay = din("hy_decay", [DEPTH, 512])
    hy_bias = din("hy_bias", [DEPTH, 2, 256])
    HY = {}
    for L_, sfx in ((256, "s"), (2048, "b")):
        HY[L_] = dict(tab=din("dft_" + sfx, [2, L_ + 1, L_ + 1], BF16), feats=din("feats_" + sfx, [17, L_]),
                      negdist=din("negdist_" + sfx, [128, L_ // 128]), aw=din("aw_" + sfx, [128, L_ // 128 + 1]),
                      bw=din("bw_" + sfx, [128, L_ // 128 + 1]))
    gdn_conv = din("gdn_conv", [DEPTH, 3, 768])
    gdn_a_log = din("gdn_a_log", [DEPTH, 8])
    gdn_dt_bias = din("gdn_dt_bias", [DEPTH, 8])
    gdn_norm = din("gdn_norm", [DEPTH, 64])
    st_gdn = din("st_gdn", [DEPTH, 2, 4, 64, 64])
    negm_d = din("negm", [2, 64, 64])
    sel_d = din("gsel", [16, 8, 128])
    sm_d = din("smask", [2, 64, 64])
    o_gdn = dout("o_gdn", [2, DEPTH, 2, 4, 64, 64])
    o_ckv = dout("o_ckv", [2, DEPTH, 256, 128])
    o_kpe = dout("o_kpe", [2, DEPTH, 256, 32])
    o_swk = dout("o_swk", [2, DEPTH, 256, 128])
    o_swv = dout("o_swv", [2, DEPTH, 256, 128])
    xT = dscr("xT", [D, TT])
    w_in_b = dscr("w_in_b", [DEPTH, D, IN_COLS], BF16)
    w_out_b = dscr("w_out_b", [DEPTH, D, D], BF16)
    w1_b = dscr("w1_b", [DEPTH, D, 4 * D], BF16)
    w2_b = dscr("w2_b", [DEPTH, 4 * D, D], BF16)
    Bcv = {}
    xTv = xT.rearrange("(c p) t -> p c t", p=128)
    dbg_out = {}

    ARENA_BYTES = 206 * 1024
    arena_t = ES.enter_context(nc.sbuf_tensor("arena", [128, ARENA_BYTES // 2], BF16))
    psum = ES.enter_context(nc.psum_tensor("psum", [128, 4096], F32))
    A = Arena(arena_t, ARENA_BYTES)
    k = K(nc, P, A, psum)
    PS = k.PS

    ident = A.alloc([128], F32)
    Bconst = Buf()
    k.memset(ident, 1.0, [Bconst])
    P.op("pool", lambda e: e.affine_select(out=ident, in_=ident, pattern=[[-1, 128]], compare_op=ALU.is_equal,
                                           fill=0.0, base=0, channel_multiplier=1), [Bconst], [Bconst])
    ones_b = A.alloc([128], BF16)
    k.memset(ones_b, 1.0, [Bconst])
    ones_f = A.alloc([128], F32)
    k.memset(ones_f, 1.0, [Bconst])

    pswapF = A.alloc([128], F32)
    k.dma("sp", pswapF, pswap_d, [], [Bconst])
    masks = A.alloc([2, 128], BF16)
    k.dma("sp", masks, masks_d.rearrange("m p q -> p m q"), [], [Bconst])
    kvg = A.alloc([DEPTH], F32)
    k.dma("sp", kvg, mla_kv_norm.rearrange("l p -> p l"), [], [Bconst], slow=True)
    sinkE = A.alloc([DEPTH * 4], F32)
    k.dma("sp", sinkE, swa_sink.rearrange("l h -> (l h)").partition_broadcast(128), [], [Bconst], slow=True)
    k.act(sinkE, sinkE, AF.Exp, [Bconst], [Bconst])
    modT = A.alloc([DEPTH, 48, 2], F32)
    A1 = A.alloc([DEPTH, 8, 2], F32)
    B1 = A.alloc([DEPTH, 8, 2], F32)
    A2 = A.alloc([DEPTH, 8, 2], F32)
    B2 = A.alloc([DEPTH, 8, 2], F32)
    Bmod = Buf()
    mark0 = A.top
    stgA = A.alloc([128], F32)
    stgB = A.alloc([128], F32)
    TAc = A.alloc([128], F32)
    TBc = A.alloc([128], F32)
    scT = A.alloc([8, 2], BF16)
    Bc_ = Buf()
    Bstg = Buf()
    k.memset(stgA, 0.0, [Bstg])
    k.memset(stgB, 0.0, [Bstg])
    k.dma("sp", stgA[0:16, :], conds.rearrange("k (c p) -> (k c) p", p=128), [], [Bstg])
    k.dma("sp", stgA[16:80, :], gvec.rearrange("g l (c p) -> (g l c) p", p=128), [], [Bstg])
    k.dma("sp", stgB[0:96, :], b_ada.rearrange("l (j p) -> (l j) p", p=128), [], [Bstg])
    k.tr(k.bank(7)[:, 0:128], stgA, ident, [Bstg, Bconst], [PS[7]])
    k.tr(k.bank(7)[:, 128:256], stgB, ident, [Bstg, Bconst], [PS[7]])
    k.copy(TAc, k.bank(7)[:, 0:128], [PS[7]], [Bc_], eng="dve")
    k.copy(TBc, k.bank(7)[:, 128:256], [PS[7]], [Bc_], eng="dve")
    gT = TAc[:, 16:80].rearrange("p (g l c) -> p g l c", g=4, l=2)
    badaT = TBc[:, 0:96].rearrange("p (l j) -> p l j", l=2)
    k.act(scT, TAc[:, 0:16].rearrange("p (k c) -> p c k", k=2), AF.Silu, [Bc_], [Bc_])
    NPIECE = 8
    PW = 6 * D // NPIECE
    wa = [A.alloc([8, PW], BF16) for _ in range(2)]
    Bwa = [Buf(), Buf()]
    for l in range(DEPTH):
        for pc in range(NPIECE):
            t = wa[pc % 2]
            bt = Bwa[pc % 2]
            k.dma("pool", t, w_ada[l].rearrange("(c p) n -> p c n", p=128)[:, :, pc * PW:(pc + 1) * PW], [], [bt])
            for jj in range(6):
                j = pc * 6 + jj
                for c in range(8):
                    k.mm(k.bank(l)[:, j * 2:(j + 1) * 2], t[:, c, jj * 128:(jj + 1) * 128], scT[:, c, :],
                         c == 0, c == 7, [bt, Bc_], [PS[l]])
        k.tt(modT[:, l], k.bank(l)[:, 0:96].rearrange("p (j k) -> p j k", k=2),
             bc(badaT[:, l].unsqueeze(2), [128, 48, 2]), ALU.add, [PS[l], Bc_], [Bmod])
        for (dst, gi, j0, plus1) in ((A1, 0, 8, True), (B1, 1, 16, False), (A2, 2, 32, True), (B2, 3, 40, False)):
            gb = bc(gT[:, gi, l].unsqueeze(2), [128, 8, 2])
            if plus1:
                k.stt(dst[:, l], modT[:, l, j0:j0 + 8, :], 1.0, gb, ALU.add, ALU.mult, [Bmod, Bc_], [Bmod])
            else:
                k.tt(dst[:, l], modT[:, l, j0:j0 + 8, :], gb, ALU.mult, [Bmod, Bc_], [Bmod])
    for l_ in range(DEPTH):
        for nm, src, dst, rows in (("w_in", w_in, w_in_b, D), ("w_out", w_out, w_out_b, D), ("w1", mlp_w1, w1_b, D), ("w2", mlp_w2, w2_b, 4 * D)):
            Bcv[(nm, l_)] = Buf()
            for r0 in range(0, rows, 128):
                k.dma("pool", dst[l_, r0:r0 + 128, :], src[l_, r0:r0 + 128, :], [], [Bcv[(nm, l_)]], bg=True)
    P.barrier()
    A.top = mark0
    dbgmod = dout("dbg_mod", [128, DEPTH * 96])
    k.dma("sp", dbgmod, modT.rearrange("p l j k -> p (l j k)"), [Bmod], [])
    if stop == "mods":
        P.emit(); ES.close(); return nc, P

    BxT = [Buf() for _ in range(TT // TB)]
    m_pro = A.top
    xs = [A.alloc([D], F32) for _ in range(2)]
    xo = [A.alloc([8, 128], F32) for _ in range(2)]
    Bxs = [Buf(), Buf()]
    Bxo = [Buf(), Buf()]
    for i in range(TT // 128):
        s = i % 2
        k.dma("sp", xs[s], x_tok[i * 128:(i + 1) * 128, :], [], [Bxs[s]])
        for c in range(8):
            bk = 2 * s + c // 4
            k.tr(k.bank(bk)[:, (c % 4) * 128:(c % 4 + 1) * 128], xs[s][:, c * 128:(c + 1) * 128], ident,
                 [Bxs[s], Bconst], [PS[bk]])
        for hh in range(2):
            bk = 2 * s + hh
            k.copy(xo[s][:, hh * 4:(hh + 1) * 4, :], k.bank(bk).rearrange("p (c t) -> p c t", c=4), [PS[bk]], [Bxo[s]])
        k.dma("sp", xTv[:, :, i * 128:(i + 1) * 128], xo[s], [Bxo[s]], [BxT[i // 4]])
    P.barrier()
    A.top = m_pro
    if stop == "pro":
        P.emit(); ES.close(); return nc, P

    hT = A.alloc([8, 2048], BF16)
    oT = A.alloc([8, 2048], BF16)
    gatesT = A.alloc([2048], F32)
    BhT = Buf()
    BoT = Buf()
    Bgates = Buf()

    def rstd_from_sq(sq, Bsq, bank_i, rstd, Brstd):
        for c in range(8):
            k.mm(k.bank(bank_i), ones_b, sq[:, c, :], c == 0, c == 7, [Bsq, Bconst], [PS[bank_i]])
        k.act(rstd, k.bank(bank_i), AF.Sqrt, [PS[bank_i]], [Brstd], scale=1.0 / D, bias=epsb)
        k.recip(rstd, rstd, [Brstd], [Brstd])

    epsb = A.alloc([1], F32)
    k.memset(epsb, EPS, [Bconst])

    def prenorm(xblk, Bx, dst, Bdst, dcols, Asc, shj, l, kc, W, gate=None):
        sq, Bsq, rstd, Brstd, tmp, Btmp = W
        k.act(sq, xblk, AF.Square, [Bx], [Bsq])
        rstd_from_sq(sq, Bsq, 0, rstd, Brstd)
        k.tt(tmp, xblk, bc(rstd.unsqueeze(1), [128, 8, TB]), ALU.mult, [Bx, Brstd], [Btmp])
        for c in range(8):
            k.act(dst[:, c, dcols], tmp[:, c, :], AF.Identity, [Btmp, Bmod], [Bdst],
                  scale=Asc[:, l, c, kc:kc + 1], bias=modT[:, l, shj + c, kc:kc + 1])
        if gate is not None:
            wg32, Bwg32 = gate
            for c in range(8):
                k.act(tmp[:, c, :], tmp[:, c, :], AF.Identity, [Btmp, Bmod], [Btmp],
                      scale=Asc[:, l, c, kc:kc + 1], bias=modT[:, l, shj + c, kc:kc + 1])
            for c in range(8):
                k.mm(k.bank(1)[0:16, :], wg32[:, c, :], tmp[:, c, :], c == 0, c == 7, [Bwg32, Btmp], [PS[1]])
            k.copy(gatesT[0:16, dcols], k.bank(1)[0:16, :], [PS[1]], [Bgates], eng="dve")


    def normalize_out(bkO, ncols, heads, l, sink, dst_cols_list, W):
        rrow, Brr, bcs, Bbcs = W
        per = ncols // len(heads)
        O = k.bank(bkO)
        if sink:
            for i, (h, e_off, dcols) in enumerate(heads):
                k.ts(rrow[64:65, i * per:(i + 1) * per], O[64:65, i * per:(i + 1) * per],
                     sinkE[64:65, l * 4 + h:l * 4 + h + 1], None, ALU.add, None, [PS[bkO], Bconst], [Brr])
            k.recip(rrow[64:65, 0:ncols], rrow[64:65, 0:ncols], [Brr], [Brr])
        else:
            k.recip(rrow[64:65, 0:ncols], O[64:65, 0:ncols], [PS[bkO]], [Brr])
        k.mm(k.bank(2)[0:64, 0:ncols], ones_f[64:65, 0:64], rrow[64:65, 0:ncols], True, True, [Brr, Bconst], [PS[2]])
        k.copy(bcs[0:64, 0:ncols], k.bank(2)[0:64, 0:ncols], [PS[2]], [Bbcs], eng="act")
        for i, (h, e_off, dcols) in enumerate(heads):
            ch = e_off // 128
            po = e_off % 128
            k.tt(oT[po:po + 64, ch, dcols], O[0:64, i * per:(i + 1) * per], bcs[0:64, i * per:(i + 1) * per],
                 ALU.mult, [PS[bkO], Bbcs], [BoT])

    def mixer_attn(l, g):
        T = g["T"]
        L = g["L"]
        nseq = g["nseq"]
        lat = g["name"] == "lat"
        nblk = T // TB
        TK = T + (512 if lat else 0)
        NKC = TK // 128
        wA = A.alloc([8, 544], BF16)
        BwA = Buf()
        k.dma("sp", wA, w_in_b[l].rearrange("(c p) n -> p c n", p=128)[:, :, 0:544], [Bcv[("w_in", l)]], [BwA])
        wukv = A.alloc([512], BF16)
        k.dma("pool", wukv, mla_w_ukv[l], [], [BwA])
        qT = A.alloc([4, T], BF16)
        BqT = Buf()
        kT = A.alloc([4, TK], BF16)
        BkT = Buf()
        ckvnT = A.alloc([TK], BF16)
        Bckv = Buf()
        Vaug = A.alloc([NKC, 4, 65], BF16)
        BV = Buf()
        k.memset(Vaug, 1.0, [BV])
        raw = A.alloc([TB], F32)
        Braw = Buf()
        sqc = A.alloc([TB], BF16)
        Bsqc = Buf()
        rs = A.alloc([TB], F32)
        Brs = Buf()
        kpf = A.alloc([TB], F32)
        Bkpf = Buf()
        tmpr = A.alloc([TB], F32)
        Btmpr = Buf()
        if lat:
            rC = A.alloc([2048], F32)
            rS = A.alloc([2048], F32)
            Brope = Buf()
            k.dma("sp", rC[64:96, :], ropeA[0, 64:96, :], [], [Brope])
            k.dma("sp", rS[64:96, :], ropeA[1, 64:96, :], [], [Brope])
        stg = A.alloc([4, 128], F32)
        Bstg = Buf()

        def rope_rows(src_f, Bsrc, r0, r1, tcols, dst, Bdst):
            k.mm(k.bank(2)[r0:r1, :], pswapF[r0:r1, r0:r1], src_f[r0:r1, :], True, True, [Bsrc, Bconst], [PS[2]])
            k.tt(tmpr[r0:r1, :], k.bank(2)[r0:r1, :], rS[r0:r1, tcols], ALU.mult, [PS[2], Brope], [Btmpr])
            k.tt(src_f[r0:r1, :], src_f[r0:r1, :], rC[r0:r1, tcols], ALU.mult, [Bsrc, Brope], [Bsrc])
            k.tt(dst, src_f[r0:r1, :], tmpr[r0:r1, :], ALU.add, [Bsrc, Btmpr], [Bdst])

        def ckv_norm_block(src_ps_bank, cols, tok_out=None):
            k.copy(raw, k.bank(src_ps_bank), [PS[src_ps_bank]], [Braw], eng="dve")
            k.act(sqc, raw, AF.Square, [Braw], [Bsqc])
            k.mm(k.bank(2), ones_b, sqc, True, True, [Bsqc, Bconst], [PS[2]])
            k.act(rs, k.bank(2), AF.Sqrt, [PS[2]], [Brs], scale=1.0 / 128, bias=epsb)
            k.recip(rs, rs, [Brs], [Brs])
            k.stt(raw, raw, kvg[:, l:l + 1], rs, ALU.mult, ALU.mult, [Braw, Brs, Bconst], [Braw])
            k.copy(ckvnT[:, cols], raw, [Braw], [Bckv], eng="act")
            if tok_out is not None:
                sq_, t0 = tok_out
                for j in range(4):
                    k.tr(k.bank(2)[:, j * 128:(j + 1) * 128], raw[:, j * 128:(j + 1) * 128], ident, [Braw, Bconst], [PS[2]])
                k.copy(stg, k.bank(2).rearrange("p (j r) -> p j r", j=4), [PS[2]], [Bstg], eng="dve")
                for j in range(4):
                    tk = t0 + j * 128
                    k.dma("sp", o_ckv[tk // 256, l, tk % 256:tk % 256 + 128, :], stg[:, j, :], [Bstg], [])

        for tb in range(nblk):
            cols = slice(tb * TB, (tb + 1) * TB)
            for c in range(8):
                k.mm(k.bank(0), wA[:, c, 384:512], hT[:, c, cols], c == 0, c == 7, [BwA, BhT], [PS[0]])
            ckv_norm_block(0, cols, tok_out=None if lat else (0, tb * TB))
            for c in range(8):
                k.mm(k.bank(1)[64:96, :], wA[:, c, 512:544], hT[:, c, cols], c == 0, c == 7, [BwA, BhT], [PS[1]])
            k.copy(kpf[64:96, :], k.bank(1)[64:96, :], [PS[1]], [Bkpf], eng="act")
            if lat:
                for h in range(4):
                    if h == 0:
                        rope_rows(kpf, Bkpf, 64, 96, cols, kT[64:96, 0, cols], BkT)
                    else:
                        k.copy(kT[64:96, h, cols], kT[64:96, 0, cols], [BkT], [BkT])
            else:
                for h in range(4):
                    k.copy(kT[64:96, h, cols], kpf[64:96, :], [Bkpf], [BkT])
                for c in range(8):
                    k.mm(k.bank(1)[0:32, :], wA[:, c, 512:544], hT[:, c, cols], c == 0, c == 7, [BwA, BhT], [PS[1]])
                k.copy(kpf[0:32, :], k.bank(1)[0:32, :], [PS[1]], [Bkpf], eng="dve")
                for j in range(4):
                    k.tr(k.bank(1)[:, j * 32:(j + 1) * 32], kpf[0:32, j * 128:(j + 1) * 128], ident[0:32, 0:32],
                         [Bkpf, Bconst], [PS[1]])
                k.copy(stg[:, 0, :], k.bank(1)[:, 0:128], [PS[1]], [Bstg], eng="dve")
                for j in range(4):
                    tk = tb * TB + j * 128
                    k.dma("sp", o_kpe[tk // 256, l, tk % 256:tk % 256 + 128, :], stg[:, 0, j * 32:(j + 1) * 32], [Bstg], [])
            for h in range(4):
                bk = h % 2
                for c in range(8):
                    k.mm(k.bank(bk)[0:96, :], wA[:, c, h * 96:(h + 1) * 96], hT[:, c, cols], c == 0, c == 7,
                         [BwA, BhT], [PS[bk]])
                if lat:
                    k.copy(raw[0:96, :], k.bank(bk)[0:96, :], [PS[bk]], [Braw], eng="act")
                    k.copy(qT[0:64, h, cols], raw[0:64, :], [Braw], [BqT], eng="dve")
                    rope_rows(raw, Braw, 64, 96, cols, qT[64:96, h, cols], BqT)
                else:
                    k.copy(qT[0:96, h, cols], k.bank(bk)[0:96, :], [PS[bk]], [BqT])
        if lat:
            for j in range(4):
                k.dma("sp", stg[:, j, :], c_ckv[l, j * 128:(j + 1) * 128, :], [], [Bstg])
            for j in range(4):
                k.tr(k.bank(0)[:, j * 128:(j + 1) * 128], stg[:, j, :], ident, [Bstg, Bconst], [PS[0]])
            k.copy(ckvnT[:, 2048:2560], k.bank(0), [PS[0]], [Bckv], eng="dve")
            for j in range(4):
                k.dma("sp", stg[:, j, 0:32], c_kpe[l, j * 128:(j + 1) * 128, :], [Bstg], [Bstg])
            for j in range(4):
                k.tr(k.bank(1)[0:32, j * 128:(j + 1) * 128], stg[:, j, 0:32], ident, [Bstg, Bconst], [PS[1]])
            for h in range(4):
                k.copy(kT[64:96, h, 2048:2560], k.bank(1)[0:32, :], [PS[1]], [BkT])
        for kb in range(TK // TB):
            cols = slice(kb * TB, (kb + 1) * TB)
            for h in range(4):
                bk = h % 2
                k.mm(k.bank(bk)[0:64, :], wukv[:, h * 128:h * 128 + 64], ckvnT[:, cols], True, True, [BwA, Bckv], [PS[bk]])
                k.copy(kT[0:64, h, cols], k.bank(bk)[0:64, :], [PS[bk]], [BkT])
        for kc in range(NKC):
            bk = kc % 2
            k.mm(k.bank(bk), ckvnT[:, kc * 128:(kc + 1) * 128], wukv, True, True, [BwA, Bckv], [PS[bk]])
            k.copy(Vaug[:, kc, :, 0:64], k.bank(bk).rearrange("p (h t e) -> p h t e", h=4, t=2)[:, :, 1, :], [PS[bk]], [BV])
        PT = [A.alloc([TB], BF16) for _ in range(3)]
        BPT = [Buf() for _ in range(3)]
        rrow = A.alloc([TB], F32)
        bcs = A.alloc([TB], F32)
        Wn = (rrow, Buf(), bcs, Buf())
        scale = 96 ** -0.5
        it = 0
        nq = 0
        QB = min(TB, L)
        for sq_ in range(nseq):
            for h in range(4):
                for qb in range(L // QB):
                    qcols = slice(sq_ * L + qb * QB, sq_ * L + (qb + 1) * QB)
                    bkO = 6 + (nq % 2)
                    nq += 1
                    kcs = list(range(sq_ * L // 128, (sq_ + 1) * L // 128)) if not lat else list(range(NKC))
                    for i, kc in enumerate(kcs):
                        bS = 3 + (it % 3)
                        pt = PT[it % 3]
                        bpt = BPT[it % 3]
                        it += 1
                        k.mm(k.bank(bS)[:, 0:QB], kT[0:96, h, kc * 128:(kc + 1) * 128], qT[0:96, h, qcols], True, True,
                             [BkT, BqT], [PS[bS]])
                        k.act(pt[:, 0:QB], k.bank(bS)[:, 0:QB], AF.Exp, [PS[bS]], [bpt], scale=scale)
                        k.mm(k.bank(bkO)[0:65, 0:QB], Vaug[:, kc, h, :], pt[:, 0:QB], i == 0, i == len(kcs) - 1,
                             [BV, bpt], [PS[bkO]])
                    normalize_out(bkO, QB, [(h, h * 64, qcols)], l, False, None, Wn)
        P.barrier()
        A.top = mB_inner = A.top
        A.top = mixer_base[0]
        wB = A.alloc([8, 512], BF16)
        BwB = Buf()
        k.dma("sp", wB, w_in_b[l].rearrange("(c p) n -> p c n", p=128)[:, :, 544:1056], [Bcv[("w_in", l)]], [BwB])
        qS = A.alloc([4, T], BF16)
        BqS = Buf()
        kS = A.alloc([2, TK], BF16)
        BkS = Buf()
        Vb = A.alloc([NKC, 2, 65], BF16)
        BVb = Buf()
        k.memset(Vb, 1.0, [BVb])
        raw = A.alloc([TB], F32)
        Braw = Buf()
        tmpr = A.alloc([TB], F32)
        Btmpr = Buf()
        stg = A.alloc([4, 128], F32)
        Bstg = Buf()
        stg2 = A.alloc([256], F32)
        Bstg2 = Buf()
        if lat:
            rC = A.alloc([2048], F32)
            rS = A.alloc([2048], F32)
            Brope = Buf()
            k.dma("sp", rC[0:64, :], ropeB[0], [], [Brope])
            k.dma("sp", rS[0:64, :], ropeB[1], [], [Brope])
        for tb in range(nblk):
            cols = slice(tb * TB, (tb + 1) * TB)
            for h in range(6):
                bk = h % 2
                c0 = h * 64 if h < 4 else 256 + (h - 4) * 64
                for c in range(8):
                    k.mm(k.bank(bk)[0:64, :], wB[:, c, c0:c0 + 64], hT[:, c, cols], c == 0, c == 7, [BwB, BhT], [PS[bk]])
                dst, Bdst = (qS[0:64, h, cols], BqS) if h < 4 else (kS[0:64, h - 4, cols], BkS)
                if lat:
                    k.copy(raw[0:64, :], k.bank(bk)[0:64, :], [PS[bk]], [Braw], eng="act")
                    rope_rows(raw, Braw, 0, 64, cols, dst, Bdst)
                else:
                    k.copy(dst, k.bank(bk)[0:64, :], [PS[bk]], [Bdst])
            for j in range(4):
                tk = tb * TB + j * 128
                kc = tk // 128
                bk = j % 2
                for c in range(8):
                    k.mm(k.bank(bk)[:, 0:256], hT[:, c, tk:tk + 128], wB[:, c, 256:512], c == 0, c == 7, [BwB, BhT], [PS[bk]])
                k.copy(Vb[:, kc, :, 0:64], k.bank(bk)[:, 128:256].rearrange("p (v e) -> p v e", v=2), [PS[bk]], [BVb], eng="act")
                if not lat:
                    k.copy(stg2, k.bank(bk)[:, 0:256], [PS[bk]], [Bstg2], eng="dve")
                    k.dma("sp", o_swk[tk // 256, l, tk % 256:tk % 256 + 128, :], stg2[:, 0:128], [Bstg2], [])
                    k.dma("sp", o_swv[tk // 256, l, tk % 256:tk % 256 + 128, :], stg2[:, 128:256], [Bstg2], [])
        if lat:
            for j in range(4):
                k.dma("sp", stg[:, j, :], c_swk[l, j * 128:(j + 1) * 128, :], [], [Bstg])
            for kv in range(2):
                for j in range(4):
                    k.tr(k.bank(kv)[0:64, j * 128:(j + 1) * 128], stg[:, j, kv * 64:(kv + 1) * 64], ident, [Bstg, Bconst], [PS[kv]])
                k.copy(kS[0:64, kv, 2048:2560], k.bank(kv)[0:64, :], [PS[kv]], [BkS])
            for j in range(4):
                k.dma("pool", Vb[:, 16 + j, :, 0:64], c_swv[l, j * 128:(j + 1) * 128, :].rearrange("p (v e) -> p v e", v=2), [], [BVb])
        PT = [A.alloc([256], BF16) for _ in range(3)]
        BPT = [Buf() for _ in range(3)]
        rrow = A.alloc([TB], F32)
        bcs = A.alloc([TB], F32)
        Wn = (rrow, Buf(), bcs, Buf())
        scale = 64 ** -0.5
        it = 0
        nq = 0
        if not lat:
            for sq_ in range(nseq):
                for h in range(4):
                    qcols = slice(sq_ * L, (sq_ + 1) * L)
                    bkO = 6 + (nq % 2)
                    nq += 1
                    kcs = list(range(sq_ * L // 128, (sq_ + 1) * L // 128))
                    for i, kc in enumerate(kcs):
                        bS = 3 + (it % 3)
                        pt = PT[it % 3]
                        bpt = BPT[it % 3]
                        it += 1
                        k.mm(k.bank(bS)[:, 0:L], kS[0:64, h // 2, kc * 128:(kc + 1) * 128], qS[0:64, h, qcols], True, True,
                             [BkS, BqS], [PS[bS]])
                        k.act(pt[:, 0:L], k.bank(bS)[:, 0:L], AF.Exp, [PS[bS]], [bpt], scale=scale)
                        k.mm(k.bank(bkO)[0:65, 0:L], Vb[:, kc, h // 2, :], pt[:, 0:L], i == 0, i == len(kcs) - 1,
                             [BVb, bpt], [PS[bkO]])
                    normalize_out(bkO, L, [(h, 256 + h * 64, qcols)], l, True, None, Wn)
        else:
            NB = L // 128
            for n in range(NB):
                qcols = slice(n * 128, (n + 1) * 128)
                for kv in range(2):
                    bkO = 6 + (nq % 2)
                    nq += 1
                    kcs = []
                    if n > 0:
                        kcs.append((n - 1, 0))
                    kcs.append((n, None))
                    if n < NB - 1:
                        kcs.append((n + 1, 1))
                    kcs += [(16 + j, None) for j in range(4)]
                    for i, (kc, mk_) in enumerate(kcs):
                        bS = 3 + (it % 3)
                        pt = PT[it % 3]
                        bpt = BPT[it % 3]
                        it += 1
                        for hh in range(2):
                            k.mm(k.bank(bS)[:, hh * 128:(hh + 1) * 128], kS[0:64, kv, kc * 128:(kc + 1) * 128],
                                 qS[0:64, 2 * kv + hh, qcols], True, True, [BkS, BqS], [PS[bS]])
                        k.act(pt, k.bank(bS)[:, 0:256], AF.Exp, [PS[bS]], [bpt], scale=scale)
                        if mk_ is not None:
                            k.tt(pt.rearrange("p (h q) -> p h q", h=2), pt.rearrange("p (h q) -> p h q", h=2),
                                 bc(masks[:, mk_, :].unsqueeze(1), [128, 2, 128]), ALU.mult, [bpt, Bconst], [bpt])
                        k.mm(k.bank(bkO)[0:65, 0:256], Vb[:, kc, kv, :], pt, i == 0, i == len(kcs) - 1, [BVb, bpt], [PS[bkO]])
                    normalize_out(bkO, 256, [(2 * kv, 256 + 2 * kv * 64, qcols), (2 * kv + 1, 256 + (2 * kv + 1) * 64, qcols)],
                                  l, True, None, Wn)


    PI = math.pi

    def wrap_sin(x, Bx, t, Bt, rows, width):
        xs = x[0:rows, 0:width]
        ts_ = t[0:rows, 0:width]
        for rep in range(2):
            k.ts(ts_, xs, PI, -2 * PI, ALU.is_gt, ALU.mult, [Bx], [Bt])
            k.tt(xs, xs, ts_, ALU.add, [Bx, Bt], [Bx])
            k.ts(ts_, xs, -PI, 2 * PI, ALU.is_lt, ALU.mult, [Bx], [Bt])
            k.tt(xs, xs, ts_, ALU.add, [Bx, Bt], [Bx])
        k.act(xs, xs, AF.Sin, [Bx], [Bx])

    def mixer_hyena(l, g):
        T = g["T"]
        L = g["L"]
        nseq = g["nseq"]
        NSC = L // 128
        NF = NSC + 1
        NW = nseq * 256
        hc = HY[L]
        tabC = hc["tab"][0]
        tabS = hc["tab"][1]
        frows = lambda fc: 128 if fc < NSC else 1
        base = A.top
        stgp = A.alloc([128], F32)
        colsT = A.alloc([128], F32)
        Bsp = Buf()
        Bcols = Buf()
        k.memset(stgp, 0.0, [Bsp])
        k.dma("sp", stgp[0:18, :], hy_conv[l].rearrange("k (c p) -> (k c) p", p=128), [], [Bsp])
        k.dma("sp", stgp[18:22, :], hy_bias[l].rearrange("o (c p) -> (o c) p", p=128), [], [Bsp])
        k.dma("sp", stgp[32:33, 0:64], hy_b1[l:l + 1, :], [], [Bsp])
        k.dma("sp", stgp[33:34, 0:64], hy_freq[l][0:1, :], [], [Bsp])
        k.dma("sp", stgp[34:35, 0:64], hy_b2[l:l + 1, :], [], [Bsp])
        k.dma("sp", stgp[35:36, 0:64], hy_freq[l][1:2, :], [], [Bsp])
        k.tr(k.bank(0)[:, 0:128], stgp, ident, [Bsp, Bconst], [PS[0]])
        k.copy(colsT, k.bank(0)[:, 0:128], [PS[0]], [Bcols], eng="dve")
        convw = colsT[:, 0:18].rearrange("p (k c) -> p k c", k=3)
        biasc = colsT[:, 18:22].rearrange("p (o c) -> p o c", o=2)
        fb = A.alloc([2], F32)
        k.tt(fb[0:64, 0:1], colsT[0:64, 32:33], colsT[0:64, 33:34], ALU.mult, [Bcols], [Bcols])
        k.tt(fb[0:64, 1:2], colsT[0:64, 34:35], colsT[0:64, 35:36], ALU.mult, [Bcols], [Bcols])
        aw = A.alloc([NF], F32)
        bw = A.alloc([NF], F32)
        negdist = A.alloc([NSC], F32)
        k.dma("sp", aw, hc["aw"], [], [Bcols])
        k.dma("sp", bw, hc["bw"], [], [Bcols])
        k.dma("sp", negdist, hc["negdist"], [], [Bcols])
        Gre = A.alloc([NF, 512], BF16)
        Gim = A.alloc([NF, 512], BF16)
        BG = Buf()
        tC = [A.alloc([NSC, 128], BF16) for _ in range(2)]
        tS = [A.alloc([NSC, 128], BF16) for _ in range(2)]
        Btab = [Buf(), Buf()]
        persist = A.top
        tmpA = A.alloc([512], F32)
        tmpB = A.alloc([512], F32)
        BtA = Buf()
        BtB = Buf()

        def load_colblock(fc, i):
            rows = frows(fc)
            k.dma("sp", tC[i][:, :, 0:rows], tabC[0:L, fc * 128:fc * 128 + rows].rearrange("(sc p) f -> p sc f", p=128),
                  [], [Btab[i]], slow=True)
            k.dma("sp", tS[i][:, :, 0:rows], tabS[0:L, fc * 128:fc * 128 + rows].rearrange("(sc p) f -> p sc f", p=128),
                  [], [Btab[i]], slow=True)

        featsT = A.alloc([L], F32)
        w1f = A.alloc([64], F32)
        w2f = A.alloc([64], F32)
        w3f = A.alloc([512], F32)
        Bfw = Buf()
        k.dma("sp", featsT[0:17, :], hc["feats"], [], [Bfw])
        k.dma("sp", w1f[0:17, :], hy_w1[l], [], [Bfw])
        k.dma("sp", w2f[0:64, :], hy_w2[l], [], [Bfw])
        k.dma("sp", w3f[0:64, :], hy_w3[l], [], [Bfw])
        h1T = A.alloc([L], F32)
        h2T = A.alloc([L], F32)
        wt = A.alloc([L], F32)
        Bh1 = Buf()
        Bh2 = Buf()
        Bwt = Buf()
        CW = min(512, L)
        for blk in range(L // CW):
            cs = slice(blk * CW, (blk + 1) * CW)
            k.mm(k.bank(blk % 2)[0:64, 0:CW], w1f[0:17, 0:64], featsT[0:17, cs], True, True, [Bfw], [PS[blk % 2]])
            k.ts(h1T[0:64, cs], k.bank(blk % 2)[0:64, 0:CW], colsT[0:64, 33:34], fb[0:64, 0:1], ALU.mult, ALU.add,
                 [PS[blk % 2], Bcols], [Bh1])
        wrap_sin(h1T, Bh1, wt, Bwt, 64, L)
        for blk in range(L // CW):
            cs = slice(blk * CW, (blk + 1) * CW)
            k.mm(k.bank(blk % 2)[0:64, 0:CW], w2f[0:64, 0:64], h1T[0:64, cs], True, True, [Bfw, Bh1], [PS[blk % 2]])
            k.ts(h2T[0:64, cs], k.bank(blk % 2)[0:64, 0:CW], colsT[0:64, 35:36], fb[0:64, 1:2], ALU.mult, ALU.add,
                 [PS[blk % 2], Bcols], [Bh2])
        wrap_sin(h2T, Bh2, wt, Bwt, 64, L)
        absdec = A.alloc([512], F32)
        Bad = Buf()
        k.dma("sp", absdec, hy_decay[l].partition_broadcast(128), [], [Bad], slow=True)
        k.act(absdec, absdec, AF.Abs, [Bad], [Bad])
        filtok = A.alloc([NSC, 512], BF16)
        Bft = Buf()
        Et = A.alloc([512], F32)
        BEt = Buf()
        for jc in range(NSC):
            bk = jc % 2
            k.mm(k.bank(bk), h2T[0:64, jc * 128:(jc + 1) * 128], w3f[0:64, :], True, True, [Bh2, Bfw], [PS[bk]])
            k.act(Et, absdec, AF.Exp, [Bad, Bcols], [BEt], scale=negdist[:, jc:jc + 1])
            k.tt(filtok[:, jc, :], k.bank(bk), Et, ALU.mult, [PS[bk], BEt], [Bft])
        for fc in range(NF):
            rows = frows(fc)
            i = fc % 2
            load_colblock(fc, i)
            ba, bb = 2 * i, 2 * i + 1
            for sc in range(NSC):
                k.mm(k.bank(ba)[0:rows, :], tC[i][:, sc, 0:rows], filtok[:, sc, :], sc == 0, sc == NSC - 1, [Btab[i], Bft], [PS[ba]])
            for sc in range(NSC):
                k.mm(k.bank(bb)[0:rows, :], tS[i][:, sc, 0:rows], filtok[:, sc, :], sc == 0, sc == NSC - 1, [Btab[i], Bft], [PS[bb]])
            k.ts(tmpA[0:rows, :], k.bank(ba)[0:rows, :], aw[0:rows, fc:fc + 1], None, ALU.mult, None, [PS[ba], Bcols], [BtA])
            k.stt(Gre[0:rows, fc, :], k.bank(bb)[0:rows, :], bw[0:rows, fc:fc + 1], tmpA[0:rows, :], ALU.mult, ALU.add,
                  [PS[bb], BtA, Bcols], [BG])
            k.ts(tmpB[0:rows, :], k.bank(bb)[0:rows, :], aw[0:rows, fc:fc + 1], None, ALU.mult, None, [PS[bb], Bcols], [BtB])
            k.stt(Gim[0:rows, fc, :], k.bank(ba)[0:rows, :], bw[0:rows, fc:fc + 1], tmpB[0:rows, :], ALU.mult, ALU.subtract,
                  [PS[ba], BtB, Bcols], [BG])
        P.barrier()
        A.top = persist
        x1T = A.alloc([2, T], BF16)
        x2T = A.alloc([2, T], BF16)
        vT = A.alloc([2, T], F32)
        Bx12 = Buf()
        BvT = Buf()
        dtok = A.alloc([NSC, NW], BF16)
        Bdt = Buf()
        conv_mark = A.top
        wH = A.alloc([8, 768], BF16)
        BwH = Buf()
        k.dma("sp", wH, w_in_b[l].rearrange("(c p) n -> p c n", p=128)[:, :, C_HU:C_HU + 768], [Bcv[("w_in", l)]], [BwH])
        hu = A.alloc([T], F32)
        uu = A.alloc([T], F32)
        Bhu = Buf()
        Buu = Buf()
        for c6 in range(6):
            for tb in range(T // TB):
                cols = slice(tb * TB, (tb + 1) * TB)
                bk = tb % 2
                for c in range(8):
                    k.mm(k.bank(bk), wH[:, c, c6 * 128:(c6 + 1) * 128], hT[:, c, cols], c == 0, c == 7, [BwH, BhT], [PS[bk]])
                k.copy(hu[:, cols], k.bank(bk), [PS[bk]], [Bhu])
            k.act(uu, hu, AF.Identity, [Bhu, Bcols], [Buu], scale=convw[:, 1, c6:c6 + 1])
            for sq_ in range(nseq):
                s0, s1 = sq_ * L, (sq_ + 1) * L
                k.stt(uu[:, s0 + 1:s1], hu[:, s0:s1 - 1], convw[:, 0, c6:c6 + 1], uu[:, s0 + 1:s1], ALU.mult, ALU.add,
                      [Bhu, Bcols, Buu], [Buu])
                k.stt(uu[:, s0:s1 - 1], hu[:, s0 + 1:s1], convw[:, 2, c6:c6 + 1], uu[:, s0:s1 - 1], ALU.mult, ALU.add,
                      [Bhu, Bcols, Buu], [Buu])
            if c6 < 2:
                k.copy(vT[:, c6, :], uu, [Buu], [BvT], eng="act")
            elif c6 < 4:
                k.copy(x1T[:, c6 - 2, :], uu, [Buu], [Bx12], eng="act")
            else:
                k.copy(x2T[:, c6 - 4, :], uu, [Buu], [Bx12], eng="act")

        def to_tokmajor():
            for sq_ in range(nseq):
                for sc in range(NSC):
                    bk = sc % 2
                    for cc in range(2):
                        t0 = sq_ * L + sc * 128
                        k.tr(k.bank(bk)[:, cc * 128:(cc + 1) * 128], vT[:, cc, t0:t0 + 128], ident, [BvT, Bconst], [PS[bk]])
                    k.copy(dtok[:, sc, sq_ * 256:(sq_ + 1) * 256], k.bank(bk)[:, 0:256], [PS[bk]], [Bdt])

        P.barrier()
        A.top = conv_mark
        Pq = A.alloc([NF, NW], BF16)
        Qq = A.alloc([NF, NW], BF16)
        BPQ = Buf()
        rC = [A.alloc([L], BF16) for _ in range(2)]
        rS = [A.alloc([L], BF16) for _ in range(2)]
        Brt = [Buf(), Buf()]
        t1 = A.alloc([NW], F32)
        t2 = A.alloc([NW], F32)
        Bt1 = Buf()
        Bt2 = Buf()
        TW = min(512, L)

        def long_conv(o, consumer):
            ocs = slice(o * 256, (o + 1) * 256)
            for fc in range(NF):
                rows = frows(fc)
                i = fc % 2
                load_colblock(fc, i)
                ba, bb = 2 * i, 2 * i + 1
                for sc in range(NSC):
                    k.mm(k.bank(ba)[0:rows, 0:NW], tC[i][:, sc, 0:rows], dtok[:, sc, :], sc == 0, sc == NSC - 1, [Btab[i], Bdt], [PS[ba]])
                for sc in range(NSC):
                    k.mm(k.bank(bb)[0:rows, 0:NW], tS[i][:, sc, 0:rows], dtok[:, sc, :], sc == 0, sc == NSC - 1, [Btab[i], Bdt], [PS[bb]])
                Av = k.bank(ba)[0:rows, 0:NW].rearrange("p (s c) -> p s c", s=nseq)
                Bv = k.bank(bb)[0:rows, 0:NW].rearrange("p (s c) -> p s c", s=nseq)
                gre = bc(Gre[0:rows, fc, ocs].unsqueeze(1), [rows, nseq, 256])
                gim = bc(Gim[0:rows, fc, ocs].unsqueeze(1), [rows, nseq, 256])
                v3 = lambda ap: ap[0:rows, :].rearrange("p (s c) -> p s c", s=nseq)
                k.tt(v3(t1), Av, gre, ALU.mult, [PS[ba], BG], [Bt1])
                k.tt(v3(t2), Bv, gim, ALU.mult, [PS[bb], BG], [Bt2])
                k.tt(Pq[0:rows, fc, :], t1[0:rows, :], t2[0:rows, :], ALU.add, [Bt1, Bt2], [BPQ])
                k.tt(v3(t1), Bv, gre, ALU.mult, [PS[bb], BG], [Bt1])
                k.tt(v3(t2), Av, gim, ALU.mult, [PS[ba], BG], [Bt2])
                k.tt(Qq[0:rows, fc, :], t1[0:rows, :], t2[0:rows, :], ALU.subtract, [Bt1, Bt2], [BPQ])
            accs = [(sq_, cc, tb) for sq_ in range(nseq) for cc in range(2) for tb in range(L // TW)]
            for fc in range(NF):
                rows = frows(fc)
                i = fc % 2
                k.dma("sp", rC[i][0:rows, :], tabC[fc * 128:fc * 128 + rows, 0:L], [], [Brt[i]])
                k.dma("sp", rS[i][0:rows, :], tabS[fc * 128:fc * 128 + rows, 0:L], [], [Brt[i]])
                for ai, (sq_, cc, tb) in enumerate(accs):
                    lc = slice(sq_ * 256 + cc * 128, sq_ * 256 + (cc + 1) * 128)
                    k.mm(k.bank(ai)[:, 0:TW], Pq[0:rows, fc, lc], rC[i][0:rows, tb * TW:(tb + 1) * TW], fc == 0, False,
                         [BPQ, Brt[i]], [PS[ai]])
                    k.mm(k.bank(ai)[:, 0:TW], Qq[0:rows, fc, lc], rS[i][0:rows, tb * TW:(tb + 1) * TW], False, fc == NF - 1,
                         [BPQ, Brt[i]], [PS[ai]])
            for ai, (sq_, cc, tb) in enumerate(accs):
                consumer(ai, cc, slice(sq_ * L + tb * TW, sq_ * L + (tb + 1) * TW))

        tcs = A.alloc([TW], F32)
        Btcs = Buf()

        def cons0(ai, cc, tcols):
            k.stt(tcs, vT[:, cc, tcols], biasc[:, 0, cc:cc + 1], k.bank(ai)[:, 0:TW], ALU.mult, ALU.add, [BvT, Bcols, PS[ai]], [Btcs])
            k.tt(vT[:, cc, tcols], tcs, x1T[:, cc, tcols], ALU.mult, [Btcs, Bx12], [BvT])

        def cons1(ai, cc, tcols):
            k.stt(tcs, vT[:, cc, tcols], biasc[:, 1, cc:cc + 1], k.bank(ai)[:, 0:TW], ALU.mult, ALU.add, [BvT, Bcols, PS[ai]], [Btcs])
            k.tt(oT[:, 6 + cc, tcols], tcs, x2T[:, cc, tcols], ALU.mult, [Btcs, Bx12], [BoT])

        to_tokmajor()
        long_conv(0, cons0)
        to_tokmajor()
        long_conv(1, cons1)
        P.barrier()
        A.top = base


    def mixer_gdn(l, g):
        GD = F32 if 'gdn32' in (dbg or ()) else BF16
        T = g["T"]
        L = g["L"]
        nseq = g["nseq"]
        lat = g["name"] == "lat"
        NCK = L // 64
        NI = nseq * 4
        NBLK = T // TB
        base = A.top
        stgp = A.alloc([128], F32)
        colsT = A.alloc([128], F32)
        Bsp = Buf()
        Bcols = Buf()
        k.memset(stgp, 0.0, [Bsp])
        k.dma("sp", stgp[0:18, :], gdn_conv[l].rearrange("k (c p) -> (k c) p", p=128), [], [Bsp])
        k.dma("sp", stgp[18:19, 0:64], gdn_norm[l:l + 1, :], [], [Bsp])
        k.dma("sp", stgp[18:19, 64:128], gdn_norm[l:l + 1, :], [], [Bsp])
        k.tr(k.bank(0)[:, 0:128], stgp, ident, [Bsp, Bconst], [PS[0]])
        k.copy(colsT, k.bank(0)[:, 0:128], [PS[0]], [Bcols], eng="dve")
        convw = colsT[:, 0:18].rearrange("p (k c) -> p k c", k=3)
        gnorm = colsT[:, 18:19]
        all16 = A.alloc([16], F32)
        k.dma("sp", all16[:, 0:8], gdn_a_log[l].partition_broadcast(128), [], [Bcols], slow=True)
        k.dma("sp", all16[:, 8:16], gdn_dt_bias[l].partition_broadcast(128), [], [Bcols], slow=True)
        k.act(all16[:, 0:8], all16[:, 0:8], AF.Exp, [Bcols], [Bcols])
        k.ts(all16[:, 0:8], all16[:, 0:8], -1.0, None, ALU.mult, None, [Bcols], [Bcols])
        neaS = A.alloc([2, 2], F32)
        dtbS = A.alloc([2, 2], F32)
        for (dst, c0) in ((neaS, 0), (dtbS, 8)):
            v8 = all16[:, c0:c0 + 8].rearrange("p (d hp two) -> p d hp two", d=2, two=2)
            k.copy(dst[0:64], v8[0:64, :, :, 0], [Bcols], [Bcols], eng="dve")
            k.copy(dst[64:128], v8[64:128, :, :, 1], [Bcols], [Bcols], eng="dve")
        blockones = A.alloc([128], BF16)
        k.memset(blockones, 0.0, [Bcols])
        k.memset(blockones[0:64, 0:64], 1.0, [Bcols])
        k.memset(blockones[64:128, 64:128], 1.0, [Bcols])
        negones = A.alloc([64], F32)
        k.memset(negones, -1.0, [Bcols])
        negm = A.alloc([2, 64], F32)
        smk = A.alloc([2, 64], F32)
        k.dma("sp", negm[0:64], negm_d.rearrange("m c s -> c m s"), [], [Bcols])
        k.dma("sp", smk[0:64], sm_d.rearrange("m c s -> c m s"), [], [Bcols])
        identb = A.alloc([64], GD)
        k.copy(identb[0:64, :], ident[0:64, 0:64], [Bconst], [Bcols], eng="dve")
        ident2 = A.alloc([64], F32)
        k.ts(ident2[0:64, :], ident[0:64, 0:64], 2.0, None, ALU.mult, None, [Bconst], [Bcols])
        startm = A.alloc([T], BF16)
        k.memset(startm, 1.0, [Bcols])
        k.memset(startm.rearrange("p (n c) -> p n c", c=64)[:, :, 0:1], 0.0, [Bcols])
        qn = A.alloc([2, T], GD)
        kn = A.alloc([2, T], GD)
        vs = A.alloc([2, T], BF16)
        ocT = A.alloc([2, T], F32)
        Bqkv = Buf()
        Boc = Buf()
        pm = A.top
        wG = A.alloc([8, 1024], BF16)
        BwG = Buf()
        k.dma("sp", wG, w_in_b[l].rearrange("(c p) n -> p c n", p=128)[:, :, C_GQKV:C_GQKV + 1024], [Bcv[("w_in", l)]], [BwG])
        hu = A.alloc([T], F32)
        uu = A.alloc([T], F32)
        Bhu = Buf()
        Buu = Buf()
        sqb = A.alloc([TB], BF16)
        Bsqb = Buf()
        rsn = A.alloc([TB], F32)
        Brsn = Buf()
        for c6 in range(6):
            hp = c6 % 2
            for tb in range(NBLK):
                cols = slice(tb * TB, (tb + 1) * TB)
                bk = tb % 2
                for c in range(8):
                    k.mm(k.bank(bk), wG[:, c, c6 * 128:(c6 + 1) * 128], hT[:, c, cols], c == 0, c == 7, [BwG, BhT], [PS[bk]])
                k.copy(hu[:, cols], k.bank(bk), [PS[bk]], [Bhu])
            k.act(uu, hu, AF.Identity, [Bhu, Bcols], [Buu], scale=convw[:, 1, c6:c6 + 1])
            for sq_ in range(nseq):
                s0, s1 = sq_ * L, (sq_ + 1) * L
                k.stt(uu[:, s0 + 1:s1], hu[:, s0:s1 - 1], convw[:, 0, c6:c6 + 1], uu[:, s0 + 1:s1], ALU.mult, ALU.add,
                      [Bhu, Bcols, Buu], [Buu])
                k.stt(uu[:, s0:s1 - 1], hu[:, s0 + 1:s1], convw[:, 2, c6:c6 + 1], uu[:, s0:s1 - 1], ALU.mult, ALU.add,
                      [Bhu, Bcols, Buu], [Buu])
            k.act(uu, uu, AF.Silu, [Buu], [Buu])
            if c6 < 4:
                dst = qn if c6 < 2 else kn
                sc_ = 64 ** -0.5 if c6 < 2 else 1.0
                for tb in range(NBLK):
                    cols = slice(tb * TB, (tb + 1) * TB)
                    k.act(sqb, uu[:, cols], AF.Square, [Buu], [Bsqb])
                    k.mm(k.bank(2), blockones, sqb, True, True, [Bsqb, Bcols], [PS[2]])
                    k.act(rsn, k.bank(2), AF.Sqrt, [PS[2], Bconst], [Brsn], bias=epsb)
                    k.recip(rsn, rsn, [Brsn], [Brsn])
                    k.stt(dst[:, hp, cols], uu[:, cols], sc_, rsn, ALU.mult, ALU.mult, [Buu, Brsn], [Bqkv])
            else:
                k.copy(vs[:, hp, :], uu, [Buu], [Bqkv], eng="act")
        P.barrier()
        A.top = pm
        gsel = A.alloc([8, 128], F32)
        Bwrep = Buf()
        k.dma("sp", gsel[0:16], sel_d, [], [Bwrep])
        Dd = A.alloc([2, T], F32)
        bet = A.alloc([2, T], BF16)
        BD = Buf()
        Bbet = Buf()
        alias_mark = A.top
        tmpf = A.alloc([T], F32)
        Btmpf = Buf()
        A.top = alias_mark
        Eb = A.alloc([2, TB], F32)
        kbT = A.alloc([2, TB], GD)
        kb32 = A.alloc([2, TB], F32)
        kbeH = A.alloc([4, TB], GD)
        qdH = A.alloc([4, TB], GD)
        D0 = A.alloc([4, TB], F32)
        ktl = A.alloc([2, TB], F32)
        vbe = A.alloc([2, TB], F32)
        Bblk = Buf()
        tails = A.alloc([NI, 8], F32)
        def stepbufs(first=[True]):
            d_ = {}
            for nm, dt in (("G", F32), ("GT", F32), ("Gs", F32), ("GsT", F32), ("PA", GD), ("PB", GD), ("TA", GD),
                           ("TB", GD), ("attnT", GD), ("vbt", F32), ("ktail", GD), ("rp", F32), ("u", GD), ("Dcol", F32),
                           ("A32", F32), ("TB32", F32), ("TBt", F32), ("E32", F32), ("TBf", F32)):
                if not first[0] and nm in ("A32", "TB32", "TBt", "E32", "TBf", "G", "Gs"):
                    continue
                d_[nm] = A.alloc([NI, 64], dt)
                d_["B" + nm] = Buf()
            first[0] = False
            return d_
        SB0 = stepbufs()
        SB = [SB0, SB0]
        S32 = A.alloc([NI, 64], F32)
        Sbf = A.alloc([NI, 64], GD)
        BS32 = Buf()
        BSbf = Buf()

        def inst_list(d, step):
            res = []
            for sq_ in range(nseq):
                j = step if d == 0 else NCK - 1 - step
                for h in range(4):
                    t0 = sq_ * L + j * 64
                    res.append((sq_ * 4 + h, sq_, h, h // 2, (h % 2) * 64, slice(t0, t0 + 64), slice(t0 % TB, t0 % TB + 64), t0 // TB, (t0 % TB) // 64))
            res.sort(key=lambda r_: (r_[4], r_[0]))
            return res

        for d in range(2):
            NEGM = negm[0:64, d, :]
            NEGMT = negm[0:64, 1 - d, :]
            SM = smk[0:64, d, :]
            SMT = smk[0:64, 1 - d, :]
            P.barrier()
            for which in range(2):
                for hp in range(2):
                    for tb in range(NBLK):
                        cols = slice(tb * TB, (tb + 1) * TB)
                        bk = tb % 2
                        k.mm(k.bank(bk), gsel[0:16, which * 4 + d * 2 + hp, :], gatesT[0:16, cols], True, True, [Bwrep, Bgates], [PS[bk]])
                        if which == 0:
                            k.act(Dd[:, hp, cols], k.bank(bk), AF.Exp, [PS[bk], Bcols], [BD], bias=dtbS[:, d, hp:hp + 1])
                        else:
                            k.act(bet[:, hp, cols], k.bank(bk), AF.Sigmoid, [PS[bk]], [Bbet])
            for hp in range(2):
                k.act(Dd[:, hp, :], Dd[:, hp, :], AF.Ln, [BD], [BD], bias=ones_f[:, 0:1])
                k.ts(Dd[:, hp, :], Dd[:, hp, :], neaS[:, d, hp:hp + 1], None, ALU.mult, None, [BD, Bcols], [BD])
                P.op("dve", (lambda hp=hp: lambda e: e.tensor_tensor_scan(out=tmpf, data0=startm, data1=Dd[:, hp, :], initial=0.0,
                                                                           op0=ALU.mult, op1=ALU.add))(), [BD, Bcols], [Btmpf])
                if d == 0:
                    k.copy(Dd[:, hp, :], tmpf, [Btmpf], [BD], eng="dve")
                else:
                    t3 = tmpf.rearrange("p (n c) -> p n c", c=64)
                    d3 = Dd[:, hp, :].rearrange("p (n c) -> p n c", c=64)
                    k.tt(Dd[:, hp, :], Dd[:, hp, :], tmpf, ALU.subtract, [BD, Btmpf], [BD])
                    k.tt(d3, d3, bc(t3[:, :, 63:64], [128, T // 64, 64]), ALU.add, [BD, Btmpf], [BD])
            P.barrier()
            if 'dumpD' in (dbg or ()) and d == 1 and l == 0 and not lat:
                ddd = dout("dbg_D", [128, 2 * T])
                k.dma("sp", ddd, Dd.rearrange("p h t -> p (h t)"), [BD], [])
                ddq = dout("dbg_qk", [128, 4 * T], BF16)
                k.dma("sp", ddq[:, 0:2 * T], qn.rearrange("p h t -> p (h t)"), [Bqkv], [])
                k.dma("sp", ddq[:, 2 * T:4 * T], kn.rearrange("p h t -> p (h t)"), [Bqkv], [])
            k.memset(S32, 0.0, [BS32], eng="dve")
            if lat:
                for h in range(4):
                    k.dma("sp", S32[0:64, h, :], st_gdn[l, d, h], [], [BS32])
            k.copy(Sbf[0:64], S32[0:64], [BS32], [BSbf], eng="act")
            lastc = 63 if d == 0 else 0
            cur_blk = None
            for step in range(NCK):
                insts = inst_list(d, step)
                blk = insts[0][7]
                if blk != cur_blk:
                    cur_blk = blk
                    bc_ = slice(blk * TB, (blk + 1) * TB)
                    k.act(Eb, Dd[:, :, bc_], AF.Exp, [BD], [Bblk])
                    k.tt(kb32, kn[:, :, bc_], bet[:, :, bc_], ALU.mult, [Bqkv, Bbet], [Bblk])
                    k.copy(kbT, kb32, [Bblk], [Bblk], eng="act")
                    k.tt(kb32, kb32, Eb, ALU.mult, [Bblk], [Bblk])
                    for h in range(4):
                        pb = (h % 2) * 64
                        k.copy(kbeH[0:64, h, :], kb32[pb:pb + 64, h // 2, :], [Bblk], [Bblk])
                        k.copy(D0[0:64, h, :], Dd[pb:pb + 64, h // 2, bc_], [BD], [Bblk])
                        k.tt(qdH[0:64, h, :], qn[pb:pb + 64, h // 2, bc_], Eb[pb:pb + 64, h // 2, :], ALU.mult, [Bqkv, Bblk], [Bblk])
                    k.tt(vbe, vs[:, :, bc_], bet[:, :, bc_], ALU.mult, [Bqkv, Bbet], [Bblk])
                    d4 = Dd[:, :, bc_].rearrange("p h (n c) -> p h n c", c=64)
                    for hp in range(2):
                        k.tt(ktl[:, hp, :].rearrange("p (n c) -> p n c", c=64), bc(d4[:, hp, :, lastc:lastc + 1], [128, 8, 64]),
                             d4[:, hp], ALU.subtract, [BD], [Bblk])
                    k.act(ktl, ktl, AF.Exp, [Bblk], [Bblk])
                    k.tt(ktl, ktl, kn[:, :, bc_], ALU.mult, [Bblk, Bqkv], [Bblk])
                    e4 = Eb.rearrange("p h (n c) -> p h n c", c=64)
                    for sq_ in range(nseq):
                        for h in range(4):
                            pb = (h % 2) * 64
                            if lat:
                                k.copy(tails[0:64, h, 0:8], e4[pb:pb + 64, h // 2, :, lastc], [Bblk], [Bblk], eng="dve")
                            else:
                                k.copy(tails[0:64, sq_ * 4 + h, sq_ * 4:(sq_ + 1) * 4], e4[pb:pb + 64, h // 2, sq_ * 4:(sq_ + 1) * 4, lastc],
                                       [Bblk], [Bblk], eng="dve")
                W = SB[step % 2]
                NW_ = NI * 64
                v3 = lambda ap: ap[0:64, 0:NW_].rearrange("p (i c) -> p i c", c=64)
                for (ii, sq_, h, hp, pb, gc, bcl, _, cib) in insts:
                    ic = slice(ii * 64, (ii + 1) * 64)
                    k.tr(k.bank(0)[0:64, ic], D0[0:64, h, bcl], ident[0:64, 0:64], [Bblk, Bconst], [PS[0]])
                k.copy(W["Dcol"][0:64, :, 0:1], v3(k.bank(0))[:, :, 0:1], [PS[0]], [W["BDcol"]], eng="act")
                for (ii, sq_, h, hp, pb, gc, bcl, _, cib) in insts:
                    k.ts(W["G"][0:64, ii, :], D0[0:64, h, bcl], W["Dcol"][0:64, ii, 0:1], 0.0, ALU.subtract, ALU.max, [Bblk, W["BDcol"]], [W["BG"]])
                    k.ts(W["GT"][0:64, ii, :], D0[0:64, h, bcl], W["Dcol"][0:64, ii, 0:1], 0.0, ALU.subtract, ALU.min, [Bblk, W["BDcol"]], [W["BGT"]])
                k.act(W["G"][0:64], W["G"][0:64], AF.Exp, [W["BG"]], [W["BG"]], scale=-1.0)
                k.act(W["GT"][0:64], W["GT"][0:64], AF.Exp, [W["BGT"]], [W["BGT"]])
                k.tt(W["GT"][0:64], W["GT"][0:64], bc(NEGMT.unsqueeze(1), [64, NI, 64]), ALU.mult, [W["BGT"], Bcols], [W["BGT"]])
                k.tt(W["Gs"][0:64], W["G"][0:64], bc(SM.unsqueeze(1), [64, NI, 64]), ALU.mult, [W["BG"], Bcols], [W["BGs"]])
                k.tt(W["GsT"][0:64], W["GT"][0:64], bc(SMT.unsqueeze(1), [64, NI, 64]), ALU.mult, [W["BGT"], Bcols], [W["BGsT"]])
                for (ii, sq_, h, hp, pb, gc, bcl, _, cib) in insts:
                    ic = slice(ii * 64, (ii + 1) * 64)
                    kb_ = kbT[pb:pb + 64, hp, bcl]
                    k_ = kn[pb:pb + 64, hp, gc]
                    q_ = qn[pb:pb + 64, hp, gc]
                    k.mm(k.bank(2)[0:64, ic], kb_, k_, True, True, [Bblk, Bqkv], [PS[2]])
                    k.mm(k.bank(3)[0:64, ic], k_, kb_, True, True, [Bblk, Bqkv], [PS[3]])
                    k.mm(k.bank(4)[0:64, ic], k_, q_, True, True, [Bqkv], [PS[4]])
                k.tt(W["A32"][0:64], v3(k.bank(2)), W["Gs"][0:64], ALU.mult, [PS[2], W["BGs"]], [W["BA32"]])
                k.copy(W["PA"][0:64], W["A32"][0:64], [W["BA32"]], [W["BPA"]], eng="act")
                k.tt(W["PB"][0:64], v3(k.bank(3)), W["GsT"][0:64], ALU.mult, [PS[3], W["BGsT"]], [W["BPB"]])
                k.tt(W["attnT"][0:64], v3(k.bank(4)), W["GT"][0:64], ALU.mult, [PS[4], W["BGT"]], [W["BattnT"]])
                idb = bc(identb[0:64, :].unsqueeze(1), [64, NI, 64])
                k.tt(W["TB"][0:64], idb, W["PB"][0:64], ALU.subtract, [Bcols, W["BPB"]], [W["BTB"]])
                def sq_mm():
                    for ii in range(NI):
                        ic = slice(ii * 64, (ii + 1) * 64)
                        k.mm(k.bank(2)[0:64, ic], W["PB"][0:64, ii, :], W["PA"][0:64, ii, :], True, True, [W["BPA"], W["BPB"]], [PS[2]])
                        k.mm(k.bank(3)[0:64, ic], W["PA"][0:64, ii, :], W["PB"][0:64, ii, :], True, True, [W["BPA"], W["BPB"]], [PS[3]])

                def sq_cp():
                    k.copy(W["PA"][0:64], v3(k.bank(2)), [PS[2]], [W["BPA"]], eng="act")
                    k.copy(W["PB"][0:64], v3(k.bank(3)), [PS[3]], [W["BPB"]], eng="dve")

                def t_mm(itn):
                    for ii in range(NI):
                        ic = slice(ii * 64, (ii + 1) * 64)
                        k.mm(k.bank(5)[0:64, ic], W["PA"][0:64, ii, :], W["TB"][0:64, ii, :], True, True, [W["BPA"], W["BTB"]], [PS[5]])

                def t_add(itn):
                    k.tt(W["TB"][0:64], W["TB"][0:64], v3(k.bank(5)), ALU.add, [W["BTB"], PS[5]], [W["BTB"]])

                NIT = 4
                sq_mm()
                sq_cp()
                for itn in range(NIT):
                    t_mm(itn)
                    if itn < NIT - 1:
                        sq_mm()
                    t_add(itn)
                    if itn < NIT - 1:
                        sq_cp()
                k.copy(W["TB32"][0:64], W["TB"][0:64], [W["BTB"]], [W["BTB32"]], eng="act")
                for ii in range(NI):
                    ic = slice(ii * 64, (ii + 1) * 64)
                    k.tr(k.bank(2)[0:64, ic], W["TB32"][0:64, ii, :], ident[0:64, 0:64], [W["BTB32"], Bconst], [PS[2]])
                    k.mm(k.bank(3)[0:64, ic], W["A32"][0:64, ii, :], W["TB32"][0:64, ii, :], True, True, [W["BA32"], W["BTB32"]], [PS[3]])
                k.copy(W["TBt"][0:64], v3(k.bank(2)), [PS[2]], [W["BTBt"]], eng="act")
                k.tt(W["E32"][0:64], bc(ident2[0:64, :].unsqueeze(1), [64, NI, 64]), W["TB32"][0:64], ALU.subtract, [Bcols, W["BTB32"]], [W["BE32"]])
                k.tt(W["E32"][0:64], W["E32"][0:64], v3(k.bank(3)), ALU.subtract, [W["BE32"], PS[3]], [W["BE32"]])
                for ii in range(NI):
                    ic = slice(ii * 64, (ii + 1) * 64)
                    k.mm(k.bank(5)[0:64, ic], W["TBt"][0:64, ii, :], W["E32"][0:64, ii, :], True, True, [W["BTBt"], W["BE32"]], [PS[5]])
                k.copy(W["TBf"][0:64], v3(k.bank(5)), [PS[5]], [W["BTBf"]], eng="dve")
                for (ii, sq_, h, hp, pb, gc, bcl, _, cib) in insts:
                    ic = slice(ii * 64, (ii + 1) * 64)
                    k.tr(k.bank(7)[0:64, ic], vbe[pb:pb + 64, hp, bcl], ident[pb:pb + 64, pb:pb + 64], [Bblk, Bconst], [PS[7]])
                    k.tr(k.bank(4)[0:64, ic], ktl[pb:pb + 64, hp, bcl], ident[pb:pb + 64, pb:pb + 64], [Bblk, Bconst], [PS[4]])
                k.copy(W["vbt"][0:64], v3(k.bank(7)), [PS[7]], [W["Bvbt"]], eng="act")
                k.copy(W["ktail"][0:64], v3(k.bank(4)), [PS[4]], [W["Bktail"]], eng="dve")
                for (ii, sq_, h, hp, pb, gc, bcl, _, cib) in insts:
                    ic = slice(ii * 64, (ii + 1) * 64)
                    k.mm(k.bank(5)[0:64, ic], kbeH[0:64, h, bcl], Sbf[0:64, ii, :], True, True, [Bblk, BSbf], [PS[5]])
                k.tt(W["rp"][0:64], W["vbt"][0:64], v3(k.bank(5)), ALU.subtract, [W["Bvbt"], PS[5]], [W["Brp"]])
                for ii in range(NI):
                    ic = slice(ii * 64, (ii + 1) * 64)
                    k.mm(k.bank(6)[0:64, ic], W["TBf"][0:64, ii, :], W["rp"][0:64, ii, :], True, True, [W["BTBf"], W["Brp"]], [PS[6]])
                k.copy(W["u"][0:64], v3(k.bank(6)), [PS[6]], [W["Bu"]], eng="act")
                for (ii, sq_, h, hp, pb, gc, bcl, _, cib) in insts:
                    ic = slice(ii * 64, (ii + 1) * 64)
                    k.mm(k.bank(7)[0:64, ic], Sbf[0:64, ii, :], qdH[0:64, h, bcl], True, True, [BSbf, Bblk], [PS[7]])
                    k.mm(k.bank(3)[0:64, ic], W["u"][0:64, ii, :], W["attnT"][0:64, ii, :], True, True, [W["Bu"], W["BattnT"]], [PS[3]])
                    k.mm(k.bank(4)[0:64, ic], W["ktail"][0:64, ii, :], W["u"][0:64, ii, :], True, True, [W["Bktail"], W["Bu"]], [PS[4]])
                for sq_ in range(nseq):
                    for par in range(2):
                        pb = par * 64
                        ii0 = sq_ * 4 + par
                        gc = [r_[5] for r_ in insts if r_[1] == sq_ and r_[2] == par][0]
                        src7 = k.bank(7)[0:64, ii0 * 64:(ii0 + 3) * 64].rearrange("p (i c) -> p i c", c=64)[:, 0:3:2, :]
                        src3 = k.bank(3)[0:64, ii0 * 64:(ii0 + 3) * 64].rearrange("p (i c) -> p i c", c=64)[:, 0:3:2, :]
                        dst = ocT[pb:pb + 64, :, gc]
                        if d == 0:
                            k.copy(dst, src7, [PS[7]], [Boc], eng="act")
                        else:
                            k.tt(dst, dst, src7, ALU.add, [Boc, PS[7]], [Boc])
                        k.tt(dst, dst, src3, ALU.add, [Boc, PS[3]], [Boc])
                cib0 = insts[0][8] if lat else None
                if lat:
                    k.tt(S32[0:64], S32[0:64], bc(tails[0:64, :, cib0:cib0 + 1], [64, NI, 64]), ALU.mult, [BS32, Bblk], [BS32])
                else:
                    for sq_ in range(nseq):
                        cb = [r_[8] for r_ in insts if r_[1] == sq_][0]
                        k.tt(S32[0:64, sq_ * 4:(sq_ + 1) * 4, :], S32[0:64, sq_ * 4:(sq_ + 1) * 4, :],
                             bc(tails[0:64, sq_ * 4:(sq_ + 1) * 4, cb:cb + 1], [64, 4, 64]), ALU.mult, [BS32, Bblk], [BS32])
                k.tt(S32[0:64], S32[0:64], v3(k.bank(4)), ALU.add, [BS32, PS[4]], [BS32])
                k.copy(Sbf[0:64], S32[0:64], [BS32], [BSbf], eng="act")
            if not lat:
                for sq_ in range(nseq):
                    for h in range(4):
                        k.dma("sp", o_gdn[sq_, l, d, h], S32[0:64, sq_ * 4 + h, :], [BS32], [])
        P.barrier()
        A.top = alias_mark
        wGz = A.alloc([8, 256], BF16)
        BwGz = Buf()
        k.dma("sp", wGz, w_in_b[l].rearrange("(c p) n -> p c n", p=128)[:, :, C_GZ:C_GZ + 256], [Bcv[("w_in", l)]], [BwGz])
        gzt = A.alloc([TB], BF16)
        Bgzt = Buf()
        sqb = A.alloc([TB], BF16)
        Bsqb = Buf()
        rsn = A.alloc([TB], F32)
        Brsn = Buf()
        for hp in range(2):
            for tb in range(NBLK):
                cols = slice(tb * TB, (tb + 1) * TB)
                for c in range(8):
                    k.mm(k.bank(3), wGz[:, c, hp * 128:(hp + 1) * 128], hT[:, c, cols], c == 0, c == 7, [BwGz, BhT], [PS[3]])
                k.act(gzt, k.bank(3), AF.Silu, [PS[3]], [Bgzt])
                k.act(sqb, ocT[:, hp, cols], AF.Square, [Boc], [Bsqb])
                k.mm(k.bank(2), blockones, sqb, True, True, [Bsqb, Bcols], [PS[2]])
                k.act(rsn, k.bank(2), AF.Sqrt, [PS[2], Bconst], [Brsn], scale=1.0 / 64, bias=epsb)
                k.recip(rsn, rsn, [Brsn], [Brsn])
                k.stt(rsn, ocT[:, hp, cols], gnorm, rsn, ALU.mult, ALU.mult, [Boc, Brsn, Bcols], [Brsn])
                k.tt(oT[:, 4 + hp, cols], rsn, gzt, ALU.mult, [Brsn, Bgzt], [BoT])
        P.barrier()
        A.top = base

    mixer_base = [0]

    groups = [dict(name="ctx", tok0=0, T=512, nseq=2, L=256, kc=0),
              dict(name="lat", tok0=512, T=2048, nseq=1, L=2048, kc=1)]

    for l in range(nlayers):
        for g in groups:
            T = g["T"]
            kc = g["kc"]
            nblk = T // TB
            mA = A.top
            xblk = [A.alloc([8, TB], F32) for _ in range(2)]
            Bxb = [Buf(), Buf()]
            sq = A.alloc([8, TB], BF16)
            rstd = A.alloc([TB], F32)
            tmp = A.alloc([8, TB], F32)
            Wn = (sq, Buf(), rstd, Buf(), tmp, Buf())
            wg32 = A.alloc([8, 16], F32)
            Bwg32 = Buf()
            k.dma("sp", wg32, w_in[l].rearrange("(c p) n -> p c n", p=128)[:, :, C_GA:C_GA + 16], [], [Bwg32], slow=True)
            for tb in range(nblk):
                gb = (g["tok0"] // TB) + tb
                s = tb % 2
                k.dma("sp", xblk[s], xTv[:, :, gb * TB:(gb + 1) * TB], [BxT[gb]], [Bxb[s]])
                prenorm(xblk[s], Bxb[s], hT, BhT, slice(tb * TB, (tb + 1) * TB), A1, 0, l, kc, Wn, gate=(wg32, Bwg32))
            P.barrier()
            A.top = mA
            if stop == "A":
                P.emit(); ES.close(); return nc, P
            if stub_mixer:
                k.copy(oT[:, :, 0:T], hT[:, :, 0:T], [BhT], [BoT], eng="dve")
            else:
                mB = A.top
                mixer_base[0] = mB
                if "noattn" not in (dbg or ()):
                    mixer_attn(l, g)
                    P.barrier()
                A.top = mB
                if "nohy" not in (dbg or ()):
                    mixer_hyena(l, g)
                if "nogdn" not in (dbg or ()):
                    mixer_gdn(l, g)
                if 'oT' in (dbg or ()) and l == 0:
                    dd = dout("dbg_oT_" + g["name"], [8, 128, T])
                    dtmp = A.alloc([T], F32)
                    Bd = Buf()
                    for c in range(8):
                        k.copy(dtmp, oT[:, c, 0:T], [BoT], [Bd], eng="dve")
                        k.dma("sp", dd[c], dtmp, [Bd], [])
                    P.barrier()
                    A.top = mB
                if stop == "B" + g["name"][0]:
                    P.emit(); ES.close(); return nc, P
            P.barrier()
            mC = A.top
            wout = A.alloc([8, D], BF16)
            Bwout = Buf()
            k.dma("sp", wout, w_out_b[l].rearrange("(c p) n -> p c n", p=128), [Bcv[("w_out", l)]], [Bwout])
            xblk = A.alloc([8, TB], F32)
            Bxb = Buf()
            xn = xblk
            Bxn = Bxb
            mT = A.alloc([8, TB], F32)
            BmT = Buf()
            sq = A.alloc([8, TB], BF16)
            Bsq = Buf()
            rstd = A.alloc([TB], F32)
            Brstd = Buf()
            h2T = A.alloc([8, TB], BF16)
            Bh2 = Buf()
            fT = hT.rearrange("p a b -> p (a b)").rearrange("p (f t) -> p f t", f=32)
            BfT = BhT
            rl = [A.alloc([TB], BF16) for _ in range(2)]
            Brl = [Buf(), Buf()]
            w1s = [A.alloc([8, 512], BF16) for _ in range(2)]
            Bw1 = [Buf(), Buf()]
            w2s = [A.alloc([4, D], BF16) for _ in range(2)]
            Bw2 = [Buf(), Buf()]
            w1v = w1_b[l].rearrange("(c p) n -> p c n", p=128)
            w2v = w2_b[l].rearrange("(c p) n -> p c n", p=128)

            def post(Bsrc_banks, Bres, res, Bsc, l, kc, dstx, Bdstx):
                for c in range(8):
                    k.copy(mT[:, c, :], k.bank(c), [PS[c]], [BmT])
                k.act(sq, mT, AF.Square, [BmT], [Bsq])
                if stop == "P1":
                    raise StopBuild()
                rstd_from_sq(sq, Bsq, 0, rstd, Brstd)
                if stop == "P2":
                    raise StopBuild()
                k.tt(mT, mT, bc(rstd.unsqueeze(1), [128, 8, TB]), ALU.mult, [BmT, Brstd], [BmT])
                if stop == "P3":
                    raise StopBuild()
                for c in range(8):
                    k.stt(dstx[:, c, :], mT[:, c, :], Bsc[:, l, c, kc:kc + 1], res[:, c, :], ALU.mult, ALU.add,
                          [BmT, Bmod, Bres], [Bdstx])

            for tb in range(nblk):
                gb = (g["tok0"] // TB) + tb
                cols = slice(tb * TB, (tb + 1) * TB)
                k.dma("sp", xblk, xTv[:, :, gb * TB:(gb + 1) * TB], [BxT[gb]], [Bxb])
                if stop == "C0a":
                    P.emit(); ES.close(); return nc, P
                for dc in range(8):
                    for ec in range(8):
                        k.mm(k.bank(dc), wout[:, ec, dc * 128:(dc + 1) * 128], oT[:, ec, cols], ec == 0, ec == 7,
                             [Bwout, BoT], [PS[dc]])
                if stop == "C0b":
                    P.emit(); ES.close(); return nc, P
                try:
                    post(None, Bxb, xblk, B1, l, kc, xn, Bxn)
                except StopBuild:
                    P.emit(); ES.close(); return nc, P
                if stop == "C1":
                    P.emit(); ES.close(); return nc, P
                Wn = (sq, Bsq, rstd, Brstd, mT, BmT)
                prenorm(xn, Bxn, h2T, Bh2, slice(0, TB), A2, 24, l, kc, Wn)
                for s8 in range(8):
                    ws = w1s[s8 % 2]
                    bw = Bw1[s8 % 2]
                    k.dma("sp", ws, w1v[:, :, s8 * 512:(s8 + 1) * 512], [Bcv[("w1", l)]], [bw])
                    for fj in range(4):
                        fc = s8 * 4 + fj
                        bk = 1 + (fc % 4)
                        for c in range(8):
                            k.mm(k.bank(bk), ws[:, c, fj * 128:(fj + 1) * 128], h2T[:, c, :], c == 0, c == 7,
                                 [bw, Bh2], [PS[bk]])
                        r = rl[fc % 2]
                        br = Brl[fc % 2]
                        k.act(r, k.bank(bk), AF.Relu, [PS[bk]], [br])
                        k.tt(fT[:, fc, :], r, r, ALU.mult, [br], [BfT])
                if stop == "C2":
                    P.emit(); ES.close(); return nc, P
                for s8 in range(8):
                    ws = w2s[s8 % 2]
                    bw = Bw2[s8 % 2]
                    k.dma("sp", ws, w2v[:, s8 * 4:(s8 + 1) * 4, :], [Bcv[("w2", l)]], [bw])
                    for fj in range(4):
                        fc = s8 * 4 + fj
                        for dc in range(8):
                            k.mm(k.bank(dc), ws[:, fj, dc * 128:(dc + 1) * 128], fT[:, fc, :], fc == 0, fc == 31,
                                 [bw, BfT], [PS[dc]])
                post(None, Bxn, xn, B2, l, kc, xblk, Bxb)
                k.dma("sp", xTv[:, :, gb * TB:(gb + 1) * TB], xblk, [Bxb], [BxT[gb]])
            P.barrier()
            A.top = mC

    xi = [A.alloc([8, 128], F32) for _ in range(2)]
    yo = [A.alloc([D], F32) for _ in range(2)]
    Bxi = [Buf(), Buf()]
    Byo = [Buf(), Buf()]
    for i in range(TT // 128):
        s = i % 2
        k.dma("sp", xi[s], xTv[:, :, i * 128:(i + 1) * 128], [BxT[i // 4]], [Bxi[s]])
        for c in range(8):
            bk = 2 * s + c // 4
            k.tr(k.bank(bk)[:, (c % 4) * 128:(c % 4 + 1) * 128], xi[s][:, c, :], ident, [Bxi[s], Bconst], [PS[bk]])
        for hh in range(2):
            bk = 2 * s + hh
            k.copy(yo[s][:, hh * 512:(hh + 1) * 512], k.bank(bk), [PS[bk]], [Byo[s]])
        k.dma("sp", y_tok[i * 128:(i + 1) * 128, :], yo[s], [Byo[s]], [])
    P.emit()
    ES.close()
    return nc, P


def _rope_tables(dim, nrows_total, row0):
    rows = 2048 // 64
    row = np.repeat(np.arange(rows), 64).astype(np.float32)
    col = np.tile(np.arange(64), rows).astype(np.float32)
    nf = dim // 4
    inv = (10000.0 ** (-np.arange(nf, dtype=np.float32) / nf)).astype(np.float32)
    ang = np.concatenate([row[:, None] * inv, col[:, None] * inv], -1).astype(np.float32)
    cos = np.cos(ang).astype(np.float32)
    sin = np.sin(ang).astype(np.float32)
    out = np.zeros((2, nrows_total, 2048), np.float32)
    for f in range(dim):
        out[0, row0 + f] = cos[:, f // 2]
        out[1, row0 + f] = sin[:, f // 2] * (-1.0 if f % 2 == 0 else 1.0)
    return out


_CONST = {}


def _consts():
    if _CONST:
        return _CONST
    _CONST["ropeA"] = _rope_tables(32, 128, 64)
    _CONST["ropeB"] = _rope_tables(64, 64, 0)
    ps = np.zeros((128, 128), np.float32)
    for m in range(128):
        ps[m ^ 1, m] = 1.0
    _CONST["pswap"] = ps
    j = np.arange(128)[:, None]
    i = np.arange(128)[None, :]
    _CONST["masks"] = np.stack([(j >= i), (j <= i)], 0).astype(np.float32).astype(ml_dtypes.bfloat16)
    r = np.arange(64)[:, None]
    c_ = np.arange(64)[None, :]
    _CONST["negm"] = np.stack([(c_ <= r), (c_ >= r)], 0).astype(np.float32)
    sel = np.zeros((16, 8, 128), np.float32)
    for which in range(2):
        for d_ in range(2):
            for hp in range(2):
                for p in range(128):
                    sel[which * 8 + d_ * 4 + 2 * hp + (p // 64), which * 4 + d_ * 2 + hp, p] = 1.0
    _CONST["gsel"] = sel
    _CONST["smask"] = np.stack([(c_ < r), (c_ > r)], 0).astype(np.float32)
    for L_, sfx in ((256, "s"), (2048, "b")):
        N = 2 * L_
        idx = np.arange(L_ + 1, dtype=np.int64)
        th = 2.0 * np.pi * ((idx[:, None] * idx[None, :]) % N).astype(np.float64) / N
        _CONST["dft_" + sfx] = np.stack([np.cos(th), np.sin(th)], 0).astype(np.float32).astype(ml_dtypes.bfloat16)
        t = np.arange(L_, dtype=np.float32)
        t01 = t / np.float32(max(L_ - 1, 1))
        w = np.float32(2.0 * math.pi) * t / np.float32(L_)
        bands = np.linspace(1e-4, 7, 8, dtype=np.float32)
        feats = np.concatenate([t01[:, None], np.cos(w[:, None] * bands), -np.sin(w[:, None] * bands)], -1).astype(np.float32)
        _CONST["feats_" + sfx] = np.ascontiguousarray(feats.T)
        dist = (np.abs(t - (L_ // 2)) / np.float32(L_ / 2)).astype(np.float32)
        _CONST["negdist_" + sfx] = np.ascontiguousarray((-dist).reshape(L_ // 128, 128).T)
        nf = L_ // 128 + 1
        f = (np.arange(nf)[None, :] * 128 + np.arange(128)[:, None])
        wfn = np.where((f == 0) | (f == L_), 1.0, 2.0) / N
        alpha = np.array([1.0, 0.0, -1.0, 0.0])[f % 4]
        beta = np.array([0.0, 1.0, 0.0, -1.0])[f % 4]
        _CONST["aw_" + sfx] = (alpha * wfn).astype(np.float32)
        _CONST["bw_" + sfx] = (beta * wfn).astype(np.float32)
    return _CONST


def core_inputs(inp, i):
    b = i % 4
    f = lambda a: np.ascontiguousarray(a, dtype=np.float32)
    d = dict(
        x_tok=f(np.concatenate([inp["x_prompt"][2 * i], inp["x_prompt"][2 * i + 1], inp["x_sample"][b]], 0)),
        conds=f(np.stack([inp["c_ctx"], inp["c"][b]], 0)),
        w_ada=f(inp["w_ada"]), b_ada=f(inp["b_ada"]),
        gvec=f(np.stack([inp["g_pre_mix"], inp["g_post_mix"], inp["g_pre_mlp"], inp["g_post_mlp"]], 0)),
        w_in=f(inp["w_in"]), w_out=f(inp["w_out"]), mlp_w1=f(inp["mlp_w1"]), mlp_w2=f(inp["mlp_w2"]),
        mla_kv_norm=f(inp["mla_kv_norm"]), mla_w_ukv=f(inp["mla_w_ukv"]), swa_sink=f(inp["swa_sink"]),
        c_ckv=f(inp["cache_mla_ckv"][b]), c_kpe=f(inp["cache_mla_kpe"][b]),
        c_swk=f(inp["cache_swa_k"][b].reshape(DEPTH, 512, 128)), c_swv=f(inp["cache_swa_v"][b].reshape(DEPTH, 512, 128)),
        gdn_conv=f(inp["gdn_conv"]), gdn_a_log=f(inp["gdn_a_log"].reshape(DEPTH, 8)), gdn_dt_bias=f(inp["gdn_dt_bias"].reshape(DEPTH, 8)),
        gdn_norm=f(inp["gdn_norm"]), st_gdn=f(inp["state_gdn"][b]),
        hy_conv=f(inp["hy_conv"]), hy_w1=f(inp["hy_w1"]), hy_b1=f(inp["hy_b1"]), hy_w2=f(inp["hy_w2"]), hy_b2=f(inp["hy_b2"]),
        hy_w3=f(inp["hy_w3"]), hy_freq=f(inp["hy_freq"]), hy_decay=f(inp["hy_decay"]), hy_bias=f(inp["hy_bias"]),
    )
    d.update(_consts())
    return d


_CACHE = {}


def kernel(**inputs):
    inp = {k_: np.asarray(v) for k_, v in inputs.items()}
    if "nc" not in _CACHE:
        _CACHE["nc"] = build_program()[0]
    nc = _CACHE["nc"]
    in_maps = [core_inputs(inp, i) for i in range(8)]
    res = run_bass_kernel_spmd(nc, in_maps, core_ids=list(range(8)))
    R = res.results
    y_prompt = np.stack([R[i]["y_tok"][s_ * 256:(s_ + 1) * 256] for i in range(8) for s_ in range(2)], 0).astype(np.float32)
    y_sample = np.stack([R[b]["y_tok"][512:2560] for b in range(4)], 0).astype(np.float32)
    cat = lambda nm: np.concatenate([R[i][nm] for i in range(8)], 0).astype(np.float32)
    new_ckv = cat("o_ckv")
    new_kpe = cat("o_kpe")
    new_k = cat("o_swk").reshape(16, DEPTH, 256, 2, 64)
    new_v = cat("o_swv").reshape(16, DEPTH, 256, 2, 64)
    new_st = cat("o_gdn")
    return (y_prompt, y_sample, new_ckv, new_kpe, new_k, new_v, new_st)
```

```python
import math
import contextlib
import numpy as np
import ml_dtypes
import concourse.bass as bass
import concourse.mybir as mybir
from concourse.bass_utils import run_bass_kernel_spmd

F32 = mybir.dt.float32
BF16 = mybir.dt.bfloat16
AF = mybir.ActivationFunctionType
ALU = mybir.AluOpType

COMPUTE = ("pe", "act", "dve", "pool")
QUEUES = ("pe", "act", "dve", "pool", "sp")
NDMASEM = 12

D = 1024
NCH = 8
TT = 2560
TB = 512
DEPTH = 2
IN_COLS = 2864
EPS = 1e-6
C_MQ, C_CKV, C_KPE, C_SQ, C_SK, C_SV, C_GQKV, C_GZ, C_GA, C_GB, C_HU = (
    0, 384, 512, 544, 800, 928, 1056, 1824, 2080, 2088, 2096)


class Buf:
    __slots__ = ("w", "r", "excl", "rg")

    def __init__(self, excl=False):
        self.w = []
        self.r = []
        self.excl = excl
        self.rg = None


class Prog:
    def __init__(self, nc):
        self.nc = nc
        self.ops = []
        self.last = {q: None for q in QUEUES}
        self.dmas_since = []
        self.force = set()

    def op(self, eng, fn, reads=(), writes=(), dma=False, pe_force=False, bg=False):
        writes = list(writes) + [b for b in reads if b.excl]
        reads = [b for b in reads if not b.excl]
        deps = set()
        for b in reads:
            deps.update(b.w)
        for b in writes:
            if dma and not b.r:
                deps.update(w for w in b.w if not self.ops[w][3])
            else:
                deps.update(b.w)
            deps.update(b.r)
        oid = len(self.ops)
        self.ops.append((eng, fn, sorted(deps), dma))
        if pe_force:
            self.force.add(oid)
        for b in reads:
            b.r.append(oid)
        for b in writes:
            if dma and not b.r and b.w and all(self.ops[w][3] for w in b.w):
                b.w = b.w + [oid]
            else:
                b.w = [oid]
            b.r = []
        if dma:
            if not bg:
                self.dmas_since.append(oid)
        else:
            self.last[eng] = oid
        return oid

    def barrier(self):
        lasts = [v for v in self.last.values() if v is not None]
        deps = sorted(set(lasts + self.dmas_since))
        self.dmas_since = []
        for q in QUEUES:
            self.ops.append((q, None, deps, False))

    def emit(self):
        nc = self.nc
        ops = self.ops
        all_dma = [i for i, o in enumerate(ops) if o[3]]
        ops.append(("sp", None, all_dma, False))
        n = len(ops)
        eng_idx = [0] * n
        cnt = {q: 0 for q in QUEUES}
        for i, o in enumerate(ops):
            cnt[o[0]] += 1
            eng_idx[i] = cnt[o[0]]
        clock = {q: ({c: 0 for c in QUEUES}, set()) for q in QUEUES}
        opclock = [None] * n
        waits = [None] * n
        needed = [False] * n
        for i, (eng, fn, deps, isdma) in enumerate(ops):
            ck, dset = clock[eng]
            w = []
            for d in deps:
                deng, dfn, _, disdma = ops[d]
                if disdma:
                    if d in dset:
                        continue
                    w.append(d)
                    needed[d] = True
                    dset.add(d)
                else:
                    if dfn is None:
                        continue
                    if deng == eng and eng == "pe" and i not in self.force:
                        continue
                    if ck[deng] >= eng_idx[d]:
                        continue
                    w.append(d)
                    needed[d] = True
                    ck[deng] = eng_idx[d]
                ock = opclock[d]
                for c in QUEUES:
                    if ock[c] > ck[c]:
                        ck[c] = ock[c]
            waits[i] = w
            opclock[i] = dict(ck)
        stack = contextlib.ExitStack()
        sems = {q: stack.enter_context(nc.semaphore("s_" + q)) for q in COMPUTE}
        dsem = {q: [stack.enter_context(nc.semaphore("d_%s_%d" % (q, j))) for j in range(NDMASEM)]
                for q in QUEUES}
        sigval = [None] * n
        ccount = {q: 0 for q in COMPUTE}
        dcount = {q: 0 for q in QUEUES}
        dslot_val = {q: [0] * NDMASEM for q in QUEUES}
        pre_wait = [None] * n
        for i, (eng, fn, deps, isdma) in enumerate(ops):
            if isdma:
                k = dcount[eng] % NDMASEM
                dcount[eng] += 1
                if dslot_val[eng][k] > 0:
                    pre_wait[i] = (dsem[eng][k], dslot_val[eng][k])
                dslot_val[eng][k] += 16
                sigval[i] = (dsem[eng][k], dslot_val[eng][k])
            elif needed[i]:
                ccount[eng] += 1
                sigval[i] = (sems[eng], ccount[eng])
        per = {q: [] for q in QUEUES}
        for i, o in enumerate(ops):
            per[o[0]].append(i)
        self.stats = {q: len(per[q]) for q in QUEUES}
        self.stats["waits"] = sum(len(w) for w in waits)
        with nc.Block() as block:
            def mk(q):
                def body(e):
                    for i in per[q]:
                        eng, fn, deps, isdma = ops[i]
                        if pre_wait[i] is not None:
                            e.wait_ge(pre_wait[i][0], pre_wait[i][1])
                        for d in waits[i]:
                            e.wait_ge(sigval[d][0], sigval[d][1])
                        if fn is None:
                            continue
                        ins = fn(e)
                        if isdma:
                            ins.then_inc(sigval[i][0], 16)
                        elif needed[i]:
                            ins.then_inc(sigval[i][0], 1)
                return body
            block.tensor(mk("pe"))
            block.scalar(mk("act"))
            block.vector(mk("dve"))
            block.gpsimd(mk("pool"))
            block.sync(mk("sp"))
        stack.close()


class StopBuild(Exception):
    pass


class Arena:
    def __init__(self, tile, nbytes):
        self.t = tile
        self.n = nbytes
        self.top = 0

    def alloc(self, shape, dt):
        n = int(np.prod(shape))
        nb = n * (2 if dt == BF16 else 4)
        off = self.top
        self.top = off + (nb + 63) // 64 * 64
        assert self.top <= self.n, ("SBUF arena overflow", self.top, self.n)
        if dt == BF16:
            v = self.t[:, off // 2: off // 2 + n]
        else:
            v = self.t[:, off // 2: off // 2 + 2 * n].bitcast(dt)
        if len(shape) == 2:
            v = v.rearrange("p (a b) -> p a b", a=shape[0])
        elif len(shape) == 3:
            v = v.rearrange("p (a b c) -> p a b c", a=shape[0], b=shape[1])
        return v


class K:
    def __init__(self, nc, P, arena, psum):
        self.nc = nc
        self.P = P
        self.A = arena
        self.psum = psum
        self.PS = [Buf(excl=True) for _ in range(8)]
        self.rr = 0

    def bank(self, i):
        return self.psum[:, i * 512:(i + 1) * 512]

    def _rg(self, stat, W):
        key = (stat.base_partition(), stat.shape[0])
        force = False
        for b in W:
            if b.excl:
                if b.rg is not None and b.rg != key:
                    force = True
                b.rg = key
        return force

    def mm(self, out, lhsT, rhs, start, stop, R, W):
        f = self._rg(lhsT, W)
        self.P.op("pe", lambda e: e.matmul(out, lhsT=lhsT, rhs=rhs, start=start, stop=stop), R, W, pe_force=f)

    def tr(self, out, in_, ident, R, W):
        f = self._rg(in_, W)
        self.P.op("pe", lambda e: e.transpose(out, in_, ident), R, W, pe_force=f)

    def act(self, out, in_, func, R, W, scale=None, bias=None):
        kw = {}
        if scale is not None:
            kw["scale"] = scale
        if bias is not None:
            kw["bias"] = bias
        self.P.op("act", lambda e: e.activation(out=out, in_=in_, func=func, **kw), R, W)

    def tt(self, out, in0, in1, op, R, W, eng="dve"):
        self.P.op(eng, lambda e: e.tensor_tensor(out=out, in0=in0, in1=in1, op=op), R, W)

    def ts(self, out, in0, s1, s2, op0, op1, R, W, eng="dve"):
        if op1 is None:
            self.P.op(eng, lambda e: e.tensor_scalar(out=out, in0=in0, scalar1=s1, scalar2=None, op0=op0), R, W)
        else:
            self.P.op(eng, lambda e: e.tensor_scalar(out=out, in0=in0, scalar1=s1, scalar2=s2, op0=op0, op1=op1), R, W)

    def stt(self, out, in0, scalar, in1, op0, op1, R, W):
        self.P.op("dve", lambda e: e.scalar_tensor_tensor(out=out, in0=in0, scalar=scalar, in1=in1, op0=op0, op1=op1), R, W)

    def copy(self, out, in_, R, W, eng=None):
        if eng is None:
            self.rr ^= 1
            eng = "dve" if self.rr else "act"
        if eng == "act":
            self.P.op("act", lambda e: e.activation(out=out, in_=in_, func=AF.Copy), R, W)
        else:
            self.P.op(eng, lambda e: e.tensor_copy(out=out, in_=in_), R, W)

    def recip(self, out, in_, R, W):
        self.P.op("dve", lambda e: e.reciprocal(out=out, in_=in_), R, W)

    def memset(self, out, val, W, eng="pool"):
        self.P.op(eng, lambda e: e.memset(out, val), (), W)

    def dma(self, q, out, in_, R, W, slow=False, bg=False):
        if bg:
            self.P.op(q, lambda e: e.dma_start(out=out, in_=in_), R, W, dma=True, bg=True)
        elif slow:
            self.P.op(q, lambda e: e.dma_start(out=out, in_=in_, allow_slow_non_contiguous=True), R, W, dma=True)
        else:
            self.P.op(q, lambda e: e.dma_start(out=out, in_=in_), R, W, dma=True)


def bc(ap, shape):
    return ap.to_broadcast(shape)


def build_program(dbg=None, stub_mixer=False, nlayers=DEPTH, stop=None):
    nc = bass.Bass("TRN2", target_bir_lowering=False)
    P = Prog(nc)
    ES = contextlib.ExitStack()

    def din(name, shape, dt=F32):
        return nc.dram_tensor(name, list(shape), dt, kind="ExternalInput").ap()

    def dout(name, shape, dt=F32):
        return nc.dram_tensor(name, list(shape), dt, kind="ExternalOutput").ap()

    def dscr(name, shape, dt=F32):
        return nc.dram_tensor(name, list(shape), dt, kind="Internal").ap()

    x_tok = din("x_tok", [TT, D])
    conds = din("conds", [2, D])
    w_ada = din("w_ada", [DEPTH, D, 6 * D])
    b_ada = din("b_ada", [DEPTH, 6 * D])
    gvec = din("gvec", [4, DEPTH, D])
    w_in = din("w_in", [DEPTH, D, IN_COLS])
    w_out = din("w_out", [DEPTH, D, D])
    mlp_w1 = din("mlp_w1", [DEPTH, D, 4 * D])
    mlp_w2 = din("mlp_w2", [DEPTH, 4 * D, D])
    y_tok = dout("y_tok", [TT, D])
    mla_kv_norm = din("mla_kv_norm", [DEPTH, 128])
    mla_w_ukv = din("mla_w_ukv", [DEPTH, 128, 512])
    swa_sink = din("swa_sink", [DEPTH, 4])
    c_ckv = din("c_ckv", [DEPTH, 512, 128])
    c_kpe = din("c_kpe", [DEPTH, 512, 32])
    c_swk = din("c_swk", [DEPTH, 512, 128])
    c_swv = din("c_swv", [DEPTH, 512, 128])
    ropeA = din("ropeA", [2, 128, 2048])
    ropeB = din("ropeB", [2, 64, 2048])
    pswap_d = din("pswap", [128, 128])
    masks_d = din("masks", [2, 128, 128], BF16)
    hy_conv = din("hy_conv", [DEPTH, 3, 768])
    hy_w1 = din("hy_w1", [DEPTH, 17, 64])
    hy_b1 = din("hy_b1", [DEPTH, 64])
    hy_w2 = din("hy_w2", [DEPTH, 64, 64])
    hy_b2 = din("hy_b2", [DEPTH, 64])
    hy_w3 = din("hy_w3", [DEPTH, 64, 512])
    hy_freq = din("hy_freq", [DEPTH, 2, 64])
    hy_decay = din("hy_decay", [DEPTH, 512])
    hy_bias = din("hy_bias", [DEPTH, 2, 256])
    HY = {}
    for L_, sfx in ((256, "s"), (2048, "b")):
        HY[L_] = dict(tab=din("dft_" + sfx, [2, L_ + 1, L_ + 1], BF16), feats=din("feats_" + sfx, [17, L_]),
                      negdist=din("negdist_" + sfx, [128, L_ // 128]), aw=din("aw_" + sfx, [128, L_ // 128 + 1]),
                      bw=din("bw_" + sfx, [128, L_ // 128 + 1]))
    gdn_conv = din("gdn_conv", [DEPTH, 3, 768])
    gdn_a_log = din("gdn_a_log", [DEPTH, 8])
    gdn_dt_bias = din("gdn_dt_bias", [DEPTH, 8])
    gdn_norm = din("gdn_norm", [DEPTH, 64])
    st_gdn = din("st_gdn", [DEPTH, 2, 4, 64, 64])
    negm_d = din("negm", [2, 64, 64])
    sel_d = din("gsel", [16, 8, 128])
    sm_d = din("smask", [2, 64, 64])
    o_gdn = dout("o_gdn", [2, DEPTH, 2, 4, 64, 64])
    o_ckv = dout("o_ckv", [2, DEPTH, 256, 128])
    o_kpe = dout("o_kpe", [2, DEPTH, 256, 32])
    o_swk = dout("o_swk", [2, DEPTH, 256, 128])
    o_swv = dout("o_swv", [2, DEPTH, 256, 128])
    xT = dscr("xT", [D, TT])
    w_in_b = dscr("w_in_b", [DEPTH, D, IN_COLS], BF16)
    w_out_b = dscr("w_out_b", [DEPTH, D, D], BF16)
    w1_b = dscr("w1_b", [DEPTH, D, 4 * D], BF16)
    w2_b = dscr("w2_b", [DEPTH, 4 * D, D], BF16)
    Bcv = {}
    xTv = xT.rearrange("(c p) t -> p c t", p=128)
    dbg_out = {}

    ARENA_BYTES = 206 * 1024
    arena_t = ES.enter_context(nc.sbuf_tensor("arena", [128, ARENA_BYTES // 2], BF16))
    psum = ES.enter_context(nc.psum_tensor("psum", [128, 4096], F32))
    A = Arena(arena_t, ARENA_BYTES)
    k = K(nc, P, A, psum)
    PS = k.PS

    ident = A.alloc([128], F32)
    Bconst = Buf()
    k.memset(ident, 1.0, [Bconst])
    P.op("pool", lambda e: e.affine_select(out=ident, in_=ident, pattern=[[-1, 128]], compare_op=ALU.is_equal,
                                           fill=0.0, base=0, channel_multiplier=1), [Bconst], [Bconst])
    ones_b = A.alloc([128], BF16)
    k.memset(ones_b, 1.0, [Bconst])
    ones_f = A.alloc([128], F32)
    k.memset(ones_f, 1.0, [Bconst])

    pswapF = A.alloc([128], F32)
    k.dma("sp", pswapF, pswap_d, [], [Bconst])
    masks = A.alloc([2, 128], BF16)
    k.dma("sp", masks, masks_d.rearrange("m p q -> p m q"), [], [Bconst])
    kvg = A.alloc([DEPTH], F32)
    k.dma("sp", kvg, mla_kv_norm.rearrange("l p -> p l"), [], [Bconst], slow=True)
    sinkE = A.alloc([DEPTH * 4], F32)
    k.dma("sp", sinkE, swa_sink.rearrange("l h -> (l h)").partition_broadcast(128), [], [Bconst], slow=True)
    k.act(sinkE, sinkE, AF.Exp, [Bconst], [Bconst])
    modT = A.alloc([DEPTH, 48, 2], F32)
    A1 = A.alloc([DEPTH, 8, 2], F32)
    B1 = A.alloc([DEPTH, 8, 2], F32)
    A2 = A.alloc([DEPTH, 8, 2], F32)
    B2 = A.alloc([DEPTH, 8, 2], F32)
    Bmod = Buf()
    mark0 = A.top
    stgA = A.alloc([128], F32)
    stgB = A.alloc([128], F32)
    TAc = A.alloc([128], F32)
    TBc = A.alloc([128], F32)
    scT = A.alloc([8, 2], BF16)
    Bc_ = Buf()
    Bstg = Buf()
    k.memset(stgA, 0.0, [Bstg])
    k.memset(stgB, 0.0, [Bstg])
    k.dma("sp", stgA[0:16, :], conds.rearrange("k (c p) -> (k c) p", p=128), [], [Bstg])
    k.dma("sp", stgA[16:80, :], gvec.rearrange("g l (c p) -> (g l c) p", p=128), [], [Bstg])
    k.dma("sp", stgB[0:96, :], b_ada.rearrange("l (j p) -> (l j) p", p=128), [], [Bstg])
    k.tr(k.bank(7)[:, 0:128], stgA, ident, [Bstg, Bconst], [PS[7]])
    k.tr(k.bank(7)[:, 128:256], stgB, ident, [Bstg, Bconst], [PS[7]])
    k.copy(TAc, k.bank(7)[:, 0:128], [PS[7]], [Bc_], eng="dve")
    k.copy(TBc, k.bank(7)[:, 128:256], [PS[7]], [Bc_], eng="dve")
    gT = TAc[:, 16:80].rearrange("p (g l c) -> p g l c", g=4, l=2)
    badaT = TBc[:, 0:96].rearrange("p (l j) -> p l j", l=2)
    k.act(scT, TAc[:, 0:16].rearrange("p (k c) -> p c k", k=2), AF.Silu, [Bc_], [Bc_])
    NPIECE = 8
    PW = 6 * D // NPIECE
    wa = [A.alloc([8, PW], BF16) for _ in range(2)]
    Bwa = [Buf(), Buf()]
    for l in range(DEPTH):
        for pc in range(NPIECE):
            t = wa[pc % 2]
            bt = Bwa[pc % 2]
            k.dma("pool", t, w_ada[l].rearrange("(c p) n -> p c n", p=128)[:, :, pc * PW:(pc + 1) * PW], [], [bt])
            for jj in range(6):
                j = pc * 6 + jj
                for c in range(8):
                    k.mm(k.bank(l)[:, j * 2:(j + 1) * 2], t[:, c, jj * 128:(jj + 1) * 128], scT[:, c, :],
                         c == 0, c == 7, [bt, Bc_], [PS[l]])
        k.tt(modT[:, l], k.bank(l)[:, 0:96].rearrange("p (j k) -> p j k", k=2),
             bc(badaT[:, l].unsqueeze(2), [128, 48, 2]), ALU.add, [PS[l], Bc_], [Bmod])
        for (dst, gi, j0, plus1) in ((A1, 0, 8, True), (B1, 1, 16, False), (A2, 2, 32, True), (B2, 3, 40, False)):
            gb = bc(gT[:, gi, l].unsqueeze(2), [128, 8, 2])
            if plus1:
                k.stt(dst[:, l], modT[:, l, j0:j0 + 8, :], 1.0, gb, ALU.add, ALU.mult, [Bmod, Bc_], [Bmod])
            else:
                k.tt(dst[:, l], modT[:, l, j0:j0 + 8, :], gb, ALU.mult, [Bmod, Bc_], [Bmod])
    for l_ in range(DEPTH):
        for nm, src, dst, rows in (("w_in", w_in, w_in_b, D), ("w_out", w_out, w_out_b, D), ("w1", mlp_w1, w1_b, D), ("w2", mlp_w2, w2_b, 4 * D)):
            Bcv[(nm, l_)] = Buf()
            for r0 in range(0, rows, 128):
                k.dma("pool", dst[l_, r0:r0 + 128, :], src[l_, r0:r0 + 128, :], [], [Bcv[(nm, l_)]], bg=True)
    P.barrier()
    A.top = mark0
    dbgmod = dout("dbg_mod", [128, DEPTH * 96])
    k.dma("sp", dbgmod, modT.rearrange("p l j k -> p (l j k)"), [Bmod], [])
    if stop == "mods":
        P.emit(); ES.close(); return nc, P

    BxT = [Buf() for _ in range(TT // TB)]
    m_pro = A.top
    xs = [A.alloc([D], F32) for _ in range(2)]
    xo = [A.alloc([8, 128], F32) for _ in range(2)]
    Bxs = [Buf(), Buf()]
    Bxo = [Buf(), Buf()]
    for i in range(TT // 128):
        s = i % 2
        k.dma("sp", xs[s], x_tok[i * 128:(i + 1) * 128, :], [], [Bxs[s]])
        for c in range(8):
            bk = 2 * s + c // 4
            k.tr(k.bank(bk)[:, (c % 4) * 128:(c % 4 + 1) * 128], xs[s][:, c * 128:(c + 1) * 128], ident,
                 [Bxs[s], Bconst], [PS[bk]])
        for hh in range(2):
            bk = 2 * s + hh
            k.copy(xo[s][:, hh * 4:(hh + 1) * 4, :], k.bank(bk).rearrange("p (c t) -> p c t", c=4), [PS[bk]], [Bxo[s]])
        k.dma("sp", xTv[:, :, i * 128:(i + 1) * 128], xo[s], [Bxo[s]], [BxT[i // 4]])
    P.barrier()
    A.top = m_pro
    if stop == "pro":
        P.emit(); ES.close(); return nc, P

    hT = A.alloc([8, 2048], BF16)
    oT = A.alloc([8, 2048], BF16)
    gatesT = A.alloc([2048], F32)
    BhT = Buf()
    BoT = Buf()
    Bgates = Buf()

    def rstd_from_sq(sq, Bsq, bank_i, rstd, Brstd):
        for c in range(8):
            k.mm(k.bank(bank_i), ones_b, sq[:, c, :], c == 0, c == 7, [Bsq, Bconst], [PS[bank_i]])
        k.act(rstd, k.bank(bank_i), AF.Sqrt, [PS[bank_i]], [Brstd], scale=1.0 / D, bias=epsb)
        k.recip(rstd, rstd, [Brstd], [Brstd])

    epsb = A.alloc([1], F32)
    k.memset(epsb, EPS, [Bconst])

    def prenorm(xblk, Bx, dst, Bdst, dcols, Asc, shj, l, kc, W, gate=None):
        sq, Bsq, rstd, Brstd, tmp, Btmp = W
        k.act(sq, xblk, AF.Square, [Bx], [Bsq])
        rstd_from_sq(sq, Bsq, 0, rstd, Brstd)
        k.tt(tmp, xblk, bc(rstd.unsqueeze(1), [128, 8, TB]), ALU.mult, [Bx, Brstd], [Btmp])
        for c in range(8):
            k.act(dst[:, c, dcols], tmp[:, c, :], AF.Identity, [Btmp, Bmod], [Bdst],
                  scale=Asc[:, l, c, kc:kc + 1], bias=modT[:, l, shj + c, kc:kc + 1])
        if gate is not None:
            wg32, Bwg32 = gate
            for c in range(8):
                k.act(tmp[:, c, :], tmp[:, c, :], AF.Identity, [Btmp, Bmod], [Btmp],
                      scale=Asc[:, l, c, kc:kc + 1], bias=modT[:, l, shj + c, kc:kc + 1])
            for c in range(8):
                k.mm(k.bank(1)[0:16, :], wg32[:, c, :], tmp[:, c, :], c == 0, c == 7, [Bwg32, Btmp], [PS[1]])
            k.copy(gatesT[0:16, dcols], k.bank(1)[0:16, :], [PS[1]], [Bgates], eng="dve")


    def normalize_out(bkO, ncols, heads, l, sink, dst_cols_list, W):
        rrow, Brr, bcs, Bbcs = W
        per = ncols // len(heads)
        O = k.bank(bkO)
        if sink:
            for i, (h, e_off, dcols) in enumerate(heads):
                k.ts(rrow[64:65, i * per:(i + 1) * per], O[64:65, i * per:(i + 1) * per],
                     sinkE[64:65, l * 4 + h:l * 4 + h + 1], None, ALU.add, None, [PS[bkO], Bconst], [Brr])
            k.recip(rrow[64:65, 0:ncols], rrow[64:65, 0:ncols], [Brr], [Brr])
        else:
            k.recip(rrow[64:65, 0:ncols], O[64:65, 0:ncols], [PS[bkO]], [Brr])
        k.mm(k.bank(2)[0:64, 0:ncols], ones_f[64:65, 0:64], rrow[64:65, 0:ncols], True, True, [Brr, Bconst], [PS[2]])
        k.copy(bcs[0:64, 0:ncols], k.bank(2)[0:64, 0:ncols], [PS[2]], [Bbcs], eng="act")
        for i, (h, e_off, dcols) in enumerate(heads):
            ch = e_off // 128
            po = e_off % 128
            k.tt(oT[po:po + 64, ch, dcols], O[0:64, i * per:(i + 1) * per], bcs[0:64, i * per:(i + 1) * per],
                 ALU.mult, [PS[bkO], Bbcs], [BoT])

    def mixer_attn(l, g):
        T = g["T"]
        L = g["L"]
        nseq = g["nseq"]
        lat = g["name"] == "lat"
        nblk = T // TB
        TK = T + (512 if lat else 0)
        NKC = TK // 128
        wA = A.alloc([8, 544], BF16)
        BwA = Buf()
        k.dma("sp", wA, w_in_b[l].rearrange("(c p) n -> p c n", p=128)[:, :, 0:544], [Bcv[("w_in", l)]], [BwA])
        wukv = A.alloc([512], BF16)
        k.dma("pool", wukv, mla_w_ukv[l], [], [BwA])
        qT = A.alloc([4, T], BF16)
        BqT = Buf()
        kT = A.alloc([4, TK], BF16)
        BkT = Buf()
        ckvnT = A.alloc([TK], BF16)
        Bckv = Buf()
        Vaug = A.alloc([NKC, 4, 65], BF16)
        BV = Buf()
        k.memset(Vaug, 1.0, [BV])
        raw = A.alloc([TB], F32)
        Braw = Buf()
        sqc = A.alloc([TB], BF16)
        Bsqc = Buf()
        rs = A.alloc([TB], F32)
        Brs = Buf()
        kpf = A.alloc([TB], F32)
        Bkpf = Buf()
        tmpr = A.alloc([TB], F32)
        Btmpr = Buf()
        if lat:
            rC = A.alloc([2048], F32)
            rS = A.alloc([2048], F32)
            Brope = Buf()
            k.dma("sp", rC[64:96, :], ropeA[0, 64:96, :], [], [Brope])
            k.dma("sp", rS[64:96, :], ropeA[1, 64:96, :], [], [Brope])
        stg = A.alloc([4, 128], F32)
        Bstg = Buf()

        def rope_rows(src_f, Bsrc, r0, r1, tcols, dst, Bdst):
            k.mm(k.bank(2)[r0:r1, :], pswapF[r0:r1, r0:r1], src_f[r0:r1, :], True, True, [Bsrc, Bconst], [PS[2]])
            k.tt(tmpr[r0:r1, :], k.bank(2)[r0:r1, :], rS[r0:r1, tcols], ALU.mult, [PS[2], Brope], [Btmpr])
            k.tt(src_f[r0:r1, :], src_f[r0:r1, :], rC[r0:r1, tcols], ALU.mult, [Bsrc, Brope], [Bsrc])
            k.tt(dst, src_f[r0:r1, :], tmpr[r0:r1, :], ALU.add, [Bsrc, Btmpr], [Bdst])

        def ckv_norm_block(src_ps_bank, cols, tok_out=None):
            k.copy(raw, k.bank(src_ps_bank), [PS[src_ps_bank]], [Braw], eng="dve")
            k.act(sqc, raw, AF.Square, [Braw], [Bsqc])
            k.mm(k.bank(2), ones_b, sqc, True, True, [Bsqc, Bconst], [PS[2]])
            k.act(rs, k.bank(2), AF.Sqrt, [PS[2]], [Brs], scale=1.0 / 128, bias=epsb)
            k.recip(rs, rs, [Brs], [Brs])
            k.stt(raw, raw, kvg[:, l:l + 1], rs, ALU.mult, ALU.mult, [Braw, Brs, Bconst], [Braw])
            k.copy(ckvnT[:, cols], raw, [Braw], [Bckv], eng="act")
            if tok_out is not None:
                sq_, t0 = tok_out
                for j in range(4):
                    k.tr(k.bank(2)[:, j * 128:(j + 1) * 128], raw[:, j * 128:(j + 1) * 128], ident, [Braw, Bconst], [PS[2]])
                k.copy(stg, k.bank(2).rearrange("p (j r) -> p j r", j=4), [PS[2]], [Bstg], eng="dve")
                for j in range(4):
                    tk = t0 + j * 128
                    k.dma("sp", o_ckv[tk // 256, l, tk % 256:tk % 256 + 128, :], stg[:, j, :], [Bstg], [])

        for tb in range(nblk):
            cols = slice(tb * TB, (tb + 1) * TB)
            for c in range(8):
                k.mm(k.bank(0), wA[:, c, 384:512], hT[:, c, cols], c == 0, c == 7, [BwA, BhT], [PS[0]])
            ckv_norm_block(0, cols, tok_out=None if lat else (0, tb * TB))
            for c in range(8):
                k.mm(k.bank(1)[64:96, :], wA[:, c, 512:544], hT[:, c, cols], c == 0, c == 7, [BwA, BhT], [PS[1]])
            k.copy(kpf[64:96, :], k.bank(1)[64:96, :], [PS[1]], [Bkpf], eng="act")
            if lat:
                for h in range(4):
                    if h == 0:
                        rope_rows(kpf, Bkpf, 64, 96, cols, kT[64:96, 0, cols], BkT)
                    else:
                        k.copy(kT[64:96, h, cols], kT[64:96, 0, cols], [BkT], [BkT])
            else:
                for h in range(4):
                    k.copy(kT[64:96, h, cols], kpf[64:96, :], [Bkpf], [BkT])
                for c in range(8):
                    k.mm(k.bank(1)[0:32, :], wA[:, c, 512:544], hT[:, c, cols], c == 0, c == 7, [BwA, BhT], [PS[1]])
                k.copy(kpf[0:32, :], k.bank(1)[0:32, :], [PS[1]], [Bkpf], eng="dve")
                for j in range(4):
                    k.tr(k.bank(1)[:, j * 32:(j + 1) * 32], kpf[0:32, j * 128:(j + 1) * 128], ident[0:32, 0:32],
                         [Bkpf, Bconst], [PS[1]])
                k.copy(stg[:, 0, :], k.bank(1)[:, 0:128], [PS[1]], [Bstg], eng="dve")
                for j in range(4):
                    tk = tb * TB + j * 128
                    k.dma("sp", o_kpe[tk // 256, l, tk % 256:tk % 256 + 128, :], stg[:, 0, j * 32:(j + 1) * 32], [Bstg], [])
            for h in range(4):
                bk = h % 2
                for c in range(8):
                    k.mm(k.bank(bk)[0:96, :], wA[:, c, h * 96:(h + 1) * 96], hT[:, c, cols], c == 0, c == 7,
                         [BwA, BhT], [PS[bk]])
                if lat:
                    k.copy(raw[0:96, :], k.bank(bk)[0:96, :], [PS[bk]], [Braw], eng="act")
                    k.copy(qT[0:64, h, cols], raw[0:64, :], [Braw], [BqT], eng="dve")
                    rope_rows(raw, Braw, 64, 96, cols, qT[64:96, h, cols], BqT)
                else:
                    k.copy(qT[0:96, h, cols], k.bank(bk)[0:96, :], [PS[bk]], [BqT])
        if lat:
            for j in range(4):
                k.dma("sp", stg[:, j, :], c_ckv[l, j * 128:(j + 1) * 128, :], [], [Bstg])
            for j in range(4):
                k.tr(k.bank(0)[:, j * 128:(j + 1) * 128], stg[:, j, :], ident, [Bstg, Bconst], [PS[0]])
            k.copy(ckvnT[:, 2048:2560], k.bank(0), [PS[0]], [Bckv], eng="dve")
            for j in range(4):
                k.dma("sp", stg[:, j, 0:32], c_kpe[l, j * 128:(j + 1) * 128, :], [Bstg], [Bstg])
            for j in range(4):
                k.tr(k.bank(1)[0:32, j * 128:(j + 1) * 128], stg[:, j, 0:32], ident, [Bstg, Bconst], [PS[1]])
            for h in range(4):
                k.copy(kT[64:96, h, 2048:2560], k.bank(1)[0:32, :], [PS[1]], [BkT])
        for kb in range(TK // TB):
            cols = slice(kb * TB, (kb + 1) * TB)
            for h in range(4):
                bk = h % 2
                k.mm(k.bank(bk)[0:64, :], wukv[:, h * 128:h * 128 + 64], ckvnT[:, cols], True, True, [BwA, Bckv], [PS[bk]])
                k.copy(kT[0:64, h, cols], k.bank(bk)[0:64, :], [PS[bk]], [BkT])
        for kc in range(NKC):
            bk = kc % 2
            k.mm(k.bank(bk), ckvnT[:, kc * 128:(kc + 1) * 128], wukv, True, True, [BwA, Bckv], [PS[bk]])
            k.copy(Vaug[:, kc, :, 0:64], k.bank(bk).rearrange("p (h t e) -> p h t e", h=4, t=2)[:, :, 1, :], [PS[bk]], [BV])
        PT = [A.alloc([TB], BF16) for _ in range(3)]
        BPT = [Buf() for _ in range(3)]
        rrow = A.alloc([TB], F32)
        bcs = A.alloc([TB], F32)
        Wn = (rrow, Buf(), bcs, Buf())
        scale = 96 ** -0.5
        it = 0
        nq = 0
        QB = min(TB, L)
        for sq_ in range(nseq):
            for h in range(4):
                for qb in range(L // QB):
                    qcols = slice(sq_ * L + qb * QB, sq_ * L + (qb + 1) * QB)
                    bkO = 6 + (nq % 2)
                    nq += 1
                    kcs = list(range(sq_ * L // 128, (sq_ + 1) * L // 128)) if not lat else list(range(NKC))
                    for i, kc in enumerate(kcs):
                        bS = 3 + (it % 3)
                        pt = PT[it % 3]
                        bpt = BPT[it % 3]
                        it += 1
                        k.mm(k.bank(bS)[:, 0:QB], kT[0:96, h, kc * 128:(kc + 1) * 128], qT[0:96, h, qcols], True, True,
                             [BkT, BqT], [PS[bS]])
                        k.act(pt[:, 0:QB], k.bank(bS)[:, 0:QB], AF.Exp, [PS[bS]], [bpt], scale=scale)
                        k.mm(k.bank(bkO)[0:65, 0:QB], Vaug[:, kc, h, :], pt[:, 0:QB], i == 0, i == len(kcs) - 1,
                             [BV, bpt], [PS[bkO]])
                    normalize_out(bkO, QB, [(h, h * 64, qcols)], l, False, None, Wn)
        P.barrier()
        A.top = mB_inner = A.top
        A.top = mixer_base[0]
        wB = A.alloc([8, 512], BF16)
        BwB = Buf()
        k.dma("sp", wB, w_in_b[l].rearrange("(c p) n -> p c n", p=128)[:, :, 544:1056], [Bcv[("w_in", l)]], [BwB])
        qS = A.alloc([4, T], BF16)
        BqS = Buf()
        kS = A.alloc([2, TK], BF16)
        BkS = Buf()
        Vb = A.alloc([NKC, 2, 65], BF16)
        BVb = Buf()
        k.memset(Vb, 1.0, [BVb])
        raw = A.alloc([TB], F32)
        Braw = Buf()
        tmpr = A.alloc([TB], F32)
        Btmpr = Buf()
        stg = A.alloc([4, 128], F32)
        Bstg = Buf()
        stg2 = A.alloc([256], F32)
        Bstg2 = Buf()
        if lat:
            rC = A.alloc([2048], F32)
            rS = A.alloc([2048], F32)
            Brope = Buf()
            k.dma("sp", rC[0:64, :], ropeB[0], [], [Brope])
            k.dma("sp", rS[0:64, :], ropeB[1], [], [Brope])
        for tb in range(nblk):
            cols = slice(tb * TB, (tb + 1) * TB)
            for h in range(6):
                bk = h % 2
                c0 = h * 64 if h < 4 else 256 + (h - 4) * 64
                for c in range(8):
                    k.mm(k.bank(bk)[0:64, :], wB[:, c, c0:c0 + 64], hT[:, c, cols], c == 0, c == 7, [BwB, BhT], [PS[bk]])
                dst, Bdst = (qS[0:64, h, cols], BqS) if h < 4 else (kS[0:64, h - 4, cols], BkS)
                if lat:
                    k.copy(raw[0:64, :], k.bank(bk)[0:64, :], [PS[bk]], [Braw], eng="act")
                    rope_rows(raw, Braw, 0, 64, cols, dst, Bdst)
                else:
                    k.copy(dst, k.bank(bk)[0:64, :], [PS[bk]], [Bdst])
            for j in range(4):
                tk = tb * TB + j * 128
                kc = tk // 128
                bk = j % 2
                for c in range(8):
                    k.mm(k.bank(bk)[:, 0:256], hT[:, c, tk:tk + 128], wB[:, c, 256:512], c == 0, c == 7, [BwB, BhT], [PS[bk]])
                k.copy(Vb[:, kc, :, 0:64], k.bank(bk)[:, 128:256].rearrange("p (v e) -> p v e", v=2), [PS[bk]], [BVb], eng="act")
                if not lat:
                    k.copy(stg2, k.bank(bk)[:, 0:256], [PS[bk]], [Bstg2], eng="dve")
                    k.dma("sp", o_swk[tk // 256, l, tk % 256:tk % 256 + 128, :], stg2[:, 0:128], [Bstg2], [])
                    k.dma("sp", o_swv[tk // 256, l, tk % 256:tk % 256 + 128, :], stg2[:, 128:256], [Bstg2], [])
        if lat:
            for j in range(4):
                k.dma("sp", stg[:, j, :], c_swk[l, j * 128:(j + 1) * 128, :], [], [Bstg])
            for kv in range(2):
                for j in range(4):
                    k.tr(k.bank(kv)[0:64, j * 128:(j + 1) * 128], stg[:, j, kv * 64:(kv + 1) * 64], ident, [Bstg, Bconst], [PS[kv]])
                k.copy(kS[0:64, kv, 2048:2560], k.bank(kv)[0:64, :], [PS[kv]], [BkS])
            for j in range(4):
                k.dma("pool", Vb[:, 16 + j, :, 0:64], c_swv[l, j * 128:(j + 1) * 128, :].rearrange("p (v e) -> p v e", v=2), [], [BVb])
        PT = [A.alloc([256], BF16) for _ in range(3)]
        BPT = [Buf() for _ in range(3)]
        rrow = A.alloc([TB], F32)
        bcs = A.alloc([TB], F32)
        Wn = (rrow, Buf(), bcs, Buf())
        scale = 64 ** -0.5
        it = 0
        nq = 0
        if not lat:
            for sq_ in range(nseq):
                for h in range(4):
                    qcols = slice(sq_ * L, (sq_ + 1) * L)
                    bkO = 6 + (nq % 2)
                    nq += 1
                    kcs = list(range(sq_ * L // 128, (sq_ + 1) * L // 128))
                    for i, kc in enumerate(kcs):
                        bS = 3 + (it % 3)
                        pt = PT[it % 3]
                        bpt = BPT[it % 3]
                        it += 1
                        k.mm(k.bank(bS)[:, 0:L], kS[0:64, h // 2, kc * 128:(kc + 1) * 128], qS[0:64, h, qcols], True, True,
                             [BkS, BqS], [PS[bS]])
                        k.act(pt[:, 0:L], k.bank(bS)[:, 0:L], AF.Exp, [PS[bS]], [bpt], scale=scale)
                        k.mm(k.bank(bkO)[0:65, 0:L], Vb[:, kc, h // 2, :], pt[:, 0:L], i == 0, i == len(kcs) - 1,
                             [BVb, bpt], [PS[bkO]])
                    normalize_out(bkO, L, [(h, 256 + h * 64, qcols)], l, True, None, Wn)
        else:
            NB = L // 128
            for n in range(NB):
                qcols = slice(n * 128, (n + 1) * 128)
                for kv in range(2):
                    bkO = 6 + (nq % 2)
                    nq += 1
                    kcs = []
                    if n > 0:
                        kcs.append((n - 1, 0))
                    kcs.append((n, None))
                    if n < NB - 1:
                        kcs.append((n + 1, 1))
                    kcs += [(16 + j, None) for j in range(4)]
                    for i, (kc, mk_) in enumerate(kcs):
                        bS = 3 + (it % 3)
                        pt = PT[it % 3]
                        bpt = BPT[it % 3]
                        it += 1
                        for hh in range(2):
                            k.mm(k.bank(bS)[:, hh * 128:(hh + 1) * 128], kS[0:64, kv, kc * 128:(kc + 1) * 128],
                                 qS[0:64, 2 * kv + hh, qcols], True, True, [BkS, BqS], [PS[bS]])
                        k.act(pt, k.bank(bS)[:, 0:256], AF.Exp, [PS[bS]], [bpt], scale=scale)
                        if mk_ is not None:
                            k.tt(pt.rearrange("p (h q) -> p h q", h=2), pt.rearrange("p (h q) -> p h q", h=2),
                                 bc(masks[:, mk_, :].unsqueeze(1), [128, 2, 128]), ALU.mult, [bpt, Bconst], [bpt])
                        k.mm(k.bank(bkO)[0:65, 0:256], Vb[:, kc, kv, :], pt, i == 0, i == len(kcs) - 1, [BVb, bpt], [PS[bkO]])
                    normalize_out(bkO, 256, [(2 * kv, 256 + 2 * kv * 64, qcols), (2 * kv + 1, 256 + (2 * kv + 1) * 64, qcols)],
                                  l, True, None, Wn)


    PI = math.pi

    def wrap_sin(x, Bx, t, Bt, rows, width):
        xs = x[0:rows, 0:width]
        ts_ = t[0:rows, 0:width]
        for rep in range(2):
            k.ts(ts_, xs, PI, -2 * PI, ALU.is_gt, ALU.mult, [Bx], [Bt])
            k.tt(xs, xs, ts_, ALU.add, [Bx, Bt], [Bx])
            k.ts(ts_, xs, -PI, 2 * PI, ALU.is_lt, ALU.mult, [Bx], [Bt])
            k.tt(xs, xs, ts_, ALU.add, [Bx, Bt], [Bx])
        k.act(xs, xs, AF.Sin, [Bx], [Bx])

    def mixer_hyena(l, g):
        T = g["T"]
        L = g["L"]
        nseq = g["nseq"]
        NSC = L // 128
        NF = NSC + 1
        NW = nseq * 256
        hc = HY[L]
        tabC = hc["tab"][0]
        tabS = hc["tab"][1]
        frows = lambda fc: 128 if fc < NSC else 1
        base = A.top
        stgp = A.alloc([128], F32)
        colsT = A.alloc([128], F32)
        Bsp = Buf()
        Bcols = Buf()
        k.memset(stgp, 0.0, [Bsp])
        k.dma("sp", stgp[0:18, :], hy_conv[l].rearrange("k (c p) -> (k c) p", p=128), [], [Bsp])
        k.dma("sp", stgp[18:22, :], hy_bias[l].rearrange("o (c p) -> (o c) p", p=128), [], [Bsp])
        k.dma("sp", stgp[32:33, 0:64], hy_b1[l:l + 1, :], [], [Bsp])
        k.dma("sp", stgp[33:34, 0:64], hy_freq[l][0:1, :], [], [Bsp])
        k.dma("sp", stgp[34:35, 0:64], hy_b2[l:l + 1, :], [], [Bsp])
        k.dma("sp", stgp[35:36, 0:64], hy_freq[l][1:2, :], [], [Bsp])
        k.tr(k.bank(0)[:, 0:128], stgp, ident, [Bsp, Bconst], [PS[0]])
        k.copy(colsT, k.bank(0)[:, 0:128], [PS[0]], [Bcols], eng="dve")
        convw = colsT[:, 0:18].rearrange("p (k c) -> p k c", k=3)
        biasc = colsT[:, 18:22].rearrange("p (o c) -> p o c", o=2)
        fb = A.alloc([2], F32)
        k.tt(fb[0:64, 0:1], colsT[0:64, 32:33], colsT[0:64, 33:34], ALU.mult, [Bcols], [Bcols])
        k.tt(fb[0:64, 1:2], colsT[0:64, 34:35], colsT[0:64, 35:36], ALU.mult, [Bcols], [Bcols])
        aw = A.alloc([NF], F32)
        bw = A.alloc([NF], F32)
        negdist = A.alloc([NSC], F32)
        k.dma("sp", aw, hc["aw"], [], [Bcols])
        k.dma("sp", bw, hc["bw"], [], [Bcols])
        k.dma("sp", negdist, hc["negdist"], [], [Bcols])
        Gre = A.alloc([NF, 512], BF16)
        Gim = A.alloc([NF, 512], BF16)
        BG = Buf()
        tC = [A.alloc([NSC, 128], BF16) for _ in range(2)]
        tS = [A.alloc([NSC, 128], BF16) for _ in range(2)]
        Btab = [Buf(), Buf()]
        persist = A.top
        tmpA = A.alloc([512], F32)
        tmpB = A.alloc([512], F32)
        BtA = Buf()
        BtB = Buf()

        def load_colblock(fc, i):
            rows = frows(fc)
            k.dma("sp", tC[i][:, :, 0:rows], tabC[0:L, fc * 128:fc * 128 + rows].rearrange("(sc p) f -> p sc f", p=128),
                  [], [Btab[i]], slow=True)
            k.dma("sp", tS[i][:, :, 0:rows], tabS[0:L, fc * 128:fc * 128 + rows].rearrange("(sc p) f -> p sc f", p=128),
                  [], [Btab[i]], slow=True)

        featsT = A.alloc([L], F32)
        w1f = A.alloc([64], F32)
        w2f = A.alloc([64], F32)
        w3f = A.alloc([512], F32)
        Bfw = Buf()
        k.dma("sp", featsT[0:17, :], hc["feats"], [], [Bfw])
        k.dma("sp", w1f[0:17, :], hy_w1[l], [], [Bfw])
        k.dma("sp", w2f[0:64, :], hy_w2[l], [], [Bfw])
        k.dma("sp", w3f[0:64, :], hy_w3[l], [], [Bfw])
        h1T = A.alloc([L], F32)
        h2T = A.alloc([L], F32)
        wt = A.alloc([L], F32)
        Bh1 = Buf()
        Bh2 = Buf()
        Bwt = Buf()
        CW = min(512, L)
        for blk in range(L // CW):
            cs = slice(blk * CW, (blk + 1) * CW)
            k.mm(k.bank(blk % 2)[0:64, 0:CW], w1f[0:17, 0:64], featsT[0:17, cs], True, True, [Bfw], [PS[blk % 2]])
            k.ts(h1T[0:64, cs], k.bank(blk % 2)[0:64, 0:CW], colsT[0:64, 33:34], fb[0:64, 0:1], ALU.mult, ALU.add,
                 [PS[blk % 2], Bcols], [Bh1])
        wrap_sin(h1T, Bh1, wt, Bwt, 64, L)
        for blk in range(L // CW):
            cs = slice(blk * CW, (blk + 1) * CW)
            k.mm(k.bank(blk % 2)[0:64, 0:CW], w2f[0:64, 0:64], h1T[0:64, cs], True, True, [Bfw, Bh1], [PS[blk % 2]])
            k.ts(h2T[0:64, cs], k.bank(blk % 2)[0:64, 0:CW], colsT[0:64, 35:36], fb[0:64, 1:2], ALU.mult, ALU.add,
                 [PS[blk % 2], Bcols], [Bh2])
        wrap_sin(h2T, Bh2, wt, Bwt, 64, L)
        absdec = A.alloc([512], F32)
        Bad = Buf()
        k.dma("sp", absdec, hy_decay[l].partition_broadcast(128), [], [Bad], slow=True)
        k.act(absdec, absdec, AF.Abs, [Bad], [Bad])
        filtok = A.alloc([NSC, 512], BF16)
        Bft = Buf()
        Et = A.alloc([512], F32)
        BEt = Buf()
        for jc in range(NSC):
            bk = jc % 2
            k.mm(k.bank(bk), h2T[0:64, jc * 128:(jc + 1) * 128], w3f[0:64, :], True, True, [Bh2, Bfw], [PS[bk]])
            k.act(Et, absdec, AF.Exp, [Bad, Bcols], [BEt], scale=negdist[:, jc:jc + 1])
            k.tt(filtok[:, jc, :], k.bank(bk), Et, ALU.mult, [PS[bk], BEt], [Bft])
        for fc in range(NF):
            rows = frows(fc)
            i = fc % 2
            load_colblock(fc, i)
            ba, bb = 2 * i, 2 * i + 1
            for sc in range(NSC):
                k.mm(k.bank(ba)[0:rows, :], tC[i][:, sc, 0:rows], filtok[:, sc, :], sc == 0, sc == NSC - 1, [Btab[i], Bft], [PS[ba]])
            for sc in range(NSC):
                k.mm(k.bank(bb)[0:rows, :], tS[i][:, sc, 0:rows], filtok[:, sc, :], sc == 0, sc == NSC - 1, [Btab[i], Bft], [PS[bb]])
            k.ts(tmpA[0:rows, :], k.bank(ba)[0:rows, :], aw[0:rows, fc:fc + 1], None, ALU.mult, None, [PS[ba], Bcols], [BtA])
            k.stt(Gre[0:rows, fc, :], k.bank(bb)[0:rows, :], bw[0:rows, fc:fc + 1], tmpA[0:rows, :], ALU.mult, ALU.add,
                  [PS[bb], BtA, Bcols], [BG])
            k.ts(tmpB[0:rows, :], k.bank(bb)[0:rows, :], aw[0:rows, fc:fc + 1], None, ALU.mult, None, [PS[bb], Bcols], [BtB])
            k.stt(Gim[0:rows, fc, :], k.bank(ba)[0:rows, :], bw[0:rows, fc:fc + 1], tmpB[0:rows, :], ALU.mult, ALU.subtract,
                  [PS[ba], BtB, Bcols], [BG])
        P.barrier()
        A.top = persist
        x1T = A.alloc([2, T], BF16)
        x2T = A.alloc([2, T], BF16)
        vT = A.alloc([2, T], F32)
        Bx12 = Buf()
        BvT = Buf()
        dtok = A.alloc([NSC, NW], BF16)
        Bdt = Buf()
        conv_mark = A.top
        wH = A.alloc([8, 768], BF16)
        BwH = Buf()
        k.dma("sp", wH, w_in_b[l].rearrange("(c p) n -> p c n", p=128)[:, :, C_HU:C_HU + 768], [Bcv[("w_in", l)]], [BwH])
        hu = A.alloc([T], F32)
        uu = A.alloc([T], F32)
        Bhu = Buf()
        Buu = Buf()
        for c6 in range(6):
            for tb in range(T // TB):
                cols = slice(tb * TB, (tb + 1) * TB)
                bk = tb % 2
                for c in range(8):
                    k.mm(k.bank(bk), wH[:, c, c6 * 128:(c6 + 1) * 128], hT[:, c, cols], c == 0, c == 7, [BwH, BhT], [PS[bk]])
                k.copy(hu[:, cols], k.bank(bk), [PS[bk]], [Bhu])
            k.act(uu, hu, AF.Identity, [Bhu, Bcols], [Buu], scale=convw[:, 1, c6:c6 + 1])
            for sq_ in range(nseq):
                s0, s1 = sq_ * L, (sq_ + 1) * L
                k.stt(uu[:, s0 + 1:s1], hu[:, s0:s1 - 1], convw[:, 0, c6:c6 + 1], uu[:, s0 + 1:s1], ALU.mult, ALU.add,
                      [Bhu, Bcols, Buu], [Buu])
                k.stt(uu[:, s0:s1 - 1], hu[:, s0 + 1:s1], convw[:, 2, c6:c6 + 1], uu[:, s0:s1 - 1], ALU.mult, ALU.add,
                      [Bhu, Bcols, Buu], [Buu])
            if c6 < 2:
                k.copy(vT[:, c6, :], uu, [Buu], [BvT], eng="act")
            elif c6 < 4:
                k.copy(x1T[:, c6 - 2, :], uu, [Buu], [Bx12], eng="act")
            else:
                k.copy(x2T[:, c6 - 4, :], uu, [Buu], [Bx12], eng="act")

        def to_tokmajor():
            for sq_ in range(nseq):
                for sc in range(NSC):
                    bk = sc % 2
                    for cc in range(2):
                        t0 = sq_ * L + sc * 128
                        k.tr(k.bank(bk)[:, cc * 128:(cc + 1) * 128], vT[:, cc, t0:t0 + 128], ident, [BvT, Bconst], [PS[bk]])
                    k.copy(dtok[:, sc, sq_ * 256:(sq_ + 1) * 256], k.bank(bk)[:, 0:256], [PS[bk]], [Bdt])

        P.barrier()
        A.top = conv_mark
        Pq = A.alloc([NF, NW], BF16)
        Qq = A.alloc([NF, NW], BF16)
        BPQ = Buf()
        rC = [A.alloc([L], BF16) for _ in range(2)]
        rS = [A.alloc([L], BF16) for _ in range(2)]
        Brt = [Buf(), Buf()]
        t1 = A.alloc([NW], F32)
        t2 = A.alloc([NW], F32)
        Bt1 = Buf()
        Bt2 = Buf()
        TW = min(512, L)

        def long_conv(o, consumer):
            ocs = slice(o * 256, (o + 1) * 256)
            for fc in range(NF):
                rows = frows(fc)
                i = fc % 2
                load_colblock(fc, i)
                ba, bb = 2 * i, 2 * i + 1
                for sc in range(NSC):
                    k.mm(k.bank(ba)[0:rows, 0:NW], tC[i][:, sc, 0:rows], dtok[:, sc, :], sc == 0, sc == NSC - 1, [Btab[i], Bdt], [PS[ba]])
                for sc in range(NSC):
                    k.mm(k.bank(bb)[0:rows, 0:NW], tS[i][:, sc, 0:rows], dtok[:, sc, :], sc == 0, sc == NSC - 1, [Btab[i], Bdt], [PS[bb]])
                Av = k.bank(ba)[0:rows, 0:NW].rearrange("p (s c) -> p s c", s=nseq)
                Bv = k.bank(bb)[0:rows, 0:NW].rearrange("p (s c) -> p s c", s=nseq)
                gre = bc(Gre[0:rows, fc, ocs].unsqueeze(1), [rows, nseq, 256])
                gim = bc(Gim[0:rows, fc, ocs].unsqueeze(1), [rows, nseq, 256])
                v3 = lambda ap: ap[0:rows, :].rearrange("p (s c) -> p s c", s=nseq)
                k.tt(v3(t1), Av, gre, ALU.mult, [PS[ba], BG], [Bt1])
                k.tt(v3(t2), Bv, gim, ALU.mult, [PS[bb], BG], [Bt2])
                k.tt(Pq[0:rows, fc, :], t1[0:rows, :], t2[0:rows, :], ALU.add, [Bt1, Bt2], [BPQ])
                k.tt(v3(t1), Bv, gre, ALU.mult, [PS[bb], BG], [Bt1])
                k.tt(v3(t2), Av, gim, ALU.mult, [PS[ba], BG], [Bt2])
                k.tt(Qq[0:rows, fc, :], t1[0:rows, :], t2[0:rows, :], ALU.subtract, [Bt1, Bt2], [BPQ])
            accs = [(sq_, cc, tb) for sq_ in range(nseq) for cc in range(2) for tb in range(L // TW)]
            for fc in range(NF):
                rows = frows(fc)
                i = fc % 2
                k.dma("sp", rC[i][0:rows, :], tabC[fc * 128:fc * 128 + rows, 0:L], [], [Brt[i]])
                k.dma("sp", rS[i][0:rows, :], tabS[fc * 128:fc * 128 + rows, 0:L], [], [Brt[i]])
                for ai, (sq_, cc, tb) in enumerate(accs):
                    lc = slice(sq_ * 256 + cc * 128, sq_ * 256 + (cc + 1) * 128)
                    k.mm(k.bank(ai)[:, 0:TW], Pq[0:rows, fc, lc], rC[i][0:rows, tb * TW:(tb + 1) * TW], fc == 0, False,
                         [BPQ, Brt[i]], [PS[ai]])
                    k.mm(k.bank(ai)[:, 0:TW], Qq[0:rows, fc, lc], rS[i][0:rows, tb * TW:(tb + 1) * TW], False, fc == NF - 1,
                         [BPQ, Brt[i]], [PS[ai]])
            for ai, (sq_, cc, tb) in enumerate(accs):
                consumer(ai, cc, slice(sq_ * L + tb * TW, sq_ * L + (tb + 1) * TW))

        tcs = A.alloc([TW], F32)
        Btcs = Buf()

        def cons0(ai, cc, tcols):
            k.stt(tcs, vT[:, cc, tcols], biasc[:, 0, cc:cc + 1], k.bank(ai)[:, 0:TW], ALU.mult, ALU.add, [BvT, Bcols, PS[ai]], [Btcs])
            k.tt(vT[:, cc, tcols], tcs, x1T[:, cc, tcols], ALU.mult, [Btcs, Bx12], [BvT])

        def cons1(ai, cc, tcols):
            k.stt(tcs, vT[:, cc, tcols], biasc[:, 1, cc:cc + 1], k.bank(ai)[:, 0:TW], ALU.mult, ALU.add, [BvT, Bcols, PS[ai]], [Btcs])
            k.tt(oT[:, 6 + cc, tcols], tcs, x2T[:, cc, tcols], ALU.mult, [Btcs, Bx12], [BoT])

        to_tokmajor()
        long_conv(0, cons0)
        to_tokmajor()
        long_conv(1, cons1)
        P.barrier()
        A.top = base


    def mixer_gdn(l, g):
        GD = F32 if 'gdn32' in (dbg or ()) else BF16
        T = g["T"]
        L = g["L"]
        nseq = g["nseq"]
        lat = g["name"] == "lat"
        NCK = L // 64
        NI = nseq * 4
        NBLK = T // TB
        base = A.top
        stgp = A.alloc([128], F32)
        colsT = A.alloc([128], F32)
        Bsp = Buf()
        Bcols = Buf()
        k.memset(stgp, 0.0, [Bsp])
        k.dma("sp", stgp[0:18, :], gdn_conv[l].rearrange("k (c p) -> (k c) p", p=128), [], [Bsp])
        k.dma("sp", stgp[18:19, 0:64], gdn_norm[l:l + 1, :], [], [Bsp])
        k.dma("sp", stgp[18:19, 64:128], gdn_norm[l:l + 1, :], [], [Bsp])
        k.tr(k.bank(0)[:, 0:128], stgp, ident, [Bsp, Bconst], [PS[0]])
        k.copy(colsT, k.bank(0)[:, 0:128], [PS[0]], [Bcols], eng="dve")
        convw = colsT[:, 0:18].rearrange("p (k c) -> p k c", k=3)
        gnorm = colsT[:, 18:19]
        all16 = A.alloc([16], F32)
        k.dma("sp", all16[:, 0:8], gdn_a_log[l].partition_broadcast(128), [], [Bcols], slow=True)
        k.dma("sp", all16[:, 8:16], gdn_dt_bias[l].partition_broadcast(128), [], [Bcols], slow=True)
        k.act(all16[:, 0:8], all16[:, 0:8], AF.Exp, [Bcols], [Bcols])
        k.ts(all16[:, 0:8], all16[:, 0:8], -1.0, None, ALU.mult, None, [Bcols], [Bcols])
        neaS = A.alloc([2, 2], F32)
        dtbS = A.alloc([2, 2], F32)
        for (dst, c0) in ((neaS, 0), (dtbS, 8)):
            v8 = all16[:, c0:c0 + 8].rearrange("p (d hp two) -> p d hp two", d=2, two=2)
            k.copy(dst[0:64], v8[0:64, :, :, 0], [Bcols], [Bcols], eng="dve")
            k.copy(dst[64:128], v8[64:128, :, :, 1], [Bcols], [Bcols], eng="dve")
        blockones = A.alloc([128], BF16)
        k.memset(blockones, 0.0, [Bcols])
        k.memset(blockones[0:64, 0:64], 1.0, [Bcols])
        k.memset(blockones[64:128, 64:128], 1.0, [Bcols])
        negones = A.alloc([64], F32)
        k.memset(negones, -1.0, [Bcols])
        negm = A.alloc([2, 64], F32)
        smk = A.alloc([2, 64], F32)
        k.dma("sp", negm[0:64], negm_d.rearrange("m c s -> c m s"), [], [Bcols])
        k.dma("sp", smk[0:64], sm_d.rearrange("m c s -> c m s"), [], [Bcols])
        mbs = A.alloc([2, 64], F32)
        mbt = A.alloc([2, 64], F32)
        k.ts(mbs[0:64], smk[0:64], -30000.0, 30000.0, ALU.mult, ALU.add, [Bcols], [Bcols])
        k.ts(mbt[0:64], negm[0:64], 30000.0, -30000.0, ALU.mult, ALU.add, [Bcols], [Bcols])
        identb = A.alloc([64], GD)
        k.copy(identb[0:64, :], ident[0:64, 0:64], [Bconst], [Bcols], eng="dve")
        ident2 = A.alloc([64], F32)
        k.ts(ident2[0:64, :], ident[0:64, 0:64], 2.0, None, ALU.mult, None, [Bconst], [Bcols])
        startm = A.alloc([T], BF16)
        k.memset(startm, 1.0, [Bcols])
        k.memset(startm.rearrange("p (n c) -> p n c", c=64)[:, :, 0:1], 0.0, [Bcols])
        qn = A.alloc([2, T], GD)
        kn = A.alloc([2, T], GD)
        vs = A.alloc([2, T], BF16)
        ocT = A.alloc([2, T], F32)
        Bqkv = Buf()
        Boc = Buf()
        pm = A.top
        wG = A.alloc([8, 1024], BF16)
        BwG = Buf()
        k.dma("sp", wG, w_in_b[l].rearrange("(c p) n -> p c n", p=128)[:, :, C_GQKV:C_GQKV + 1024], [Bcv[("w_in", l)]], [BwG])
        hu = A.alloc([T], F32)
        uu = A.alloc([T], F32)
        Bhu = Buf()
        Buu = Buf()
        sqb = A.alloc([TB], BF16)
        Bsqb = Buf()
        rsn = A.alloc([TB], F32)
        Brsn = Buf()
        for c6 in range(6):
            hp = c6 % 2
            for tb in range(NBLK):
                cols = slice(tb * TB, (tb + 1) * TB)
                bk = tb % 2
                for c in range(8):
                    k.mm(k.bank(bk), wG[:, c, c6 * 128:(c6 + 1) * 128], hT[:, c, cols], c == 0, c == 7, [BwG, BhT], [PS[bk]])
                k.copy(hu[:, cols], k.bank(bk), [PS[bk]], [Bhu])
            k.act(uu, hu, AF.Identity, [Bhu, Bcols], [Buu], scale=convw[:, 1, c6:c6 + 1])
            for sq_ in range(nseq):
                s0, s1 = sq_ * L, (sq_ + 1) * L
                k.stt(uu[:, s0 + 1:s1], hu[:, s0:s1 - 1], convw[:, 0, c6:c6 + 1], uu[:, s0 + 1:s1], ALU.mult, ALU.add,
                      [Bhu, Bcols, Buu], [Buu])
                k.stt(uu[:, s0:s1 - 1], hu[:, s0 + 1:s1], convw[:, 2, c6:c6 + 1], uu[:, s0:s1 - 1], ALU.mult, ALU.add,
                      [Bhu, Bcols, Buu], [Buu])
            k.act(uu, uu, AF.Silu, [Buu], [Buu])
            if c6 < 4:
                dst = qn if c6 < 2 else kn
                sc_ = 64 ** -0.5 if c6 < 2 else 1.0
                for tb in range(NBLK):
                    cols = slice(tb * TB, (tb + 1) * TB)
                    k.act(sqb, uu[:, cols], AF.Square, [Buu], [Bsqb])
                    k.mm(k.bank(2), blockones, sqb, True, True, [Bsqb, Bcols], [PS[2]])
                    k.act(rsn, k.bank(2), AF.Sqrt, [PS[2], Bconst], [Brsn], bias=epsb)
                    k.recip(rsn, rsn, [Brsn], [Brsn])
                    k.stt(dst[:, hp, cols], uu[:, cols], sc_, rsn, ALU.mult, ALU.mult, [Buu, Brsn], [Bqkv])
            else:
                k.copy(vs[:, hp, :], uu, [Buu], [Bqkv], eng="act")
        P.barrier()
        A.top = pm
        gsel = A.alloc([8, 128], F32)
        Bwrep = Buf()
        k.dma("sp", gsel[0:16], sel_d, [], [Bwrep])
        Dd = A.alloc([2, T], F32)
        bet = A.alloc([2, T], BF16)
        BD = Buf()
        Bbet = Buf()
        alias_mark = A.top
        tmpf = A.alloc([T], F32)
        Btmpf = Buf()
        A.top = alias_mark
        Eb = A.alloc([2, TB], F32)
        kbT = A.alloc([2, TB], GD)
        kb32 = A.alloc([2, TB], F32)
        kbeH = A.alloc([4, TB], GD)
        qdH = A.alloc([4, TB], GD)
        D0 = A.alloc([4, TB], F32)
        ktl = A.alloc([2, TB], F32)
        vbe = A.alloc([2, TB], F32)
        Bblk = Buf()
        tails = A.alloc([NI, 8], F32)
        def stepbufs(first=[True]):
            d_ = {}
            for nm, dt in (("G", F32), ("GT", F32), ("Gs", F32), ("GsT", F32), ("PA", GD), ("PB", GD), ("TA", GD),
                           ("TB", GD), ("attnT", GD), ("vbt", F32), ("ktail", GD), ("rp", F32), ("u", GD), ("Dcol", F32),
                           ("A32", F32), ("TB32", F32), ("TBt", F32), ("E32", F32), ("TBf", F32)):
                if not first[0] and nm in ("A32", "TB32", "TBt", "E32", "TBf", "G", "Gs"):
                    continue
                d_[nm] = A.alloc([NI, 64], dt)
                d_["B" + nm] = Buf()
            first[0] = False
            return d_
        SB0 = stepbufs()
        SB = [SB0, SB0]
        S32 = A.alloc([NI, 64], F32)
        Sbf = A.alloc([NI, 64], GD)
        BS32 = Buf()
        BSbf = Buf()

        def inst_list(d, step):
            res = []
            for sq_ in range(nseq):
                j = step if d == 0 else NCK - 1 - step
                for h in range(4):
                    t0 = sq_ * L + j * 64
                    res.append((sq_ * 4 + h, sq_, h, h // 2, (h % 2) * 64, slice(t0, t0 + 64), slice(t0 % TB, t0 % TB + 64), t0 // TB, (t0 % TB) // 64))
            res.sort(key=lambda r_: (r_[4], r_[0]))
            return res

        for d in range(2):
            NEGM = negm[0:64, d, :]
            NEGMT = negm[0:64, 1 - d, :]
            SM = smk[0:64, d, :]
            SMT = smk[0:64, 1 - d, :]
            MBS = mbs[0:64, d, :]
            MBT = mbt[0:64, 1 - d, :]
            P.barrier()
            for which in range(2):
                for hp in range(2):
                    for tb in range(NBLK):
                        cols = slice(tb * TB, (tb + 1) * TB)
                        bk = tb % 2
                        k.mm(k.bank(bk), gsel[0:16, which * 4 + d * 2 + hp, :], gatesT[0:16, cols], True, True, [Bwrep, Bgates], [PS[bk]])
                        if which == 0:
                            k.act(Dd[:, hp, cols], k.bank(bk), AF.Exp, [PS[bk], Bcols], [BD], bias=dtbS[:, d, hp:hp + 1])
                        else:
                            k.act(bet[:, hp, cols], k.bank(bk), AF.Sigmoid, [PS[bk]], [Bbet])
            for hp in range(2):
                k.act(Dd[:, hp, :], Dd[:, hp, :], AF.Ln, [BD], [BD], bias=ones_f[:, 0:1])
                k.ts(Dd[:, hp, :], Dd[:, hp, :], neaS[:, d, hp:hp + 1], None, ALU.mult, None, [BD, Bcols], [BD])
                P.op("dve", (lambda hp=hp: lambda e: e.tensor_tensor_scan(out=tmpf, data0=startm, data1=Dd[:, hp, :], initial=0.0,
                                                                           op0=ALU.mult, op1=ALU.add))(), [BD, Bcols], [Btmpf])
                if d == 0:
                    k.copy(Dd[:, hp, :], tmpf, [Btmpf], [BD], eng="dve")
                else:
                    t3 = tmpf.rearrange("p (n c) -> p n c", c=64)
                    d3 = Dd[:, hp, :].rearrange("p (n c) -> p n c", c=64)
                    k.tt(Dd[:, hp, :], Dd[:, hp, :], tmpf, ALU.subtract, [BD, Btmpf], [BD])
                    k.tt(d3, d3, bc(t3[:, :, 63:64], [128, T // 64, 64]), ALU.add, [BD, Btmpf], [BD])
            P.barrier()
            if 'dumpD' in (dbg or ()) and d == 1 and l == 0 and not lat:
                ddd = dout("dbg_D", [128, 2 * T])
                k.dma("sp", ddd, Dd.rearrange("p h t -> p (h t)"), [BD], [])
                ddq = dout("dbg_qk", [128, 4 * T], BF16)
                k.dma("sp", ddq[:, 0:2 * T], qn.rearrange("p h t -> p (h t)"), [Bqkv], [])
                k.dma("sp", ddq[:, 2 * T:4 * T], kn.rearrange("p h t -> p (h t)"), [Bqkv], [])
            k.memset(S32, 0.0, [BS32], eng="dve")
            if lat:
                for h in range(4):
                    k.dma("sp", S32[0:64, h, :], st_gdn[l, d, h], [], [BS32])
            k.copy(Sbf[0:64], S32[0:64], [BS32], [BSbf], eng="act")
            lastc = 63 if d == 0 else 0
            cur_blk = None
            for step in range(NCK):
                insts = inst_list(d, step)
                blk = insts[0][7]
                if blk != cur_blk:
                    cur_blk = blk
                    bc_ = slice(blk * TB, (blk + 1) * TB)
                    k.act(Eb, Dd[:, :, bc_], AF.Exp, [BD], [Bblk])
                    k.tt(kb32, kn[:, :, bc_], bet[:, :, bc_], ALU.mult, [Bqkv, Bbet], [Bblk])
                    k.copy(kbT, kb32, [Bblk], [Bblk], eng="act")
                    k.tt(kb32, kb32, Eb, ALU.mult, [Bblk], [Bblk])
                    for h in range(4):
                        pb = (h % 2) * 64
                        k.copy(kbeH[0:64, h, :], kb32[pb:pb + 64, h // 2, :], [Bblk], [Bblk])
                        k.copy(D0[0:64, h, :], Dd[pb:pb + 64, h // 2, bc_], [BD], [Bblk])
                        k.tt(qdH[0:64, h, :], qn[pb:pb + 64, h // 2, bc_], Eb[pb:pb + 64, h // 2, :], ALU.mult, [Bqkv, Bblk], [Bblk])
                    k.tt(vbe, vs[:, :, bc_], bet[:, :, bc_], ALU.mult, [Bqkv, Bbet], [Bblk])
                    d4 = Dd[:, :, bc_].rearrange("p h (n c) -> p h n c", c=64)
                    for hp in range(2):
                        k.tt(ktl[:, hp, :].rearrange("p (n c) -> p n c", c=64), bc(d4[:, hp, :, lastc:lastc + 1], [128, 8, 64]),
                             d4[:, hp], ALU.subtract, [BD], [Bblk])
                    k.act(ktl, ktl, AF.Exp, [Bblk], [Bblk])
                    k.tt(ktl, ktl, kn[:, :, bc_], ALU.mult, [Bblk, Bqkv], [Bblk])
                    e4 = Eb.rearrange("p h (n c) -> p h n c", c=64)
                    for sq_ in range(nseq):
                        for h in range(4):
                            pb = (h % 2) * 64
                            if lat:
                                k.copy(tails[0:64, h, 0:8], e4[pb:pb + 64, h // 2, :, lastc], [Bblk], [Bblk], eng="dve")
                            else:
                                k.copy(tails[0:64, sq_ * 4 + h, sq_ * 4:(sq_ + 1) * 4], e4[pb:pb + 64, h // 2, sq_ * 4:(sq_ + 1) * 4, lastc],
                                       [Bblk], [Bblk], eng="dve")
                W = SB[step % 2]
                NW_ = NI * 64
                v3 = lambda ap: ap[0:64, 0:NW_].rearrange("p (i c) -> p i c", c=64)
                for (ii, sq_, h, hp, pb, gc, bcl, _, cib) in insts:
                    ic = slice(ii * 64, (ii + 1) * 64)
                    k.tr(k.bank(0)[0:64, ic], D0[0:64, h, bcl], ident[0:64, 0:64], [Bblk, Bconst], [PS[0]])
                k.copy(W["Dcol"][0:64, :, 0:1], v3(k.bank(0))[:, :, 0:1], [PS[0]], [W["BDcol"]], eng="act")
                for (ii, sq_, h, hp, pb, gc, bcl, _, cib) in insts:
                    k.stt(W["Gs"][0:64, ii, :], D0[0:64, h, bcl], W["Dcol"][0:64, ii, 0:1], MBS, ALU.subtract, ALU.max,
                          [Bblk, W["BDcol"], Bcols], [W["BGs"]])
                    k.stt(W["GT"][0:64, ii, :], D0[0:64, h, bcl], W["Dcol"][0:64, ii, 0:1], MBT, ALU.subtract, ALU.min,
                          [Bblk, W["BDcol"], Bcols], [W["BGT"]])
                k.act(W["Gs"][0:64], W["Gs"][0:64], AF.Exp, [W["BGs"]], [W["BGs"]], scale=-1.0)
                k.act(W["GT"][0:64], W["GT"][0:64], AF.Exp, [W["BGT"]], [W["BGT"]])
                k.tt(W["GsT"][0:64], W["GT"][0:64], bc(SMT.unsqueeze(1), [64, NI, 64]), ALU.mult, [W["BGT"], Bcols], [W["BGsT"]])
                for (ii, sq_, h, hp, pb, gc, bcl, _, cib) in insts:
                    ic = slice(ii * 64, (ii + 1) * 64)
                    kb_ = kbT[pb:pb + 64, hp, bcl]
                    k_ = kn[pb:pb + 64, hp, gc]
                    q_ = qn[pb:pb + 64, hp, gc]
                    k.mm(k.bank(2)[0:64, ic], kb_, k_, True, True, [Bblk, Bqkv], [PS[2]])
                    k.mm(k.bank(3)[0:64, ic], k_, kb_, True, True, [Bblk, Bqkv], [PS[3]])
                    k.mm(k.bank(4)[0:64, ic], k_, q_, True, True, [Bqkv], [PS[4]])
                k.tt(W["A32"][0:64], v3(k.bank(2)), W["Gs"][0:64], ALU.mult, [PS[2], W["BGs"]], [W["BA32"]])
                k.copy(W["PA"][0:64], W["A32"][0:64], [W["BA32"]], [W["BPA"]], eng="act")
                k.tt(W["PB"][0:64], v3(k.bank(3)), W["GsT"][0:64], ALU.mult, [PS[3], W["BGsT"]], [W["BPB"]])
                k.tt(W["attnT"][0:64], v3(k.bank(4)), W["GT"][0:64], ALU.mult, [PS[4], W["BGT"]], [W["BattnT"]])
                idb = bc(identb[0:64, :].unsqueeze(1), [64, NI, 64])
                k.tt(W["TB"][0:64], idb, W["PB"][0:64], ALU.subtract, [Bcols, W["BPB"]], [W["BTB"]])
                def sq_mm():
                    for ii in range(NI):
                        ic = slice(ii * 64, (ii + 1) * 64)
                        k.mm(k.bank(2)[0:64, ic], W["PB"][0:64, ii, :], W["PA"][0:64, ii, :], True, True, [W["BPA"], W["BPB"]], [PS[2]])
                        k.mm(k.bank(3)[0:64, ic], W["PA"][0:64, ii, :], W["PB"][0:64, ii, :], True, True, [W["BPA"], W["BPB"]], [PS[3]])

                def sq_cp():
                    k.copy(W["PA"][0:64], v3(k.bank(2)), [PS[2]], [W["BPA"]], eng="act")
                    k.copy(W["PB"][0:64], v3(k.bank(3)), [PS[3]], [W["BPB"]], eng="dve")

                def t_mm(itn):
                    for ii in range(NI):
                        ic = slice(ii * 64, (ii + 1) * 64)
                        k.mm(k.bank(5)[0:64, ic], W["PA"][0:64, ii, :], W["TB"][0:64, ii, :], True, True, [W["BPA"], W["BTB"]], [PS[5]])

                def t_add(itn):
                    k.tt(W["TB"][0:64], W["TB"][0:64], v3(k.bank(5)), ALU.add, [W["BTB"], PS[5]], [W["BTB"]])

                NIT = 4
                sq_mm()
                sq_cp()
                for itn in range(NIT):
                    t_mm(itn)
                    if itn < NIT - 1:
                        sq_mm()
                    t_add(itn)
                    if itn < NIT - 1:
                        sq_cp()
                k.copy(W["TB32"][0:64], W["TB"][0:64], [W["BTB"]], [W["BTB32"]], eng="act")
                for ii in range(NI):
                    ic = slice(ii * 64, (ii + 1) * 64)
                    k.tr(k.bank(2)[0:64, ic], W["TB32"][0:64, ii, :], ident[0:64, 0:64], [W["BTB32"], Bconst], [PS[2]])
                    k.mm(k.bank(3)[0:64, ic], W["A32"][0:64, ii, :], W["TB32"][0:64, ii, :], True, True, [W["BA32"], W["BTB32"]], [PS[3]])
                k.copy(W["TBt"][0:64], v3(k.bank(2)), [PS[2]], [W["BTBt"]], eng="act")
                k.tt(W["E32"][0:64], bc(ident2[0:64, :].unsqueeze(1), [64, NI, 64]), W["TB32"][0:64], ALU.subtract, [Bcols, W["BTB32"]], [W["BE32"]])
                k.tt(W["E32"][0:64], W["E32"][0:64], v3(k.bank(3)), ALU.subtract, [W["BE32"], PS[3]], [W["BE32"]])
                for ii in range(NI):
                    ic = slice(ii * 64, (ii + 1) * 64)
                    k.mm(k.bank(5)[0:64, ic], W["TBt"][0:64, ii, :], W["E32"][0:64, ii, :], True, True, [W["BTBt"], W["BE32"]], [PS[5]])
                k.copy(W["TBf"][0:64], v3(k.bank(5)), [PS[5]], [W["BTBf"]], eng="dve")
                for (ii, sq_, h, hp, pb, gc, bcl, _, cib) in insts:
                    ic = slice(ii * 64, (ii + 1) * 64)
                    k.tr(k.bank(7)[0:64, ic], vbe[pb:pb + 64, hp, bcl], ident[pb:pb + 64, pb:pb + 64], [Bblk, Bconst], [PS[7]])
                    k.tr(k.bank(4)[0:64, ic], ktl[pb:pb + 64, hp, bcl], ident[pb:pb + 64, pb:pb + 64], [Bblk, Bconst], [PS[4]])
                k.copy(W["vbt"][0:64], v3(k.bank(7)), [PS[7]], [W["Bvbt"]], eng="act")
                k.copy(W["ktail"][0:64], v3(k.bank(4)), [PS[4]], [W["Bktail"]], eng="dve")
                for (ii, sq_, h, hp, pb, gc, bcl, _, cib) in insts:
                    ic = slice(ii * 64, (ii + 1) * 64)
                    k.mm(k.bank(5)[0:64, ic], kbeH[0:64, h, bcl], Sbf[0:64, ii, :], True, True, [Bblk, BSbf], [PS[5]])
                k.tt(W["rp"][0:64], W["vbt"][0:64], v3(k.bank(5)), ALU.subtract, [W["Bvbt"], PS[5]], [W["Brp"]])
                for ii in range(NI):
                    ic = slice(ii * 64, (ii + 1) * 64)
                    k.mm(k.bank(6)[0:64, ic], W["TBf"][0:64, ii, :], W["rp"][0:64, ii, :], True, True, [W["BTBf"], W["Brp"]], [PS[6]])
                k.copy(W["u"][0:64], v3(k.bank(6)), [PS[6]], [W["Bu"]], eng="act")
                for (ii, sq_, h, hp, pb, gc, bcl, _, cib) in insts:
                    ic = slice(ii * 64, (ii + 1) * 64)
                    k.mm(k.bank(7)[0:64, ic], Sbf[0:64, ii, :], qdH[0:64, h, bcl], True, False, [BSbf, Bblk], [PS[7]])
                    k.mm(k.bank(7)[0:64, ic], W["u"][0:64, ii, :], W["attnT"][0:64, ii, :], False, True, [W["Bu"], W["BattnT"]], [PS[7]])
                    k.mm(k.bank(4)[0:64, ic], W["ktail"][0:64, ii, :], W["u"][0:64, ii, :], True, True, [W["Bktail"], W["Bu"]], [PS[4]])
                for sq_ in range(nseq):
                    for par in range(2):
                        pb = par * 64
                        ii0 = sq_ * 4 + par
                        gc = [r_[5] for r_ in insts if r_[1] == sq_ and r_[2] == par][0]
                        src7 = k.bank(7)[0:64, ii0 * 64:(ii0 + 3) * 64].rearrange("p (i c) -> p i c", c=64)[:, 0:3:2, :]
                        dst = ocT[pb:pb + 64, :, gc]
                        if d == 0:
                            k.copy(dst, src7, [PS[7]], [Boc], eng="act")
                        else:
                            k.tt(dst, dst, src7, ALU.add, [Boc, PS[7]], [Boc])
                cib0 = insts[0][8] if lat else None
                if lat:
                    k.tt(S32[0:64], S32[0:64], bc(tails[0:64, :, cib0:cib0 + 1], [64, NI, 64]), ALU.mult, [BS32, Bblk], [BS32])
                else:
                    for sq_ in range(nseq):
                        cb = [r_[8] for r_ in insts if r_[1] == sq_][0]
                        k.tt(S32[0:64, sq_ * 4:(sq_ + 1) * 4, :], S32[0:64, sq_ * 4:(sq_ + 1) * 4, :],
                             bc(tails[0:64, sq_ * 4:(sq_ + 1) * 4, cb:cb + 1], [64, 4, 64]), ALU.mult, [BS32, Bblk], [BS32])
                k.tt(S32[0:64], S32[0:64], v3(k.bank(4)), ALU.add, [BS32, PS[4]], [BS32])
                k.copy(Sbf[0:64], S32[0:64], [BS32], [BSbf], eng="act")
            if not lat:
                for sq_ in range(nseq):
                    for h in range(4):
                        k.dma("sp", o_gdn[sq_, l, d, h], S32[0:64, sq_ * 4 + h, :], [BS32], [])
        P.barrier()
        A.top = alias_mark
        wGz = A.alloc([8, 256], BF16)
        BwGz = Buf()
        k.dma("sp", wGz, w_in_b[l].rearrange("(c p) n -> p c n", p=128)[:, :, C_GZ:C_GZ + 256], [Bcv[("w_in", l)]], [BwGz])
        gzt = A.alloc([TB], BF16)
        Bgzt = Buf()
        sqb = A.alloc([TB], BF16)
        Bsqb = Buf()
        rsn = A.alloc([TB], F32)
        Brsn = Buf()
        for hp in range(2):
            for tb in range(NBLK):
                cols = slice(tb * TB, (tb + 1) * TB)
                for c in range(8):
                    k.mm(k.bank(3), wGz[:, c, hp * 128:(hp + 1) * 128], hT[:, c, cols], c == 0, c == 7, [BwGz, BhT], [PS[3]])
                k.act(gzt, k.bank(3), AF.Silu, [PS[3]], [Bgzt])
                k.act(sqb, ocT[:, hp, cols], AF.Square, [Boc], [Bsqb])
                k.mm(k.bank(2), blockones, sqb, True, True, [Bsqb, Bcols], [PS[2]])
                k.act(rsn, k.bank(2), AF.Sqrt, [PS[2], Bconst], [Brsn], scale=1.0 / 64, bias=epsb)
                k.recip(rsn, rsn, [Brsn], [Brsn])
                k.stt(rsn, ocT[:, hp, cols], gnorm, rsn, ALU.mult, ALU.mult, [Boc, Brsn, Bcols], [Brsn])
                k.tt(oT[:, 4 + hp, cols], rsn, gzt, ALU.mult, [Brsn, Bgzt], [BoT])
        P.barrier()
        A.top = base

    mixer_base = [0]

    groups = [dict(name="ctx", tok0=0, T=512, nseq=2, L=256, kc=0),
              dict(name="lat", tok0=512, T=2048, nseq=1, L=2048, kc=1)]

    for l in range(nlayers):
        for g in groups:
            T = g["T"]
            kc = g["kc"]
            nblk = T // TB
            mA = A.top
            xblk = [A.alloc([8, TB], F32) for _ in range(2)]
            Bxb = [Buf(), Buf()]
            sq = A.alloc([8, TB], BF16)
            rstd = A.alloc([TB], F32)
            tmp = A.alloc([8, TB], F32)
            Wn = (sq, Buf(), rstd, Buf(), tmp, Buf())
            wg32 = A.alloc([8, 16], F32)
            Bwg32 = Buf()
            k.dma("sp", wg32, w_in[l].rearrange("(c p) n -> p c n", p=128)[:, :, C_GA:C_GA + 16], [], [Bwg32], slow=True)
            for tb in range(nblk):
                gb = (g["tok0"] // TB) + tb
                s = tb % 2
                k.dma("sp", xblk[s], xTv[:, :, gb * TB:(gb + 1) * TB], [BxT[gb]], [Bxb[s]])
                prenorm(xblk[s], Bxb[s], hT, BhT, slice(tb * TB, (tb + 1) * TB), A1, 0, l, kc, Wn, gate=(wg32, Bwg32))
            P.barrier()
            A.top = mA
            if stop == "A":
                P.emit(); ES.close(); return nc, P
            if stub_mixer:
                k.copy(oT[:, :, 0:T], hT[:, :, 0:T], [BhT], [BoT], eng="dve")
            else:
                mB = A.top
                mixer_base[0] = mB
                if "noattn" not in (dbg or ()):
                    mixer_attn(l, g)
                    P.barrier()
                A.top = mB
                if "nohy" not in (dbg or ()):
                    mixer_hyena(l, g)
                if "nogdn" not in (dbg or ()):
                    mixer_gdn(l, g)
                if 'oT' in (dbg or ()) and l == 0:
                    dd = dout("dbg_oT_" + g["name"], [8, 128, T])
                    dtmp = A.alloc([T], F32)
                    Bd = Buf()
                    for c in range(8):
                        k.copy(dtmp, oT[:, c, 0:T], [BoT], [Bd], eng="dve")
                        k.dma("sp", dd[c], dtmp, [Bd], [])
                    P.barrier()
                    A.top = mB
                if stop == "B" + g["name"][0]:
                    P.emit(); ES.close(); return nc, P
            P.barrier()
            mC = A.top
            wout = A.alloc([8, D], BF16)
            Bwout = Buf()
            k.dma("sp", wout, w_out_b[l].rearrange("(c p) n -> p c n", p=128), [Bcv[("w_out", l)]], [Bwout])
            xblk = A.alloc([8, TB], F32)
            Bxb = Buf()
            xn = xblk
            Bxn = Bxb
            mT = A.alloc([8, TB], F32)
            BmT = Buf()
            sq = A.alloc([8, TB], BF16)
            Bsq = Buf()
            rstd = A.alloc([TB], F32)
            Brstd = Buf()
            h2T = A.alloc([8, TB], BF16)
            Bh2 = Buf()
            fT = hT.rearrange("p a b -> p (a b)").rearrange("p (f t) -> p f t", f=32)
            BfT = BhT
            rl = [A.alloc([TB], BF16) for _ in range(2)]
            Brl = [Buf(), Buf()]
            w1s = [A.alloc([8, 512], BF16) for _ in range(2)]
            Bw1 = [Buf(), Buf()]
            w2s = [A.alloc([4, D], BF16) for _ in range(2)]
            Bw2 = [Buf(), Buf()]
            w1v = w1_b[l].rearrange("(c p) n -> p c n", p=128)
            w2v = w2_b[l].rearrange("(c p) n -> p c n", p=128)

            def post(Bsrc_banks, Bres, res, Bsc, l, kc, dstx, Bdstx):
                for c in range(8):
                    k.copy(mT[:, c, :], k.bank(c), [PS[c]], [BmT])
                k.act(sq, mT, AF.Square, [BmT], [Bsq])
                if stop == "P1":
                    raise StopBuild()
                rstd_from_sq(sq, Bsq, 0, rstd, Brstd)
                if stop == "P2":
                    raise StopBuild()
                k.tt(mT, mT, bc(rstd.unsqueeze(1), [128, 8, TB]), ALU.mult, [BmT, Brstd], [BmT])
                if stop == "P3":
                    raise StopBuild()
                for c in range(8):
                    k.stt(dstx[:, c, :], mT[:, c, :], Bsc[:, l, c, kc:kc + 1], res[:, c, :], ALU.mult, ALU.add,
                          [BmT, Bmod, Bres], [Bdstx])

            for tb in range(nblk):
                gb = (g["tok0"] // TB) + tb
                cols = slice(tb * TB, (tb + 1) * TB)
                k.dma("sp", xblk, xTv[:, :, gb * TB:(gb + 1) * TB], [BxT[gb]], [Bxb])
                if stop == "C0a":
                    P.emit(); ES.close(); return nc, P
                for dc in range(8):
                    for ec in range(8):
                        k.mm(k.bank(dc), wout[:, ec, dc * 128:(dc + 1) * 128], oT[:, ec, cols], ec == 0, ec == 7,
                             [Bwout, BoT], [PS[dc]])
                if stop == "C0b":
                    P.emit(); ES.close(); return nc, P
                try:
                    post(None, Bxb, xblk, B1, l, kc, xn, Bxn)
                except StopBuild:
                    P.emit(); ES.close(); return nc, P
                if stop == "C1":
                    P.emit(); ES.close(); return nc, P
                Wn = (sq, Bsq, rstd, Brstd, mT, BmT)
                prenorm(xn, Bxn, h2T, Bh2, slice(0, TB), A2, 24, l, kc, Wn)
                for s8 in range(8):
                    ws = w1s[s8 % 2]
                    bw = Bw1[s8 % 2]
                    k.dma("sp", ws, w1v[:, :, s8 * 512:(s8 + 1) * 512], [Bcv[("w1", l)]], [bw])
                    for fj in range(4):
                        fc = s8 * 4 + fj
                        bk = 1 + (fc % 4)
                        for c in range(8):
                            k.mm(k.bank(bk), ws[:, c, fj * 128:(fj + 1) * 128], h2T[:, c, :], c == 0, c == 7,
                                 [bw, Bh2], [PS[bk]])
                        r = rl[fc % 2]
                        br = Brl[fc % 2]
                        k.act(r, k.bank(bk), AF.Relu, [PS[bk]], [br])
                        k.tt(fT[:, fc, :], r, r, ALU.mult, [br], [BfT])
                if stop == "C2":
                    P.emit(); ES.close(); return nc, P
                for s8 in range(8):
                    ws = w2s[s8 % 2]
                    bw = Bw2[s8 % 2]
                    k.dma("sp", ws, w2v[:, s8 * 4:(s8 + 1) * 4, :], [Bcv[("w2", l)]], [bw])
                    for fj in range(4):
                        fc = s8 * 4 + fj
                        for dc in range(8):
                            k.mm(k.bank(dc), ws[:, fj, dc * 128:(dc + 1) * 128], fT[:, fc, :], fc == 0, fc == 31,
                                 [bw, BfT], [PS[dc]])
                post(None, Bxn, xn, B2, l, kc, xblk, Bxb)
                k.dma("sp", xTv[:, :, gb * TB:(gb + 1) * TB], xblk, [Bxb], [BxT[gb]])
            P.barrier()
            A.top = mC

    xi = [A.alloc([8, 128], F32) for _ in range(2)]
    yo = [A.alloc([D], F32) for _ in range(2)]
    Bxi = [Buf(), Buf()]
    Byo = [Buf(), Buf()]
    for i in range(TT // 128):
        s = i % 2
        k.dma("sp", xi[s], xTv[:, :, i * 128:(i + 1) * 128], [BxT[i // 4]], [Bxi[s]])
        for c in range(8):
            bk = 2 * s + c // 4
            k.tr(k.bank(bk)[:, (c % 4) * 128:(c % 4 + 1) * 128], xi[s][:, c, :], ident, [Bxi[s], Bconst], [PS[bk]])
        for hh in range(2):
            bk = 2 * s + hh
            k.copy(yo[s][:, hh * 512:(hh + 1) * 512], k.bank(bk), [PS[bk]], [Byo[s]])
        k.dma("sp", y_tok[i * 128:(i + 1) * 128, :], yo[s], [Byo[s]], [])
    P.emit()
    ES.close()
    return nc, P


def _rope_tables(dim, nrows_total, row0):
    rows = 2048 // 64
    row = np.repeat(np.arange(rows), 64).astype(np.float32)
    col = np.tile(np.arange(64), rows).astype(np.float32)
    nf = dim // 4
    inv = (10000.0 ** (-np.arange(nf, dtype=np.float32) / nf)).astype(np.float32)
    ang = np.concatenate([row[:, None] * inv, col[:, None] * inv], -1).astype(np.float32)
    cos = np.cos(ang).astype(np.float32)
    sin = np.sin(ang).astype(np.float32)
    out = np.zeros((2, nrows_total, 2048), np.float32)
    for f in range(dim):
        out[0, row0 + f] = cos[:, f // 2]
        out[1, row0 + f] = sin[:, f // 2] * (-1.0 if f % 2 == 0 else 1.0)
    return out


_CONST = {}


def _consts():
    if _CONST:
        return _CONST
    _CONST["ropeA"] = _rope_tables(32, 128, 64)
    _CONST["ropeB"] = _rope_tables(64, 64, 0)
    ps = np.zeros((128, 128), np.float32)
    for m in range(128):
        ps[m ^ 1, m] = 1.0
    _CONST["pswap"] = ps
    j = np.arange(128)[:, None]
    i = np.arange(128)[None, :]
    _CONST["masks"] = np.stack([(j >= i), (j <= i)], 0).astype(np.float32).astype(ml_dtypes.bfloat16)
    r = np.arange(64)[:, None]
    c_ = np.arange(64)[None, :]
    _CONST["negm"] = np.stack([(c_ <= r), (c_ >= r)], 0).astype(np.float32)
    sel = np.zeros((16, 8, 128), np.float32)
    for which in range(2):
        for d_ in range(2):
            for hp in range(2):
                for p in range(128):
                    sel[which * 8 + d_ * 4 + 2 * hp + (p // 64), which * 4 + d_ * 2 + hp, p] = 1.0
    _CONST["gsel"] = sel
    _CONST["smask"] = np.stack([(c_ < r), (c_ > r)], 0).astype(np.float32)
    for L_, sfx in ((256, "s"), (2048, "b")):
        N = 2 * L_
        idx = np.arange(L_ + 1, dtype=np.int64)
        th = 2.0 * np.pi * ((idx[:, None] * idx[None, :]) % N).astype(np.float64) / N
        _CONST["dft_" + sfx] = np.stack([np.cos(th), np.sin(th)], 0).astype(np.float32).astype(ml_dtypes.bfloat16)
        t = np.arange(L_, dtype=np.float32)
        t01 = t / np.float32(max(L_ - 1, 1))
        w = np.float32(2.0 * math.pi) * t / np.float32(L_)
        bands = np.linspace(1e-4, 7, 8, dtype=np.float32)
        feats = np.concatenate([t01[:, None], np.cos(w[:, None] * bands), -np.sin(w[:, None] * bands)], -1).astype(np.float32)
        _CONST["feats_" + sfx] = np.ascontiguousarray(feats.T)
        dist = (np.abs(t - (L_ // 2)) / np.float32(L_ / 2)).astype(np.float32)
        _CONST["negdist_" + sfx] = np.ascontiguousarray((-dist).reshape(L_ // 128, 128).T)
        nf = L_ // 128 + 1
        f = (np.arange(nf)[None, :] * 128 + np.arange(128)[:, None])
        wfn = np.where((f == 0) | (f == L_), 1.0, 2.0) / N
        alpha = np.array([1.0, 0.0, -1.0, 0.0])[f % 4]
        beta = np.array([0.0, 1.0, 0.0, -1.0])[f % 4]
        _CONST["aw_" + sfx] = (alpha * wfn).astype(np.float32)
        _CONST["bw_" + sfx] = (beta * wfn).astype(np.float32)
    return _CONST


def core_inputs(inp, i):
    b = i % 4
    f = lambda a: np.ascontiguousarray(a, dtype=np.float32)
    d = dict(
        x_tok=f(np.concatenate([inp["x_prompt"][2 * i], inp["x_prompt"][2 * i + 1], inp["x_sample"][b]], 0)),
        conds=f(np.stack([inp["c_ctx"], inp["c"][b]], 0)),
        w_ada=f(inp["w_ada"]), b_ada=f(inp["b_ada"]),
        gvec=f(np.stack([inp["g_pre_mix"], inp["g_post_mix"], inp["g_pre_mlp"], inp["g_post_mlp"]], 0)),
        w_in=f(inp["w_in"]), w_out=f(inp["w_out"]), mlp_w1=f(inp["mlp_w1"]), mlp_w2=f(inp["mlp_w2"]),
        mla_kv_norm=f(inp["mla_kv_norm"]), mla_w_ukv=f(inp["mla_w_ukv"]), swa_sink=f(inp["swa_sink"]),
        c_ckv=f(inp["cache_mla_ckv"][b]), c_kpe=f(inp["cache_mla_kpe"][b]),
        c_swk=f(inp["cache_swa_k"][b].reshape(DEPTH, 512, 128)), c_swv=f(inp["cache_swa_v"][b].reshape(DEPTH, 512, 128)),
        gdn_conv=f(inp["gdn_conv"]), gdn_a_log=f(inp["gdn_a_log"].reshape(DEPTH, 8)), gdn_dt_bias=f(inp["gdn_dt_bias"].reshape(DEPTH, 8)),
        gdn_norm=f(inp["gdn_norm"]), st_gdn=f(inp["state_gdn"][b]),
        hy_conv=f(inp["hy_conv"]), hy_w1=f(inp["hy_w1"]), hy_b1=f(inp["hy_b1"]), hy_w2=f(inp["hy_w2"]), hy_b2=f(inp["hy_b2"]),
        hy_w3=f(inp["hy_w3"]), hy_freq=f(inp["hy_freq"]), hy_decay=f(inp["hy_decay"]), hy_bias=f(inp["hy_bias"]),
    )
    d.update(_consts())
    return d


_CACHE = {}


def kernel(**inputs):
    inp = {k_: np.asarray(v) for k_, v in inputs.items()}
    if "nc" not in _CACHE:
        _CACHE["nc"] = build_program()[0]
    nc = _CACHE["nc"]
    in_maps = [core_inputs(inp, i) for i in range(8)]
    res = run_bass_kernel_spmd(nc, in_maps, core_ids=list(range(8)))
    R = res.results
    y_prompt = np.stack([R[i]["y_tok"][s_ * 256:(s_ + 1) * 256] for i in range(8) for s_ in range(2)], 0).astype(np.float32)
    y_sample = np.stack([R[b]["y_tok"][512:2560] for b in range(4)], 0).astype(np.float32)
    cat = lambda nm: np.concatenate([R[i][nm] for i in range(8)], 0).astype(np.float32)
    new_ckv = cat("o_ckv")
    new_kpe = cat("o_kpe")
    new_k = cat("o_swk").reshape(16, DEPTH, 256, 2, 64)
    new_v = cat("o_swv").reshape(16, DEPTH, 256, 2, 64)
    new_st = cat("o_gdn")
    return (y_prompt, y_sample, new_ckv, new_kpe, new_k, new_v, new_st)
```

```python
import math
import contextlib
import numpy as np
import ml_dtypes
import concourse.bass as bass
import concourse.mybir as mybir
from concourse.bass_utils import run_bass_kernel_spmd

F32 = mybir.dt.float32
BF16 = mybir.dt.bfloat16
AF = mybir.ActivationFunctionType
ALU = mybir.AluOpType

COMPUTE = ("pe", "act", "dve", "pool")
QUEUES = ("pe", "act", "dve", "pool", "sp")
NDMASEM = 12

D = 1024
NCH = 8
TT = 2560
TB = 512
DEPTH = 2
IN_COLS = 2864
EPS = 1e-6
C_MQ, C_CKV, C_KPE, C_SQ, C_SK, C_SV, C_GQKV, C_GZ, C_GA, C_GB, C_HU = (
    0, 384, 512, 544, 800, 928, 1056, 1824, 2080, 2088, 2096)


class Buf:
    __slots__ = ("w", "r", "excl", "rg")

    def __init__(self, excl=False):
        self.w = []
        self.r = []
        self.excl = excl
        self.rg = None


class Prog:
    def __init__(self, nc):
        self.nc = nc
        self.ops = []
        self.last = {q: None for q in QUEUES}
        self.dmas_since = []
        self.force = set()

    def op(self, eng, fn, reads=(), writes=(), dma=False, pe_force=False, bg=False):
        writes = list(writes) + [b for b in reads if b.excl]
        reads = [b for b in reads if not b.excl]
        deps = set()
        for b in reads:
            deps.update(b.w)
        for b in writes:
            if dma and not b.r:
                deps.update(w for w in b.w if not self.ops[w][3])
            else:
                deps.update(b.w)
            deps.update(b.r)
        oid = len(self.ops)
        self.ops.append((eng, fn, sorted(deps), dma))
        if pe_force:
            self.force.add(oid)
        for b in reads:
            b.r.append(oid)
        for b in writes:
            if dma and not b.r and b.w and all(self.ops[w][3] for w in b.w):
                b.w = b.w + [oid]
            else:
                b.w = [oid]
            b.r = []
        if dma:
            if not bg:
                self.dmas_since.append(oid)
        else:
            self.last[eng] = oid
        return oid

    def barrier(self):
        lasts = [v for v in self.last.values() if v is not None]
        deps = sorted(set(lasts + self.dmas_since))
        self.dmas_since = []
        for q in QUEUES:
            self.ops.append((q, None, deps, False))

    def emit(self):
        nc = self.nc
        ops = self.ops
        all_dma = [i for i, o in enumerate(ops) if o[3]]
        ops.append(("sp", None, all_dma, False))
        n = len(ops)
        eng_idx = [0] * n
        cnt = {q: 0 for q in QUEUES}
        for i, o in enumerate(ops):
            cnt[o[0]] += 1
            eng_idx[i] = cnt[o[0]]
        clock = {q: ({c: 0 for c in QUEUES}, set()) for q in QUEUES}
        opclock = [None] * n
        waits = [None] * n
        needed = [False] * n
        for i, (eng, fn, deps, isdma) in enumerate(ops):
            ck, dset = clock[eng]
            w = []
            for d in deps:
                deng, dfn, _, disdma = ops[d]
                if disdma:
                    if d in dset:
                        continue
                    w.append(d)
                    needed[d] = True
                    dset.add(d)
                else:
                    if dfn is None:
                        continue
                    if deng == eng and eng == "pe" and i not in self.force:
                        continue
                    if ck[deng] >= eng_idx[d]:
                        continue
                    w.append(d)
                    needed[d] = True
                    ck[deng] = eng_idx[d]
                ock = opclock[d]
                for c in QUEUES:
                    if ock[c] > ck[c]:
                        ck[c] = ock[c]
            waits[i] = w
            opclock[i] = dict(ck)
        stack = contextlib.ExitStack()
        sems = {q: stack.enter_context(nc.semaphore("s_" + q)) for q in COMPUTE}
        dsem = {q: [stack.enter_context(nc.semaphore("d_%s_%d" % (q, j))) for j in range(NDMASEM)]
                for q in QUEUES}
        sigval = [None] * n
        ccount = {q: 0 for q in COMPUTE}
        dcount = {q: 0 for q in QUEUES}
        dslot_val = {q: [0] * NDMASEM for q in QUEUES}
        pre_wait = [None] * n
        for i, (eng, fn, deps, isdma) in enumerate(ops):
            if isdma:
                k = dcount[eng] % NDMASEM
                dcount[eng] += 1
                if dslot_val[eng][k] > 0:
                    pre_wait[i] = (dsem[eng][k], dslot_val[eng][k])
                dslot_val[eng][k] += 16
                sigval[i] = (dsem[eng][k], dslot_val[eng][k])
            elif needed[i]:
                ccount[eng] += 1
                sigval[i] = (sems[eng], ccount[eng])
        per = {q: [] for q in QUEUES}
        for i, o in enumerate(ops):
            per[o[0]].append(i)
        self.stats = {q: len(per[q]) for q in QUEUES}
        self.stats["waits"] = sum(len(w) for w in waits)
        with nc.Block() as block:
            def mk(q):
                def body(e):
                    for i in per[q]:
                        eng, fn, deps, isdma = ops[i]
                        if pre_wait[i] is not None:
                            e.wait_ge(pre_wait[i][0], pre_wait[i][1])
                        for d in waits[i]:
                            e.wait_ge(sigval[d][0], sigval[d][1])
                        if fn is None:
                            continue
                        ins = fn(e)
                        if isdma:
                            ins.then_inc(sigval[i][0], 16)
                        elif needed[i]:
                            ins.then_inc(sigval[i][0], 1)
                return body
            block.tensor(mk("pe"))
            block.scalar(mk("act"))
            block.vector(mk("dve"))
            block.gpsimd(mk("pool"))
            block.sync(mk("sp"))
        stack.close()


class StopBuild(Exception):
    pass


class Arena:
    def __init__(self, tile, nbytes):
        self.t = tile
        self.n = nbytes
        self.top = 0

    def alloc(self, shape, dt):
        n = int(np.prod(shape))
        nb = n * (2 if dt == BF16 else 4)
        off = self.top
        self.top = off + (nb + 63) // 64 * 64
        assert self.top <= self.n, ("SBUF arena overflow", self.top, self.n)
        if dt == BF16:
            v = self.t[:, off // 2: off // 2 + n]
        else:
            v = self.t[:, off // 2: off // 2 + 2 * n].bitcast(dt)
        if len(shape) == 2:
            v = v.rearrange("p (a b) -> p a b", a=shape[0])
        elif len(shape) == 3:
            v = v.rearrange("p (a b c) -> p a b c", a=shape[0], b=shape[1])
        return v


class K:
    def __init__(self, nc, P, arena, psum):
        self.nc = nc
        self.P = P
        self.A = arena
        self.psum = psum
        self.PS = [Buf(excl=True) for _ in range(8)]
        self.rr = 0

    def bank(self, i):
        return self.psum[:, i * 512:(i + 1) * 512]

    def _rg(self, stat, W):
        key = (stat.base_partition(), stat.shape[0])
        force = False
        for b in W:
            if b.excl:
                if b.rg is not None and b.rg != key:
                    force = True
                b.rg = key
        return force

    def mm(self, out, lhsT, rhs, start, stop, R, W):
        f = self._rg(lhsT, W)
        self.P.op("pe", lambda e: e.matmul(out, lhsT=lhsT, rhs=rhs, start=start, stop=stop), R, W, pe_force=f)

    def tr(self, out, in_, ident, R, W):
        f = self._rg(in_, W)
        self.P.op("pe", lambda e: e.transpose(out, in_, ident), R, W, pe_force=f)

    def act(self, out, in_, func, R, W, scale=None, bias=None):
        kw = {}
        if scale is not None:
            kw["scale"] = scale
        if bias is not None:
            kw["bias"] = bias
        self.P.op("act", lambda e: e.activation(out=out, in_=in_, func=func, **kw), R, W)

    def tt(self, out, in0, in1, op, R, W, eng="dve"):
        self.P.op(eng, lambda e: e.tensor_tensor(out=out, in0=in0, in1=in1, op=op), R, W)

    def ts(self, out, in0, s1, s2, op0, op1, R, W, eng="dve"):
        if op1 is None:
            self.P.op(eng, lambda e: e.tensor_scalar(out=out, in0=in0, scalar1=s1, scalar2=None, op0=op0), R, W)
        else:
            self.P.op(eng, lambda e: e.tensor_scalar(out=out, in0=in0, scalar1=s1, scalar2=s2, op0=op0, op1=op1), R, W)

    def stt(self, out, in0, scalar, in1, op0, op1, R, W):
        self.P.op("dve", lambda e: e.scalar_tensor_tensor(out=out, in0=in0, scalar=scalar, in1=in1, op0=op0, op1=op1), R, W)

    def copy(self, out, in_, R, W, eng=None):
        if eng is None:
            self.rr ^= 1
            eng = "dve" if self.rr else "act"
        if eng == "act":
            self.P.op("act", lambda e: e.activation(out=out, in_=in_, func=AF.Copy), R, W)
        else:
            self.P.op(eng, lambda e: e.tensor_copy(out=out, in_=in_), R, W)

    def recip(self, out, in_, R, W):
        self.P.op("dve", lambda e: e.reciprocal(out=out, in_=in_), R, W)

    def memset(self, out, val, W, eng="pool"):
        self.P.op(eng, lambda e: e.memset(out, val), (), W)

    def dma(self, q, out, in_, R, W, slow=False, bg=False):
        if bg:
            self.P.op(q, lambda e: e.dma_start(out=out, in_=in_), R, W, dma=True, bg=True)
        elif slow:
            self.P.op(q, lambda e: e.dma_start(out=out, in_=in_, allow_slow_non_contiguous=True), R, W, dma=True)
        else:
            self.P.op(q, lambda e: e.dma_start(out=out, in_=in_), R, W, dma=True)


def bc(ap, shape):
    return ap.to_broadcast(shape)


def build_program(dbg=None, stub_mixer=False, nlayers=DEPTH, stop=None):
    nc = bass.Bass("TRN2", target_bir_lowering=False)
    P = Prog(nc)
    ES = contextlib.ExitStack()

    def din(name, shape, dt=F32):
        return nc.dram_tensor(name, list(shape), dt, kind="ExternalInput").ap()

    def dout(name, shape, dt=F32):
        return nc.dram_tensor(name, list(shape), dt, kind="ExternalOutput").ap()

    def dscr(name, shape, dt=F32):
        return nc.dram_tensor(name, list(shape), dt, kind="Internal").ap()

    x_tok = din("x_tok", [TT, D])
    conds = din("conds", [2, D])
    w_ada = din("w_ada", [DEPTH, D, 6 * D])
    b_ada = din("b_ada", [DEPTH, 6 * D])
    gvec = din("gvec", [4, DEPTH, D])
    w_in = din("w_in", [DEPTH, D, IN_COLS])
    w_out = din("w_out", [DEPTH, D, D])
    mlp_w1 = din("mlp_w1", [DEPTH, D, 4 * D])
    mlp_w2 = din("mlp_w2", [DEPTH, 4 * D, D])
    y_tok = dout("y_tok", [TT, D])
    mla_kv_norm = din("mla_kv_norm", [DEPTH, 128])
    mla_w_ukv = din("mla_w_ukv", [DEPTH, 128, 512])
    swa_sink = din("swa_sink", [DEPTH, 4])
    c_ckv = din("c_ckv", [DEPTH, 512, 128])
    c_kpe = din("c_kpe", [DEPTH, 512, 32])
    c_swk = din("c_swk", [DEPTH, 512, 128])
    c_swv = din("c_swv", [DEPTH, 512, 128])
    ropeA = din("ropeA", [2, 128, 2048])
    ropeB = din("ropeB", [2, 64, 2048])
    pswap_d = din("pswap", [128, 128])
    masks_d = din("masks", [2, 128, 128], BF16)
    hy_conv = din("hy_conv", [DEPTH, 3, 768])
    hy_w1 = din("hy_w1", [DEPTH, 17, 64])
    hy_b1 = din("hy_b1", [DEPTH, 64])
    hy_w2 = din("hy_w2", [DEPTH, 64, 64])
    hy_b2 = din("hy_b2", [DEPTH, 64])
    hy_w3 = din("hy_w3", [DEPTH, 64, 512])
    hy_freq = din("hy_freq", [DEPTH, 2, 64])
    hy_decay = din("hy_decay", [DEPTH, 512])
    hy_bias = din("hy_bias", [DEPTH, 2, 256])
    HY = {}
    for L_, sfx in ((256, "s"), (2048, "b")):
        HY[L_] = dict(tab=din("dft_" + sfx, [2, L_ + 1, L_ + 1], BF16), feats=din("feats_" + sfx, [17, L_]),
                      negdist=din("negdist_" + sfx, [128, L_ // 128]), aw=din("aw_" + sfx, [128, L_ // 128 + 1]),
                      bw=din("bw_" + sfx, [128, L_ // 128 + 1]))
    gdn_conv = din("gdn_conv", [DEPTH, 3, 768])
    gdn_a_log = din("gdn_a_log", [DEPTH, 8])
    gdn_dt_bias = din("gdn_dt_bias", [DEPTH, 8])
    gdn_norm = din("gdn_norm", [DEPTH, 64])
    st_gdn = din("st_gdn", [DEPTH, 2, 4, 64, 64])
    negm_d = din("negm", [2, 64, 64])
    sel_d = din("gsel", [16, 8, 128])
    sm_d = din("smask", [2, 64, 64])
    o_gdn = dout("o_gdn", [2, DEPTH, 2, 4, 64, 64])
    o_ckv = dout("o_ckv", [2, DEPTH, 256, 128])
    o_kpe = dout("o_kpe", [2, DEPTH, 256, 32])
    o_swk = dout("o_swk", [2, DEPTH, 256, 128])
    o_swv = dout("o_swv", [2, DEPTH, 256, 128])
    xT = dscr("xT", [D, TT])
    w_in_b = dscr("w_in_b", [DEPTH, D, IN_COLS], BF16)
    w_out_b = dscr("w_out_b", [DEPTH, D, D], BF16)
    w1_b = dscr("w1_b", [DEPTH, D, 4 * D], BF16)
    w2_b = dscr("w2_b", [DEPTH, 4 * D, D], BF16)
    Bcv = {}
    xTv = xT.rearrange("(c p) t -> p c t", p=128)
    dbg_out = {}

    ARENA_BYTES = 206 * 1024
    arena_t = ES.enter_context(nc.sbuf_tensor("arena", [128, ARENA_BYTES // 2], BF16))
    psum = ES.enter_context(nc.psum_tensor("psum", [128, 4096], F32))
    A = Arena(arena_t, ARENA_BYTES)
    k = K(nc, P, A, psum)
    PS = k.PS

    ident = A.alloc([128], F32)
    Bconst = Buf()
    k.memset(ident, 1.0, [Bconst])
    P.op("pool", lambda e: e.affine_select(out=ident, in_=ident, pattern=[[-1, 128]], compare_op=ALU.is_equal,
                                           fill=0.0, base=0, channel_multiplier=1), [Bconst], [Bconst])
    ones_b = A.alloc([128], BF16)
    k.memset(ones_b, 1.0, [Bconst])
    ones_f = A.alloc([128], F32)
    k.memset(ones_f, 1.0, [Bconst])

    pswapF = A.alloc([128], F32)
    k.dma("sp", pswapF, pswap_d, [], [Bconst])
    masks = A.alloc([2, 128], BF16)
    k.dma("sp", masks, masks_d.rearrange("m p q -> p m q"), [], [Bconst])
    kvg = A.alloc([DEPTH], F32)
    k.dma("sp", kvg, mla_kv_norm.rearrange("l p -> p l"), [], [Bconst], slow=True)
    sinkE = A.alloc([DEPTH * 4], F32)
    k.dma("sp", sinkE, swa_sink.rearrange("l h -> (l h)").partition_broadcast(128), [], [Bconst], slow=True)
    k.act(sinkE, sinkE, AF.Exp, [Bconst], [Bconst])
    modT = A.alloc([DEPTH, 48, 2], F32)
    A1 = A.alloc([DEPTH, 8, 2], F32)
    B1 = A.alloc([DEPTH, 8, 2], F32)
    A2 = A.alloc([DEPTH, 8, 2], F32)
    B2 = A.alloc([DEPTH, 8, 2], F32)
    Bmod = Buf()
    mark0 = A.top
    stgA = A.alloc([128], F32)
    stgB = A.alloc([128], F32)
    TAc = A.alloc([128], F32)
    TBc = A.alloc([128], F32)
    scT = A.alloc([8, 2], BF16)
    Bc_ = Buf()
    Bstg = Buf()
    k.memset(stgA, 0.0, [Bstg])
    k.memset(stgB, 0.0, [Bstg])
    k.dma("sp", stgA[0:16, :], conds.rearrange("k (c p) -> (k c) p", p=128), [], [Bstg])
    k.dma("sp", stgA[16:80, :], gvec.rearrange("g l (c p) -> (g l c) p", p=128), [], [Bstg])
    k.dma("sp", stgB[0:96, :], b_ada.rearrange("l (j p) -> (l j) p", p=128), [], [Bstg])
    k.tr(k.bank(7)[:, 0:128], stgA, ident, [Bstg, Bconst], [PS[7]])
    k.tr(k.bank(7)[:, 128:256], stgB, ident, [Bstg, Bconst], [PS[7]])
    k.copy(TAc, k.bank(7)[:, 0:128], [PS[7]], [Bc_], eng="dve")
    k.copy(TBc, k.bank(7)[:, 128:256], [PS[7]], [Bc_], eng="dve")
    gT = TAc[:, 16:80].rearrange("p (g l c) -> p g l c", g=4, l=2)
    badaT = TBc[:, 0:96].rearrange("p (l j) -> p l j", l=2)
    k.act(scT, TAc[:, 0:16].rearrange("p (k c) -> p c k", k=2), AF.Silu, [Bc_], [Bc_])
    NPIECE = 8
    PW = 6 * D // NPIECE
    wa = [A.alloc([8, PW], BF16) for _ in range(2)]
    Bwa = [Buf(), Buf()]
    for l in range(DEPTH):
        for pc in range(NPIECE):
            t = wa[pc % 2]
            bt = Bwa[pc % 2]
            k.dma("pool", t, w_ada[l].rearrange("(c p) n -> p c n", p=128)[:, :, pc * PW:(pc + 1) * PW], [], [bt])
            for jj in range(6):
                j = pc * 6 + jj
                for c in range(8):
                    k.mm(k.bank(l)[:, j * 2:(j + 1) * 2], t[:, c, jj * 128:(jj + 1) * 128], scT[:, c, :],
                         c == 0, c == 7, [bt, Bc_], [PS[l]])
        k.tt(modT[:, l], k.bank(l)[:, 0:96].rearrange("p (j k) -> p j k", k=2),
             bc(badaT[:, l].unsqueeze(2), [128, 48, 2]), ALU.add, [PS[l], Bc_], [Bmod])
        for (dst, gi, j0, plus1) in ((A1, 0, 8, True), (B1, 1, 16, False), (A2, 2, 32, True), (B2, 3, 40, False)):
            gb = bc(gT[:, gi, l].unsqueeze(2), [128, 8, 2])
            if plus1:
                k.stt(dst[:, l], modT[:, l, j0:j0 + 8, :], 1.0, gb, ALU.add, ALU.mult, [Bmod, Bc_], [Bmod])
            else:
                k.tt(dst[:, l], modT[:, l, j0:j0 + 8, :], gb, ALU.mult, [Bmod, Bc_], [Bmod])
    for l_ in range(DEPTH):
        for nm, src, dst, rows in (("w_in", w_in, w_in_b, D), ("w_out", w_out, w_out_b, D), ("w1", mlp_w1, w1_b, D), ("w2", mlp_w2, w2_b, 4 * D)):
            Bcv[(nm, l_)] = Buf()
            for r0 in range(0, rows, 128):
                k.dma("pool", dst[l_, r0:r0 + 128, :], src[l_, r0:r0 + 128, :], [], [Bcv[(nm, l_)]], bg=True)
    P.barrier()
    A.top = mark0
    dbgmod = dout("dbg_mod", [128, DEPTH * 96])
    k.dma("sp", dbgmod, modT.rearrange("p l j k -> p (l j k)"), [Bmod], [])
    if stop == "mods":
        P.emit(); ES.close(); return nc, P

    BxT = [Buf() for _ in range(TT // TB)]
    m_pro = A.top
    xs = [A.alloc([D], F32) for _ in range(2)]
    xo = [A.alloc([8, 128], F32) for _ in range(2)]
    Bxs = [Buf(), Buf()]
    Bxo = [Buf(), Buf()]
    for i in range(TT // 128):
        s = i % 2
        k.dma("sp", xs[s], x_tok[i * 128:(i + 1) * 128, :], [], [Bxs[s]])
        for c in range(8):
            bk = 2 * s + c // 4
            k.tr(k.bank(bk)[:, (c % 4) * 128:(c % 4 + 1) * 128], xs[s][:, c * 128:(c + 1) * 128], ident,
                 [Bxs[s], Bconst], [PS[bk]])
        for hh in range(2):
            bk = 2 * s + hh
            k.copy(xo[s][:, hh * 4:(hh + 1) * 4, :], k.bank(bk).rearrange("p (c t) -> p c t", c=4), [PS[bk]], [Bxo[s]])
        k.dma("sp", xTv[:, :, i * 128:(i + 1) * 128], xo[s], [Bxo[s]], [BxT[i // 4]])
    P.barrier()
    A.top = m_pro
    if stop == "pro":
        P.emit(); ES.close(); return nc, P

    hT = A.alloc([8, 2048], BF16)
    oT = A.alloc([8, 2048], BF16)
    gatesT = A.alloc([2048], F32)
    BhT = Buf()
    BoT = Buf()
    Bgates = Buf()

    def rstd_from_sq(sq, Bsq, bank_i, rstd, Brstd):
        for c in range(8):
            k.mm(k.bank(bank_i), ones_b, sq[:, c, :], c == 0, c == 7, [Bsq, Bconst], [PS[bank_i]])
        k.act(rstd, k.bank(bank_i), AF.Sqrt, [PS[bank_i]], [Brstd], scale=1.0 / D, bias=epsb)
        k.recip(rstd, rstd, [Brstd], [Brstd])

    epsb = A.alloc([1], F32)
    k.memset(epsb, EPS, [Bconst])

    def prenorm(xblk, Bx, dst, Bdst, dcols, Asc, shj, l, kc, W, gate=None):
        sq, Bsq, rstd, Brstd, tmp, Btmp = W
        k.act(sq, xblk, AF.Square, [Bx], [Bsq])
        rstd_from_sq(sq, Bsq, 0, rstd, Brstd)
        k.tt(tmp, xblk, bc(rstd.unsqueeze(1), [128, 8, TB]), ALU.mult, [Bx, Brstd], [Btmp])
        for c in range(8):
            k.act(dst[:, c, dcols], tmp[:, c, :], AF.Identity, [Btmp, Bmod], [Bdst],
                  scale=Asc[:, l, c, kc:kc + 1], bias=modT[:, l, shj + c, kc:kc + 1])
        if gate is not None:
            wg32, Bwg32 = gate
            for c in range(8):
                k.act(tmp[:, c, :], tmp[:, c, :], AF.Identity, [Btmp, Bmod], [Btmp],
                      scale=Asc[:, l, c, kc:kc + 1], bias=modT[:, l, shj + c, kc:kc + 1])
            for c in range(8):
                k.mm(k.bank(1)[0:16, :], wg32[:, c, :], tmp[:, c, :], c == 0, c == 7, [Bwg32, Btmp], [PS[1]])
            k.copy(gatesT[0:16, dcols], k.bank(1)[0:16, :], [PS[1]], [Bgates], eng="dve")


    def normalize_out(bkO, ncols, heads, l, sink, dst_cols_list, W):
        rrow, Brr, bcs, Bbcs = W
        per = ncols // len(heads)
        O = k.bank(bkO)
        if sink:
            for i, (h, e_off, dcols) in enumerate(heads):
                k.ts(rrow[64:65, i * per:(i + 1) * per], O[64:65, i * per:(i + 1) * per],
                     sinkE[64:65, l * 4 + h:l * 4 + h + 1], None, ALU.add, None, [PS[bkO], Bconst], [Brr])
            k.recip(rrow[64:65, 0:ncols], rrow[64:65, 0:ncols], [Brr], [Brr])
        else:
            k.recip(rrow[64:65, 0:ncols], O[64:65, 0:ncols], [PS[bkO]], [Brr])
        k.mm(k.bank(2)[0:64, 0:ncols], ones_f[64:65, 0:64], rrow[64:65, 0:ncols], True, True, [Brr, Bconst], [PS[2]])
        k.copy(bcs[0:64, 0:ncols], k.bank(2)[0:64, 0:ncols], [PS[2]], [Bbcs], eng="act")
        for i, (h, e_off, dcols) in enumerate(heads):
            ch = e_off // 128
            po = e_off % 128
            k.tt(oT[po:po + 64, ch, dcols], O[0:64, i * per:(i + 1) * per], bcs[0:64, i * per:(i + 1) * per],
                 ALU.mult, [PS[bkO], Bbcs], [BoT])

    def mixer_attn(l, g):
        T = g["T"]
        L = g["L"]
        nseq = g["nseq"]
        lat = g["name"] == "lat"
        nblk = T // TB
        TK = T + (512 if lat else 0)
        NKC = TK // 128
        wA = A.alloc([8, 544], BF16)
        BwA = Buf()
        k.dma("sp", wA, w_in_b[l].rearrange("(c p) n -> p c n", p=128)[:, :, 0:544], [Bcv[("w_in", l)]], [BwA])
        wukv = A.alloc([512], BF16)
        k.dma("pool", wukv, mla_w_ukv[l], [], [BwA])
        qT = A.alloc([4, T], BF16)
        BqT = Buf()
        kT = A.alloc([4, TK], BF16)
        BkT = Buf()
        ckvnT = A.alloc([TK], BF16)
        Bckv = Buf()
        Vaug = A.alloc([NKC, 4, 65], BF16)
        BV = Buf()
        k.memset(Vaug, 1.0, [BV])
        raw = A.alloc([TB], F32)
        Braw = Buf()
        sqc = A.alloc([TB], BF16)
        Bsqc = Buf()
        rs = A.alloc([TB], F32)
        Brs = Buf()
        kpf = A.alloc([TB], F32)
        Bkpf = Buf()
        tmpr = A.alloc([TB], F32)
        Btmpr = Buf()
        if lat:
            rC = A.alloc([2048], F32)
            rS = A.alloc([2048], F32)
            Brope = Buf()
            k.dma("sp", rC[64:96, :], ropeA[0, 64:96, :], [], [Brope])
            k.dma("sp", rS[64:96, :], ropeA[1, 64:96, :], [], [Brope])
        stg = A.alloc([4, 128], F32)
        Bstg = Buf()

        def rope_rows(src_f, Bsrc, r0, r1, tcols, dst, Bdst):
            k.mm(k.bank(2)[r0:r1, :], pswapF[r0:r1, r0:r1], src_f[r0:r1, :], True, True, [Bsrc, Bconst], [PS[2]])
            k.tt(tmpr[r0:r1, :], k.bank(2)[r0:r1, :], rS[r0:r1, tcols], ALU.mult, [PS[2], Brope], [Btmpr])
            k.tt(src_f[r0:r1, :], src_f[r0:r1, :], rC[r0:r1, tcols], ALU.mult, [Bsrc, Brope], [Bsrc])
            k.tt(dst, src_f[r0:r1, :], tmpr[r0:r1, :], ALU.add, [Bsrc, Btmpr], [Bdst])

        def ckv_norm_block(src_ps_bank, cols, tok_out=None):
            k.copy(raw, k.bank(src_ps_bank), [PS[src_ps_bank]], [Braw], eng="dve")
            k.act(sqc, raw, AF.Square, [Braw], [Bsqc])
            k.mm(k.bank(2), ones_b, sqc, True, True, [Bsqc, Bconst], [PS[2]])
            k.act(rs, k.bank(2), AF.Sqrt, [PS[2]], [Brs], scale=1.0 / 128, bias=epsb)
            k.recip(rs, rs, [Brs], [Brs])
            k.stt(raw, raw, kvg[:, l:l + 1], rs, ALU.mult, ALU.mult, [Braw, Brs, Bconst], [Braw])
            k.copy(ckvnT[:, cols], raw, [Braw], [Bckv], eng="act")
            if tok_out is not None:
                sq_, t0 = tok_out
                for j in range(4):
                    k.tr(k.bank(2)[:, j * 128:(j + 1) * 128], raw[:, j * 128:(j + 1) * 128], ident, [Braw, Bconst], [PS[2]])
                k.copy(stg, k.bank(2).rearrange("p (j r) -> p j r", j=4), [PS[2]], [Bstg], eng="dve")
                for j in range(4):
                    tk = t0 + j * 128
                    k.dma("sp", o_ckv[tk // 256, l, tk % 256:tk % 256 + 128, :], stg[:, j, :], [Bstg], [])

        for tb in range(nblk):
            cols = slice(tb * TB, (tb + 1) * TB)
            for c in range(8):
                k.mm(k.bank(0), wA[:, c, 384:512], hT[:, c, cols], c == 0, c == 7, [BwA, BhT], [PS[0]])
            ckv_norm_block(0, cols, tok_out=None if lat else (0, tb * TB))
            for c in range(8):
                k.mm(k.bank(1)[64:96, :], wA[:, c, 512:544], hT[:, c, cols], c == 0, c == 7, [BwA, BhT], [PS[1]])
            k.copy(kpf[64:96, :], k.bank(1)[64:96, :], [PS[1]], [Bkpf], eng="act")
            if lat:
                for h in range(4):
                    if h == 0:
                        rope_rows(kpf, Bkpf, 64, 96, cols, kT[64:96, 0, cols], BkT)
                    else:
                        k.copy(kT[64:96, h, cols], kT[64:96, 0, cols], [BkT], [BkT])
            else:
                for h in range(4):
                    k.copy(kT[64:96, h, cols], kpf[64:96, :], [Bkpf], [BkT])
                for c in range(8):
                    k.mm(k.bank(1)[0:32, :], wA[:, c, 512:544], hT[:, c, cols], c == 0, c == 7, [BwA, BhT], [PS[1]])
                k.copy(kpf[0:32, :], k.bank(1)[0:32, :], [PS[1]], [Bkpf], eng="dve")
                for j in range(4):
                    k.tr(k.bank(1)[:, j * 32:(j + 1) * 32], kpf[0:32, j * 128:(j + 1) * 128], ident[0:32, 0:32],
                         [Bkpf, Bconst], [PS[1]])
                k.copy(stg[:, 0, :], k.bank(1)[:, 0:128], [PS[1]], [Bstg], eng="dve")
                for j in range(4):
                    tk = tb * TB + j * 128
                    k.dma("sp", o_kpe[tk // 256, l, tk % 256:tk % 256 + 128, :], stg[:, 0, j * 32:(j + 1) * 32], [Bstg], [])
            for h in range(4):
                bk = h % 2
                for c in range(8):
                    k.mm(k.bank(bk)[0:96, :], wA[:, c, h * 96:(h + 1) * 96], hT[:, c, cols], c == 0, c == 7,
                         [BwA, BhT], [PS[bk]])
                if lat:
                    k.copy(raw[0:96, :], k.bank(bk)[0:96, :], [PS[bk]], [Braw], eng="act")
                    k.copy(qT[0:64, h, cols], raw[0:64, :], [Braw], [BqT], eng="dve")
                    rope_rows(raw, Braw, 64, 96, cols, qT[64:96, h, cols], BqT)
                else:
                    k.copy(qT[0:96, h, cols], k.bank(bk)[0:96, :], [PS[bk]], [BqT])
        if lat:
            for j in range(4):
                k.dma("sp", stg[:, j, :], c_ckv[l, j * 128:(j + 1) * 128, :], [], [Bstg])
            for j in range(4):
                k.tr(k.bank(0)[:, j * 128:(j + 1) * 128], stg[:, j, :], ident, [Bstg, Bconst], [PS[0]])
            k.copy(ckvnT[:, 2048:2560], k.bank(0), [PS[0]], [Bckv], eng="dve")
            for j in range(4):
                k.dma("sp", stg[:, j, 0:32], c_kpe[l, j * 128:(j + 1) * 128, :], [Bstg], [Bstg])
            for j in range(4):
                k.tr(k.bank(1)[0:32, j * 128:(j + 1) * 128], stg[:, j, 0:32], ident, [Bstg, Bconst], [PS[1]])
            for h in range(4):
                k.copy(kT[64:96, h, 2048:2560], k.bank(1)[0:32, :], [PS[1]], [BkT])
        for kb in range(TK // TB):
            cols = slice(kb * TB, (kb + 1) * TB)
            for h in range(4):
                bk = h % 2
                k.mm(k.bank(bk)[0:64, :], wukv[:, h * 128:h * 128 + 64], ckvnT[:, cols], True, True, [BwA, Bckv], [PS[bk]])
                k.copy(kT[0:64, h, cols], k.bank(bk)[0:64, :], [PS[bk]], [BkT])
        for kc in range(NKC):
            bk = kc % 2
            k.mm(k.bank(bk), ckvnT[:, kc * 128:(kc + 1) * 128], wukv, True, True, [BwA, Bckv], [PS[bk]])
            k.copy(Vaug[:, kc, :, 0:64], k.bank(bk).rearrange("p (h t e) -> p h t e", h=4, t=2)[:, :, 1, :], [PS[bk]], [BV])
        PT = [A.alloc([TB], BF16) for _ in range(3)]
        BPT = [Buf() for _ in range(3)]
        rrow = A.alloc([TB], F32)
        bcs = A.alloc([TB], F32)
        Wn = (rrow, Buf(), bcs, Buf())
        scale = 96 ** -0.5
        it = 0
        nq = 0
        QB = min(TB, L)
        for sq_ in range(nseq):
            for h in range(4):
                for qb in range(L // QB):
                    qcols = slice(sq_ * L + qb * QB, sq_ * L + (qb + 1) * QB)
                    bkO = 6 + (nq % 2)
                    nq += 1
                    kcs = list(range(sq_ * L // 128, (sq_ + 1) * L // 128)) if not lat else list(range(NKC))
                    for i, kc in enumerate(kcs):
                        bS = 3 + (it % 3)
                        pt = PT[it % 3]
                        bpt = BPT[it % 3]
                        it += 1
                        k.mm(k.bank(bS)[:, 0:QB], kT[0:96, h, kc * 128:(kc + 1) * 128], qT[0:96, h, qcols], True, True,
                             [BkT, BqT], [PS[bS]])
                        k.act(pt[:, 0:QB], k.bank(bS)[:, 0:QB], AF.Exp, [PS[bS]], [bpt], scale=scale)
                        k.mm(k.bank(bkO)[0:65, 0:QB], Vaug[:, kc, h, :], pt[:, 0:QB], i == 0, i == len(kcs) - 1,
                             [BV, bpt], [PS[bkO]])
                    normalize_out(bkO, QB, [(h, h * 64, qcols)], l, False, None, Wn)
        P.barrier()
        A.top = mB_inner = A.top
        A.top = mixer_base[0]
        wB = A.alloc([8, 512], BF16)
        BwB = Buf()
        k.dma("sp", wB, w_in_b[l].rearrange("(c p) n -> p c n", p=128)[:, :, 544:1056], [Bcv[("w_in", l)]], [BwB])
        qS = A.alloc([4, T], BF16)
        BqS = Buf()
        kS = A.alloc([2, TK], BF16)
        BkS = Buf()
        Vb = A.alloc([NKC, 2, 65], BF16)
        BVb = Buf()
        k.memset(Vb, 1.0, [BVb])
        raw = A.alloc([TB], F32)
        Braw = Buf()
        tmpr = A.alloc([TB], F32)
        Btmpr = Buf()
        stg = A.alloc([4, 128], F32)
        Bstg = Buf()
        stg2 = A.alloc([256], F32)
        Bstg2 = Buf()
        if lat:
            rC = A.alloc([2048], F32)
            rS = A.alloc([2048], F32)
            Brope = Buf()
            k.dma("sp", rC[0:64, :], ropeB[0], [], [Brope])
            k.dma("sp", rS[0:64, :], ropeB[1], [], [Brope])
        for tb in range(nblk):
            cols = slice(tb * TB, (tb + 1) * TB)
            for h in range(6):
                bk = h % 2
                c0 = h * 64 if h < 4 else 256 + (h - 4) * 64
                for c in range(8):
                    k.mm(k.bank(bk)[0:64, :], wB[:, c, c0:c0 + 64], hT[:, c, cols], c == 0, c == 7, [BwB, BhT], [PS[bk]])
                dst, Bdst = (qS[0:64, h, cols], BqS) if h < 4 else (kS[0:64, h - 4, cols], BkS)
                if lat:
                    k.copy(raw[0:64, :], k.bank(bk)[0:64, :], [PS[bk]], [Braw], eng="act")
                    rope_rows(raw, Braw, 0, 64, cols, dst, Bdst)
                else:
                    k.copy(dst, k.bank(bk)[0:64, :], [PS[bk]], [Bdst])
            for j in range(4):
                tk = tb * TB + j * 128
                kc = tk // 128
                bk = j % 2
                for c in range(8):
                    k.mm(k.bank(bk)[:, 0:256], hT[:, c, tk:tk + 128], wB[:, c, 256:512], c == 0, c == 7, [BwB, BhT], [PS[bk]])
                k.copy(Vb[:, kc, :, 0:64], k.bank(bk)[:, 128:256].rearrange("p (v e) -> p v e", v=2), [PS[bk]], [BVb], eng="act")
                if not lat:
                    k.copy(stg2, k.bank(bk)[:, 0:256], [PS[bk]], [Bstg2], eng="dve")
                    k.dma("sp", o_swk[tk // 256, l, tk % 256:tk % 256 + 128, :], stg2[:, 0:128], [Bstg2], [])
                    k.dma("sp", o_swv[tk // 256, l, tk % 256:tk % 256 + 128, :], stg2[:, 128:256], [Bstg2], [])
        if lat:
            for j in range(4):
                k.dma("sp", stg[:, j, :], c_swk[l, j * 128:(j + 1) * 128, :], [], [Bstg])
            for kv in range(2):
                for j in range(4):
                    k.tr(k.bank(kv)[0:64, j * 128:(j + 1) * 128], stg[:, j, kv * 64:(kv + 1) * 64], ident, [Bstg, Bconst], [PS[kv]])
                k.copy(kS[0:64, kv, 2048:2560], k.bank(kv)[0:64, :], [PS[kv]], [BkS])
            for j in range(4):
                k.dma("pool", Vb[:, 16 + j, :, 0:64], c_swv[l, j * 128:(j + 1) * 128, :].rearrange("p (v e) -> p v e", v=2), [], [BVb])
        PT = [A.alloc([256], BF16) for _ in range(3)]
        BPT = [Buf() for _ in range(3)]
        rrow = A.alloc([TB], F32)
        bcs = A.alloc([TB], F32)
        Wn = (rrow, Buf(), bcs, Buf())
        scale = 64 ** -0.5
        it = 0
        nq = 0
        if not lat:
            for sq_ in range(nseq):
                for h in range(4):
                    qcols = slice(sq_ * L, (sq_ + 1) * L)
                    bkO = 6 + (nq % 2)
                    nq += 1
                    kcs = list(range(sq_ * L // 128, (sq_ + 1) * L // 128))
                    for i, kc in enumerate(kcs):
                        bS = 3 + (it % 3)
                        pt = PT[it % 3]
                        bpt = BPT[it % 3]
                        it += 1
                        k.mm(k.bank(bS)[:, 0:L], kS[0:64, h // 2, kc * 128:(kc + 1) * 128], qS[0:64, h, qcols], True, True,
                             [BkS, BqS], [PS[bS]])
                        k.act(pt[:, 0:L], k.bank(bS)[:, 0:L], AF.Exp, [PS[bS]], [bpt], scale=scale)
                        k.mm(k.bank(bkO)[0:65, 0:L], Vb[:, kc, h // 2, :], pt[:, 0:L], i == 0, i == len(kcs) - 1,
                             [BVb, bpt], [PS[bkO]])
                    normalize_out(bkO, L, [(h, 256 + h * 64, qcols)], l, True, None, Wn)
        else:
            NB = L // 128
            for n in range(NB):
                qcols = slice(n * 128, (n + 1) * 128)
                for kv in range(2):
                    bkO = 6 + (nq % 2)
                    nq += 1
                    kcs = []
                    if n > 0:
                        kcs.append((n - 1, 0))
                    kcs.append((n, None))
                    if n < NB - 1:
                        kcs.append((n + 1, 1))
                    kcs += [(16 + j, None) for j in range(4)]
                    for i, (kc, mk_) in enumerate(kcs):
                        bS = 3 + (it % 3)
                        pt = PT[it % 3]
                        bpt = BPT[it % 3]
                        it += 1
                        for hh in range(2):
                            k.mm(k.bank(bS)[:, hh * 128:(hh + 1) * 128], kS[0:64, kv, kc * 128:(kc + 1) * 128],
                                 qS[0:64, 2 * kv + hh, qcols], True, True, [BkS, BqS], [PS[bS]])
                        k.act(pt, k.bank(bS)[:, 0:256], AF.Exp, [PS[bS]], [bpt], scale=scale)
                        if mk_ is not None:
                            k.tt(pt.rearrange("p (h q) -> p h q", h=2), pt.rearrange("p (h q) -> p h q", h=2),
                                 bc(masks[:, mk_, :].unsqueeze(1), [128, 2, 128]), ALU.mult, [bpt, Bconst], [bpt])
                        k.mm(k.bank(bkO)[0:65, 0:256], Vb[:, kc, kv, :], pt, i == 0, i == len(kcs) - 1, [BVb, bpt], [PS[bkO]])
                    normalize_out(bkO, 256, [(2 * kv, 256 + 2 * kv * 64, qcols), (2 * kv + 1, 256 + (2 * kv + 1) * 64, qcols)],
                                  l, True, None, Wn)


    PI = math.pi

    def wrap_sin(x, Bx, t, Bt, rows, width):
        xs = x[0:rows, 0:width]
        ts_ = t[0:rows, 0:width]
        for rep in range(2):
            k.ts(ts_, xs, PI, -2 * PI, ALU.is_gt, ALU.mult, [Bx], [Bt])
            k.tt(xs, xs, ts_, ALU.add, [Bx, Bt], [Bx])
            k.ts(ts_, xs, -PI, 2 * PI, ALU.is_lt, ALU.mult, [Bx], [Bt])
            k.tt(xs, xs, ts_, ALU.add, [Bx, Bt], [Bx])
        k.act(xs, xs, AF.Sin, [Bx], [Bx])

    def mixer_hyena(l, g):
        T = g["T"]
        L = g["L"]
        nseq = g["nseq"]
        NSC = L // 128
        NF = NSC + 1
        NW = nseq * 256
        hc = HY[L]
        tabC = hc["tab"][0]
        tabS = hc["tab"][1]
        frows = lambda fc: 128 if fc < NSC else 1
        base = A.top
        stgp = A.alloc([128], F32)
        colsT = A.alloc([128], F32)
        Bsp = Buf()
        Bcols = Buf()
        k.memset(stgp, 0.0, [Bsp])
        k.dma("sp", stgp[0:18, :], hy_conv[l].rearrange("k (c p) -> (k c) p", p=128), [], [Bsp])
        k.dma("sp", stgp[18:22, :], hy_bias[l].rearrange("o (c p) -> (o c) p", p=128), [], [Bsp])
        k.dma("sp", stgp[32:33, 0:64], hy_b1[l:l + 1, :], [], [Bsp])
        k.dma("sp", stgp[33:34, 0:64], hy_freq[l][0:1, :], [], [Bsp])
        k.dma("sp", stgp[34:35, 0:64], hy_b2[l:l + 1, :], [], [Bsp])
        k.dma("sp", stgp[35:36, 0:64], hy_freq[l][1:2, :], [], [Bsp])
        k.tr(k.bank(0)[:, 0:128], stgp, ident, [Bsp, Bconst], [PS[0]])
        k.copy(colsT, k.bank(0)[:, 0:128], [PS[0]], [Bcols], eng="dve")
        convw = colsT[:, 0:18].rearrange("p (k c) -> p k c", k=3)
        biasc = colsT[:, 18:22].rearrange("p (o c) -> p o c", o=2)
        fb = A.alloc([2], F32)
        k.tt(fb[0:64, 0:1], colsT[0:64, 32:33], colsT[0:64, 33:34], ALU.mult, [Bcols], [Bcols])
        k.tt(fb[0:64, 1:2], colsT[0:64, 34:35], colsT[0:64, 35:36], ALU.mult, [Bcols], [Bcols])
        aw = A.alloc([NF], F32)
        bw = A.alloc([NF], F32)
        negdist = A.alloc([NSC], F32)
        k.dma("sp", aw, hc["aw"], [], [Bcols])
        k.dma("sp", bw, hc["bw"], [], [Bcols])
        k.dma("sp", negdist, hc["negdist"], [], [Bcols])
        Gre = A.alloc([NF, 512], BF16)
        Gim = A.alloc([NF, 512], BF16)
        BG = Buf()
        tC = [A.alloc([NSC, 128], BF16) for _ in range(2)]
        tS = [A.alloc([NSC, 128], BF16) for _ in range(2)]
        Btab = [Buf(), Buf()]
        persist = A.top
        tmpA = A.alloc([512], F32)
        tmpB = A.alloc([512], F32)
        BtA = Buf()
        BtB = Buf()

        def load_colblock(fc, i):
            rows = frows(fc)
            k.dma("sp", tC[i][:, :, 0:rows], tabC[0:L, fc * 128:fc * 128 + rows].rearrange("(sc p) f -> p sc f", p=128),
                  [], [Btab[i]], slow=True)
            k.dma("sp", tS[i][:, :, 0:rows], tabS[0:L, fc * 128:fc * 128 + rows].rearrange("(sc p) f -> p sc f", p=128),
                  [], [Btab[i]], slow=True)

        featsT = A.alloc([L], F32)
        w1f = A.alloc([64], F32)
        w2f = A.alloc([64], F32)
        w3f = A.alloc([512], F32)
        Bfw = Buf()
        k.dma("sp", featsT[0:17, :], hc["feats"], [], [Bfw])
        k.dma("sp", w1f[0:17, :], hy_w1[l], [], [Bfw])
        k.dma("sp", w2f[0:64, :], hy_w2[l], [], [Bfw])
        k.dma("sp", w3f[0:64, :], hy_w3[l], [], [Bfw])
        h1T = A.alloc([L], F32)
        h2T = A.alloc([L], F32)
        wt = A.alloc([L], F32)
        Bh1 = Buf()
        Bh2 = Buf()
        Bwt = Buf()
        CW = min(512, L)
        for blk in range(L // CW):
            cs = slice(blk * CW, (blk + 1) * CW)
            k.mm(k.bank(blk % 2)[0:64, 0:CW], w1f[0:17, 0:64], featsT[0:17, cs], True, True, [Bfw], [PS[blk % 2]])
            k.ts(h1T[0:64, cs], k.bank(blk % 2)[0:64, 0:CW], colsT[0:64, 33:34], fb[0:64, 0:1], ALU.mult, ALU.add,
                 [PS[blk % 2], Bcols], [Bh1])
        wrap_sin(h1T, Bh1, wt, Bwt, 64, L)
        for blk in range(L // CW):
            cs = slice(blk * CW, (blk + 1) * CW)
            k.mm(k.bank(blk % 2)[0:64, 0:CW], w2f[0:64, 0:64], h1T[0:64, cs], True, True, [Bfw, Bh1], [PS[blk % 2]])
            k.ts(h2T[0:64, cs], k.bank(blk % 2)[0:64, 0:CW], colsT[0:64, 35:36], fb[0:64, 1:2], ALU.mult, ALU.add,
                 [PS[blk % 2], Bcols], [Bh2])
        wrap_sin(h2T, Bh2, wt, Bwt, 64, L)
        absdec = A.alloc([512], F32)
        Bad = Buf()
        k.dma("sp", absdec, hy_decay[l].partition_broadcast(128), [], [Bad], slow=True)
        k.act(absdec, absdec, AF.Abs, [Bad], [Bad])
        filtok = A.alloc([NSC, 512], BF16)
        Bft = Buf()
        Et = A.alloc([512], F32)
        BEt = Buf()
        for jc in range(NSC):
            bk = jc % 2
            k.mm(k.bank(bk), h2T[0:64, jc * 128:(jc + 1) * 128], w3f[0:64, :], True, True, [Bh2, Bfw], [PS[bk]])
            k.act(Et, absdec, AF.Exp, [Bad, Bcols], [BEt], scale=negdist[:, jc:jc + 1])
            k.tt(filtok[:, jc, :], k.bank(bk), Et, ALU.mult, [PS[bk], BEt], [Bft])
        for fc in range(NF):
            rows = frows(fc)
            i = fc % 2
            load_colblock(fc, i)
            ba, bb = 2 * i, 2 * i + 1
            for sc in range(NSC):
                k.mm(k.bank(ba)[0:rows, :], tC[i][:, sc, 0:rows], filtok[:, sc, :], sc == 0, sc == NSC - 1, [Btab[i], Bft], [PS[ba]])
            for sc in range(NSC):
                k.mm(k.bank(bb)[0:rows, :], tS[i][:, sc, 0:rows], filtok[:, sc, :], sc == 0, sc == NSC - 1, [Btab[i], Bft], [PS[bb]])
            k.ts(tmpA[0:rows, :], k.bank(ba)[0:rows, :], aw[0:rows, fc:fc + 1], None, ALU.mult, None, [PS[ba], Bcols], [BtA])
            k.stt(Gre[0:rows, fc, :], k.bank(bb)[0:rows, :], bw[0:rows, fc:fc + 1], tmpA[0:rows, :], ALU.mult, ALU.add,
                  [PS[bb], BtA, Bcols], [BG])
            k.ts(tmpB[0:rows, :], k.bank(bb)[0:rows, :], aw[0:rows, fc:fc + 1], None, ALU.mult, None, [PS[bb], Bcols], [BtB])
            k.stt(Gim[0:rows, fc, :], k.bank(ba)[0:rows, :], bw[0:rows, fc:fc + 1], tmpB[0:rows, :], ALU.mult, ALU.subtract,
                  [PS[ba], BtB, Bcols], [BG])
        P.barrier()
        A.top = persist
        x1T = A.alloc([2, T], BF16)
        x2T = A.alloc([2, T], BF16)
        vT = A.alloc([2, T], F32)
        Bx12 = Buf()
        BvT = Buf()
        dtok = A.alloc([NSC, NW], BF16)
        Bdt = Buf()
        conv_mark = A.top
        wH = A.alloc([8, 768], BF16)
        BwH = Buf()
        k.dma("sp", wH, w_in_b[l].rearrange("(c p) n -> p c n", p=128)[:, :, C_HU:C_HU + 768], [Bcv[("w_in", l)]], [BwH])
        hu = A.alloc([T], F32)
        uu = A.alloc([T], F32)
        Bhu = Buf()
        Buu = Buf()
        for c6 in range(6):
            for tb in range(T // TB):
                cols = slice(tb * TB, (tb + 1) * TB)
                bk = tb % 2
                for c in range(8):
                    k.mm(k.bank(bk), wH[:, c, c6 * 128:(c6 + 1) * 128], hT[:, c, cols], c == 0, c == 7, [BwH, BhT], [PS[bk]])
                k.copy(hu[:, cols], k.bank(bk), [PS[bk]], [Bhu])
            k.act(uu, hu, AF.Identity, [Bhu, Bcols], [Buu], scale=convw[:, 1, c6:c6 + 1])
            for sq_ in range(nseq):
                s0, s1 = sq_ * L, (sq_ + 1) * L
                k.stt(uu[:, s0 + 1:s1], hu[:, s0:s1 - 1], convw[:, 0, c6:c6 + 1], uu[:, s0 + 1:s1], ALU.mult, ALU.add,
                      [Bhu, Bcols, Buu], [Buu])
                k.stt(uu[:, s0:s1 - 1], hu[:, s0 + 1:s1], convw[:, 2, c6:c6 + 1], uu[:, s0:s1 - 1], ALU.mult, ALU.add,
                      [Bhu, Bcols, Buu], [Buu])
            if c6 < 2:
                k.copy(vT[:, c6, :], uu, [Buu], [BvT], eng="act")
            elif c6 < 4:
                k.copy(x1T[:, c6 - 2, :], uu, [Buu], [Bx12], eng="act")
            else:
                k.copy(x2T[:, c6 - 4, :], uu, [Buu], [Bx12], eng="act")

        def to_tokmajor():
            for sq_ in range(nseq):
                for sc in range(NSC):
                    bk = sc % 2
                    for cc in range(2):
                        t0 = sq_ * L + sc * 128
                        k.tr(k.bank(bk)[:, cc * 128:(cc + 1) * 128], vT[:, cc, t0:t0 + 128], ident, [BvT, Bconst], [PS[bk]])
                    k.copy(dtok[:, sc, sq_ * 256:(sq_ + 1) * 256], k.bank(bk)[:, 0:256], [PS[bk]], [Bdt])

        P.barrier()
        A.top = conv_mark
        Pq = A.alloc([NF, NW], BF16)
        Qq = A.alloc([NF, NW], BF16)
        BPQ = Buf()
        rC = [A.alloc([L], BF16) for _ in range(2)]
        rS = [A.alloc([L], BF16) for _ in range(2)]
        Brt = [Buf(), Buf()]
        t1 = A.alloc([NW], F32)
        t2 = A.alloc([NW], F32)
        Bt1 = Buf()
        Bt2 = Buf()
        TW = min(512, L)

        def long_conv(o, consumer):
            ocs = slice(o * 256, (o + 1) * 256)
            for fc in range(NF):
                rows = frows(fc)
                i = fc % 2
                load_colblock(fc, i)
                ba, bb = 2 * i, 2 * i + 1
                for sc in range(NSC):
                    k.mm(k.bank(ba)[0:rows, 0:NW], tC[i][:, sc, 0:rows], dtok[:, sc, :], sc == 0, sc == NSC - 1, [Btab[i], Bdt], [PS[ba]])
                for sc in range(NSC):
                    k.mm(k.bank(bb)[0:rows, 0:NW], tS[i][:, sc, 0:rows], dtok[:, sc, :], sc == 0, sc == NSC - 1, [Btab[i], Bdt], [PS[bb]])
                Av = k.bank(ba)[0:rows, 0:NW].rearrange("p (s c) -> p s c", s=nseq)
                Bv = k.bank(bb)[0:rows, 0:NW].rearrange("p (s c) -> p s c", s=nseq)
                gre = bc(Gre[0:rows, fc, ocs].unsqueeze(1), [rows, nseq, 256])
                gim = bc(Gim[0:rows, fc, ocs].unsqueeze(1), [rows, nseq, 256])
                v3 = lambda ap: ap[0:rows, :].rearrange("p (s c) -> p s c", s=nseq)
                k.tt(v3(t1), Av, gre, ALU.mult, [PS[ba], BG], [Bt1])
                k.tt(v3(t2), Bv, gim, ALU.mult, [PS[bb], BG], [Bt2])
                k.tt(Pq[0:rows, fc, :], t1[0:rows, :], t2[0:rows, :], ALU.add, [Bt1, Bt2], [BPQ])
                k.tt(v3(t1), Bv, gre, ALU.mult, [PS[bb], BG], [Bt1])
                k.tt(v3(t2), Av, gim, ALU.mult, [PS[ba], BG], [Bt2])
                k.tt(Qq[0:rows, fc, :], t1[0:rows, :], t2[0:rows, :], ALU.subtract, [Bt1, Bt2], [BPQ])
            accs = [(sq_, cc, tb) for sq_ in range(nseq) for cc in range(2) for tb in range(L // TW)]
            for fc in range(NF):
                rows = frows(fc)
                i = fc % 2
                k.dma("sp", rC[i][0:rows, :], tabC[fc * 128:fc * 128 + rows, 0:L], [], [Brt[i]])
                k.dma("sp", rS[i][0:rows, :], tabS[fc * 128:fc * 128 + rows, 0:L], [], [Brt[i]])
                for ai, (sq_, cc, tb) in enumerate(accs):
                    lc = slice(sq_ * 256 + cc * 128, sq_ * 256 + (cc + 1) * 128)
                    k.mm(k.bank(ai)[:, 0:TW], Pq[0:rows, fc, lc], rC[i][0:rows, tb * TW:(tb + 1) * TW], fc == 0, False,
                         [BPQ, Brt[i]], [PS[ai]])
                    k.mm(k.bank(ai)[:, 0:TW], Qq[0:rows, fc, lc], rS[i][0:rows, tb * TW:(tb + 1) * TW], False, fc == NF - 1,
                         [BPQ, Brt[i]], [PS[ai]])
            for ai, (sq_, cc, tb) in enumerate(accs):
                consumer(ai, cc, slice(sq_ * L + tb * TW, sq_ * L + (tb + 1) * TW))

        tcs = A.alloc([TW], F32)
        Btcs = Buf()

        def cons0(ai, cc, tcols):
            k.stt(tcs, vT[:, cc, tcols], biasc[:, 0, cc:cc + 1], k.bank(ai)[:, 0:TW], ALU.mult, ALU.add, [BvT, Bcols, PS[ai]], [Btcs])
            k.tt(vT[:, cc, tcols], tcs, x1T[:, cc, tcols], ALU.mult, [Btcs, Bx12], [BvT])

        def cons1(ai, cc, tcols):
            k.stt(tcs, vT[:, cc, tcols], biasc[:, 1, cc:cc + 1], k.bank(ai)[:, 0:TW], ALU.mult, ALU.add, [BvT, Bcols, PS[ai]], [Btcs])
            k.tt(oT[:, 6 + cc, tcols], tcs, x2T[:, cc, tcols], ALU.mult, [Btcs, Bx12], [BoT])

        to_tokmajor()
        long_conv(0, cons0)
        to_tokmajor()
        long_conv(1, cons1)
        P.barrier()
        A.top = base


    def mixer_gdn(l, g):
        GD = F32 if 'gdn32' in (dbg or ()) else BF16
        T = g["T"]
        L = g["L"]
        nseq = g["nseq"]
        lat = g["name"] == "lat"
        NCK = L // 64
        NI = nseq * 4
        NBLK = T // TB
        base = A.top
        stgp = A.alloc([128], F32)
        colsT = A.alloc([128], F32)
        Bsp = Buf()
        Bcols = Buf()
        k.memset(stgp, 0.0, [Bsp])
        k.dma("sp", stgp[0:18, :], gdn_conv[l].rearrange("k (c p) -> (k c) p", p=128), [], [Bsp])
        k.dma("sp", stgp[18:19, 0:64], gdn_norm[l:l + 1, :], [], [Bsp])
        k.dma("sp", stgp[18:19, 64:128], gdn_norm[l:l + 1, :], [], [Bsp])
        k.tr(k.bank(0)[:, 0:128], stgp, ident, [Bsp, Bconst], [PS[0]])
        k.copy(colsT, k.bank(0)[:, 0:128], [PS[0]], [Bcols], eng="dve")
        convw = colsT[:, 0:18].rearrange("p (k c) -> p k c", k=3)
        gnorm = colsT[:, 18:19]
        all16 = A.alloc([16], F32)
        k.dma("sp", all16[:, 0:8], gdn_a_log[l].partition_broadcast(128), [], [Bcols], slow=True)
        k.dma("sp", all16[:, 8:16], gdn_dt_bias[l].partition_broadcast(128), [], [Bcols], slow=True)
        k.act(all16[:, 0:8], all16[:, 0:8], AF.Exp, [Bcols], [Bcols])
        k.ts(all16[:, 0:8], all16[:, 0:8], -1.0, None, ALU.mult, None, [Bcols], [Bcols])
        neaS = A.alloc([2, 2], F32)
        dtbS = A.alloc([2, 2], F32)
        for (dst, c0) in ((neaS, 0), (dtbS, 8)):
            v8 = all16[:, c0:c0 + 8].rearrange("p (d hp two) -> p d hp two", d=2, two=2)
            k.copy(dst[0:64], v8[0:64, :, :, 0], [Bcols], [Bcols], eng="dve")
            k.copy(dst[64:128], v8[64:128, :, :, 1], [Bcols], [Bcols], eng="dve")
        blockones = A.alloc([128], BF16)
        k.memset(blockones, 0.0, [Bcols])
        k.memset(blockones[0:64, 0:64], 1.0, [Bcols])
        k.memset(blockones[64:128, 64:128], 1.0, [Bcols])
        negones = A.alloc([64], F32)
        k.memset(negones, -1.0, [Bcols])
        negm = A.alloc([2, 64], F32)
        smk = A.alloc([2, 64], F32)
        k.dma("sp", negm[0:64], negm_d.rearrange("m c s -> c m s"), [], [Bcols])
        k.dma("sp", smk[0:64], sm_d.rearrange("m c s -> c m s"), [], [Bcols])
        mbs = A.alloc([2, 64], F32)
        mbt = A.alloc([2, 64], F32)
        k.ts(mbs[0:64], smk[0:64], -30000.0, 30000.0, ALU.mult, ALU.add, [Bcols], [Bcols])
        k.ts(mbt[0:64], negm[0:64], 30000.0, -30000.0, ALU.mult, ALU.add, [Bcols], [Bcols])
        identb = A.alloc([64], GD)
        k.copy(identb[0:64, :], ident[0:64, 0:64], [Bconst], [Bcols], eng="dve")
        ident2 = A.alloc([64], F32)
        k.ts(ident2[0:64, :], ident[0:64, 0:64], 2.0, None, ALU.mult, None, [Bconst], [Bcols])
        startm = A.alloc([T], BF16)
        k.memset(startm, 1.0, [Bcols])
        k.memset(startm.rearrange("p (n c) -> p n c", c=64)[:, :, 0:1], 0.0, [Bcols])
        qn = A.alloc([2, T], GD)
        kn = A.alloc([2, T], GD)
        vs = A.alloc([2, T], BF16)
        ocT = A.alloc([2, T], F32)
        Bqkv = Buf()
        Boc = Buf()
        pm = A.top
        wG = A.alloc([8, 1024], BF16)
        BwG = Buf()
        k.dma("sp", wG, w_in_b[l].rearrange("(c p) n -> p c n", p=128)[:, :, C_GQKV:C_GQKV + 1024], [Bcv[("w_in", l)]], [BwG])
        hu = A.alloc([T], F32)
        uu = A.alloc([T], F32)
        Bhu = Buf()
        Buu = Buf()
        sqb = A.alloc([TB], BF16)
        Bsqb = Buf()
        rsn = A.alloc([TB], F32)
        Brsn = Buf()
        for c6 in range(6):
            hp = c6 % 2
            for tb in range(NBLK):
                cols = slice(tb * TB, (tb + 1) * TB)
                bk = tb % 2
                for c in range(8):
                    k.mm(k.bank(bk), wG[:, c, c6 * 128:(c6 + 1) * 128], hT[:, c, cols], c == 0, c == 7, [BwG, BhT], [PS[bk]])
                k.copy(hu[:, cols], k.bank(bk), [PS[bk]], [Bhu])
            k.act(uu, hu, AF.Identity, [Bhu, Bcols], [Buu], scale=convw[:, 1, c6:c6 + 1])
            for sq_ in range(nseq):
                s0, s1 = sq_ * L, (sq_ + 1) * L
                k.stt(uu[:, s0 + 1:s1], hu[:, s0:s1 - 1], convw[:, 0, c6:c6 + 1], uu[:, s0 + 1:s1], ALU.mult, ALU.add,
                      [Bhu, Bcols, Buu], [Buu])
                k.stt(uu[:, s0:s1 - 1], hu[:, s0 + 1:s1], convw[:, 2, c6:c6 + 1], uu[:, s0:s1 - 1], ALU.mult, ALU.add,
                      [Bhu, Bcols, Buu], [Buu])
            k.act(uu, uu, AF.Silu, [Buu], [Buu])
            if c6 < 4:
                dst = qn if c6 < 2 else kn
                sc_ = 64 ** -0.5 if c6 < 2 else 1.0
                for tb in range(NBLK):
                    cols = slice(tb * TB, (tb + 1) * TB)
                    k.act(sqb, uu[:, cols], AF.Square, [Buu], [Bsqb])
                    k.mm(k.bank(2), blockones, sqb, True, True, [Bsqb, Bcols], [PS[2]])
                    k.act(rsn, k.bank(2), AF.Sqrt, [PS[2], Bconst], [Brsn], bias=epsb)
                    k.recip(rsn, rsn, [Brsn], [Brsn])
                    k.stt(dst[:, hp, cols], uu[:, cols], sc_, rsn, ALU.mult, ALU.mult, [Buu, Brsn], [Bqkv])
            else:
                k.copy(vs[:, hp, :], uu, [Buu], [Bqkv], eng="act")
        P.barrier()
        A.top = pm
        gsel = A.alloc([8, 128], F32)
        Bwrep = Buf()
        k.dma("sp", gsel[0:16], sel_d, [], [Bwrep])
        Dd = A.alloc([2, T], F32)
        bet = A.alloc([2, T], BF16)
        BD = Buf()
        Bbet = Buf()
        alias_mark = A.top
        tmpf = A.alloc([T], F32)
        Btmpf = Buf()
        A.top = alias_mark
        Eb = A.alloc([2, TB], F32)
        kbT = A.alloc([2, TB], GD)
        kb32 = A.alloc([2, TB], F32)
        kbeH = A.alloc([4, TB], GD)
        qdH = A.alloc([4, TB], GD)
        D0 = A.alloc([4, TB], F32)
        ktl = A.alloc([2, TB], F32)
        vbe = A.alloc([2, TB], F32)
        Bblk = Buf()
        tails = A.alloc([NI, 8], F32)
        def stepbufs(first=[True]):
            d_ = {}
            for nm, dt in (("G", F32), ("GT", F32), ("Gs", F32), ("GsT", F32), ("PA", GD), ("PB", GD), ("TA", GD),
                           ("TB", GD), ("attnT", GD), ("vbt", F32), ("ktail", GD), ("rp", GD), ("u", GD), ("Dcol", F32),
                           ("A32", F32), ("TB32", F32), ("TBt", F32), ("E32", F32), ("TBf", GD)):
                if not first[0] and nm in ("A32", "TB32", "TBt", "E32", "TBf", "G", "Gs"):
                    continue
                d_[nm] = A.alloc([NI, 64], dt)
                d_["B" + nm] = Buf()
            first[0] = False
            return d_
        SB0 = stepbufs()
        SB = [SB0, SB0]
        S32 = A.alloc([NI, 64], F32)
        Sbf = A.alloc([NI, 64], GD)
        BS32 = Buf()
        BSbf = Buf()

        def inst_list(d, step):
            res = []
            for sq_ in range(nseq):
                j = step if d == 0 else NCK - 1 - step
                for h in range(4):
                    t0 = sq_ * L + j * 64
                    res.append((sq_ * 4 + h, sq_, h, h // 2, (h % 2) * 64, slice(t0, t0 + 64), slice(t0 % TB, t0 % TB + 64), t0 // TB, (t0 % TB) // 64))
            res.sort(key=lambda r_: (r_[4], r_[0]))
            return res

        for d in range(2):
            NEGM = negm[0:64, d, :]
            NEGMT = negm[0:64, 1 - d, :]
            SM = smk[0:64, d, :]
            SMT = smk[0:64, 1 - d, :]
            MBS = mbs[0:64, d, :]
            MBT = mbt[0:64, 1 - d, :]
            P.barrier()
            for which in range(2):
                for hp in range(2):
                    for tb in range(NBLK):
                        cols = slice(tb * TB, (tb + 1) * TB)
                        bk = tb % 2
                        k.mm(k.bank(bk), gsel[0:16, which * 4 + d * 2 + hp, :], gatesT[0:16, cols], True, True, [Bwrep, Bgates], [PS[bk]])
                        if which == 0:
                            k.act(Dd[:, hp, cols], k.bank(bk), AF.Exp, [PS[bk], Bcols], [BD], bias=dtbS[:, d, hp:hp + 1])
                        else:
                            k.act(bet[:, hp, cols], k.bank(bk), AF.Sigmoid, [PS[bk]], [Bbet])
            for hp in range(2):
                k.act(Dd[:, hp, :], Dd[:, hp, :], AF.Ln, [BD], [BD], bias=ones_f[:, 0:1])
                k.ts(Dd[:, hp, :], Dd[:, hp, :], neaS[:, d, hp:hp + 1], None, ALU.mult, None, [BD, Bcols], [BD])
                P.op("dve", (lambda hp=hp: lambda e: e.tensor_tensor_scan(out=tmpf, data0=startm, data1=Dd[:, hp, :], initial=0.0,
                                                                           op0=ALU.mult, op1=ALU.add))(), [BD, Bcols], [Btmpf])
                if d == 0:
                    k.copy(Dd[:, hp, :], tmpf, [Btmpf], [BD], eng="dve")
                else:
                    t3 = tmpf.rearrange("p (n c) -> p n c", c=64)
                    d3 = Dd[:, hp, :].rearrange("p (n c) -> p n c", c=64)
                    k.tt(Dd[:, hp, :], Dd[:, hp, :], tmpf, ALU.subtract, [BD, Btmpf], [BD])
                    k.tt(d3, d3, bc(t3[:, :, 63:64], [128, T // 64, 64]), ALU.add, [BD, Btmpf], [BD])
            P.barrier()
            if 'dumpD' in (dbg or ()) and d == 1 and l == 0 and not lat:
                ddd = dout("dbg_D", [128, 2 * T])
                k.dma("sp", ddd, Dd.rearrange("p h t -> p (h t)"), [BD], [])
                ddq = dout("dbg_qk", [128, 4 * T], BF16)
                k.dma("sp", ddq[:, 0:2 * T], qn.rearrange("p h t -> p (h t)"), [Bqkv], [])
                k.dma("sp", ddq[:, 2 * T:4 * T], kn.rearrange("p h t -> p (h t)"), [Bqkv], [])
            k.memset(S32, 0.0, [BS32], eng="dve")
            if lat:
                for h in range(4):
                    k.dma("sp", S32[0:64, h, :], st_gdn[l, d, h], [], [BS32])
            k.copy(Sbf[0:64], S32[0:64], [BS32], [BSbf], eng="act")
            lastc = 63 if d == 0 else 0
            cur_blk = None
            for step in range(NCK):
                insts = inst_list(d, step)
                blk = insts[0][7]
                if blk != cur_blk:
                    cur_blk = blk
                    bc_ = slice(blk * TB, (blk + 1) * TB)
                    k.act(Eb, Dd[:, :, bc_], AF.Exp, [BD], [Bblk])
                    k.tt(kb32, kn[:, :, bc_], bet[:, :, bc_], ALU.mult, [Bqkv, Bbet], [Bblk])
                    k.copy(kbT, kb32, [Bblk], [Bblk], eng="act")
                    k.tt(kb32, kb32, Eb, ALU.mult, [Bblk], [Bblk])
                    for h in range(4):
                        pb = (h % 2) * 64
                        k.copy(kbeH[0:64, h, :], kb32[pb:pb + 64, h // 2, :], [Bblk], [Bblk])
                        k.copy(D0[0:64, h, :], Dd[pb:pb + 64, h // 2, bc_], [BD], [Bblk])
                        k.tt(qdH[0:64, h, :], qn[pb:pb + 64, h // 2, bc_], Eb[pb:pb + 64, h // 2, :], ALU.mult, [Bqkv, Bblk], [Bblk])
                    k.tt(vbe, vs[:, :, bc_], bet[:, :, bc_], ALU.mult, [Bqkv, Bbet], [Bblk])
                    d4 = Dd[:, :, bc_].rearrange("p h (n c) -> p h n c", c=64)
                    for hp in range(2):
                        k.tt(ktl[:, hp, :].rearrange("p (n c) -> p n c", c=64), bc(d4[:, hp, :, lastc:lastc + 1], [128, 8, 64]),
                             d4[:, hp], ALU.subtract, [BD], [Bblk])
                    k.act(ktl, ktl, AF.Exp, [Bblk], [Bblk])
                    k.tt(ktl, ktl, kn[:, :, bc_], ALU.mult, [Bblk, Bqkv], [Bblk])
                    e4 = Eb.rearrange("p h (n c) -> p h n c", c=64)
                    for sq_ in range(nseq):
                        for h in range(4):
                            pb = (h % 2) * 64
                            if lat:
                                k.copy(tails[0:64, h, 0:8], e4[pb:pb + 64, h // 2, :, lastc], [Bblk], [Bblk], eng="dve")
                            else:
                                k.copy(tails[0:64, sq_ * 4 + h, sq_ * 4:(sq_ + 1) * 4], e4[pb:pb + 64, h // 2, sq_ * 4:(sq_ + 1) * 4, lastc],
                                       [Bblk], [Bblk], eng="dve")
                W = SB[step % 2]
                NW_ = NI * 64
                v3 = lambda ap: ap[0:64, 0:NW_].rearrange("p (i c) -> p i c", c=64)
                for (ii, sq_, h, hp, pb, gc, bcl, _, cib) in insts:
                    ic = slice(ii * 64, (ii + 1) * 64)
                    k.tr(k.bank(0)[0:64, ic], D0[0:64, h, bcl], ident[0:64, 0:64], [Bblk, Bconst], [PS[0]])
                k.copy(W["Dcol"][0:64, :, 0:1], v3(k.bank(0))[:, :, 0:1], [PS[0]], [W["BDcol"]], eng="act")
                for (ii, sq_, h, hp, pb, gc, bcl, _, cib) in insts:
                    k.stt(W["Gs"][0:64, ii, :], D0[0:64, h, bcl], W["Dcol"][0:64, ii, 0:1], MBS, ALU.subtract, ALU.max,
                          [Bblk, W["BDcol"], Bcols], [W["BGs"]])
                    k.stt(W["GT"][0:64, ii, :], D0[0:64, h, bcl], W["Dcol"][0:64, ii, 0:1], MBT, ALU.subtract, ALU.min,
                          [Bblk, W["BDcol"], Bcols], [W["BGT"]])
                k.act(W["Gs"][0:64], W["Gs"][0:64], AF.Exp, [W["BGs"]], [W["BGs"]], scale=-1.0)
                k.act(W["GT"][0:64], W["GT"][0:64], AF.Exp, [W["BGT"]], [W["BGT"]])
                k.tt(W["GsT"][0:64], W["GT"][0:64], bc(SMT.unsqueeze(1), [64, NI, 64]), ALU.mult, [W["BGT"], Bcols], [W["BGsT"]])
                for (ii, sq_, h, hp, pb, gc, bcl, _, cib) in insts:
                    ic = slice(ii * 64, (ii + 1) * 64)
                    kb_ = kbT[pb:pb + 64, hp, bcl]
                    k_ = kn[pb:pb + 64, hp, gc]
                    q_ = qn[pb:pb + 64, hp, gc]
                    k.mm(k.bank(2)[0:64, ic], kb_, k_, True, True, [Bblk, Bqkv], [PS[2]])
                    k.mm(k.bank(3)[0:64, ic], k_, kb_, True, True, [Bblk, Bqkv], [PS[3]])
                    k.mm(k.bank(4)[0:64, ic], k_, q_, True, True, [Bqkv], [PS[4]])
                k.tt(W["A32"][0:64], v3(k.bank(2)), W["Gs"][0:64], ALU.mult, [PS[2], W["BGs"]], [W["BA32"]])
                k.copy(W["PA"][0:64], W["A32"][0:64], [W["BA32"]], [W["BPA"]], eng="act")
                k.tt(W["PB"][0:64], v3(k.bank(3)), W["GsT"][0:64], ALU.mult, [PS[3], W["BGsT"]], [W["BPB"]])
                k.tt(W["attnT"][0:64], v3(k.bank(4)), W["GT"][0:64], ALU.mult, [PS[4], W["BGT"]], [W["BattnT"]])
                idb = bc(identb[0:64, :].unsqueeze(1), [64, NI, 64])
                k.tt(W["TB"][0:64], idb, W["PB"][0:64], ALU.subtract, [Bcols, W["BPB"]], [W["BTB"]])
                def sq_mm():
                    for ii in range(NI):
                        ic = slice(ii * 64, (ii + 1) * 64)
                        k.mm(k.bank(2)[0:64, ic], W["PB"][0:64, ii, :], W["PA"][0:64, ii, :], True, True, [W["BPA"], W["BPB"]], [PS[2]])
                        k.mm(k.bank(3)[0:64, ic], W["PA"][0:64, ii, :], W["PB"][0:64, ii, :], True, True, [W["BPA"], W["BPB"]], [PS[3]])

                def sq_cp():
                    k.copy(W["PA"][0:64], v3(k.bank(2)), [PS[2]], [W["BPA"]], eng="act")
                    k.copy(W["PB"][0:64], v3(k.bank(3)), [PS[3]], [W["BPB"]], eng="dve")

                def t_mm(itn):
                    for ii in range(NI):
                        ic = slice(ii * 64, (ii + 1) * 64)
                        k.mm(k.bank(5)[0:64, ic], W["PA"][0:64, ii, :], W["TB"][0:64, ii, :], True, True, [W["BPA"], W["BTB"]], [PS[5]])

                def t_add(itn):
                    k.tt(W["TB"][0:64], W["TB"][0:64], v3(k.bank(5)), ALU.add, [W["BTB"], PS[5]], [W["BTB"]])

                NIT = 4
                sq_mm()
                sq_cp()
                for itn in range(NIT):
                    t_mm(itn)
                    if itn < NIT - 1:
                        sq_mm()
                    t_add(itn)
                    if itn < NIT - 1:
                        sq_cp()
                k.copy(W["TB32"][0:64], W["TB"][0:64], [W["BTB"]], [W["BTB32"]], eng="act")
                for ii in range(NI):
                    ic = slice(ii * 64, (ii + 1) * 64)
                    k.tr(k.bank(2)[0:64, ic], W["TB32"][0:64, ii, :], ident[0:64, 0:64], [W["BTB32"], Bconst], [PS[2]])
                    k.mm(k.bank(3)[0:64, ic], W["A32"][0:64, ii, :], W["TB32"][0:64, ii, :], True, True, [W["BA32"], W["BTB32"]], [PS[3]])
                k.copy(W["TBt"][0:64], v3(k.bank(2)), [PS[2]], [W["BTBt"]], eng="act")
                k.tt(W["E32"][0:64], bc(ident2[0:64, :].unsqueeze(1), [64, NI, 64]), W["TB32"][0:64], ALU.subtract, [Bcols, W["BTB32"]], [W["BE32"]])
                k.tt(W["E32"][0:64], W["E32"][0:64], v3(k.bank(3)), ALU.subtract, [W["BE32"], PS[3]], [W["BE32"]])
                for ii in range(NI):
                    ic = slice(ii * 64, (ii + 1) * 64)
                    k.mm(k.bank(5)[0:64, ic], W["TBt"][0:64, ii, :], W["E32"][0:64, ii, :], True, True, [W["BTBt"], W["BE32"]], [PS[5]])
                k.copy(W["TBf"][0:64], v3(k.bank(5)), [PS[5]], [W["BTBf"]], eng="dve")
                for (ii, sq_, h, hp, pb, gc, bcl, _, cib) in insts:
                    ic = slice(ii * 64, (ii + 1) * 64)
                    k.tr(k.bank(7)[0:64, ic], vbe[pb:pb + 64, hp, bcl], ident[pb:pb + 64, pb:pb + 64], [Bblk, Bconst], [PS[7]])
                    k.tr(k.bank(4)[0:64, ic], ktl[pb:pb + 64, hp, bcl], ident[pb:pb + 64, pb:pb + 64], [Bblk, Bconst], [PS[4]])
                k.copy(W["vbt"][0:64], v3(k.bank(7)), [PS[7]], [W["Bvbt"]], eng="act")
                k.copy(W["ktail"][0:64], v3(k.bank(4)), [PS[4]], [W["Bktail"]], eng="dve")
                for (ii, sq_, h, hp, pb, gc, bcl, _, cib) in insts:
                    ic = slice(ii * 64, (ii + 1) * 64)
                    k.mm(k.bank(5)[0:64, ic], kbeH[0:64, h, bcl], Sbf[0:64, ii, :], True, True, [Bblk, BSbf], [PS[5]])
                k.tt(W["rp"][0:64], W["vbt"][0:64], v3(k.bank(5)), ALU.subtract, [W["Bvbt"], PS[5]], [W["Brp"]])
                for ii in range(NI):
                    ic = slice(ii * 64, (ii + 1) * 64)
                    k.mm(k.bank(6)[0:64, ic], W["TBf"][0:64, ii, :], W["rp"][0:64, ii, :], True, True, [W["BTBf"], W["Brp"]], [PS[6]])
                k.copy(W["u"][0:64], v3(k.bank(6)), [PS[6]], [W["Bu"]], eng="act")
                for (ii, sq_, h, hp, pb, gc, bcl, _, cib) in insts:
                    ic = slice(ii * 64, (ii + 1) * 64)
                    k.mm(k.bank(7)[0:64, ic], Sbf[0:64, ii, :], qdH[0:64, h, bcl], True, False, [BSbf, Bblk], [PS[7]])
                    k.mm(k.bank(7)[0:64, ic], W["u"][0:64, ii, :], W["attnT"][0:64, ii, :], False, True, [W["Bu"], W["BattnT"]], [PS[7]])
                    k.mm(k.bank(4)[0:64, ic], W["ktail"][0:64, ii, :], W["u"][0:64, ii, :], True, True, [W["Bktail"], W["Bu"]], [PS[4]])
                for sq_ in range(nseq):
                    for par in range(2):
                        pb = par * 64
                        ii0 = sq_ * 4 + par
                        gc = [r_[5] for r_ in insts if r_[1] == sq_ and r_[2] == par][0]
                        src7 = k.bank(7)[0:64, ii0 * 64:(ii0 + 3) * 64].rearrange("p (i c) -> p i c", c=64)[:, 0:3:2, :]
                        dst = ocT[pb:pb + 64, :, gc]
                        if d == 0:
                            k.copy(dst, src7, [PS[7]], [Boc], eng="act")
                        else:
                            k.tt(dst, dst, src7, ALU.add, [Boc, PS[7]], [Boc])
                cib0 = insts[0][8] if lat else None
                if lat:
                    k.tt(S32[0:64], S32[0:64], bc(tails[0:64, :, cib0:cib0 + 1], [64, NI, 64]), ALU.mult, [BS32, Bblk], [BS32])
                else:
                    for sq_ in range(nseq):
                        cb = [r_[8] for r_ in insts if r_[1] == sq_][0]
                        k.tt(S32[0:64, sq_ * 4:(sq_ + 1) * 4, :], S32[0:64, sq_ * 4:(sq_ + 1) * 4, :],
                             bc(tails[0:64, sq_ * 4:(sq_ + 1) * 4, cb:cb + 1], [64, 4, 64]), ALU.mult, [BS32, Bblk], [BS32])
                k.tt(S32[0:64], S32[0:64], v3(k.bank(4)), ALU.add, [BS32, PS[4]], [BS32])
                k.copy(Sbf[0:64], S32[0:64], [BS32], [BSbf], eng="act")
            if not lat:
                for sq_ in range(nseq):
                    for h in range(4):
                        k.dma("sp", o_gdn[sq_, l, d, h], S32[0:64, sq_ * 4 + h, :], [BS32], [])
        P.barrier()
        A.top = alias_mark
        wGz = A.alloc([8, 256], BF16)
        BwGz = Buf()
        k.dma("sp", wGz, w_in_b[l].rearrange("(c p) n -> p c n", p=128)[:, :, C_GZ:C_GZ + 256], [Bcv[("w_in", l)]], [BwGz])
        gzt = A.alloc([TB], BF16)
        Bgzt = Buf()
        sqb = A.alloc([TB], BF16)
        Bsqb = Buf()
        rsn = A.alloc([TB], F32)
        Brsn = Buf()
        for hp in range(2):
            for tb in range(NBLK):
                cols = slice(tb * TB, (tb + 1) * TB)
                for c in range(8):
                    k.mm(k.bank(3), wGz[:, c, hp * 128:(hp + 1) * 128], hT[:, c, cols], c == 0, c == 7, [BwGz, BhT], [PS[3]])
                k.act(gzt, k.bank(3), AF.Silu, [PS[3]], [Bgzt])
                k.act(sqb, ocT[:, hp, cols], AF.Square, [Boc], [Bsqb])
                k.mm(k.bank(2), blockones, sqb, True, True, [Bsqb, Bcols], [PS[2]])
                k.act(rsn, k.bank(2), AF.Sqrt, [PS[2], Bconst], [Brsn], scale=1.0 / 64, bias=epsb)
                k.recip(rsn, rsn, [Brsn], [Brsn])
                k.stt(rsn, ocT[:, hp, cols], gnorm, rsn, ALU.mult, ALU.mult, [Boc, Brsn, Bcols], [Brsn])
                k.tt(oT[:, 4 + hp, cols], rsn, gzt, ALU.mult, [Brsn, Bgzt], [BoT])
        P.barrier()
        A.top = base

    mixer_base = [0]

    groups = [dict(name="ctx", tok0=0, T=512, nseq=2, L=256, kc=0),
              dict(name="lat", tok0=512, T=2048, nseq=1, L=2048, kc=1)]

    for l in range(nlayers):
        for g in groups:
            T = g["T"]
            kc = g["kc"]
            nblk = T // TB
            mA = A.top
            xblk = [A.alloc([8, TB], F32) for _ in range(2)]
            Bxb = [Buf(), Buf()]
            sq = A.alloc([8, TB], BF16)
            rstd = A.alloc([TB], F32)
            tmp = A.alloc([8, TB], F32)
            Wn = (sq, Buf(), rstd, Buf(), tmp, Buf())
            wg32 = A.alloc([8, 16], F32)
            Bwg32 = Buf()
            k.dma("sp", wg32, w_in[l].rearrange("(c p) n -> p c n", p=128)[:, :, C_GA:C_GA + 16], [], [Bwg32], slow=True)
            for tb in range(nblk):
                gb = (g["tok0"] // TB) + tb
                s = tb % 2
                k.dma("sp", xblk[s], xTv[:, :, gb * TB:(gb + 1) * TB], [BxT[gb]], [Bxb[s]])
                prenorm(xblk[s], Bxb[s], hT, BhT, slice(tb * TB, (tb + 1) * TB), A1, 0, l, kc, Wn, gate=(wg32, Bwg32))
            P.barrier()
            A.top = mA
            if stop == "A":
                P.emit(); ES.close(); return nc, P
            if stub_mixer:
                k.copy(oT[:, :, 0:T], hT[:, :, 0:T], [BhT], [BoT], eng="dve")
            else:
                mB = A.top
                mixer_base[0] = mB
                if "noattn" not in (dbg or ()):
                    mixer_attn(l, g)
                    P.barrier()
                A.top = mB
                if "nohy" not in (dbg or ()):
                    mixer_hyena(l, g)
                if "nogdn" not in (dbg or ()):
                    mixer_gdn(l, g)
                if 'oT' in (dbg or ()) and l == 0:
                    dd = dout("dbg_oT_" + g["name"], [8, 128, T])
                    dtmp = A.alloc([T], F32)
                    Bd = Buf()
                    for c in range(8):
                        k.copy(dtmp, oT[:, c, 0:T], [BoT], [Bd], eng="dve")
                        k.dma("sp", dd[c], dtmp, [Bd], [])
                    P.barrier()
                    A.top = mB
                if stop == "B" + g["name"][0]:
                    P.emit(); ES.close(); return nc, P
            P.barrier()
            mC = A.top
            wout = A.alloc([8, D], BF16)
            Bwout = Buf()
            k.dma("sp", wout, w_out_b[l].rearrange("(c p) n -> p c n", p=128), [Bcv[("w_out", l)]], [Bwout])
            xblk = A.alloc([8, TB], F32)
            Bxb = Buf()
            xn = xblk
            Bxn = Bxb
            mT = A.alloc([8, TB], F32)
            BmT = Buf()
            sq = A.alloc([8, TB], BF16)
            Bsq = Buf()
            rstd = A.alloc([TB], F32)
            Brstd = Buf()
            h2T = A.alloc([8, TB], BF16)
            Bh2 = Buf()
            fT = hT.rearrange("p a b -> p (a b)").rearrange("p (f t) -> p f t", f=32)
            BfT = BhT
            rl = [A.alloc([TB], BF16) for _ in range(2)]
            Brl = [Buf(), Buf()]
            w1s = [A.alloc([8, 512], BF16) for _ in range(2)]
            Bw1 = [Buf(), Buf()]
            w2s = [A.alloc([4, D], BF16) for _ in range(2)]
            Bw2 = [Buf(), Buf()]
            w1v = w1_b[l].rearrange("(c p) n -> p c n", p=128)
            w2v = w2_b[l].rearrange("(c p) n -> p c n", p=128)

            def post(Bsrc_banks, Bres, res, Bsc, l, kc, dstx, Bdstx):
                for c in range(8):
                    k.copy(mT[:, c, :], k.bank(c), [PS[c]], [BmT])
                k.act(sq, mT, AF.Square, [BmT], [Bsq])
                if stop == "P1":
                    raise StopBuild()
                rstd_from_sq(sq, Bsq, 0, rstd, Brstd)
                if stop == "P2":
                    raise StopBuild()
                k.tt(mT, mT, bc(rstd.unsqueeze(1), [128, 8, TB]), ALU.mult, [BmT, Brstd], [BmT])
                if stop == "P3":
                    raise StopBuild()
                for c in range(8):
                    k.stt(dstx[:, c, :], mT[:, c, :], Bsc[:, l, c, kc:kc + 1], res[:, c, :], ALU.mult, ALU.add,
                          [BmT, Bmod, Bres], [Bdstx])

            for tb in range(nblk):
                gb = (g["tok0"] // TB) + tb
                cols = slice(tb * TB, (tb + 1) * TB)
                k.dma("sp", xblk, xTv[:, :, gb * TB:(gb + 1) * TB], [BxT[gb]], [Bxb])
                if stop == "C0a":
                    P.emit(); ES.close(); return nc, P
                for dc in range(8):
                    for ec in range(8):
                        k.mm(k.bank(dc), wout[:, ec, dc * 128:(dc + 1) * 128], oT[:, ec, cols], ec == 0, ec == 7,
                             [Bwout, BoT], [PS[dc]])
                if stop == "C0b":
                    P.emit(); ES.close(); return nc, P
                try:
                    post(None, Bxb, xblk, B1, l, kc, xn, Bxn)
                except StopBuild:
                    P.emit(); ES.close(); return nc, P
                if stop == "C1":
                    P.emit(); ES.close(); return nc, P
                Wn = (sq, Bsq, rstd, Brstd, mT, BmT)
                prenorm(xn, Bxn, h2T, Bh2, slice(0, TB), A2, 24, l, kc, Wn)
                for s8 in range(8):
                    ws = w1s[s8 % 2]
                    bw = Bw1[s8 % 2]
                    k.dma("sp", ws, w1v[:, :, s8 * 512:(s8 + 1) * 512], [Bcv[("w1", l)]], [bw])
                    for fj in range(4):
                        fc = s8 * 4 + fj
                        bk = 1 + (fc % 4)
                        for c in range(8):
                            k.mm(k.bank(bk), ws[:, c, fj * 128:(fj + 1) * 128], h2T[:, c, :], c == 0, c == 7,
                                 [bw, Bh2], [PS[bk]])
                        r = rl[fc % 2]
                        br = Brl[fc % 2]
                        k.act(r, k.bank(bk), AF.Relu, [PS[bk]], [br])
                        k.tt(fT[:, fc, :], r, r, ALU.mult, [br], [BfT])
                if stop == "C2":
                    P.emit(); ES.close(); return nc, P
                for s8 in range(8):
                    ws = w2s[s8 % 2]
                    bw = Bw2[s8 % 2]
                    k.dma("sp", ws, w2v[:, s8 * 4:(s8 + 1) * 4, :], [Bcv[("w2", l)]], [bw])
                    for fj in range(4):
                        fc = s8 * 4 + fj
                        for dc in range(8):
                            k.mm(k.bank(dc), ws[:, fj, dc * 128:(dc + 1) * 128], fT[:, fc, :], fc == 0, fc == 31,
                                 [bw, BfT], [PS[dc]])
                post(None, Bxn, xn, B2, l, kc, xblk, Bxb)
                k.dma("sp", xTv[:, :, gb * TB:(gb + 1) * TB], xblk, [Bxb], [BxT[gb]])
            P.barrier()
            A.top = mC

    xi = [A.alloc([8, 128], F32) for _ in range(2)]
    yo = [A.alloc([D], F32) for _ in range(2)]
    Bxi = [Buf(), Buf()]
    Byo = [Buf(), Buf()]
    for i in range(TT // 128):
        s = i % 2
        k.dma("sp", xi[s], xTv[:, :, i * 128:(i + 1) * 128], [BxT[i // 4]], [Bxi[s]])
        for c in range(8):
            bk = 2 * s + c // 4
            k.tr(k.bank(bk)[:, (c % 4) * 128:(c % 4 + 1) * 128], xi[s][:, c, :], ident, [Bxi[s], Bconst], [PS[bk]])
        for hh in range(2):
            bk = 2 * s + hh
            k.copy(yo[s][:, hh * 512:(hh + 1) * 512], k.bank(bk), [PS[bk]], [Byo[s]])
        k.dma("sp", y_tok[i * 128:(i + 1) * 128, :], yo[s], [Byo[s]], [])
    P.emit()
    ES.close()
    return nc, P


def _rope_tables(dim, nrows_total, row0):
    rows = 2048 // 64
    row = np.repeat(np.arange(rows), 64).astype(np.float32)
    col = np.tile(np.arange(64), rows).astype(np.float32)
    nf = dim // 4
    inv = (10000.0 ** (-np.arange(nf, dtype=np.float32) / nf)).astype(np.float32)
    ang = np.concatenate([row[:, None] * inv, col[:, None] * inv], -1).astype(np.float32)
    cos = np.cos(ang).astype(np.float32)
    sin = np.sin(ang).astype(np.float32)
    out = np.zeros((2, nrows_total, 2048), np.float32)
    for f in range(dim):
        out[0, row0 + f] = cos[:, f // 2]
        out[1, row0 + f] = sin[:, f // 2] * (-1.0 if f % 2 == 0 else 1.0)
    return out


_CONST = {}


def _consts():
    if _CONST:
        return _CONST
    _CONST["ropeA"] = _rope_tables(32, 128, 64)
    _CONST["ropeB"] = _rope_tables(64, 64, 0)
    ps = np.zeros((128, 128), np.float32)
    for m in range(128):
        ps[m ^ 1, m] = 1.0
    _CONST["pswap"] = ps
    j = np.arange(128)[:, None]
    i = np.arange(128)[None, :]
    _CONST["masks"] = np.stack([(j >= i), (j <= i)], 0).astype(np.float32).astype(ml_dtypes.bfloat16)
    r = np.arange(64)[:, None]
    c_ = np.arange(64)[None, :]
    _CONST["negm"] = np.stack([(c_ <= r), (c_ >= r)], 0).astype(np.float32)
    sel = np.zeros((16, 8, 128), np.float32)
    for which in range(2):
        for d_ in range(2):
            for hp in range(2):
                for p in range(128):
                    sel[which * 8 + d_ * 4 + 2 * hp + (p // 64), which * 4 + d_ * 2 + hp, p] = 1.0
    _CONST["gsel"] = sel
    _CONST["smask"] = np.stack([(c_ < r), (c_ > r)], 0).astype(np.float32)
    for L_, sfx in ((256, "s"), (2048, "b")):
        N = 2 * L_
        idx = np.arange(L_ + 1, dtype=np.int64)
        th = 2.0 * np.pi * ((idx[:, None] * idx[None, :]) % N).astype(np.float64) / N
        _CONST["dft_" + sfx] = np.stack([np.cos(th), np.sin(th)], 0).astype(np.float32).astype(ml_dtypes.bfloat16)
        t = np.arange(L_, dtype=np.float32)
        t01 = t / np.float32(max(L_ - 1, 1))
        w = np.float32(2.0 * math.pi) * t / np.float32(L_)
        bands = np.linspace(1e-4, 7, 8, dtype=np.float32)
        feats = np.concatenate([t01[:, None], np.cos(w[:, None] * bands), -np.sin(w[:, None] * bands)], -1).astype(np.float32)
        _CONST["feats_" + sfx] = np.ascontiguousarray(feats.T)
        dist = (np.abs(t - (L_ // 2)) / np.float32(L_ / 2)).astype(np.float32)
        _CONST["negdist_" + sfx] = np.ascontiguousarray((-dist).reshape(L_ // 128, 128).T)
        nf = L_ // 128 + 1
        f = (np.arange(nf)[None, :] * 128 + np.arange(128)[:, None])
        wfn = np.where((f == 0) | (f == L_), 1.0, 2.0) / N
        alpha = np.array([1.0, 0.0, -1.0, 0.0])[f % 4]
        beta = np.array([0.0, 1.0, 0.0, -1.0])[f % 4]
        _CONST["aw_" + sfx] = (alpha * wfn).astype(np.float32)
        _CONST["bw_" + sfx] = (beta * wfn).astype(np.float32)
    return _CONST


def core_inputs(inp, i):
    b = i % 4
    f = lambda a: np.ascontiguousarray(a, dtype=np.float32)
    d = dict(
        x_tok=f(np.concatenate([inp["x_prompt"][2 * i], inp["x_prompt"][2 * i + 1], inp["x_sample"][b]], 0)),
        conds=f(np.stack([inp["c_ctx"], inp["c"][b]], 0)),
        w_ada=f(inp["w_ada"]), b_ada=f(inp["b_ada"]),
        gvec=f(np.stack([inp["g_pre_mix"], inp["g_post_mix"], inp["g_pre_mlp"], inp["g_post_mlp"]], 0)),
        w_in=f(inp["w_in"]), w_out=f(inp["w_out"]), mlp_w1=f(inp["mlp_w1"]), mlp_w2=f(inp["mlp_w2"]),
        mla_kv_norm=f(inp["mla_kv_norm"]), mla_w_ukv=f(inp["mla_w_ukv"]), swa_sink=f(inp["swa_sink"]),
        c_ckv=f(inp["cache_mla_ckv"][b]), c_kpe=f(inp["cache_mla_kpe"][b]),
        c_swk=f(inp["cache_swa_k"][b].reshape(DEPTH, 512, 128)), c_swv=f(inp["cache_swa_v"][b].reshape(DEPTH, 512, 128)),
        gdn_conv=f(inp["gdn_conv"]), gdn_a_log=f(inp["gdn_a_log"].reshape(DEPTH, 8)), gdn_dt_bias=f(inp["gdn_dt_bias"].reshape(DEPTH, 8)),
        gdn_norm=f(inp["gdn_norm"]), st_gdn=f(inp["state_gdn"][b]),
        hy_conv=f(inp["hy_conv"]), hy_w1=f(inp["hy_w1"]), hy_b1=f(inp["hy_b1"]), hy_w2=f(inp["hy_w2"]), hy_b2=f(inp["hy_b2"]),
        hy_w3=f(inp["hy_w3"]), hy_freq=f(inp["hy_freq"]), hy_decay=f(inp["hy_decay"]), hy_bias=f(inp["hy_bias"]),
    )
    d.update(_consts())
    return d


_CACHE = {}


def kernel(**inputs):
    inp = {k_: np.asarray(v) for k_, v in inputs.items()}
    if "nc" not in _CACHE:
        _CACHE["nc"] = build_program()[0]
    nc = _CACHE["nc"]
    in_maps = [core_inputs(inp, i) for i in range(8)]
    res = run_bass_kernel_spmd(nc, in_maps, core_ids=list(range(8)))
    R = res.results
    y_prompt = np.stack([R[i]["y_tok"][s_ * 256:(s_ + 1) * 256] for i in range(8) for s_ in range(2)], 0).astype(np.float32)
    y_sample = np.stack([R[b]["y_tok"][512:2560] for b in range(4)], 0).astype(np.float32)
    cat = lambda nm: np.concatenate([R[i][nm] for i in range(8)], 0).astype(np.float32)
    new_ckv = cat("o_ckv")
    new_kpe = cat("o_kpe")
    new_k = cat("o_swk").reshape(16, DEPTH, 256, 2, 64)
    new_v = cat("o_swv").reshape(16, DEPTH, 256, 2, 64)
    new_st = cat("o_gdn")
    return (y_prompt, y_sample, new_ckv, new_kpe, new_k, new_v, new_st)
```

```python
import math
import contextlib
import numpy as np
import ml_dtypes
import concourse.bass as bass
import concourse.mybir as mybir
from concourse.bass_utils import run_bass_kernel_spmd

F32 = mybir.dt.float32
BF16 = mybir.dt.bfloat16
AF = mybir.ActivationFunctionType
ALU = mybir.AluOpType

COMPUTE = ("pe", "act", "dve", "pool")
QUEUES = ("pe", "act", "dve", "pool", "sp")
NDMASEM = 12

D = 1024
NCH = 8
TT = 2560
TB = 512
DEPTH = 2
IN_COLS = 2864
EPS = 1e-6
C_MQ, C_CKV, C_KPE, C_SQ, C_SK, C_SV, C_GQKV, C_GZ, C_GA, C_GB, C_HU = (
    0, 384, 512, 544, 800, 928, 1056, 1824, 2080, 2088, 2096)


class Buf:
    __slots__ = ("w", "r", "excl", "rg")

    def __init__(self, excl=False):
        self.w = []
        self.r = []
        self.excl = excl
        self.rg = None


class Prog:
    def __init__(self, nc):
        self.nc = nc
        self.ops = []
        self.last = {q: None for q in QUEUES}
        self.dmas_since = []
        self.force = set()

    def op(self, eng, fn, reads=(), writes=(), dma=False, pe_force=False, bg=False):
        writes = list(writes) + [b for b in reads if b.excl]
        reads = [b for b in reads if not b.excl]
        deps = set()
        for b in reads:
            deps.update(b.w)
        for b in writes:
            if dma and not b.r:
                deps.update(w for w in b.w if not self.ops[w][3])
            else:
                deps.update(b.w)
            deps.update(b.r)
        oid = len(self.ops)
        self.ops.append((eng, fn, sorted(deps), dma))
        if pe_force:
            self.force.add(oid)
        for b in reads:
            b.r.append(oid)
        for b in writes:
            if dma and not b.r and b.w and all(self.ops[w][3] for w in b.w):
                b.w = b.w + [oid]
            else:
                b.w = [oid]
            b.r = []
        if dma:
            if not bg:
                self.dmas_since.append(oid)
        else:
            self.last[eng] = oid
        return oid

    def barrier(self):
        lasts = [v for v in self.last.values() if v is not None]
        deps = sorted(set(lasts + self.dmas_since))
        self.dmas_since = []
        for q in QUEUES:
            self.ops.append((q, None, deps, False))

    def emit(self):
        nc = self.nc
        ops = self.ops
        all_dma = [i for i, o in enumerate(ops) if o[3]]
        ops.append(("sp", None, all_dma, False))
        n = len(ops)
        eng_idx = [0] * n
        cnt = {q: 0 for q in QUEUES}
        for i, o in enumerate(ops):
            cnt[o[0]] += 1
            eng_idx[i] = cnt[o[0]]
        clock = {q: ({c: 0 for c in QUEUES}, set()) for q in QUEUES}
        opclock = [None] * n
        waits = [None] * n
        needed = [False] * n
        for i, (eng, fn, deps, isdma) in enumerate(ops):
            ck, dset = clock[eng]
            w = []
            for d in deps:
                deng, dfn, _, disdma = ops[d]
                if disdma:
                    if d in dset:
                        continue
                    w.append(d)
                    needed[d] = True
                    dset.add(d)
                else:
                    if dfn is None:
                        continue
                    if deng == eng and eng == "pe" and i not in self.force:
                        continue
                    if ck[deng] >= eng_idx[d]:
                        continue
                    w.append(d)
                    needed[d] = True
                    ck[deng] = eng_idx[d]
                ock = opclock[d]
                for c in QUEUES:
                    if ock[c] > ck[c]:
                        ck[c] = ock[c]
            waits[i] = w
            opclock[i] = dict(ck)
        stack = contextlib.ExitStack()
        sems = {q: stack.enter_context(nc.semaphore("s_" + q)) for q in COMPUTE}
        dsem = {q: [stack.enter_context(nc.semaphore("d_%s_%d" % (q, j))) for j in range(NDMASEM)]
                for q in QUEUES}
        sigval = [None] * n
        ccount = {q: 0 for q in COMPUTE}
        dcount = {q: 0 for q in QUEUES}
        dslot_val = {q: [0] * NDMASEM for q in QUEUES}
        pre_wait = [None] * n
        for i, (eng, fn, deps, isdma) in enumerate(ops):
            if isdma:
                k = dcount[eng] % NDMASEM
                dcount[eng] += 1
                if dslot_val[eng][k] > 0:
                    pre_wait[i] = (dsem[eng][k], dslot_val[eng][k])
                dslot_val[eng][k] += 16
                sigval[i] = (dsem[eng][k], dslot_val[eng][k])
            elif needed[i]:
                ccount[eng] += 1
                sigval[i] = (sems[eng], ccount[eng])
        per = {q: [] for q in QUEUES}
        for i, o in enumerate(ops):
            per[o[0]].append(i)
        self.stats = {q: len(per[q]) for q in QUEUES}
        self.stats["waits"] = sum(len(w) for w in waits)
        with nc.Block() as block:
            def mk(q):
                def body(e):
                    for i in per[q]:
                        eng, fn, deps, isdma = ops[i]
                        if pre_wait[i] is not None:
                            e.wait_ge(pre_wait[i][0], pre_wait[i][1])
                        for d in waits[i]:
                            e.wait_ge(sigval[d][0], sigval[d][1])
                        if fn is None:
                            continue
                        ins = fn(e)
                        if isdma:
                            ins.then_inc(sigval[i][0], 16)
                        elif needed[i]:
                            ins.then_inc(sigval[i][0], 1)
                return body
            block.tensor(mk("pe"))
            block.scalar(mk("act"))
            block.vector(mk("dve"))
            block.gpsimd(mk("pool"))
            block.sync(mk("sp"))
        stack.close()


class StopBuild(Exception):
    pass


class Arena:
    def __init__(self, tile, nbytes):
        self.t = tile
        self.n = nbytes
        self.top = 0

    def alloc(self, shape, dt):
        n = int(np.prod(shape))
        nb = n * (2 if dt == BF16 else 4)
        off = self.top
        self.top = off + (nb + 63) // 64 * 64
        assert self.top <= self.n, ("SBUF arena overflow", self.top, self.n)
        if dt == BF16:
            v = self.t[:, off // 2: off // 2 + n]
        else:
            v = self.t[:, off // 2: off // 2 + 2 * n].bitcast(dt)
        if len(shape) == 2:
            v = v.rearrange("p (a b) -> p a b", a=shape[0])
        elif len(shape) == 3:
            v = v.rearrange("p (a b c) -> p a b c", a=shape[0], b=shape[1])
        return v


class K:
    def __init__(self, nc, P, arena, psum):
        self.nc = nc
        self.P = P
        self.A = arena
        self.psum = psum
        self.PS = [Buf(excl=True) for _ in range(8)]
        self.rr = 0

    def bank(self, i):
        return self.psum[:, i * 512:(i + 1) * 512]

    def _rg(self, stat, W):
        key = (stat.base_partition(), stat.shape[0])
        force = False
        for b in W:
            if b.excl:
                if b.rg is not None and b.rg != key:
                    force = True
                b.rg = key
        return force

    def mm(self, out, lhsT, rhs, start, stop, R, W):
        f = self._rg(lhsT, W)
        self.P.op("pe", lambda e: e.matmul(out, lhsT=lhsT, rhs=rhs, start=start, stop=stop), R, W, pe_force=f)

    def tr(self, out, in_, ident, R, W):
        f = self._rg(in_, W)
        self.P.op("pe", lambda e: e.transpose(out, in_, ident), R, W, pe_force=f)

    def act(self, out, in_, func, R, W, scale=None, bias=None):
        kw = {}
        if scale is not None:
            kw["scale"] = scale
        if bias is not None:
            kw["bias"] = bias
        self.P.op("act", lambda e: e.activation(out=out, in_=in_, func=func, **kw), R, W)

    def tt(self, out, in0, in1, op, R, W, eng="dve"):
        self.P.op(eng, lambda e: e.tensor_tensor(out=out, in0=in0, in1=in1, op=op), R, W)

    def ts(self, out, in0, s1, s2, op0, op1, R, W, eng="dve"):
        if op1 is None:
            self.P.op(eng, lambda e: e.tensor_scalar(out=out, in0=in0, scalar1=s1, scalar2=None, op0=op0), R, W)
        else:
            self.P.op(eng, lambda e: e.tensor_scalar(out=out, in0=in0, scalar1=s1, scalar2=s2, op0=op0, op1=op1), R, W)

    def stt(self, out, in0, scalar, in1, op0, op1, R, W):
        self.P.op("dve", lambda e: e.scalar_tensor_tensor(out=out, in0=in0, scalar=scalar, in1=in1, op0=op0, op1=op1), R, W)

    def copy(self, out, in_, R, W, eng=None):
        if eng is None:
            self.rr ^= 1
            eng = "dve" if self.rr else "act"
        if eng == "act":
            self.P.op("act", lambda e: e.activation(out=out, in_=in_, func=AF.Copy), R, W)
        else:
            self.P.op(eng, lambda e: e.tensor_copy(out=out, in_=in_), R, W)

    def recip(self, out, in_, R, W):
        self.P.op("dve", lambda e: e.reciprocal(out=out, in_=in_), R, W)

    def memset(self, out, val, W, eng="pool"):
        self.P.op(eng, lambda e: e.memset(out, val), (), W)

    def dma(self, q, out, in_, R, W, slow=False, bg=False):
        if bg:
            self.P.op(q, lambda e: e.dma_start(out=out, in_=in_), R, W, dma=True, bg=True)
        elif slow:
            self.P.op(q, lambda e: e.dma_start(out=out, in_=in_, allow_slow_non_contiguous=True), R, W, dma=True)
        else:
            self.P.op(q, lambda e: e.dma_start(out=out, in_=in_), R, W, dma=True)


def bc(ap, shape):
    return ap.to_broadcast(shape)


def build_program(dbg=None, stub_mixer=False, nlayers=DEPTH, stop=None):
    nc = bass.Bass("TRN2", target_bir_lowering=False)
    P = Prog(nc)
    ES = contextlib.ExitStack()

    def din(name, shape, dt=F32):
        return nc.dram_tensor(name, list(shape), dt, kind="ExternalInput").ap()

    def dout(name, shape, dt=F32):
        return nc.dram_tensor(name, list(shape), dt, kind="ExternalOutput").ap()

    def dscr(name, shape, dt=F32):
        return nc.dram_tensor(name, list(shape), dt, kind="Internal").ap()

    x_tok = din("x_tok", [TT, D])
    conds = din("conds", [2, D])
    w_ada = din("w_ada", [DEPTH, D, 6 * D])
    b_ada = din("b_ada", [DEPTH, 6 * D])
    gvec = din("gvec", [4, DEPTH, D])
    w_in = din("w_in", [DEPTH, D, IN_COLS])
    w_out = din("w_out", [DEPTH, D, D])
    mlp_w1 = din("mlp_w1", [DEPTH, D, 4 * D])
    mlp_w2 = din("mlp_w2", [DEPTH, 4 * D, D])
    y_tok = dout("y_tok", [TT, D])
    mla_kv_norm = din("mla_kv_norm", [DEPTH, 128])
    mla_w_ukv = din("mla_w_ukv", [DEPTH, 128, 512])
    swa_sink = din("swa_sink", [DEPTH, 4])
    c_ckv = din("c_ckv", [DEPTH, 512, 128])
    c_kpe = din("c_kpe", [DEPTH, 512, 32])
    c_swk = din("c_swk", [DEPTH, 512, 128])
    c_swv = din("c_swv", [DEPTH, 512, 128])
    ropeA = din("ropeA", [2, 128, 2048])
    ropeB = din("ropeB", [2, 64, 2048])
    pswap_d = din("pswap", [128, 128])
    masks_d = din("masks", [2, 128, 128], BF16)
    hy_conv = din("hy_conv", [DEPTH, 3, 768])
    hy_w1 = din("hy_w1", [DEPTH, 17, 64])
    hy_b1 = din("hy_b1", [DEPTH, 64])
    hy_w2 = din("hy_w2", [DEPTH, 64, 64])
    hy_b2 = din("hy_b2", [DEPTH, 64])
    hy_w3 = din("hy_w3", [DEPTH, 64, 512])
    hy_freq = din("hy_freq", [DEPTH, 2, 64])
    hy_decay = din("hy_decay", [DEPTH, 512])
    hy_bias = din("hy_bias", [DEPTH, 2, 256])
    HY = {}
    for L_, sfx in ((256, "s"), (2048, "b")):
        HY[L_] = dict(tab=din("dft_" + sfx, [2, L_ + 1, L_ + 1], BF16), feats=din("feats_" + sfx, [17, L_]),
                      negdist=din("negdist_" + sfx, [128, L_ // 128]), aw=din("aw_" + sfx, [128, L_ // 128 + 1]),
                      bw=din("bw_" + sfx, [128, L_ // 128 + 1]))
    gdn_conv = din("gdn_conv", [DEPTH, 3, 768])
    gdn_a_log = din("gdn_a_log", [DEPTH, 8])
    gdn_dt_bias = din("gdn_dt_bias", [DEPTH, 8])
    gdn_norm = din("gdn_norm", [DEPTH, 64])
    st_gdn = din("st_gdn", [DEPTH, 2, 4, 64, 64])
    negm_d = din("negm", [2, 64, 64])
    sel_d = din("gsel", [16, 8, 128])
    sm_d = din("smask", [2, 64, 64])
    o_gdn = dout("o_gdn", [2, DEPTH, 2, 4, 64, 64])
    o_ckv = dout("o_ckv", [2, DEPTH, 256, 128])
    o_kpe = dout("o_kpe", [2, DEPTH, 256, 32])
    o_swk = dout("o_swk", [2, DEPTH, 256, 128])
    o_swv = dout("o_swv", [2, DEPTH, 256, 128])
    xT = dscr("xT", [D, TT])
    w_in_b = dscr("w_in_b", [DEPTH, D, IN_COLS], BF16)
    w_out_b = dscr("w_out_b", [DEPTH, D, D], BF16)
    w1_b = dscr("w1_b", [DEPTH, D, 4 * D], BF16)
    w2_b = dscr("w2_b", [DEPTH, 4 * D, D], BF16)
    Bcv = {}
    xTv = xT.rearrange("(c p) t -> p c t", p=128)
    dbg_out = {}

    ARENA_BYTES = 206 * 1024
    arena_t = ES.enter_context(nc.sbuf_tensor("arena", [128, ARENA_BYTES // 2], BF16))
    psum = ES.enter_context(nc.psum_tensor("psum", [128, 4096], F32))
    A = Arena(arena_t, ARENA_BYTES)
    k = K(nc, P, A, psum)
    PS = k.PS

    ident = A.alloc([128], F32)
    Bconst = Buf()
    k.memset(ident, 1.0, [Bconst])
    P.op("pool", lambda e: e.affine_select(out=ident, in_=ident, pattern=[[-1, 128]], compare_op=ALU.is_equal,
                                           fill=0.0, base=0, channel_multiplier=1), [Bconst], [Bconst])
    ones_b = A.alloc([128], BF16)
    k.memset(ones_b, 1.0, [Bconst])
    ones_f = A.alloc([128], F32)
    k.memset(ones_f, 1.0, [Bconst])

    pswapF = A.alloc([128], F32)
    k.dma("sp", pswapF, pswap_d, [], [Bconst])
    masks = A.alloc([2, 128], BF16)
    k.dma("sp", masks, masks_d.rearrange("m p q -> p m q"), [], [Bconst])
    kvg = A.alloc([DEPTH], F32)
    k.dma("sp", kvg, mla_kv_norm.rearrange("l p -> p l"), [], [Bconst], slow=True)
    sinkE = A.alloc([DEPTH * 4], F32)
    k.dma("sp", sinkE, swa_sink.rearrange("l h -> (l h)").partition_broadcast(128), [], [Bconst], slow=True)
    k.act(sinkE, sinkE, AF.Exp, [Bconst], [Bconst])
    modT = A.alloc([DEPTH, 48, 2], F32)
    A1 = A.alloc([DEPTH, 8, 2], F32)
    B1 = A.alloc([DEPTH, 8, 2], F32)
    A2 = A.alloc([DEPTH, 8, 2], F32)
    B2 = A.alloc([DEPTH, 8, 2], F32)
    Bmod = Buf()
    mark0 = A.top
    stgA = A.alloc([128], F32)
    stgB = A.alloc([128], F32)
    TAc = A.alloc([128], F32)
    TBc = A.alloc([128], F32)
    scT = A.alloc([8, 2], BF16)
    Bc_ = Buf()
    Bstg = Buf()
    k.memset(stgA, 0.0, [Bstg])
    k.memset(stgB, 0.0, [Bstg])
    k.dma("sp", stgA[0:16, :], conds.rearrange("k (c p) -> (k c) p", p=128), [], [Bstg])
    k.dma("sp", stgA[16:80, :], gvec.rearrange("g l (c p) -> (g l c) p", p=128), [], [Bstg])
    k.dma("sp", stgB[0:96, :], b_ada.rearrange("l (j p) -> (l j) p", p=128), [], [Bstg])
    k.tr(k.bank(7)[:, 0:128], stgA, ident, [Bstg, Bconst], [PS[7]])
    k.tr(k.bank(7)[:, 128:256], stgB, ident, [Bstg, Bconst], [PS[7]])
    k.copy(TAc, k.bank(7)[:, 0:128], [PS[7]], [Bc_], eng="dve")
    k.copy(TBc, k.bank(7)[:, 128:256], [PS[7]], [Bc_], eng="dve")
    gT = TAc[:, 16:80].rearrange("p (g l c) -> p g l c", g=4, l=2)
    badaT = TBc[:, 0:96].rearrange("p (l j) -> p l j", l=2)
    k.act(scT, TAc[:, 0:16].rearrange("p (k c) -> p c k", k=2), AF.Silu, [Bc_], [Bc_])
    NPIECE = 8
    PW = 6 * D // NPIECE
    wa = [A.alloc([8, PW], BF16) for _ in range(2)]
    Bwa = [Buf(), Buf()]
    for l in range(DEPTH):
        for pc in range(NPIECE):
            t = wa[pc % 2]
            bt = Bwa[pc % 2]
            k.dma("pool", t, w_ada[l].rearrange("(c p) n -> p c n", p=128)[:, :, pc * PW:(pc + 1) * PW], [], [bt])
            for jj in range(6):
                j = pc * 6 + jj
                for c in range(8):
                    k.mm(k.bank(l)[:, j * 2:(j + 1) * 2], t[:, c, jj * 128:(jj + 1) * 128], scT[:, c, :],
                         c == 0, c == 7, [bt, Bc_], [PS[l]])
        k.tt(modT[:, l], k.bank(l)[:, 0:96].rearrange("p (j k) -> p j k", k=2),
             bc(badaT[:, l].unsqueeze(2), [128, 48, 2]), ALU.add, [PS[l], Bc_], [Bmod])
        for (dst, gi, j0, plus1) in ((A1, 0, 8, True), (B1, 1, 16, False), (A2, 2, 32, True), (B2, 3, 40, False)):
            gb = bc(gT[:, gi, l].unsqueeze(2), [128, 8, 2])
            if plus1:
                k.stt(dst[:, l], modT[:, l, j0:j0 + 8, :], 1.0, gb, ALU.add, ALU.mult, [Bmod, Bc_], [Bmod])
            else:
                k.tt(dst[:, l], modT[:, l, j0:j0 + 8, :], gb, ALU.mult, [Bmod, Bc_], [Bmod])
    for l_ in range(DEPTH):
        for nm, src, dst, rows in (("w_in", w_in, w_in_b, D), ("w_out", w_out, w_out_b, D), ("w1", mlp_w1, w1_b, D), ("w2", mlp_w2, w2_b, 4 * D)):
            Bcv[(nm, l_)] = Buf()
            for r0 in range(0, rows, 128):
                k.dma("pool", dst[l_, r0:r0 + 128, :], src[l_, r0:r0 + 128, :], [], [Bcv[(nm, l_)]], bg=True)
    P.barrier()
    A.top = mark0
    dbgmod = dout("dbg_mod", [128, DEPTH * 96])
    k.dma("sp", dbgmod, modT.rearrange("p l j k -> p (l j k)"), [Bmod], [])
    if stop == "mods":
        P.emit(); ES.close(); return nc, P

    BxT = [Buf() for _ in range(TT // TB)]
    m_pro = A.top
    xs = [A.alloc([D], F32) for _ in range(2)]
    xo = [A.alloc([8, 128], F32) for _ in range(2)]
    Bxs = [Buf(), Buf()]
    Bxo = [Buf(), Buf()]
    for i in range(TT // 128):
        s = i % 2
        k.dma("sp", xs[s], x_tok[i * 128:(i + 1) * 128, :], [], [Bxs[s]])
        for c in range(8):
            bk = 2 * s + c // 4
            k.tr(k.bank(bk)[:, (c % 4) * 128:(c % 4 + 1) * 128], xs[s][:, c * 128:(c + 1) * 128], ident,
                 [Bxs[s], Bconst], [PS[bk]])
        for hh in range(2):
            bk = 2 * s + hh
            k.copy(xo[s][:, hh * 4:(hh + 1) * 4, :], k.bank(bk).rearrange("p (c t) -> p c t", c=4), [PS[bk]], [Bxo[s]])
        k.dma("sp", xTv[:, :, i * 128:(i + 1) * 128], xo[s], [Bxo[s]], [BxT[i // 4]])
    P.barrier()
    A.top = m_pro
    if stop == "pro":
        P.emit(); ES.close(); return nc, P

    hT = A.alloc([8, 2048], BF16)
    oT = A.alloc([8, 2048], BF16)
    gatesT = A.alloc([2048], F32)
    BhT = Buf()
    BoT = Buf()
    Bgates = Buf()

    def rstd_from_sq(sq, Bsq, bank_i, rstd, Brstd):
        for c in range(8):
            k.mm(k.bank(bank_i), ones_b, sq[:, c, :], c == 0, c == 7, [Bsq, Bconst], [PS[bank_i]])
        k.act(rstd, k.bank(bank_i), AF.Sqrt, [PS[bank_i]], [Brstd], scale=1.0 / D, bias=epsb)
        k.recip(rstd, rstd, [Brstd], [Brstd])

    epsb = A.alloc([1], F32)
    k.memset(epsb, EPS, [Bconst])

    def prenorm(xblk, Bx, dst, Bdst, dcols, Asc, shj, l, kc, W, gate=None):
        sq, Bsq, rstd, Brstd, tmp, Btmp = W
        k.act(sq, xblk, AF.Square, [Bx], [Bsq])
        rstd_from_sq(sq, Bsq, 0, rstd, Brstd)
        k.tt(tmp, xblk, bc(rstd.unsqueeze(1), [128, 8, TB]), ALU.mult, [Bx, Brstd], [Btmp])
        for c in range(8):
            k.act(dst[:, c, dcols], tmp[:, c, :], AF.Identity, [Btmp, Bmod], [Bdst],
                  scale=Asc[:, l, c, kc:kc + 1], bias=modT[:, l, shj + c, kc:kc + 1])
        if gate is not None:
            wg32, Bwg32 = gate
            for c in range(8):
                k.act(tmp[:, c, :], tmp[:, c, :], AF.Identity, [Btmp, Bmod], [Btmp],
                      scale=Asc[:, l, c, kc:kc + 1], bias=modT[:, l, shj + c, kc:kc + 1])
            for c in range(8):
                k.mm(k.bank(1)[0:16, :], wg32[:, c, :], tmp[:, c, :], c == 0, c == 7, [Bwg32, Btmp], [PS[1]])
            k.copy(gatesT[0:16, dcols], k.bank(1)[0:16, :], [PS[1]], [Bgates], eng="dve")


    def normalize_out(bkO, ncols, heads, l, sink, dst_cols_list, W):
        rrow, Brr, bcs, Bbcs = W
        per = ncols // len(heads)
        O = k.bank(bkO)
        if sink:
            for i, (h, e_off, dcols) in enumerate(heads):
                k.ts(rrow[64:65, i * per:(i + 1) * per], O[64:65, i * per:(i + 1) * per],
                     sinkE[64:65, l * 4 + h:l * 4 + h + 1], None, ALU.add, None, [PS[bkO], Bconst], [Brr])
            k.recip(rrow[64:65, 0:ncols], rrow[64:65, 0:ncols], [Brr], [Brr])
        else:
            k.recip(rrow[64:65, 0:ncols], O[64:65, 0:ncols], [PS[bkO]], [Brr])
        k.mm(k.bank(2)[0:64, 0:ncols], ones_f[64:65, 0:64], rrow[64:65, 0:ncols], True, True, [Brr, Bconst], [PS[2]])
        k.copy(bcs[0:64, 0:ncols], k.bank(2)[0:64, 0:ncols], [PS[2]], [Bbcs], eng="act")
        for i, (h, e_off, dcols) in enumerate(heads):
            ch = e_off // 128
            po = e_off % 128
            k.tt(oT[po:po + 64, ch, dcols], O[0:64, i * per:(i + 1) * per], bcs[0:64, i * per:(i + 1) * per],
                 ALU.mult, [PS[bkO], Bbcs], [BoT])

    def mixer_attn(l, g):
        T = g["T"]
        L = g["L"]
        nseq = g["nseq"]
        lat = g["name"] == "lat"
        nblk = T // TB
        TK = T + (512 if lat else 0)
        NKC = TK // 128
        wA = A.alloc([8, 544], BF16)
        BwA = Buf()
        k.dma("sp", wA, w_in_b[l].rearrange("(c p) n -> p c n", p=128)[:, :, 0:544], [Bcv[("w_in", l)]], [BwA])
        wukv = A.alloc([512], BF16)
        k.dma("pool", wukv, mla_w_ukv[l], [], [BwA])
        qT = A.alloc([4, T], BF16)
        BqT = Buf()
        kT = A.alloc([4, TK], BF16)
        BkT = Buf()
        ckvnT = A.alloc([TK], BF16)
        Bckv = Buf()
        Vaug = A.alloc([NKC, 4, 65], BF16)
        BV = Buf()
        k.memset(Vaug, 1.0, [BV])
        raw = A.alloc([TB], F32)
        Braw = Buf()
        sqc = A.alloc([TB], BF16)
        Bsqc = Buf()
        rs = A.alloc([TB], F32)
        Brs = Buf()
        kpf = A.alloc([TB], F32)
        Bkpf = Buf()
        tmpr = A.alloc([TB], F32)
        Btmpr = Buf()
        if lat:
            rC = A.alloc([2048], F32)
            rS = A.alloc([2048], F32)
            Brope = Buf()
            k.dma("sp", rC[64:96, :], ropeA[0, 64:96, :], [], [Brope])
            k.dma("sp", rS[64:96, :], ropeA[1, 64:96, :], [], [Brope])
        stg = A.alloc([4, 128], F32)
        Bstg = Buf()

        def rope_rows(src_f, Bsrc, r0, r1, tcols, dst, Bdst):
            k.mm(k.bank(2)[r0:r1, :], pswapF[r0:r1, r0:r1], src_f[r0:r1, :], True, True, [Bsrc, Bconst], [PS[2]])
            k.tt(tmpr[r0:r1, :], k.bank(2)[r0:r1, :], rS[r0:r1, tcols], ALU.mult, [PS[2], Brope], [Btmpr])
            k.tt(src_f[r0:r1, :], src_f[r0:r1, :], rC[r0:r1, tcols], ALU.mult, [Bsrc, Brope], [Bsrc])
            k.tt(dst, src_f[r0:r1, :], tmpr[r0:r1, :], ALU.add, [Bsrc, Btmpr], [Bdst])

        def ckv_norm_block(src_ps_bank, cols, tok_out=None):
            k.copy(raw, k.bank(src_ps_bank), [PS[src_ps_bank]], [Braw], eng="dve")
            k.act(sqc, raw, AF.Square, [Braw], [Bsqc])
            k.mm(k.bank(2), ones_b, sqc, True, True, [Bsqc, Bconst], [PS[2]])
            k.act(rs, k.bank(2), AF.Sqrt, [PS[2]], [Brs], scale=1.0 / 128, bias=epsb)
            k.recip(rs, rs, [Brs], [Brs])
            k.stt(raw, raw, kvg[:, l:l + 1], rs, ALU.mult, ALU.mult, [Braw, Brs, Bconst], [Braw])
            k.copy(ckvnT[:, cols], raw, [Braw], [Bckv], eng="act")
            if tok_out is not None:
                sq_, t0 = tok_out
                for j in range(4):
                    k.tr(k.bank(2)[:, j * 128:(j + 1) * 128], raw[:, j * 128:(j + 1) * 128], ident, [Braw, Bconst], [PS[2]])
                k.copy(stg, k.bank(2).rearrange("p (j r) -> p j r", j=4), [PS[2]], [Bstg], eng="dve")
                for j in range(4):
                    tk = t0 + j * 128
                    k.dma("sp", o_ckv[tk // 256, l, tk % 256:tk % 256 + 128, :], stg[:, j, :], [Bstg], [])

        for tb in range(nblk):
            cols = slice(tb * TB, (tb + 1) * TB)
            for c in range(8):
                k.mm(k.bank(0), wA[:, c, 384:512], hT[:, c, cols], c == 0, c == 7, [BwA, BhT], [PS[0]])
            ckv_norm_block(0, cols, tok_out=None if lat else (0, tb * TB))
            for c in range(8):
                k.mm(k.bank(1)[64:96, :], wA[:, c, 512:544], hT[:, c, cols], c == 0, c == 7, [BwA, BhT], [PS[1]])
            k.copy(kpf[64:96, :], k.bank(1)[64:96, :], [PS[1]], [Bkpf], eng="act")
            if lat:
                for h in range(4):
                    if h == 0:
                        rope_rows(kpf, Bkpf, 64, 96, cols, kT[64:96, 0, cols], BkT)
                    else:
                        k.copy(kT[64:96, h, cols], kT[64:96, 0, cols], [BkT], [BkT])
            else:
                for h in range(4):
                    k.copy(kT[64:96, h, cols], kpf[64:96, :], [Bkpf], [BkT])
                for c in range(8):
                    k.mm(k.bank(1)[0:32, :], wA[:, c, 512:544], hT[:, c, cols], c == 0, c == 7, [BwA, BhT], [PS[1]])
                k.copy(kpf[0:32, :], k.bank(1)[0:32, :], [PS[1]], [Bkpf], eng="dve")
                for j in range(4):
                    k.tr(k.bank(1)[:, j * 32:(j + 1) * 32], kpf[0:32, j * 128:(j + 1) * 128], ident[0:32, 0:32],
                         [Bkpf, Bconst], [PS[1]])
                k.copy(stg[:, 0, :], k.bank(1)[:, 0:128], [PS[1]], [Bstg], eng="dve")
                for j in range(4):
                    tk = tb * TB + j * 128
                    k.dma("sp", o_kpe[tk // 256, l, tk % 256:tk % 256 + 128, :], stg[:, 0, j * 32:(j + 1) * 32], [Bstg], [])
            for h in range(4):
                bk = h % 2
                for c in range(8):
                    k.mm(k.bank(bk)[0:96, :], wA[:, c, h * 96:(h + 1) * 96], hT[:, c, cols], c == 0, c == 7,
                         [BwA, BhT], [PS[bk]])
                if lat:
                    k.copy(raw[0:96, :], k.bank(bk)[0:96, :], [PS[bk]], [Braw], eng="act")
                    k.copy(qT[0:64, h, cols], raw[0:64, :], [Braw], [BqT], eng="dve")
                    rope_rows(raw, Braw, 64, 96, cols, qT[64:96, h, cols], BqT)
                else:
                    k.copy(qT[0:96, h, cols], k.bank(bk)[0:96, :], [PS[bk]], [BqT])
        if lat:
            for j in range(4):
                k.dma("sp", stg[:, j, :], c_ckv[l, j * 128:(j + 1) * 128, :], [], [Bstg])
            for j in range(4):
                k.tr(k.bank(0)[:, j * 128:(j + 1) * 128], stg[:, j, :], ident, [Bstg, Bconst], [PS[0]])
            k.copy(ckvnT[:, 2048:2560], k.bank(0), [PS[0]], [Bckv], eng="dve")
            for j in range(4):
                k.dma("sp", stg[:, j, 0:32], c_kpe[l, j * 128:(j + 1) * 128, :], [Bstg], [Bstg])
            for j in range(4):
                k.tr(k.bank(1)[0:32, j * 128:(j + 1) * 128], stg[:, j, 0:32], ident, [Bstg, Bconst], [PS[1]])
            for h in range(4):
                k.copy(kT[64:96, h, 2048:2560], k.bank(1)[0:32, :], [PS[1]], [BkT])
        for kb in range(TK // TB):
            cols = slice(kb * TB, (kb + 1) * TB)
            for h in range(4):
                bk = h % 2
                k.mm(k.bank(bk)[0:64, :], wukv[:, h * 128:h * 128 + 64], ckvnT[:, cols], True, True, [BwA, Bckv], [PS[bk]])
                k.copy(kT[0:64, h, cols], k.bank(bk)[0:64, :], [PS[bk]], [BkT])
        for kc in range(NKC):
            bk = kc % 2
            k.mm(k.bank(bk), ckvnT[:, kc * 128:(kc + 1) * 128], wukv, True, True, [BwA, Bckv], [PS[bk]])
            k.copy(Vaug[:, kc, :, 0:64], k.bank(bk).rearrange("p (h t e) -> p h t e", h=4, t=2)[:, :, 1, :], [PS[bk]], [BV])
        PT = [A.alloc([TB], BF16) for _ in range(3)]
        BPT = [Buf() for _ in range(3)]
        rrow = A.alloc([TB], F32)
        bcs = A.alloc([TB], F32)
        Wn = (rrow, Buf(), bcs, Buf())
        scale = 96 ** -0.5
        it = 0
        nq = 0
        QB = min(TB, L)
        for sq_ in range(nseq):
            for h in range(4):
                for qb in range(L // QB):
                    qcols = slice(sq_ * L + qb * QB, sq_ * L + (qb + 1) * QB)
                    bkO = 6 + (nq % 2)
                    nq += 1
                    kcs = list(range(sq_ * L // 128, (sq_ + 1) * L // 128)) if not lat else list(range(NKC))
                    for i, kc in enumerate(kcs):
                        bS = 3 + (it % 3)
                        pt = PT[it % 3]
                        bpt = BPT[it % 3]
                        it += 1
                        k.mm(k.bank(bS)[:, 0:QB], kT[0:96, h, kc * 128:(kc + 1) * 128], qT[0:96, h, qcols], True, True,
                             [BkT, BqT], [PS[bS]])
                        k.act(pt[:, 0:QB], k.bank(bS)[:, 0:QB], AF.Exp, [PS[bS]], [bpt], scale=scale)
                        k.mm(k.bank(bkO)[0:65, 0:QB], Vaug[:, kc, h, :], pt[:, 0:QB], i == 0, i == len(kcs) - 1,
                             [BV, bpt], [PS[bkO]])
                    normalize_out(bkO, QB, [(h, h * 64, qcols)], l, False, None, Wn)
        P.barrier()
        A.top = mB_inner = A.top
        A.top = mixer_base[0]
        wB = A.alloc([8, 512], BF16)
        BwB = Buf()
        k.dma("sp", wB, w_in_b[l].rearrange("(c p) n -> p c n", p=128)[:, :, 544:1056], [Bcv[("w_in", l)]], [BwB])
        qS = A.alloc([4, T], BF16)
        BqS = Buf()
        kS = A.alloc([2, TK], BF16)
        BkS = Buf()
        Vb = A.alloc([NKC, 2, 65], BF16)
        BVb = Buf()
        k.memset(Vb, 1.0, [BVb])
        raw = A.alloc([TB], F32)
        Braw = Buf()
        tmpr = A.alloc([TB], F32)
        Btmpr = Buf()
        stg = A.alloc([4, 128], F32)
        Bstg = Buf()
        stg2 = A.alloc([256], F32)
        Bstg2 = Buf()
        if lat:
            rC = A.alloc([2048], F32)
            rS = A.alloc([2048], F32)
            Brope = Buf()
            k.dma("sp", rC[0:64, :], ropeB[0], [], [Brope])
            k.dma("sp", rS[0:64, :], ropeB[1], [], [Brope])
        for tb in range(nblk):
            cols = slice(tb * TB, (tb + 1) * TB)
            for h in range(6):
                bk = h % 2
                c0 = h * 64 if h < 4 else 256 + (h - 4) * 64
                for c in range(8):
                    k.mm(k.bank(bk)[0:64, :], wB[:, c, c0:c0 + 64], hT[:, c, cols], c == 0, c == 7, [BwB, BhT], [PS[bk]])
                dst, Bdst = (qS[0:64, h, cols], BqS) if h < 4 else (kS[0:64, h - 4, cols], BkS)
                if lat:
                    k.copy(raw[0:64, :], k.bank(bk)[0:64, :], [PS[bk]], [Braw], eng="act")
                    rope_rows(raw, Braw, 0, 64, cols, dst, Bdst)
                else:
                    k.copy(dst, k.bank(bk)[0:64, :], [PS[bk]], [Bdst])
            for j in range(4):
                tk = tb * TB + j * 128
                kc = tk // 128
                bk = j % 2
                for c in range(8):
                    k.mm(k.bank(bk)[:, 0:256], hT[:, c, tk:tk + 128], wB[:, c, 256:512], c == 0, c == 7, [BwB, BhT], [PS[bk]])
                k.copy(Vb[:, kc, :, 0:64], k.bank(bk)[:, 128:256].rearrange("p (v e) -> p v e", v=2), [PS[bk]], [BVb], eng="act")
                if not lat:
                    k.copy(stg2, k.bank(bk)[:, 0:256], [PS[bk]], [Bstg2], eng="dve")
                    k.dma("sp", o_swk[tk // 256, l, tk % 256:tk % 256 + 128, :], stg2[:, 0:128], [Bstg2], [])
                    k.dma("sp", o_swv[tk // 256, l, tk % 256:tk % 256 + 128, :], stg2[:, 128:256], [Bstg2], [])
        if lat:
            for j in range(4):
                k.dma("sp", stg[:, j, :], c_swk[l, j * 128:(j + 1) * 128, :], [], [Bstg])
            for kv in range(2):
                for j in range(4):
                    k.tr(k.bank(kv)[0:64, j * 128:(j + 1) * 128], stg[:, j, kv * 64:(kv + 1) * 64], ident, [Bstg, Bconst], [PS[kv]])
                k.copy(kS[0:64, kv, 2048:2560], k.bank(kv)[0:64, :], [PS[kv]], [BkS])
            for j in range(4):
                k.dma("pool", Vb[:, 16 + j, :, 0:64], c_swv[l, j * 128:(j + 1) * 128, :].rearrange("p (v e) -> p v e", v=2), [], [BVb])
        PT = [A.alloc([256], BF16) for _ in range(3)]
        BPT = [Buf() for _ in range(3)]
        rrow = A.alloc([TB], F32)
        bcs = A.alloc([TB], F32)
        Wn = (rrow, Buf(), bcs, Buf())
        scale = 64 ** -0.5
        it = 0
        nq = 0
        if not lat:
            for sq_ in range(nseq):
                for h in range(4):
                    qcols = slice(sq_ * L, (sq_ + 1) * L)
                    bkO = 6 + (nq % 2)
                    nq += 1
                    kcs = list(range(sq_ * L // 128, (sq_ + 1) * L // 128))
                    for i, kc in enumerate(kcs):
                        bS = 3 + (it % 3)
                        pt = PT[it % 3]
                        bpt = BPT[it % 3]
                        it += 1
                        k.mm(k.bank(bS)[:, 0:L], kS[0:64, h // 2, kc * 128:(kc + 1) * 128], qS[0:64, h, qcols], True, True,
                             [BkS, BqS], [PS[bS]])
                        k.act(pt[:, 0:L], k.bank(bS)[:, 0:L], AF.Exp, [PS[bS]], [bpt], scale=scale)
                        k.mm(k.bank(bkO)[0:65, 0:L], Vb[:, kc, h // 2, :], pt[:, 0:L], i == 0, i == len(kcs) - 1,
                             [BVb, bpt], [PS[bkO]])
                    normalize_out(bkO, L, [(h, 256 + h * 64, qcols)], l, True, None, Wn)
        else:
            NB = L // 128
            for n in range(NB):
                qcols = slice(n * 128, (n + 1) * 128)
                for kv in range(2):
                    bkO = 6 + (nq % 2)
                    nq += 1
                    kcs = []
                    if n > 0:
                        kcs.append((n - 1, 0))
                    kcs.append((n, None))
                    if n < NB - 1:
                        kcs.append((n + 1, 1))
                    kcs += [(16 + j, None) for j in range(4)]
                    for i, (kc, mk_) in enumerate(kcs):
                        bS = 3 + (it % 3)
                        pt = PT[it % 3]
                        bpt = BPT[it % 3]
                        it += 1
                        k.mm(k.bank(bS)[:, 0:256], kS[0:64, kv, kc * 128:(kc + 1) * 128],
                             qS[0:64, 2 * kv:2 * kv + 2, qcols], True, True, [BkS, BqS], [PS[bS]])
                        k.act(pt, k.bank(bS)[:, 0:256], AF.Exp, [PS[bS]], [bpt], scale=scale)
                        if mk_ is not None:
                            k.tt(pt.rearrange("p (h q) -> p h q", h=2), pt.rearrange("p (h q) -> p h q", h=2),
                                 bc(masks[:, mk_, :].unsqueeze(1), [128, 2, 128]), ALU.mult, [bpt, Bconst], [bpt])
                        k.mm(k.bank(bkO)[0:65, 0:256], Vb[:, kc, kv, :], pt, i == 0, i == len(kcs) - 1, [BVb, bpt], [PS[bkO]])
                    normalize_out(bkO, 256, [(2 * kv, 256 + 2 * kv * 64, qcols), (2 * kv + 1, 256 + (2 * kv + 1) * 64, qcols)],
                                  l, True, None, Wn)


    PI = math.pi

    def wrap_sin(x, Bx, t, Bt, rows, width):
        xs = x[0:rows, 0:width]
        ts_ = t[0:rows, 0:width]
        for rep in range(2):
            k.ts(ts_, xs, PI, -2 * PI, ALU.is_gt, ALU.mult, [Bx], [Bt])
            k.tt(xs, xs, ts_, ALU.add, [Bx, Bt], [Bx])
            k.ts(ts_, xs, -PI, 2 * PI, ALU.is_lt, ALU.mult, [Bx], [Bt])
            k.tt(xs, xs, ts_, ALU.add, [Bx, Bt], [Bx])
        k.act(xs, xs, AF.Sin, [Bx], [Bx])

    def mixer_hyena(l, g):
        T = g["T"]
        L = g["L"]
        nseq = g["nseq"]
        NSC = L // 128
        NF = NSC + 1
        NW = nseq * 256
        hc = HY[L]
        tabC = hc["tab"][0]
        tabS = hc["tab"][1]
        frows = lambda fc: 128 if fc < NSC else 1
        base = A.top
        stgp = A.alloc([128], F32)
        colsT = A.alloc([128], F32)
        Bsp = Buf()
        Bcols = Buf()
        k.memset(stgp, 0.0, [Bsp])
        k.dma("sp", stgp[0:18, :], hy_conv[l].rearrange("k (c p) -> (k c) p", p=128), [], [Bsp])
        k.dma("sp", stgp[18:22, :], hy_bias[l].rearrange("o (c p) -> (o c) p", p=128), [], [Bsp])
        k.dma("sp", stgp[32:33, 0:64], hy_b1[l:l + 1, :], [], [Bsp])
        k.dma("sp", stgp[33:34, 0:64], hy_freq[l][0:1, :], [], [Bsp])
        k.dma("sp", stgp[34:35, 0:64], hy_b2[l:l + 1, :], [], [Bsp])
        k.dma("sp", stgp[35:36, 0:64], hy_freq[l][1:2, :], [], [Bsp])
        k.tr(k.bank(0)[:, 0:128], stgp, ident, [Bsp, Bconst], [PS[0]])
        k.copy(colsT, k.bank(0)[:, 0:128], [PS[0]], [Bcols], eng="dve")
        convw = colsT[:, 0:18].rearrange("p (k c) -> p k c", k=3)
        biasc = colsT[:, 18:22].rearrange("p (o c) -> p o c", o=2)
        fb = A.alloc([2], F32)
        k.tt(fb[0:64, 0:1], colsT[0:64, 32:33], colsT[0:64, 33:34], ALU.mult, [Bcols], [Bcols])
        k.tt(fb[0:64, 1:2], colsT[0:64, 34:35], colsT[0:64, 35:36], ALU.mult, [Bcols], [Bcols])
        aw = A.alloc([NF], F32)
        bw = A.alloc([NF], F32)
        negdist = A.alloc([NSC], F32)
        k.dma("sp", aw, hc["aw"], [], [Bcols])
        k.dma("sp", bw, hc["bw"], [], [Bcols])
        k.dma("sp", negdist, hc["negdist"], [], [Bcols])
        Gre = A.alloc([NF, 512], BF16)
        Gim = A.alloc([NF, 512], BF16)
        BG = Buf()
        tC = [A.alloc([NSC, 128], BF16) for _ in range(2)]
        tS = [A.alloc([NSC, 128], BF16) for _ in range(2)]
        Btab = [Buf(), Buf()]
        persist = A.top
        tmpA = A.alloc([512], F32)
        tmpB = A.alloc([512], F32)
        BtA = Buf()
        BtB = Buf()

        def load_colblock(fc, i):
            rows = frows(fc)
            k.dma("sp", tC[i][:, :, 0:rows], tabC[0:L, fc * 128:fc * 128 + rows].rearrange("(sc p) f -> p sc f", p=128),
                  [], [Btab[i]], slow=True)
            k.dma("sp", tS[i][:, :, 0:rows], tabS[0:L, fc * 128:fc * 128 + rows].rearrange("(sc p) f -> p sc f", p=128),
                  [], [Btab[i]], slow=True)

        featsT = A.alloc([L], F32)
        w1f = A.alloc([64], F32)
        w2f = A.alloc([64], F32)
        w3f = A.alloc([512], F32)
        Bfw = Buf()
        k.dma("sp", featsT[0:17, :], hc["feats"], [], [Bfw])
        k.dma("sp", w1f[0:17, :], hy_w1[l], [], [Bfw])
        k.dma("sp", w2f[0:64, :], hy_w2[l], [], [Bfw])
        k.dma("sp", w3f[0:64, :], hy_w3[l], [], [Bfw])
        h1T = A.alloc([L], F32)
        h2T = A.alloc([L], F32)
        wt = A.alloc([L], F32)
        Bh1 = Buf()
        Bh2 = Buf()
        Bwt = Buf()
        CW = min(512, L)
        for blk in range(L // CW):
            cs = slice(blk * CW, (blk + 1) * CW)
            k.mm(k.bank(blk % 2)[0:64, 0:CW], w1f[0:17, 0:64], featsT[0:17, cs], True, True, [Bfw], [PS[blk % 2]])
            k.ts(h1T[0:64, cs], k.bank(blk % 2)[0:64, 0:CW], colsT[0:64, 33:34], fb[0:64, 0:1], ALU.mult, ALU.add,
                 [PS[blk % 2], Bcols], [Bh1])
        wrap_sin(h1T, Bh1, wt, Bwt, 64, L)
        for blk in range(L // CW):
            cs = slice(blk * CW, (blk + 1) * CW)
            k.mm(k.bank(blk % 2)[0:64, 0:CW], w2f[0:64, 0:64], h1T[0:64, cs], True, True, [Bfw, Bh1], [PS[blk % 2]])
            k.ts(h2T[0:64, cs], k.bank(blk % 2)[0:64, 0:CW], colsT[0:64, 35:36], fb[0:64, 1:2], ALU.mult, ALU.add,
                 [PS[blk % 2], Bcols], [Bh2])
        wrap_sin(h2T, Bh2, wt, Bwt, 64, L)
        absdec = A.alloc([512], F32)
        Bad = Buf()
        k.dma("sp", absdec, hy_decay[l].partition_broadcast(128), [], [Bad], slow=True)
        k.act(absdec, absdec, AF.Abs, [Bad], [Bad])
        filtok = A.alloc([NSC, 512], BF16)
        Bft = Buf()
        Et = A.alloc([512], F32)
        BEt = Buf()
        for jc in range(NSC):
            bk = jc % 2
            k.mm(k.bank(bk), h2T[0:64, jc * 128:(jc + 1) * 128], w3f[0:64, :], True, True, [Bh2, Bfw], [PS[bk]])
            k.act(Et, absdec, AF.Exp, [Bad, Bcols], [BEt], scale=negdist[:, jc:jc + 1])
            k.tt(filtok[:, jc, :], k.bank(bk), Et, ALU.mult, [PS[bk], BEt], [Bft])
        for fc in range(NF):
            rows = frows(fc)
            i = fc % 2
            load_colblock(fc, i)
            ba, bb = 2 * i, 2 * i + 1
            for sc in range(NSC):
                k.mm(k.bank(ba)[0:rows, :], tC[i][:, sc, 0:rows], filtok[:, sc, :], sc == 0, sc == NSC - 1, [Btab[i], Bft], [PS[ba]])
            for sc in range(NSC):
                k.mm(k.bank(bb)[0:rows, :], tS[i][:, sc, 0:rows], filtok[:, sc, :], sc == 0, sc == NSC - 1, [Btab[i], Bft], [PS[bb]])
            k.ts(tmpA[0:rows, :], k.bank(ba)[0:rows, :], aw[0:rows, fc:fc + 1], None, ALU.mult, None, [PS[ba], Bcols], [BtA])
            k.stt(Gre[0:rows, fc, :], k.bank(bb)[0:rows, :], bw[0:rows, fc:fc + 1], tmpA[0:rows, :], ALU.mult, ALU.add,
                  [PS[bb], BtA, Bcols], [BG])
            k.ts(tmpB[0:rows, :], k.bank(bb)[0:rows, :], aw[0:rows, fc:fc + 1], None, ALU.mult, None, [PS[bb], Bcols], [BtB])
            k.stt(Gim[0:rows, fc, :], k.bank(ba)[0:rows, :], bw[0:rows, fc:fc + 1], tmpB[0:rows, :], ALU.mult, ALU.subtract,
                  [PS[ba], BtB, Bcols], [BG])
        P.barrier()
        A.top = persist
        x1T = A.alloc([2, T], BF16)
        x2T = A.alloc([2, T], BF16)
        vT = A.alloc([2, T], F32)
        Bx12 = Buf()
        BvT = Buf()
        dtok = A.alloc([NSC, NW], BF16)
        Bdt = Buf()
        conv_mark = A.top
        wH = A.alloc([8, 768], BF16)
        BwH = Buf()
        k.dma("sp", wH, w_in_b[l].rearrange("(c p) n -> p c n", p=128)[:, :, C_HU:C_HU + 768], [Bcv[("w_in", l)]], [BwH])
        hu = A.alloc([T], F32)
        uu = A.alloc([T], F32)
        Bhu = Buf()
        Buu = Buf()
        for c6 in range(6):
            for tb in range(T // TB):
                cols = slice(tb * TB, (tb + 1) * TB)
                bk = tb % 2
                for c in range(8):
                    k.mm(k.bank(bk), wH[:, c, c6 * 128:(c6 + 1) * 128], hT[:, c, cols], c == 0, c == 7, [BwH, BhT], [PS[bk]])
                k.copy(hu[:, cols], k.bank(bk), [PS[bk]], [Bhu])
            k.act(uu, hu, AF.Identity, [Bhu, Bcols], [Buu], scale=convw[:, 1, c6:c6 + 1])
            for sq_ in range(nseq):
                s0, s1 = sq_ * L, (sq_ + 1) * L
                k.stt(uu[:, s0 + 1:s1], hu[:, s0:s1 - 1], convw[:, 0, c6:c6 + 1], uu[:, s0 + 1:s1], ALU.mult, ALU.add,
                      [Bhu, Bcols, Buu], [Buu])
                k.stt(uu[:, s0:s1 - 1], hu[:, s0 + 1:s1], convw[:, 2, c6:c6 + 1], uu[:, s0:s1 - 1], ALU.mult, ALU.add,
                      [Bhu, Bcols, Buu], [Buu])
            if c6 < 2:
                k.copy(vT[:, c6, :], uu, [Buu], [BvT], eng="act")
            elif c6 < 4:
                k.copy(x1T[:, c6 - 2, :], uu, [Buu], [Bx12], eng="act")
            else:
                k.copy(x2T[:, c6 - 4, :], uu, [Buu], [Bx12], eng="act")

        def to_tokmajor():
            for sq_ in range(nseq):
                for sc in range(NSC):
                    bk = sc % 2
                    for cc in range(2):
                        t0 = sq_ * L + sc * 128
                        k.tr(k.bank(bk)[:, cc * 128:(cc + 1) * 128], vT[:, cc, t0:t0 + 128], ident, [BvT, Bconst], [PS[bk]])
                    k.copy(dtok[:, sc, sq_ * 256:(sq_ + 1) * 256], k.bank(bk)[:, 0:256], [PS[bk]], [Bdt])

        P.barrier()
        A.top = conv_mark
        Pq = A.alloc([NF, NW], BF16)
        Qq = A.alloc([NF, NW], BF16)
        BPQ = Buf()
        rC = [A.alloc([L], BF16) for _ in range(2)]
        rS = [A.alloc([L], BF16) for _ in range(2)]
        Brt = [Buf(), Buf()]
        t1 = A.alloc([NW], F32)
        t2 = A.alloc([NW], F32)
        Bt1 = Buf()
        Bt2 = Buf()
        TW = min(512, L)

        def long_conv(o, consumer):
            ocs = slice(o * 256, (o + 1) * 256)
            for fc in range(NF):
                rows = frows(fc)
                i = fc % 2
                load_colblock(fc, i)
                ba, bb = 2 * i, 2 * i + 1
                for sc in range(NSC):
                    k.mm(k.bank(ba)[0:rows, 0:NW], tC[i][:, sc, 0:rows], dtok[:, sc, :], sc == 0, sc == NSC - 1, [Btab[i], Bdt], [PS[ba]])
                for sc in range(NSC):
                    k.mm(k.bank(bb)[0:rows, 0:NW], tS[i][:, sc, 0:rows], dtok[:, sc, :], sc == 0, sc == NSC - 1, [Btab[i], Bdt], [PS[bb]])
                Av = k.bank(ba)[0:rows, 0:NW].rearrange("p (s c) -> p s c", s=nseq)
                Bv = k.bank(bb)[0:rows, 0:NW].rearrange("p (s c) -> p s c", s=nseq)
                gre = bc(Gre[0:rows, fc, ocs].unsqueeze(1), [rows, nseq, 256])
                gim = bc(Gim[0:rows, fc, ocs].unsqueeze(1), [rows, nseq, 256])
                v3 = lambda ap: ap[0:rows, :].rearrange("p (s c) -> p s c", s=nseq)
                k.tt(v3(t1), Av, gre, ALU.mult, [PS[ba], BG], [Bt1])
                k.tt(v3(t2), Bv, gim, ALU.mult, [PS[bb], BG], [Bt2])
                k.tt(Pq[0:rows, fc, :], t1[0:rows, :], t2[0:rows, :], ALU.add, [Bt1, Bt2], [BPQ])
                k.tt(v3(t1), Bv, gre, ALU.mult, [PS[bb], BG], [Bt1])
                k.tt(v3(t2), Av, gim, ALU.mult, [PS[ba], BG], [Bt2])
                k.tt(Qq[0:rows, fc, :], t1[0:rows, :], t2[0:rows, :], ALU.subtract, [Bt1, Bt2], [BPQ])
            accs = [(sq_, cc, tb) for sq_ in range(nseq) for cc in range(2) for tb in range(L // TW)]
            for fc in range(NF):
                rows = frows(fc)
                i = fc % 2
                k.dma("sp", rC[i][0:rows, :], tabC[fc * 128:fc * 128 + rows, 0:L], [], [Brt[i]])
                k.dma("sp", rS[i][0:rows, :], tabS[fc * 128:fc * 128 + rows, 0:L], [], [Brt[i]])
                for ai, (sq_, cc, tb) in enumerate(accs):
                    lc = slice(sq_ * 256 + cc * 128, sq_ * 256 + (cc + 1) * 128)
                    k.mm(k.bank(ai)[:, 0:TW], Pq[0:rows, fc, lc], rC[i][0:rows, tb * TW:(tb + 1) * TW], fc == 0, False,
                         [BPQ, Brt[i]], [PS[ai]])
                    k.mm(k.bank(ai)[:, 0:TW], Qq[0:rows, fc, lc], rS[i][0:rows, tb * TW:(tb + 1) * TW], False, fc == NF - 1,
                         [BPQ, Brt[i]], [PS[ai]])
            for ai, (sq_, cc, tb) in enumerate(accs):
                consumer(ai, cc, slice(sq_ * L + tb * TW, sq_ * L + (tb + 1) * TW))

        tcs = A.alloc([TW], F32)
        Btcs = Buf()

        def cons0(ai, cc, tcols):
            k.stt(tcs, vT[:, cc, tcols], biasc[:, 0, cc:cc + 1], k.bank(ai)[:, 0:TW], ALU.mult, ALU.add, [BvT, Bcols, PS[ai]], [Btcs])
            k.tt(vT[:, cc, tcols], tcs, x1T[:, cc, tcols], ALU.mult, [Btcs, Bx12], [BvT])

        def cons1(ai, cc, tcols):
            k.stt(tcs, vT[:, cc, tcols], biasc[:, 1, cc:cc + 1], k.bank(ai)[:, 0:TW], ALU.mult, ALU.add, [BvT, Bcols, PS[ai]], [Btcs])
            k.tt(oT[:, 6 + cc, tcols], tcs, x2T[:, cc, tcols], ALU.mult, [Btcs, Bx12], [BoT])

        to_tokmajor()
        long_conv(0, cons0)
        to_tokmajor()
        long_conv(1, cons1)
        P.barrier()
        A.top = base


    def mixer_gdn(l, g):
        GD = F32 if 'gdn32' in (dbg or ()) else BF16
        T = g["T"]
        L = g["L"]
        nseq = g["nseq"]
        lat = g["name"] == "lat"
        NCK = L // 64
        NI = nseq * 4
        NBLK = T // TB
        base = A.top
        stgp = A.alloc([128], F32)
        colsT = A.alloc([128], F32)
        Bsp = Buf()
        Bcols = Buf()
        k.memset(stgp, 0.0, [Bsp])
        k.dma("sp", stgp[0:18, :], gdn_conv[l].rearrange("k (c p) -> (k c) p", p=128), [], [Bsp])
        k.dma("sp", stgp[18:19, 0:64], gdn_norm[l:l + 1, :], [], [Bsp])
        k.dma("sp", stgp[18:19, 64:128], gdn_norm[l:l + 1, :], [], [Bsp])
        k.tr(k.bank(0)[:, 0:128], stgp, ident, [Bsp, Bconst], [PS[0]])
        k.copy(colsT, k.bank(0)[:, 0:128], [PS[0]], [Bcols], eng="dve")
        convw = colsT[:, 0:18].rearrange("p (k c) -> p k c", k=3)
        gnorm = colsT[:, 18:19]
        all16 = A.alloc([16], F32)
        k.dma("sp", all16[:, 0:8], gdn_a_log[l].partition_broadcast(128), [], [Bcols], slow=True)
        k.dma("sp", all16[:, 8:16], gdn_dt_bias[l].partition_broadcast(128), [], [Bcols], slow=True)
        k.act(all16[:, 0:8], all16[:, 0:8], AF.Exp, [Bcols], [Bcols])
        k.ts(all16[:, 0:8], all16[:, 0:8], -1.0, None, ALU.mult, None, [Bcols], [Bcols])
        neaS = A.alloc([2, 2], F32)
        dtbS = A.alloc([2, 2], F32)
        for (dst, c0) in ((neaS, 0), (dtbS, 8)):
            v8 = all16[:, c0:c0 + 8].rearrange("p (d hp two) -> p d hp two", d=2, two=2)
            k.copy(dst[0:64], v8[0:64, :, :, 0], [Bcols], [Bcols], eng="dve")
            k.copy(dst[64:128], v8[64:128, :, :, 1], [Bcols], [Bcols], eng="dve")
        blockones = A.alloc([128], BF16)
        k.memset(blockones, 0.0, [Bcols])
        k.memset(blockones[0:64, 0:64], 1.0, [Bcols])
        k.memset(blockones[64:128, 64:128], 1.0, [Bcols])
        negones = A.alloc([64], F32)
        k.memset(negones, -1.0, [Bcols])
        negm = A.alloc([2, 64], F32)
        smk = A.alloc([2, 64], F32)
        k.dma("sp", negm[0:64], negm_d.rearrange("m c s -> c m s"), [], [Bcols])
        k.dma("sp", smk[0:64], sm_d.rearrange("m c s -> c m s"), [], [Bcols])
        mbs = A.alloc([2, 64], F32)
        mbt = A.alloc([2, 64], F32)
        k.ts(mbs[0:64], smk[0:64], -30000.0, 30000.0, ALU.mult, ALU.add, [Bcols], [Bcols])
        k.ts(mbt[0:64], negm[0:64], 30000.0, -30000.0, ALU.mult, ALU.add, [Bcols], [Bcols])
        identb = A.alloc([64], GD)
        k.copy(identb[0:64, :], ident[0:64, 0:64], [Bconst], [Bcols], eng="dve")
        ident2 = A.alloc([64], F32)
        k.ts(ident2[0:64, :], ident[0:64, 0:64], 2.0, None, ALU.mult, None, [Bconst], [Bcols])
        startm = A.alloc([T], BF16)
        k.memset(startm, 1.0, [Bcols])
        k.memset(startm.rearrange("p (n c) -> p n c", c=64)[:, :, 0:1], 0.0, [Bcols])
        qn = A.alloc([2, T], GD)
        kn = A.alloc([2, T], GD)
        vs = A.alloc([2, T], BF16)
        ocT = A.alloc([2, T], F32)
        Bqkv = Buf()
        Boc = Buf()
        pm = A.top
        wG = A.alloc([8, 1024], BF16)
        BwG = Buf()
        k.dma("sp", wG, w_in_b[l].rearrange("(c p) n -> p c n", p=128)[:, :, C_GQKV:C_GQKV + 1024], [Bcv[("w_in", l)]], [BwG])
        hu = A.alloc([T], F32)
        uu = A.alloc([T], F32)
        Bhu = Buf()
        Buu = Buf()
        sqb = A.alloc([TB], BF16)
        Bsqb = Buf()
        rsn = A.alloc([TB], F32)
        Brsn = Buf()
        for c6 in range(6):
            hp = c6 % 2
            for tb in range(NBLK):
                cols = slice(tb * TB, (tb + 1) * TB)
                bk = tb % 2
                for c in range(8):
                    k.mm(k.bank(bk), wG[:, c, c6 * 128:(c6 + 1) * 128], hT[:, c, cols], c == 0, c == 7, [BwG, BhT], [PS[bk]])
                k.copy(hu[:, cols], k.bank(bk), [PS[bk]], [Bhu])
            k.act(uu, hu, AF.Identity, [Bhu, Bcols], [Buu], scale=convw[:, 1, c6:c6 + 1])
            for sq_ in range(nseq):
                s0, s1 = sq_ * L, (sq_ + 1) * L
                k.stt(uu[:, s0 + 1:s1], hu[:, s0:s1 - 1], convw[:, 0, c6:c6 + 1], uu[:, s0 + 1:s1], ALU.mult, ALU.add,
                      [Bhu, Bcols, Buu], [Buu])
                k.stt(uu[:, s0:s1 - 1], hu[:, s0 + 1:s1], convw[:, 2, c6:c6 + 1], uu[:, s0:s1 - 1], ALU.mult, ALU.add,
                      [Bhu, Bcols, Buu], [Buu])
            k.act(uu, uu, AF.Silu, [Buu], [Buu])
            if c6 < 4:
                dst = qn if c6 < 2 else kn
                sc_ = 64 ** -0.5 if c6 < 2 else 1.0
                for tb in range(NBLK):
                    cols = slice(tb * TB, (tb + 1) * TB)
                    k.act(sqb, uu[:, cols], AF.Square, [Buu], [Bsqb])
                    k.mm(k.bank(2), blockones, sqb, True, True, [Bsqb, Bcols], [PS[2]])
                    k.act(rsn, k.bank(2), AF.Sqrt, [PS[2], Bconst], [Brsn], bias=epsb)
                    k.recip(rsn, rsn, [Brsn], [Brsn])
                    k.stt(dst[:, hp, cols], uu[:, cols], sc_, rsn, ALU.mult, ALU.mult, [Buu, Brsn], [Bqkv])
            else:
                k.copy(vs[:, hp, :], uu, [Buu], [Bqkv], eng="act")
        P.barrier()
        A.top = pm
        gsel = A.alloc([8, 128], F32)
        Bwrep = Buf()
        k.dma("sp", gsel[0:16], sel_d, [], [Bwrep])
        Dd = A.alloc([2, T], F32)
        bet = A.alloc([2, T], BF16)
        BD = Buf()
        Bbet = Buf()
        alias_mark = A.top
        tmpf = A.alloc([T], F32)
        Btmpf = Buf()
        A.top = alias_mark
        Eb = A.alloc([2, TB], F32)
        kbT = A.alloc([2, TB], GD)
        kb32 = A.alloc([2, TB], F32)
        kbeH = A.alloc([4, TB], GD)
        qdH = A.alloc([4, TB], GD)
        D0 = A.alloc([4, TB], F32)
        ktl = A.alloc([2, TB], F32)
        vbe = A.alloc([2, TB], F32)
        Bblk = Buf()
        tails = A.alloc([NI, 8], F32)
        def stepbufs(first=[True]):
            d_ = {}
            for nm, dt in (("G", F32), ("GT", F32), ("Gs", F32), ("GsT", F32), ("PA", GD), ("PB", GD), ("TA", GD),
                           ("TB", GD), ("attnT", GD), ("vbt", F32), ("ktail", GD), ("rp", GD), ("u", GD), ("Dcol", F32),
                           ("A32", F32), ("TB32", F32), ("TBt", F32), ("E32", F32), ("TBf", GD)):
                if not first[0] and nm in ("A32", "TB32", "TBt", "E32", "TBf", "G", "Gs"):
                    continue
                d_[nm] = A.alloc([NI, 64], dt)
                d_["B" + nm] = Buf()
            first[0] = False
            return d_
        SB0 = stepbufs()
        SB = [SB0, SB0]
        S32 = A.alloc([NI, 64], F32)
        Sbf = A.alloc([NI, 64], GD)
        BS32 = Buf()
        BSbf = Buf()

        def inst_list(d, step):
            res = []
            for sq_ in range(nseq):
                j = step if d == 0 else NCK - 1 - step
                for h in range(4):
                    t0 = sq_ * L + j * 64
                    res.append((sq_ * 4 + h, sq_, h, h // 2, (h % 2) * 64, slice(t0, t0 + 64), slice(t0 % TB, t0 % TB + 64), t0 // TB, (t0 % TB) // 64))
            res.sort(key=lambda r_: (r_[4], r_[0]))
            return res

        for d in range(2):
            NEGM = negm[0:64, d, :]
            NEGMT = negm[0:64, 1 - d, :]
            SM = smk[0:64, d, :]
            SMT = smk[0:64, 1 - d, :]
            MBS = mbs[0:64, d, :]
            MBT = mbt[0:64, 1 - d, :]
            P.barrier()
            for which in range(2):
                for hp in range(2):
                    for tb in range(NBLK):
                        cols = slice(tb * TB, (tb + 1) * TB)
                        bk = tb % 2
                        k.mm(k.bank(bk), gsel[0:16, which * 4 + d * 2 + hp, :], gatesT[0:16, cols], True, True, [Bwrep, Bgates], [PS[bk]])
                        if which == 0:
                            k.act(Dd[:, hp, cols], k.bank(bk), AF.Exp, [PS[bk], Bcols], [BD], bias=dtbS[:, d, hp:hp + 1])
                        else:
                            k.act(bet[:, hp, cols], k.bank(bk), AF.Sigmoid, [PS[bk]], [Bbet])
            for hp in range(2):
                k.act(Dd[:, hp, :], Dd[:, hp, :], AF.Ln, [BD], [BD], bias=ones_f[:, 0:1])
                k.ts(Dd[:, hp, :], Dd[:, hp, :], neaS[:, d, hp:hp + 1], None, ALU.mult, None, [BD, Bcols], [BD])
                P.op("dve", (lambda hp=hp: lambda e: e.tensor_tensor_scan(out=tmpf, data0=startm, data1=Dd[:, hp, :], initial=0.0,
                                                                           op0=ALU.mult, op1=ALU.add))(), [BD, Bcols], [Btmpf])
                if d == 0:
                    k.copy(Dd[:, hp, :], tmpf, [Btmpf], [BD], eng="dve")
                else:
                    t3 = tmpf.rearrange("p (n c) -> p n c", c=64)
                    d3 = Dd[:, hp, :].rearrange("p (n c) -> p n c", c=64)
                    k.tt(Dd[:, hp, :], Dd[:, hp, :], tmpf, ALU.subtract, [BD, Btmpf], [BD])
                    k.tt(d3, d3, bc(t3[:, :, 63:64], [128, T // 64, 64]), ALU.add, [BD, Btmpf], [BD])
            P.barrier()
            if 'dumpD' in (dbg or ()) and d == 1 and l == 0 and not lat:
                ddd = dout("dbg_D", [128, 2 * T])
                k.dma("sp", ddd, Dd.rearrange("p h t -> p (h t)"), [BD], [])
                ddq = dout("dbg_qk", [128, 4 * T], BF16)
                k.dma("sp", ddq[:, 0:2 * T], qn.rearrange("p h t -> p (h t)"), [Bqkv], [])
                k.dma("sp", ddq[:, 2 * T:4 * T], kn.rearrange("p h t -> p (h t)"), [Bqkv], [])
            k.memset(S32, 0.0, [BS32], eng="dve")
            if lat:
                for h in range(4):
                    k.dma("sp", S32[0:64, h, :], st_gdn[l, d, h], [], [BS32])
            k.copy(Sbf[0:64], S32[0:64], [BS32], [BSbf], eng="act")
            lastc = 63 if d == 0 else 0
            cur_blk = None
            for step in range(NCK):
                insts = inst_list(d, step)
                blk = insts[0][7]
                if blk != cur_blk:
                    cur_blk = blk
                    bc_ = slice(blk * TB, (blk + 1) * TB)
                    k.act(Eb, Dd[:, :, bc_], AF.Exp, [BD], [Bblk])
                    k.tt(kb32, kn[:, :, bc_], bet[:, :, bc_], ALU.mult, [Bqkv, Bbet], [Bblk])
                    k.copy(kbT, kb32, [Bblk], [Bblk], eng="act")
                    k.tt(kb32, kb32, Eb, ALU.mult, [Bblk], [Bblk])
                    for h in range(4):
                        pb = (h % 2) * 64
                        k.copy(kbeH[0:64, h, :], kb32[pb:pb + 64, h // 2, :], [Bblk], [Bblk])
                        k.copy(D0[0:64, h, :], Dd[pb:pb + 64, h // 2, bc_], [BD], [Bblk])
                        k.tt(qdH[0:64, h, :], qn[pb:pb + 64, h // 2, bc_], Eb[pb:pb + 64, h // 2, :], ALU.mult, [Bqkv, Bblk], [Bblk])
                    k.tt(vbe, vs[:, :, bc_], bet[:, :, bc_], ALU.mult, [Bqkv, Bbet], [Bblk])
                    d4 = Dd[:, :, bc_].rearrange("p h (n c) -> p h n c", c=64)
                    for hp in range(2):
                        k.tt(ktl[:, hp, :].rearrange("p (n c) -> p n c", c=64), bc(d4[:, hp, :, lastc:lastc + 1], [128, 8, 64]),
                             d4[:, hp], ALU.subtract, [BD], [Bblk])
                    k.act(ktl, ktl, AF.Exp, [Bblk], [Bblk])
                    k.tt(ktl, ktl, kn[:, :, bc_], ALU.mult, [Bblk, Bqkv], [Bblk])
                    e4 = Eb.rearrange("p h (n c) -> p h n c", c=64)
                    for sq_ in range(nseq):
                        for h in range(4):
                            pb = (h % 2) * 64
                            if lat:
                                k.copy(tails[0:64, h, 0:8], e4[pb:pb + 64, h // 2, :, lastc], [Bblk], [Bblk], eng="dve")
                            else:
                                k.copy(tails[0:64, sq_ * 4 + h, sq_ * 4:(sq_ + 1) * 4], e4[pb:pb + 64, h // 2, sq_ * 4:(sq_ + 1) * 4, lastc],
                                       [Bblk], [Bblk], eng="dve")
                W = SB[step % 2]
                NW_ = NI * 64
                v3 = lambda ap: ap[0:64, 0:NW_].rearrange("p (i c) -> p i c", c=64)
                for (ii, sq_, h, hp, pb, gc, bcl, _, cib) in insts:
                    ic = slice(ii * 64, (ii + 1) * 64)
                    k.tr(k.bank(0)[0:64, ic], D0[0:64, h, bcl], ident[0:64, 0:64], [Bblk, Bconst], [PS[0]])
                k.copy(W["Dcol"][0:64, :, 0:1], v3(k.bank(0))[:, :, 0:1], [PS[0]], [W["BDcol"]], eng="act")
                for (ii, sq_, h, hp, pb, gc, bcl, _, cib) in insts:
                    k.stt(W["Gs"][0:64, ii, :], D0[0:64, h, bcl], W["Dcol"][0:64, ii, 0:1], MBS, ALU.subtract, ALU.max,
                          [Bblk, W["BDcol"], Bcols], [W["BGs"]])
                    k.stt(W["GT"][0:64, ii, :], D0[0:64, h, bcl], W["Dcol"][0:64, ii, 0:1], MBT, ALU.subtract, ALU.min,
                          [Bblk, W["BDcol"], Bcols], [W["BGT"]])
                k.act(W["Gs"][0:64], W["Gs"][0:64], AF.Exp, [W["BGs"]], [W["BGs"]], scale=-1.0)
                k.act(W["GT"][0:64], W["GT"][0:64], AF.Exp, [W["BGT"]], [W["BGT"]])
                k.tt(W["GsT"][0:64], W["GT"][0:64], bc(SMT.unsqueeze(1), [64, NI, 64]), ALU.mult, [W["BGT"], Bcols], [W["BGsT"]])
                for (ii, sq_, h, hp, pb, gc, bcl, _, cib) in insts:
                    ic = slice(ii * 64, (ii + 1) * 64)
                    kb_ = kbT[pb:pb + 64, hp, bcl]
                    k_ = kn[pb:pb + 64, hp, gc]
                    q_ = qn[pb:pb + 64, hp, gc]
                    k.mm(k.bank(2)[0:64, ic], kb_, k_, True, True, [Bblk, Bqkv], [PS[2]])
                    k.mm(k.bank(3)[0:64, ic], k_, kb_, True, True, [Bblk, Bqkv], [PS[3]])
                    k.mm(k.bank(4)[0:64, ic], k_, q_, True, True, [Bqkv], [PS[4]])
                k.tt(W["A32"][0:64], v3(k.bank(2)), W["Gs"][0:64], ALU.mult, [PS[2], W["BGs"]], [W["BA32"]])
                k.copy(W["PA"][0:64], W["A32"][0:64], [W["BA32"]], [W["BPA"]], eng="act")
                k.tt(W["PB"][0:64], v3(k.bank(3)), W["GsT"][0:64], ALU.mult, [PS[3], W["BGsT"]], [W["BPB"]])
                k.tt(W["attnT"][0:64], v3(k.bank(4)), W["GT"][0:64], ALU.mult, [PS[4], W["BGT"]], [W["BattnT"]])
                idb = bc(identb[0:64, :].unsqueeze(1), [64, NI, 64])
                k.tt(W["TB"][0:64], idb, W["PB"][0:64], ALU.subtract, [Bcols, W["BPB"]], [W["BTB"]])
                def sq_mm():
                    for ii in range(NI):
                        ic = slice(ii * 64, (ii + 1) * 64)
                        k.mm(k.bank(2)[0:64, ic], W["PB"][0:64, ii, :], W["PA"][0:64, ii, :], True, True, [W["BPA"], W["BPB"]], [PS[2]])
                        k.mm(k.bank(3)[0:64, ic], W["PA"][0:64, ii, :], W["PB"][0:64, ii, :], True, True, [W["BPA"], W["BPB"]], [PS[3]])

                def sq_cp():
                    k.copy(W["PA"][0:64], v3(k.bank(2)), [PS[2]], [W["BPA"]], eng="act")
                    k.copy(W["PB"][0:64], v3(k.bank(3)), [PS[3]], [W["BPB"]], eng="dve")

                def t_mm(itn):
                    for ii in range(NI):
                        ic = slice(ii * 64, (ii + 1) * 64)
                        k.mm(k.bank(5)[0:64, ic], W["PA"][0:64, ii, :], W["TB"][0:64, ii, :], True, True, [W["BPA"], W["BTB"]], [PS[5]])

                def t_add(itn):
                    k.tt(W["TB"][0:64], W["TB"][0:64], v3(k.bank(5)), ALU.add, [W["BTB"], PS[5]], [W["BTB"]])

                NIT = 4
                sq_mm()
                sq_cp()
                for itn in range(NIT):
                    t_mm(itn)
                    if itn < NIT - 1:
                        sq_mm()
                    t_add(itn)
                    if itn < NIT - 1:
                        sq_cp()
                k.copy(W["TB32"][0:64], W["TB"][0:64], [W["BTB"]], [W["BTB32"]], eng="act")
                for ii in range(NI):
                    ic = slice(ii * 64, (ii + 1) * 64)
                    k.tr(k.bank(2)[0:64, ic], W["TB32"][0:64, ii, :], ident[0:64, 0:64], [W["BTB32"], Bconst], [PS[2]])
                    k.mm(k.bank(3)[0:64, ic], W["A32"][0:64, ii, :], W["TB32"][0:64, ii, :], True, True, [W["BA32"], W["BTB32"]], [PS[3]])
                k.copy(W["TBt"][0:64], v3(k.bank(2)), [PS[2]], [W["BTBt"]], eng="act")
                k.tt(W["E32"][0:64], bc(ident2[0:64, :].unsqueeze(1), [64, NI, 64]), W["TB32"][0:64], ALU.subtract, [Bcols, W["BTB32"]], [W["BE32"]])
                k.tt(W["E32"][0:64], W["E32"][0:64], v3(k.bank(3)), ALU.subtract, [W["BE32"], PS[3]], [W["BE32"]])
                for ii in range(NI):
                    ic = slice(ii * 64, (ii + 1) * 64)
                    k.mm(k.bank(5)[0:64, ic], W["TBt"][0:64, ii, :], W["E32"][0:64, ii, :], True, True, [W["BTBt"], W["BE32"]], [PS[5]])
                k.copy(W["TBf"][0:64], v3(k.bank(5)), [PS[5]], [W["BTBf"]], eng="dve")
                for (ii, sq_, h, hp, pb, gc, bcl, _, cib) in insts:
                    ic = slice(ii * 64, (ii + 1) * 64)
                    k.tr(k.bank(7)[0:64, ic], vbe[pb:pb + 64, hp, bcl], ident[pb:pb + 64, pb:pb + 64], [Bblk, Bconst], [PS[7]])
                    k.tr(k.bank(4)[0:64, ic], ktl[pb:pb + 64, hp, bcl], ident[pb:pb + 64, pb:pb + 64], [Bblk, Bconst], [PS[4]])
                k.copy(W["vbt"][0:64], v3(k.bank(7)), [PS[7]], [W["Bvbt"]], eng="act")
                k.copy(W["ktail"][0:64], v3(k.bank(4)), [PS[4]], [W["Bktail"]], eng="dve")
                for (ii, sq_, h, hp, pb, gc, bcl, _, cib) in insts:
                    ic = slice(ii * 64, (ii + 1) * 64)
                    k.mm(k.bank(5)[0:64, ic], kbeH[0:64, h, bcl], Sbf[0:64, ii, :], True, True, [Bblk, BSbf], [PS[5]])
                k.tt(W["rp"][0:64], W["vbt"][0:64], v3(k.bank(5)), ALU.subtract, [W["Bvbt"], PS[5]], [W["Brp"]])
                for ii in range(NI):
                    ic = slice(ii * 64, (ii + 1) * 64)
                    k.mm(k.bank(6)[0:64, ic], W["TBf"][0:64, ii, :], W["rp"][0:64, ii, :], True, True, [W["BTBf"], W["Brp"]], [PS[6]])
                k.copy(W["u"][0:64], v3(k.bank(6)), [PS[6]], [W["Bu"]], eng="act")
                for (ii, sq_, h, hp, pb, gc, bcl, _, cib) in insts:
                    ic = slice(ii * 64, (ii + 1) * 64)
                    k.mm(k.bank(7)[0:64, ic], Sbf[0:64, ii, :], qdH[0:64, h, bcl], True, False, [BSbf, Bblk], [PS[7]])
                    k.mm(k.bank(7)[0:64, ic], W["u"][0:64, ii, :], W["attnT"][0:64, ii, :], False, True, [W["Bu"], W["BattnT"]], [PS[7]])
                    k.mm(k.bank(4)[0:64, ic], W["ktail"][0:64, ii, :], W["u"][0:64, ii, :], True, True, [W["Bktail"], W["Bu"]], [PS[4]])
                for sq_ in range(nseq):
                    for par in range(2):
                        pb = par * 64
                        ii0 = sq_ * 4 + par
                        gc = [r_[5] for r_ in insts if r_[1] == sq_ and r_[2] == par][0]
                        src7 = k.bank(7)[0:64, ii0 * 64:(ii0 + 3) * 64].rearrange("p (i c) -> p i c", c=64)[:, 0:3:2, :]
                        dst = ocT[pb:pb + 64, :, gc]
                        if d == 0:
                            k.copy(dst, src7, [PS[7]], [Boc], eng="act")
                        else:
                            k.tt(dst, dst, src7, ALU.add, [Boc, PS[7]], [Boc])
                cib0 = insts[0][8] if lat else None
                if lat:
                    k.tt(S32[0:64], S32[0:64], bc(tails[0:64, :, cib0:cib0 + 1], [64, NI, 64]), ALU.mult, [BS32, Bblk], [BS32])
                else:
                    for sq_ in range(nseq):
                        cb = [r_[8] for r_ in insts if r_[1] == sq_][0]
                        k.tt(S32[0:64, sq_ * 4:(sq_ + 1) * 4, :], S32[0:64, sq_ * 4:(sq_ + 1) * 4, :],
                             bc(tails[0:64, sq_ * 4:(sq_ + 1) * 4, cb:cb + 1], [64, 4, 64]), ALU.mult, [BS32, Bblk], [BS32])
                k.tt(S32[0:64], S32[0:64], v3(k.bank(4)), ALU.add, [BS32, PS[4]], [BS32])
                k.copy(Sbf[0:64], S32[0:64], [BS32], [BSbf], eng="act")
            if not lat:
                for sq_ in range(nseq):
                    for h in range(4):
                        k.dma("sp", o_gdn[sq_, l, d, h], S32[0:64, sq_ * 4 + h, :], [BS32], [])
        P.barrier()
        A.top = alias_mark
        wGz = A.alloc([8, 256], BF16)
        BwGz = Buf()
        k.dma("sp", wGz, w_in_b[l].rearrange("(c p) n -> p c n", p=128)[:, :, C_GZ:C_GZ + 256], [Bcv[("w_in", l)]], [BwGz])
        gzt = A.alloc([TB], BF16)
        Bgzt = Buf()
        sqb = A.alloc([TB], BF16)
        Bsqb = Buf()
        rsn = A.alloc([TB], F32)
        Brsn = Buf()
        for hp in range(2):
            for tb in range(NBLK):
                cols = slice(tb * TB, (tb + 1) * TB)
                for c in range(8):
                    k.mm(k.bank(3), wGz[:, c, hp * 128:(hp + 1) * 128], hT[:, c, cols], c == 0, c == 7, [BwGz, BhT], [PS[3]])
                k.act(gzt, k.bank(3), AF.Silu, [PS[3]], [Bgzt])
                k.act(sqb, ocT[:, hp, cols], AF.Square, [Boc], [Bsqb])
                k.mm(k.bank(2), blockones, sqb, True, True, [Bsqb, Bcols], [PS[2]])
                k.act(rsn, k.bank(2), AF.Sqrt, [PS[2], Bconst], [Brsn], scale=1.0 / 64, bias=epsb)
                k.recip(rsn, rsn, [Brsn], [Brsn])
                k.stt(rsn, ocT[:, hp, cols], gnorm, rsn, ALU.mult, ALU.mult, [Boc, Brsn, Bcols], [Brsn])
                k.tt(oT[:, 4 + hp, cols], rsn, gzt, ALU.mult, [Brsn, Bgzt], [BoT])
        P.barrier()
        A.top = base

    mixer_base = [0]

    groups = [dict(name="ctx", tok0=0, T=512, nseq=2, L=256, kc=0),
              dict(name="lat", tok0=512, T=2048, nseq=1, L=2048, kc=1)]

    for l in range(nlayers):
        for g in groups:
            T = g["T"]
            kc = g["kc"]
            nblk = T // TB
            mA = A.top
            xblk = [A.alloc([8, TB], F32) for _ in range(2)]
            Bxb = [Buf(), Buf()]
            sq = A.alloc([8, TB], BF16)
            rstd = A.alloc([TB], F32)
            tmp = A.alloc([8, TB], F32)
            Wn = (sq, Buf(), rstd, Buf(), tmp, Buf())
            wg32 = A.alloc([8, 16], F32)
            Bwg32 = Buf()
            k.dma("sp", wg32, w_in[l].rearrange("(c p) n -> p c n", p=128)[:, :, C_GA:C_GA + 16], [], [Bwg32], slow=True)
            for tb in range(nblk):
                gb = (g["tok0"] // TB) + tb
                s = tb % 2
                k.dma("sp", xblk[s], xTv[:, :, gb * TB:(gb + 1) * TB], [BxT[gb]], [Bxb[s]])
                prenorm(xblk[s], Bxb[s], hT, BhT, slice(tb * TB, (tb + 1) * TB), A1, 0, l, kc, Wn, gate=(wg32, Bwg32))
            P.barrier()
            A.top = mA
            if stop == "A":
                P.emit(); ES.close(); return nc, P
            if stub_mixer:
                k.copy(oT[:, :, 0:T], hT[:, :, 0:T], [BhT], [BoT], eng="dve")
            else:
                mB = A.top
                mixer_base[0] = mB
                if "noattn" not in (dbg or ()):
                    mixer_attn(l, g)
                    P.barrier()
                A.top = mB
                if "nohy" not in (dbg or ()):
                    mixer_hyena(l, g)
                if "nogdn" not in (dbg or ()):
                    mixer_gdn(l, g)
                if 'oT' in (dbg or ()) and l == 0:
                    dd = dout("dbg_oT_" + g["name"], [8, 128, T])
                    dtmp = A.alloc([T], F32)
                    Bd = Buf()
                    for c in range(8):
                        k.copy(dtmp, oT[:, c, 0:T], [BoT], [Bd], eng="dve")
                        k.dma("sp", dd[c], dtmp, [Bd], [])
                    P.barrier()
                    A.top = mB
                if stop == "B" + g["name"][0]:
                    P.emit(); ES.close(); return nc, P
            P.barrier()
            mC = A.top
            wout = A.alloc([8, D], BF16)
            Bwout = Buf()
            k.dma("sp", wout, w_out_b[l].rearrange("(c p) n -> p c n", p=128), [Bcv[("w_out", l)]], [Bwout])
            xblk = A.alloc([8, TB], F32)
            Bxb = Buf()
            xn = xblk
            Bxn = Bxb
            mT = A.alloc([8, TB], F32)
            BmT = Buf()
            sq = A.alloc([8, TB], BF16)
            Bsq = Buf()
            rstd = A.alloc([TB], F32)
            Brstd = Buf()
            h2T = A.alloc([8, TB], BF16)
            Bh2 = Buf()
            fT = hT.rearrange("p a b -> p (a b)").rearrange("p (f t) -> p f t", f=32)
            BfT = BhT
            rl = [A.alloc([TB], BF16) for _ in range(2)]
            Brl = [Buf(), Buf()]
            w1s = [A.alloc([8, 512], BF16) for _ in range(2)]
            Bw1 = [Buf(), Buf()]
            w2s = [A.alloc([4, D], BF16) for _ in range(2)]
            Bw2 = [Buf(), Buf()]
            w1v = w1_b[l].rearrange("(c p) n -> p c n", p=128)
            w2v = w2_b[l].rearrange("(c p) n -> p c n", p=128)

            def post(Bsrc_banks, Bres, res, Bsc, l, kc, dstx, Bdstx):
                for c in range(8):
                    k.copy(mT[:, c, :], k.bank(c), [PS[c]], [BmT])
                k.act(sq, mT, AF.Square, [BmT], [Bsq])
                if stop == "P1":
                    raise StopBuild()
                rstd_from_sq(sq, Bsq, 0, rstd, Brstd)
                if stop == "P2":
                    raise StopBuild()
                k.tt(mT, mT, bc(rstd.unsqueeze(1), [128, 8, TB]), ALU.mult, [BmT, Brstd], [BmT])
                if stop == "P3":
                    raise StopBuild()
                for c in range(8):
                    k.stt(dstx[:, c, :], mT[:, c, :], Bsc[:, l, c, kc:kc + 1], res[:, c, :], ALU.mult, ALU.add,
                          [BmT, Bmod, Bres], [Bdstx])

            for tb in range(nblk):
                gb = (g["tok0"] // TB) + tb
                cols = slice(tb * TB, (tb + 1) * TB)
                k.dma("sp", xblk, xTv[:, :, gb * TB:(gb + 1) * TB], [BxT[gb]], [Bxb])
                if stop == "C0a":
                    P.emit(); ES.close(); return nc, P
                for dc in range(8):
                    for ec in range(8):
                        k.mm(k.bank(dc), wout[:, ec, dc * 128:(dc + 1) * 128], oT[:, ec, cols], ec == 0, ec == 7,
                             [Bwout, BoT], [PS[dc]])
                if stop == "C0b":
                    P.emit(); ES.close(); return nc, P
                try:
                    post(None, Bxb, xblk, B1, l, kc, xn, Bxn)
                except StopBuild:
                    P.emit(); ES.close(); return nc, P
                if stop == "C1":
                    P.emit(); ES.close(); return nc, P
                Wn = (sq, Bsq, rstd, Brstd, mT, BmT)
                prenorm(xn, Bxn, h2T, Bh2, slice(0, TB), A2, 24, l, kc, Wn)
                for s8 in range(8):
                    ws = w1s[s8 % 2]
                    bw = Bw1[s8 % 2]
                    k.dma("sp", ws, w1v[:, :, s8 * 512:(s8 + 1) * 512], [Bcv[("w1", l)]], [bw])
                    for fj in range(4):
                        fc = s8 * 4 + fj
                        bk = 1 + (fc % 4)
                        for c in range(8):
                            k.mm(k.bank(bk), ws[:, c, fj * 128:(fj + 1) * 128], h2T[:, c, :], c == 0, c == 7,
                                 [bw, Bh2], [PS[bk]])
                        r = rl[fc % 2]
                        br = Brl[fc % 2]
                        k.act(r, k.bank(bk), AF.Relu, [PS[bk]], [br])
                        k.tt(fT[:, fc, :], r, r, ALU.mult, [br], [BfT])
                if stop == "C2":
                    P.emit(); ES.close(); return nc, P
                for s8 in range(8):
                    ws = w2s[s8 % 2]
                    bw = Bw2[s8 % 2]
                    k.dma("sp", ws, w2v[:, s8 * 4:(s8 + 1) * 4, :], [Bcv[("w2", l)]], [bw])
                    for fj in range(4):
                        fc = s8 * 4 + fj
                        for dc in range(8):
                            k.mm(k.bank(dc), ws[:, fj, dc * 128:(dc + 1) * 128], fT[:, fc, :], fc == 0, fc == 31,
                                 [bw, BfT], [PS[dc]])
                post(None, Bxn, xn, B2, l, kc, xblk, Bxb)
                k.dma("sp", xTv[:, :, gb * TB:(gb + 1) * TB], xblk, [Bxb], [BxT[gb]])
            P.barrier()
            A.top = mC

    xi = [A.alloc([8, 128], F32) for _ in range(2)]
    yo = [A.alloc([D], F32) for _ in range(2)]
    Bxi = [Buf(), Buf()]
    Byo = [Buf(), Buf()]
    for i in range(TT // 128):
        s = i % 2
        k.dma("sp", xi[s], xTv[:, :, i * 128:(i + 1) * 128], [BxT[i // 4]], [Bxi[s]])
        for c in range(8):
            bk = 2 * s + c // 4
            k.tr(k.bank(bk)[:, (c % 4) * 128:(c % 4 + 1) * 128], xi[s][:, c, :], ident, [Bxi[s], Bconst], [PS[bk]])
        for hh in range(2):
            bk = 2 * s + hh
            k.copy(yo[s][:, hh * 512:(hh + 1) * 512], k.bank(bk), [PS[bk]], [Byo[s]])
        k.dma("sp", y_tok[i * 128:(i + 1) * 128, :], yo[s], [Byo[s]], [])
    P.emit()
    ES.close()
    return nc, P


def _rope_tables(dim, nrows_total, row0):
    rows = 2048 // 64
    row = np.repeat(np.arange(rows), 64).astype(np.float32)
    col = np.tile(np.arange(64), rows).astype(np.float32)
    nf = dim // 4
    inv = (10000.0 ** (-np.arange(nf, dtype=np.float32) / nf)).astype(np.float32)
    ang = np.concatenate([row[:, None] * inv, col[:, None] * inv], -1).astype(np.float32)
    cos = np.cos(ang).astype(np.float32)
    sin = np.sin(ang).astype(np.float32)
    out = np.zeros((2, nrows_total, 2048), np.float32)
    for f in range(dim):
        out[0, row0 + f] = cos[:, f // 2]
        out[1, row0 + f] = sin[:, f // 2] * (-1.0 if f % 2 == 0 else 1.0)
    return out


_CONST = {}


def _consts():
    if _CONST:
        return _CONST
    _CONST["ropeA"] = _rope_tables(32, 128, 64)
    _CONST["ropeB"] = _rope_tables(64, 64, 0)
    ps = np.zeros((128, 128), np.float32)
    for m in range(128):
        ps[m ^ 1, m] = 1.0
    _CONST["pswap"] = ps
    j = np.arange(128)[:, None]
    i = np.arange(128)[None, :]
    _CONST["masks"] = np.stack([(j >= i), (j <= i)], 0).astype(np.float32).astype(ml_dtypes.bfloat16)
    r = np.arange(64)[:, None]
    c_ = np.arange(64)[None, :]
    _CONST["negm"] = np.stack([(c_ <= r), (c_ >= r)], 0).astype(np.float32)
    sel = np.zeros((16, 8, 128), np.float32)
    for which in range(2):
        for d_ in range(2):
            for hp in range(2):
                for p in range(128):
                    sel[which * 8 + d_ * 4 + 2 * hp + (p // 64), which * 4 + d_ * 2 + hp, p] = 1.0
    _CONST["gsel"] = sel
    _CONST["smask"] = np.stack([(c_ < r), (c_ > r)], 0).astype(np.float32)
    for L_, sfx in ((256, "s"), (2048, "b")):
        N = 2 * L_
        idx = np.arange(L_ + 1, dtype=np.int64)
        th = 2.0 * np.pi * ((idx[:, None] * idx[None, :]) % N).astype(np.float64) / N
        _CONST["dft_" + sfx] = np.stack([np.cos(th), np.sin(th)], 0).astype(np.float32).astype(ml_dtypes.bfloat16)
        t = np.arange(L_, dtype=np.float32)
        t01 = t / np.float32(max(L_ - 1, 1))
        w = np.float32(2.0 * math.pi) * t / np.float32(L_)
        bands = np.linspace(1e-4, 7, 8, dtype=np.float32)
        feats = np.concatenate([t01[:, None], np.cos(w[:, None] * bands), -np.sin(w[:, None] * bands)], -1).astype(np.float32)
        _CONST["feats_" + sfx] = np.ascontiguousarray(feats.T)
        dist = (np.abs(t - (L_ // 2)) / np.float32(L_ / 2)).astype(np.float32)
        _CONST["negdist_" + sfx] = np.ascontiguousarray((-dist).reshape(L_ // 128, 128).T)
        nf = L_ // 128 + 1
        f = (np.arange(nf)[None, :] * 128 + np.arange(128)[:, None])
        wfn = np.where((f == 0) | (f == L_), 1.0, 2.0) / N
        alpha = np.array([1.0, 0.0, -1.0, 0.0])[f % 4]
        beta = np.array([0.0, 1.0, 0.0, -1.0])[f % 4]
        _CONST["aw_" + sfx] = (alpha * wfn).astype(np.float32)
        _CONST["bw_" + sfx] = (beta * wfn).astype(np.float32)
    return _CONST


def core_inputs(inp, i):
    b = i % 4
    f = lambda a: np.ascontiguousarray(a, dtype=np.float32)
    d = dict(
        x_tok=f(np.concatenate([inp["x_prompt"][2 * i], inp["x_prompt"][2 * i + 1], inp["x_sample"][b]], 0)),
        conds=f(np.stack([inp["c_ctx"], inp["c"][b]], 0)),
        w_ada=f(inp["w_ada"]), b_ada=f(inp["b_ada"]),
        gvec=f(np.stack([inp["g_pre_mix"], inp["g_post_mix"], inp["g_pre_mlp"], inp["g_post_mlp"]], 0)),
        w_in=f(inp["w_in"]), w_out=f(inp["w_out"]), mlp_w1=f(inp["mlp_w1"]), mlp_w2=f(inp["mlp_w2"]),
        mla_kv_norm=f(inp["mla_kv_norm"]), mla_w_ukv=f(inp["mla_w_ukv"]), swa_sink=f(inp["swa_sink"]),
        c_ckv=f(inp["cache_mla_ckv"][b]), c_kpe=f(inp["cache_mla_kpe"][b]),
        c_swk=f(inp["cache_swa_k"][b].reshape(DEPTH, 512, 128)), c_swv=f(inp["cache_swa_v"][b].reshape(DEPTH, 512, 128)),
        gdn_conv=f(inp["gdn_conv"]), gdn_a_log=f(inp["gdn_a_log"].reshape(DEPTH, 8)), gdn_dt_bias=f(inp["gdn_dt_bias"].reshape(DEPTH, 8)),
        gdn_norm=f(inp["gdn_norm"]), st_gdn=f(inp["state_gdn"][b]),
        hy_conv=f(inp["hy_conv"]), hy_w1=f(inp["hy_w1"]), hy_b1=f(inp["hy_b1"]), hy_w2=f(inp["hy_w2"]), hy_b2=f(inp["hy_b2"]),
        hy_w3=f(inp["hy_w3"]), hy_freq=f(inp["hy_freq"]), hy_decay=f(inp["hy_decay"]), hy_bias=f(inp["hy_bias"]),
    )
    d.update(_consts())
    return d


_CACHE = {}


def kernel(**inputs):
    inp = {k_: np.asarray(v) for k_, v in inputs.items()}
    if "nc" not in _CACHE:
        _CACHE["nc"] = build_program()[0]
    nc = _CACHE["nc"]
    in_maps = [core_inputs(inp, i) for i in range(8)]
    res = run_bass_kernel_spmd(nc, in_maps, core_ids=list(range(8)))
    R = res.results
    y_prompt = np.stack([R[i]["y_tok"][s_ * 256:(s_ + 1) * 256] for i in range(8) for s_ in range(2)], 0).astype(np.float32)
    y_sample = np.stack([R[b]["y_tok"][512:2560] for b in range(4)], 0).astype(np.float32)
    cat = lambda nm: np.concatenate([R[i][nm] for i in range(8)], 0).astype(np.float32)
    new_ckv = cat("o_ckv")
    new_kpe = cat("o_kpe")
    new_k = cat("o_swk").reshape(16, DEPTH, 256, 2, 64)
    new_v = cat("o_swv").reshape(16, DEPTH, 256, 2, 64)
    new_st = cat("o_gdn")
    return (y_prompt, y_sample, new_ckv, new_kpe, new_k, new_v, new_st)
```
